# Optimizing a Trainium2 kernel written in Bass

```python
import jax, jax.numpy as jnp
from jax import lax
import numpy as np

D_MODEL = 1024
BATCH = 16
SEQ = 4096
DEPTH = 1

N_META = 16
HEAD_DIM = 64
N_HEADS = D_MODEL // HEAD_DIM
N_KV_HEADS = N_HEADS // 4
GQA_GROUP = N_HEADS // N_KV_HEADS
WINDOW = 128
ATTN_BLOCK = 128
ROPE_THETA = 500000.0
ROPE_DIM = HEAD_DIM // 4
ML_HEADS = 4
ML_V_DIM = D_MODEL // ML_HEADS
ML_QK_DIM = ML_V_DIM // 2
ML_CHUNK = 64
CONV_WIDTH = 4
RMS_EPS = 1e-6
LN_EPS = 1e-6
NEG_BIG = -1e30

ATTN_WIDTH = N_HEADS * HEAD_DIM
KV_WIDTH = N_KV_HEADS * HEAD_DIM
ML_QK_WIDTH = ML_HEADS * ML_QK_DIM
ML_WIDTH = ML_HEADS * ML_V_DIM
SPLIT_SIZES = (ATTN_WIDTH, KV_WIDTH, KV_WIDTH, ATTN_WIDTH,
               2 * ML_QK_WIDTH, ML_WIDTH, ML_HEADS, ML_HEADS, ML_WIDTH, ML_WIDTH,
               D_MODEL, D_MODEL)
IN_WIDTH = sum(SPLIT_SIZES)
SPLIT_POINTS = tuple(int(c) for c in np.cumsum(SPLIT_SIZES)[:-1])

kernel_name = "hybrid_swa_sink_mlstm_gated_merge"


def _rmsnorm(x, g):
    xf = x.astype(jnp.float32)
    y = xf * lax.rsqrt(jnp.mean(xf * xf, axis=-1, keepdims=True) + RMS_EPS)
    return (y * g.astype(jnp.float32)).astype(x.dtype)


def _partial_rope(t, pos):
    half = ROPE_DIM // 2
    inv_freq = ROPE_THETA ** (-jnp.arange(0, ROPE_DIM, 2, dtype=jnp.float32) / ROPE_DIM)
    ang = pos[:, None] * inv_freq[None, :]
    cos = jnp.cos(ang)[None, :, None, :]
    sin = jnp.sin(ang)[None, :, None, :]
    tf = t.astype(jnp.float32)
    t1, t2, rest = tf[..., :half], tf[..., half:ROPE_DIM], tf[..., ROPE_DIM:]
    out = jnp.concatenate([t1 * cos - t2 * sin, t2 * cos + t1 * sin, rest], axis=-1)
    return out.astype(t.dtype)


def _sink_attention(q, k, v, mask, sink):
    s = jnp.einsum('bqhgd,bkhd->bhgqk', q, k).astype(jnp.float32)
    s = jnp.where(mask, s, NEG_BIG)
    sink_b = sink.astype(jnp.float32)[None, :, :, None, None]
    mx = jnp.maximum(jnp.max(s, axis=-1, keepdims=True), sink_b)
    p = jnp.exp(s - mx)
    denom = jnp.sum(p, axis=-1, keepdims=True) + jnp.exp(sink_b - mx)
    p = (p / denom).astype(v.dtype)
    return jnp.einsum('bhgqk,bkhd->bqhgd', p, v)


def _swa_sink_branch(q, k, v, sinks):
    B, L = q.shape[0], q.shape[1]
    S = L - N_META
    NB = S // ATTN_BLOCK
    pos = jnp.arange(L, dtype=jnp.float32)
    q = _partial_rope(q, pos) * (HEAD_DIM ** -0.5)
    k = _partial_rope(k, pos)
    q = q.reshape(B, L, N_KV_HEADS, GQA_GROUP, HEAD_DIM)
    sink = sinks.reshape(N_KV_HEADS, GQA_GROUP)
    qm, km, vm = q[:, :N_META], k[:, :N_META], v[:, :N_META]
    meta_mask = jnp.tril(jnp.ones((N_META, N_META), dtype=bool))
    out_meta = _sink_attention(qm, km, vm, meta_mask, sink)

    def blocks(t):
        return t[:, N_META:].reshape((B, NB, ATTN_BLOCK) + t.shape[2:])

    qb, kb, vb = blocks(q), blocks(k), blocks(v)

    def band_keys(tb, tm):
        prev = jnp.concatenate([jnp.zeros_like(tb[:, :1]), tb[:, :-1]], axis=1)
        meta = jnp.broadcast_to(tm[:, None], (B, NB) + tm.shape[1:])
        return jnp.concatenate([meta, prev, tb], axis=2).swapaxes(0, 1)

    kband, vband = band_keys(kb, km), band_keys(vb, vm)
    qi = jnp.arange(ATTN_BLOCK)[:, None]
    kj = jnp.arange(2 * ATTN_BLOCK)[None, :]
    rel = qi + ATTN_BLOCK - kj
    blk = jnp.arange(NB)[:, None, None]
    local = (rel >= 0) & (rel < WINDOW) & (blk * ATTN_BLOCK + kj - ATTN_BLOCK >= 0)
    mask = jnp.concatenate([jnp.ones((NB, ATTN_BLOCK, N_META), dtype=bool), local], axis=-1)
    out_blocks = lax.map(lambda a: _sink_attention(a[0], a[1], a[2], a[3], sink),
                         (qb.swapaxes(0, 1), kband, vband, mask))
    out_real = out_blocks.swapaxes(0, 1).reshape(B, S, ATTN_WIDTH)
    return jnp.concatenate([out_meta.reshape(B, N_META, ATTN_WIDTH), out_real], axis=1)


def _causal_conv(t, w, b):
    out = lax.conv_general_dilated(
        t, w[:, None, :].astype(t.dtype), window_strides=(1,),
        padding=((CONV_WIDTH - 1, 0),), dimension_numbers=('NWC', 'WIO', 'NWC'),
        feature_group_count=t.shape[-1])
    return out + b.astype(t.dtype)


def _mlstm_cell(q, k, v, i_pre, f_pre):
    B, L = q.shape[0], q.shape[1]
    f32 = jnp.float32
    pad = (-N_META) % ML_CHUNK
    q = q.astype(f32)
    k = k.astype(f32) * (ML_QK_DIM ** -0.5)
    v = v.astype(f32)
    li = i_pre.astype(f32)
    lf = jax.nn.log_sigmoid(f_pre.astype(f32))

    def padt(t, val):
        return jnp.pad(t, ((0, 0), (pad, 0)) + ((0, 0),) * (t.ndim - 2), constant_values=val)

    q, k, v, lf = padt(q, 0.0), padt(k, 0.0), padt(v, 0.0), padt(lf, 0.0)
    li = padt(li, NEG_BIG)
    NC = (L + pad) // ML_CHUNK

    def chunks(t):
        t = t.reshape((B, NC, ML_CHUNK) + t.shape[2:])
        return t.transpose((1, 0, 3, 2) + tuple(range(4, t.ndim)))

    xs = (chunks(q), chunks(k), chunks(v), chunks(li), chunks(lf))
    causal = jnp.tril(jnp.ones((ML_CHUNK, ML_CHUNK), dtype=bool))

    def body(carry, inp):
        C, n, m = carry
        qc, kc, vc, lic, lfc = inp
        bcum = jnp.cumsum(lfc, axis=-1)
        Dm = jnp.where(causal, bcum[..., :, None] - bcum[..., None, :] + lic[..., None, :], NEG_BIG)
        inter = bcum + m[..., None]
        m_row = jnp.maximum(inter, jnp.max(Dm, axis=-1))
        w_inter = jnp.exp(inter - m_row)
        sc = jnp.einsum('bhtd,bhsd->bhts', qc, kc) * jnp.exp(Dm - m_row[..., None])
        num = w_inter[..., None] * jnp.einsum('bhtd,bhvd->bhtv', qc, C) + jnp.einsum('bhts,bhsv->bhtv', sc, vc)
        den = w_inter * jnp.einsum('bhtd,bhd->bht', qc, n) + jnp.sum(sc, axis=-1)
        h = num / jnp.maximum(jnp.abs(den), jnp.exp(-m_row))[..., None]
        b_last = bcum[..., -1]
        a = b_last[..., None] - bcum + lic
        m_new = jnp.maximum(b_last + m, jnp.max(a, axis=-1))
        w_c = jnp.exp(b_last + m - m_new)
        w_a = jnp.exp(a - m_new[..., None])
        C = w_c[..., None, None] * C + jnp.einsum('bhs,bhsv,bhsd->bhvd', w_a, vc, kc)
        n = w_c[..., None] * n + jnp.einsum('bhs,bhsd->bhd', w_a, kc)
        return (C, n, m_new), h

    init = (jnp.zeros((B, ML_HEADS, ML_V_DIM, ML_QK_DIM), f32),
            jnp.zeros((B, ML_HEADS, ML_QK_DIM), f32),
            jnp.zeros((B, ML_HEADS), f32))
    _, hs = lax.scan(body, init, xs)
    hs = hs.transpose(1, 0, 3, 2, 4).reshape(B, NC * ML_CHUNK, ML_HEADS, ML_V_DIM)
    return hs[:, pad:]


def _head_layernorm(h, g):
    mu = jnp.mean(h, axis=-1, keepdims=True)
    hc = h - mu
    y = hc * lax.rsqrt(jnp.mean(hc * hc, axis=-1, keepdims=True) + LN_EPS)
    y = y * g.astype(jnp.float32).reshape(ML_HEADS, ML_V_DIM)
    return y.reshape(h.shape[0], h.shape[1], ML_WIDTH)


def _hybrid_mixer(u, w_in, attn_sinks, conv_w, conv_b, gate_bias, head_norm,
                  w_attn_out, w_mlstm_out, w_out):
    B, L, _ = u.shape
    proj = u @ w_in.astype(u.dtype)
    (a_q, a_k, a_v, a_z, m_qk, m_v, m_i, m_f, m_o, m_z, g_a, g_m) = jnp.split(proj, SPLIT_POINTS, axis=-1)
    att = _swa_sink_branch(a_q.reshape(B, L, N_HEADS, HEAD_DIM),
                           a_k.reshape(B, L, N_KV_HEADS, HEAD_DIM),
                           a_v.reshape(B, L, N_KV_HEADS, HEAD_DIM), attn_sinks)
    y_a = (att * jax.nn.silu(a_z)) @ w_attn_out.astype(u.dtype)
    m_qk = jax.nn.silu(_causal_conv(m_qk, conv_w, conv_b))
    m_q, m_k = jnp.split(m_qk, 2, axis=-1)
    cell = _mlstm_cell(m_q.reshape(B, L, ML_HEADS, ML_QK_DIM),
                       m_k.reshape(B, L, ML_HEADS, ML_QK_DIM),
                       m_v.reshape(B, L, ML_HEADS, ML_V_DIM),
                       m_i + gate_bias[:ML_HEADS].astype(u.dtype),
                       m_f + gate_bias[ML_HEADS:].astype(u.dtype))
    o_gate = jax.nn.sigmoid(m_o.astype(jnp.float32)).reshape(B, L, ML_HEADS, ML_V_DIM)
    hm = _head_layernorm(o_gate * cell, head_norm).astype(u.dtype)
    y_m = (hm * jax.nn.silu(m_z)) @ w_mlstm_out.astype(u.dtype)
    merged = jax.nn.sigmoid(g_a) * y_a + jax.nn.sigmoid(g_m) * y_m
    return merged @ w_out.astype(u.dtype)


def setup_inputs(seed: int = 0) -> dict:
    key = jax.random.key(seed)
    ks = jax.random.split(key, 16)
    nrm = jax.random.normal
    x = nrm(ks[0], (BATCH, SEQ, D_MODEL), jnp.float32)
    meta_tokens = nrm(ks[1], (N_META, D_MODEL), jnp.float32)
    norm_pre = 1.0 + 0.01 * nrm(ks[2], (DEPTH, D_MODEL), jnp.float32)
    w_in = nrm(ks[3], (DEPTH, D_MODEL, IN_WIDTH), jnp.float32) * D_MODEL ** -0.5
    attn_sinks = 0.5 * nrm(ks[4], (DEPTH, N_HEADS), jnp.float32)
    conv_w = nrm(ks[5], (DEPTH, CONV_WIDTH, 2 * ML_QK_WIDTH), jnp.float32) * CONV_WIDTH ** -0.5
    conv_b = 0.01 * nrm(ks[6], (DEPTH, 2 * ML_QK_WIDTH), jnp.float32)
    i_bias = 0.1 * nrm(ks[7], (DEPTH, ML_HEADS), jnp.float32)
    f_bias = 3.0 + 3.0 * jax.random.uniform(ks[8], (DEPTH, ML_HEADS), jnp.float32)
    mlstm_gate_bias = jnp.concatenate([i_bias, f_bias], axis=-1)
    mlstm_head_norm = 1.0 + 0.01 * nrm(ks[9], (DEPTH, ML_WIDTH), jnp.float32)
    w_attn_out = nrm(ks[10], (DEPTH, ATTN_WIDTH, D_MODEL), jnp.float32) * ATTN_WIDTH ** -0.5
    w_mlstm_out = nrm(ks[11], (DEPTH, ML_WIDTH, D_MODEL), jnp.float32) * ML_WIDTH ** -0.5
    w_out = nrm(ks[12], (DEPTH, D_MODEL, D_MODEL), jnp.float32) * D_MODEL ** -0.5
    norm_post = 1.0 + 0.01 * nrm(ks[13], (DEPTH, D_MODEL), jnp.float32)
    return {"x": x, "meta_tokens": meta_tokens, "norm_pre": norm_pre, "w_in": w_in,
            "attn_sinks": attn_sinks, "conv_w": conv_w, "conv_b": conv_b,
            "mlstm_gate_bias": mlstm_gate_bias, "mlstm_head_norm": mlstm_head_norm,
            "w_attn_out": w_attn_out, "w_mlstm_out": w_mlstm_out, "w_out": w_out,
            "norm_post": norm_post}


def reference(x, meta_tokens, norm_pre, w_in, attn_sinks, conv_w, conv_b, mlstm_gate_bias,
              mlstm_head_norm, w_attn_out, w_mlstm_out, w_out, norm_post):
    B = x.shape[0]
    meta = jnp.broadcast_to(meta_tokens.astype(x.dtype)[None], (B,) + meta_tokens.shape)
    h = jnp.concatenate([meta, x], axis=1)
    for layer in range(DEPTH):
        u = _rmsnorm(h, norm_pre[layer])
        y = _hybrid_mixer(u, w_in[layer], attn_sinks[layer], conv_w[layer], conv_b[layer],
                          mlstm_gate_bias[layer], mlstm_head_norm[layer],
                          w_attn_out[layer], w_mlstm_out[layer], w_out[layer])
        h = h + _rmsnorm(y, norm_post[layer])
    return h[:, N_META:]
```

```python
import contextlib
import math
import numpy as np
import concourse.bass as bass
import concourse.mybir as mybir
from concourse.bass_utils import run_bass_kernel_spmd

F32 = mybir.dt.float32
BF16 = mybir.dt.bfloat16
ALU = mybir.AluOpType
AF = mybir.ActivationFunctionType

ENGS = ("pe", "act", "dve", "pool", "sp")


class Res:
    __slots__ = ("name", "w", "readers")

    def __init__(self, name):
        self.name = name
        self.w = None
        self.readers = []


class Op:
    __slots__ = ("eng", "fn", "semkey", "inc", "deps", "signal", "idx", "is_dma")

    def __init__(self, eng, fn, semkey, inc, is_dma):
        self.eng = eng
        self.fn = fn
        self.semkey = semkey
        self.inc = inc
        self.deps = []
        self.signal = False
        self.idx = 0
        self.is_dma = is_dma


class Prog:
    def __init__(self, nc):
        self.nc = nc
        self.streams = {e: [] for e in ENGS}
        self.by_sem = {}

    def res(self, name):
        return Res(name)

    def op(self, eng, fn, reads=(), writes=(), dma=None):
        is_dma = dma is not None
        semkey = ("dma", dma) if is_dma else eng
        o = Op(eng, fn, semkey, 16 if is_dma else 1, is_dma)
        if is_dma:
            o.signal = True
        deps = {}

        def need(p, kind):
            if p is None:
                return
            if (not p.is_dma) and (not is_dma) and p.eng == eng:
                if eng == "pe":
                    return
            deps[id(p)] = p

        for r in reads:
            need(r.w, "raw")
        for w in writes:
            need(w.w, "waw")
            for t in w.readers:
                need(t, "war")
        o.deps = list(deps.values())
        for p in o.deps:
            p.signal = True
        for w in writes:
            w.w = o
            w.readers = []
        for r in reads:
            if not is_dma:
                r.readers = [t for t in r.readers if t.is_dma or t.eng != eng]
            r.readers.append(o)
        self.streams[eng].append(o)
        self.by_sem.setdefault(semkey, []).append(o)
        return o

    def emit(self):
        nc = self.nc
        for k, ops in self.by_sem.items():
            c = 0
            for o in ops:
                if o.signal:
                    c += o.inc
                o.idx = c
        with contextlib.ExitStack() as st:
            sems = {}
            for k in self.by_sem:
                nm = "s_" + (k if isinstance(k, str) else "d_" + str(k[1]))
                sems[k] = st.enter_context(nc.semaphore(nm))
            block = st.enter_context(nc.Block())
            prog = self

            def run(eng_name, handle):
                waited = {}
                for o in prog.streams[eng_name]:
                    need = {}
                    for p in o.deps:
                        if p.idx > need.get(p.semkey, 0):
                            need[p.semkey] = p.idx
                    for k, v in need.items():
                        if waited.get(k, 0) >= v:
                            continue
                        handle.wait_ge(sems[k], v)
                        waited[k] = v
                    ins = o.fn(handle)
                    if o.signal:
                        ins.then_inc(sems[o.semkey], o.inc)

            @block.tensor
            def _(e):
                run("pe", e)

            @block.scalar
            def _(e):
                run("act", e)

            @block.vector
            def _(e):
                run("dve", e)

            @block.gpsimd
            def _(e):
                run("pool", e)

            @block.sync
            def _(e):
                run("sp", e)


C_AQ, C_AK, C_AV, C_AZ = 0, 1024, 1280, 1536
C_MQK, C_MV, C_MI, C_MF, C_MO, C_MZ, C_GA, C_GM = 2560, 3584, 4608, 4612, 4616, 5640, 6664, 7688
IN_W = 8712
T = 512
LOGK = -0.5 * math.log(128.0)


def build_nc(NSEQ, NST):
    nc = bass.Bass("TRN2", target_bir_lowering=False)
    S = NST * T

    def din(name, shape, dt=F32):
        return nc.dram_tensor(name, shape, dt, kind="ExternalInput").ap()

    x = din("x", [NSEQ, S, 1024])
    meta = din("meta_tokens", [16, 1024])
    norm_pre = din("norm_pre", [1024])
    w_in = din("w_in", [1024, IN_W])
    sinks = din("attn_sinks", [16])
    conv_w = din("conv_w", [4, 1024])
    conv_b = din("conv_b", [1024])
    gate_bias = din("mlstm_gate_bias", [8])
    head_norm = din("mlstm_head_norm", [1024])
    w_ao = din("w_attn_out", [1024, 1024])
    w_mo = din("w_mlstm_out", [1024, 1024])
    w_out = din("w_out", [1024, 1024])
    norm_post = din("norm_post", [1024])
    c_ident = din("c_ident", [128, 128])
    c_triu = din("c_triu", [128, 128])
    c_pit = din("c_pit", [128, 128])
    c_cos = din("c_cos", [NST, 128, T])
    c_sin = din("c_sin", [NST, 128, T])
    c_cosm = din("c_cosm", [128, 16])
    c_sinm = din("c_sinm", [128, 16])
    out = nc.dram_tensor("out", [NSEQ, S, 1024], F32, kind="ExternalOutput").ap()

    def dscr(name, shape):
        return nc.dram_tensor(name, shape, BF16, kind="Internal").ap()

    NWB = 24
    wbt = dscr("wbt", [NWB, 128, 8, 512])
    WB = {"K": 0, "V": 1, "Q": 2, "Z": 4, "GA": 6, "MQK": 8, "MV": 10, "MO": 12, "MZ": 14, "GM": 16, "AO": 18, "MOUT": 20, "OUT": 22}

    P = Prog(nc)
    with contextlib.ExitStack() as st:
        def sb(name, shape, dt=F32):
            return st.enter_context(nc.sbuf_tensor(name, shape, dt))

        NB = 7
        banks = [st.enter_context(nc.psum_tensor("pb%d" % i, [128, 512], F32)) for i in range(NB)]
        bank_res = [P.res("pb%d" % i) for i in range(NB)]
        pst = st.enter_context(nc.psum_tensor("pst", [128, 8, 128], BF16))
        r_pst = P.res("pst")
        bctr = [0]

        def bank():
            i = bctr[0] % NB
            bctr[0] += 1
            return banks[i], bank_res[i]

        ident_f = sb("ident_f", [128, 128]); r_identf = P.res("identf")
        ident_b = sb("ident_b", [128, 128], BF16); r_ident = P.res("ident")
        triu_f = sb("triu_f", [128, 128]); r_triu = P.res("triu")
        pit_f = sb("pit_f", [128, 128]); r_pitf = P.res("pitf")
        pit_b = sb("pit_b", [128, 128], BF16); r_pit = P.res("pit")
        ones_f = sb("ones_f", [128, 128]); r_onesf = P.res("onesf")
        ones_b = sb("ones_b", [128, 128], BF16); r_onesb = P.res("onesb")
        mask_b = sb("mask_b", [128, 4, 128], BF16); r_mask = P.res("mask")
        maskp_b = sb("maskp_b", [128, 4, 128], BF16); r_maskp = P.res("maskp")
        gB = sb("gB", [128, 1024]); r_gB = P.res("gB")
        hnB = sb("hnB", [128, 1024]); r_hnB = P.res("hnB")
        npB = sb("npB", [128, 1024]); r_npB = P.res("npB")
        gbias = sb("gbias", [128, 8]); r_gbias = P.res("gbias")
        cw = sb("cw", [128, 8, 4]); r_cw = P.res("cw")
        cb = sb("cb", [128, 8]); r_cb = P.res("cb")
        sk = sb("sk", [33, 16]); r_sk = P.res("sk")
        wgate = sb("wgate", [128, 8, 8], BF16); r_wgate = P.res("wgate")
        cosT = sb("cosT", [128, T]); r_cos = P.res("cos")
        sinT = sb("sinT", [128, T]); r_sin = P.res("sin")
        cosm = sb("cosm", [128, 16]); r_cosm = P.res("cosm")
        sinm = sb("sinm", [128, 16]); r_sinm = P.res("sinm")

        NSLOT = 3
        wring = [sb("wring%d" % i, [128, 8, 512], BF16) for i in range(NSLOT)]
        r_wring = [P.res("wring%d" % i) for i in range(NSLOT)]
        wctr = [0]

        xs = [sb("xs%d" % j, [128, 1024]) for j in range(4)]
        r_xs = [P.res("xs%d" % j) for j in range(4)]
        xbs = [sb("xb%d" % i, [128, 1024], BF16) for i in range(2)]
        r_xbs = [P.res("xb%d" % i) for i in range(2)]
        mhalf = sb("mhalf", [128, 8]); r_mhalf = P.res("mhalf")
        stt = [sb("stt%d" % j, [128, 4]) for j in range(4)]
        r_stt = [P.res("stt%d" % j) for j in range(4)]
        uT = sb("uT", [128, 8, T], BF16); r_uT = P.res("uT")
        bufA = sb("bufA", [128, 8, T], BF16); r_bufA = P.res("bufA")
        bufB = sb("bufB", [128, 8, T], BF16); r_bufB = P.res("bufB")
        bufC = sb("bufC", [128, 8, T], BF16); r_bufC = P.res("bufC")
        hzT = sb("hzT", [128, 8, T], BF16); r_hzT = P.res("hzT")
        mrg = sb("mrg", [128, 8, T], BF16); r_mrg = P.res("mrg")
        Kbuf = sb("Kbuf", [128, 4, 640], BF16); r_Kbuf = P.res("Kbuf")
        KmT = sb("KmT", [128, 4, 16], BF16); r_KmT = P.res("KmT")
        Vdup = [sb("Vaug%d" % i, [128, 4, 66], BF16) for i in range(5)]
        r_Vdup = [P.res("Vaug%d" % i) for i in range(5)]
        Vmeta = sb("Vmeta", [33, 4, 66], BF16); r_Vmeta = P.res("Vmeta")
        att_tok = [sb("att_tok%d" % i, [128, 1024], BF16) for i in range(2)]
        r_att_tok = [P.res("att_tok%d" % i) for i in range(2)]
        rec4 = [sb("rec4_%d" % i, [128, 4]) for i in range(2)]
        r_rec4 = [P.res("rec4_%d" % i) for i in range(2)]
        ptc = [sb("ptc%d" % i, [128, 512], BF16) for i in range(2)]
        r_ptc = [P.res("ptc%d" % i) for i in range(2)]
        ptp = [sb("ptp%d" % i, [128, 512], BF16) for i in range(2)]
        r_ptp = [P.res("ptp%d" % i) for i in range(2)]
        ptm = [sb("ptm%d" % g, [33, 512], BF16) for g in range(4)]
        r_ptm = [P.res("ptm%d" % g) for g in range(4)]
        raw = [sb("raw%d" % i, [128, 512], BF16) for i in range(2)]
        r_raw = [P.res("raw%d" % i) for i in range(2)]
        t1 = [sb("t1_%d" % i, [128, 512]) for i in range(2)]
        r_t1 = [P.res("t1_%d" % i) for i in range(2)]
        t2 = [sb("t2_%d" % i, [128, 512]) for i in range(2)]
        r_t2 = [P.res("t2_%d" % i) for i in range(2)]
        tzt = [sb("tzt%d" % i, [128, 512], BF16) for i in range(2)]
        r_tzt = [P.res("tzt%d" % i) for i in range(2)]
        zst = [sb("zst%d" % i, [128, 512], BF16) for i in range(2)]
        r_zst = [P.res("zst%d" % i) for i in range(2)]
        pre = [sb("pre%d" % i, [128, 515]) for i in range(2)]
        r_pre = [P.res("pre%d" % i) for i in range(2)]
        acc = [sb("acc%d" % i, [128, 512]) for i in range(2)]
        r_acc = [P.res("acc%d" % i) for i in range(2)]
        halo = sb("halo", [128, 8, 3]); r_halo = P.res("halo")
        halo_m = sb("halo_m", [128, 8, 3]); r_halom = P.res("halom")
        gsb = sb("gsb", [128, 4, 8]); r_gsb = P.res("gsb")
        e1 = sb("e1", [128, 4, 4]); r_e1 = P.res("e1")
        nlf = sb("nlf", [128, 16]); r_nlf = P.res("nlf")
        thr = sb("thr", [128, 16]); r_thr = P.res("thr")
        gs = sb("gs", [128, 16]); r_gs = P.res("gs")
        vsc = sb("vsc", [128, 16]); r_vsc = P.res("vsc")
        eB = sb("eB", [128, 16]); r_eB = P.res("eB")
        vaug = [sb("vaug%d" % j, [128, 4, 272], BF16) for j in range(4)]
        r_vaug = [P.res("vaug%d" % j) for j in range(4)]
        ktok = sb("ktok", [128, 4, 128], BF16); r_ktok = P.res("ktok")
        PTm = sb("PTm", [128, 512], BF16); r_PTm = P.res("PTm")
        C32 = sb("C32", [128, 4, 257]); r_C32 = [P.res("C32_%d" % h) for h in range(4)]
        Cbf = sb("Cbf", [128, 4, 272], BF16); r_Cbf = [P.res("Cbf_%d" % h) for h in range(4)]
        Cm32 = sb("Cm32", [128, 4, 257]); r_Cm32 = P.res("Cm32")
        Ue = [sb("Ue%d" % i, [128, 257]) for i in range(2)]
        r_Ue = [P.res("Ue%d" % i) for i in range(2)]
        dd = sb("dd", [128, 4, 4]); r_dd = [P.res("dd%d" % h) for h in range(4)]
        tmpN = sb("tmpN", [128, 1024]); r_tmpN = P.res("tmpN")
        ho = sb("ho", [128, 1024]); r_ho = P.res("ho")
        st6 = sb("st6", [128, 4, 6]); r_st6 = P.res("st6")
        mv = sb("mv", [128, 4, 2]); r_mv = P.res("mv")
        lnv = sb("lnv", [128, 8]); r_lnv = P.res("lnv")
        hzs = [sb("hz%d" % i, [128, 1024], BF16) for i in range(2)]
        r_hzs = [P.res("hz%d" % i) for i in range(2)]
        hz, r_hz = hzs[0], r_hzs[0]
        ot = [sb("ot%d" % i, [128, 1024]) for i in range(2)]
        r_ot = [P.res("ot%d" % i) for i in range(2)]
        ss = [sb("ss%d" % i, [128, 8]) for i in range(2)]
        r_ss = [P.res("ss%d" % i) for i in range(2)]
        r_outs = []

        def new_out():
            r = P.res("out%d" % len(r_outs))
            r_outs.append(r)
            return r

        ctr = {"raw": 0, "tz": 0, "pre": 0, "pt": 0, "ot": 0, "xb": 0, "hz": 0, "rec4": 0, "ue": 0}

        def rr(key, n=2):
            i = ctr[key] % n
            ctr[key] += 1
            return i

        A = P.op

        r_castb = {}

        def cast_blk(bi, src, c0, n):
            r = P.res("cast%d" % bi)
            A("pool", lambda e: e.dma_start(out=wbt[bi][:, :, 0:n], in_=src[:, c0:c0 + n].rearrange("(k p) n -> p k n", p=128)),
              writes=[r], dma="cast%d" % bi)
            r_castb[bi] = [r]

        rk = []
        for g in range(4):
            for d in range(2):
                r = P.res("castK%d%d" % (g, d))
                A("pool", lambda e, g=g, d=d: e.dma_start(out=wbt[0][:, :, g * 128 + d * 64: g * 128 + d * 64 + 64],
                                                      in_=w_in[:, C_AK + g * 64: C_AK + g * 64 + 64].rearrange("(k p) n -> p k n", p=128)),
                  writes=[r], dma="castK")
                rk.append(r)
        r_castb[0] = rk
        cast_blk(1, w_in, C_AV, 256)
        for nm, c0 in (("Q", C_AQ), ("Z", C_AZ), ("GA", C_GA)):
            for i in range(2):
                cast_blk(WB[nm] + i, w_in, c0 + i * 512, 512)
        for i in range(2):
            cast_blk(WB["AO"] + i, w_ao, i * 512, 512)
        r_wgate_c = P.res("wgate_c")
        for nm, c0 in (("MQK", C_MQK), ("MV", C_MV), ("MO", C_MO), ("MZ", C_MZ), ("GM", C_GM)):
            for i in range(2):
                cast_blk(WB[nm] + i, w_in, c0 + i * 512, 512)
        for i in range(2):
            cast_blk(WB["MOUT"] + i, w_mo, i * 512, 512)
        for i in range(2):
            cast_blk(WB["OUT"] + i, w_out, i * 512, 512)

        def ld(dst, src, r, name, **kw):
            A("sp", lambda e: e.dma_start(out=dst, in_=src, **kw), writes=[r], dma=name)

        ld(ident_f[:], c_ident, r_identf, "identf")
        ld(triu_f[:], c_triu, r_triu, "triu")
        ld(pit_f[:], c_pit, r_pitf, "pitf")
        ld(gB[:], norm_pre.partition_broadcast(128), r_gB, "gB")
        ld(hnB[:], head_norm.partition_broadcast(128), r_hnB, "hnB")
        ld(npB[:], norm_post.partition_broadcast(128), r_npB, "npB")
        ld(gbias[:], gate_bias.partition_broadcast(128), r_gbias, "gbias")
        for jj in range(4):
            ld(cw[:, :, jj], conv_w[jj].rearrange("(b p) -> p b", p=128), r_cw, "cw", allow_slow_non_contiguous=True)
        ld(cb[:], conv_b.rearrange("(b p) -> p b", p=128), r_cb, "cb", allow_slow_non_contiguous=True)
        ld(sk[32:33, :], sinks.rearrange("(o n) -> o n", o=1), r_sk, "sk")
        ld(cosm[:], c_cosm, r_cosm, "cosm")
        ld(sinm[:], c_sinm, r_sinm, "sinm")
        A("pool", lambda e: e.dma_start(out=wgate[:], in_=w_in[:, C_MI:C_MI + 8].rearrange("(k p) n -> p k n", p=128),
                                        allow_slow_non_contiguous=True), writes=[r_wgate], dma="wgate")
        A("dve", lambda e: e.tensor_copy(out=ident_b[:], in_=ident_f[:]), [r_identf], [r_ident])
        A("dve", lambda e: e.tensor_copy(out=pit_b[:], in_=pit_f[:]), [r_pitf], [r_pit])
        A("dve", lambda e: e.memset(ones_f[:], 1.0), [], [r_onesf])
        A("dve", lambda e: e.memset(ones_b[:], 1.0), [], [r_onesb])
        A("dve", lambda e: e.tensor_copy(out=mask_b[:], in_=triu_f[:].unsqueeze(1).to_broadcast([128, 4, 128])), [r_triu], [r_mask])
        A("dve", lambda e: e.tensor_scalar(out=maskp_b[:], in0=triu_f[:].unsqueeze(1).to_broadcast([128, 4, 128]),
                                           scalar1=-1.0, scalar2=1.0, op0=ALU.mult, op1=ALU.add), [r_triu], [r_maskp])
        A("dve", lambda e: e.tensor_scalar(out=cw[:], in0=cw[:], scalar1=0.5, scalar2=None, op0=ALU.mult), [r_cw], [r_cw])
        A("dve", lambda e: e.tensor_scalar(out=cb[:], in0=cb[:], scalar1=0.5, scalar2=None, op0=ALU.mult), [r_cb], [r_cb])
        A("pool", lambda e: e.memset(Vmeta[:], 0.0), [], [r_Vmeta])
        A("pool", lambda e: e.memset(Vmeta[0:16, :, 64:65], 1.0), [], [r_Vmeta])
        A("pool", lambda e: e.memset(Vmeta[32:33, :, 64:65], 1.0), [], [r_Vmeta])
        for i in range(5):
            A("pool", lambda e, i=i: e.memset(Vdup[i][:, :, 64:65], 1.0), [], [r_Vdup[i]])
        A("pool", lambda e: e.memset(mhalf[:], -0.5), [], [r_mhalf])
        A("dve", lambda e: e.tensor_scalar(out=gB[:], in0=gB[:], scalar1=32.0, scalar2=None, op0=ALU.mult), [r_gB], [r_gB])
        A("dve", lambda e: e.tensor_scalar(out=npB[:], in0=npB[:], scalar1=32.0, scalar2=None, op0=ALU.mult), [r_npB], [r_npB])
        for g in range(4):
            A("pool", lambda e, g=g: e.memset(ptm[g][:], 0.0), [], [r_ptm[g]])
            A("act", lambda e, g=g: e.activation(out=ptm[g][32:33, :].rearrange("p (h q) -> p h q", h=4),
                                                 in_=sk[32:33, 4 * g:4 * g + 4].unsqueeze(2).to_broadcast([1, 4, 128]),
                                                 func=AF.Exp), [r_sk, r_ptm[g]], [r_ptm[g]])
        for j in range(4):
            A("pool", lambda e, j=j: e.memset(vaug[j][:], 0.0), [], [r_vaug[j]])
        A("pool", lambda e: e.memset(Cbf[:], 0.0), [], r_Cbf)
        A("pool", lambda e: e.memset(halo_m[:], 0.0), [], [r_halom])

        def load_w(bi, n=512):
            i = wctr[0] % NSLOT
            wctr[0] += 1
            A("sp", lambda e: e.dma_start(out=wring[i][:, :, 0:n], in_=wbt[bi][:, :, 0:n]),
              reads=r_castb[bi], writes=[r_wring[i]], dma="w%d" % i)
            return wring[i], r_wring[i]

        def proj_fm(w, c0, rhs, r_rhs, N, n0=0):
            wt, r_w = w
            b, rb = bank()
            for kc in range(8):
                A("pe", lambda e, kc=kc: e.matmul(b[:, 0:N], lhsT=wt[:, kc, c0:c0 + 128], rhs=rhs[:, kc, n0:n0 + N],
                                                 start=(kc == 0), stop=(kc == 7)), [r_w, r_rhs], [rb])
            return b, rb

        def proj_tm(w, n, lhs, r_lhs, t0, nt):
            wt, r_w = w
            b, rb = bank()
            for kc in range(8):
                A("pe", lambda e, kc=kc: e.matmul(b[0:nt, 0:n], lhsT=lhs[:, kc, t0:t0 + nt], rhs=wt[:, kc, 0:n],
                                                 start=(kc == 0), stop=(kc == 7)), [r_w, r_lhs], [rb])
            return b, rb

        pending = []

        def flush():
            while pending:
                pending.pop(0)()

        def rope(b, rb, N, cos_ap, sin_ap, r_tabs, scale, dst, r_dst):
            i = rr("raw")
            A("act", lambda e: e.activation(out=raw[i][:, 0:N], in_=b[:, 0:N], func=AF.Copy, scale=scale), [rb], [r_raw[i]])

            def stage_b():
                b2, rb2 = bank()
                A("pe", lambda e: e.matmul(b2[:, 0:N], lhsT=pit_b[:], rhs=raw[i][:, 0:N], start=True, stop=True), [r_pit, r_raw[i]], [rb2])
                A("dve", lambda e: e.tensor_tensor(out=t1[i][:, 0:N], in0=raw[i][:, 0:N], in1=cos_ap, op=ALU.mult), [r_raw[i]] + r_tabs, [r_t1[i]])
                A("dve", lambda e: e.tensor_tensor(out=t2[i][:, 0:N], in0=b2[:, 0:N], in1=sin_ap, op=ALU.mult), [rb2] + r_tabs, [r_t2[i]])
                A("pool", lambda e: e.tensor_tensor(out=dst, in0=t1[i][:, 0:N], in1=t2[i][:, 0:N], op=ALU.add), [r_t1[i], r_t2[i]], [r_dst])
            pending.append(stage_b)

        def rmsnorm_front(src, r_src, n, stat, r_stat):
            xi = rr("xb")
            xb_, r_xb_ = xbs[xi], r_xbs[xi]
            A("act", lambda e: e.activation(out=xb_[0:n, :], in_=src, func=AF.Square, accum_out=stat[0:n, 0:1]), [r_src], [r_xb_, r_stat])
            A("pool", lambda e: e.tensor_scalar(out=stat[0:n, 1:2], in0=stat[0:n, 0:1], scalar1=1024 * 1e-6, scalar2=None, op0=ALU.add), [r_stat], [r_stat])
            A("pool", lambda e: e.tensor_tensor(out=stat[0:n, 2:3], in0=stat[0:n, 1:2], in1=mhalf[0:n, 0:1], op=ALU.pow), [r_stat, r_mhalf], [r_stat])
            A("dve", lambda e: e.scalar_tensor_tensor(out=xb_[0:n, :], in0=src, scalar=stat[0:n, 2:3], in1=gB[0:n, :],
                                                      op0=ALU.mult, op1=ALU.mult), [r_src, r_stat, r_gB], [r_xb_])
            return xb_, r_xb_

        def rmsnorm_back(xbr, n, dstT, r_dstT, t0):
            xb_, r_xb_ = xbr
            for k in range(8):
                A("pe", lambda e, k=k: e.transpose(out=pst[:, k, 0:n], in_=xb_[0:n, k * 128:(k + 1) * 128], identity=ident_b[0:n, 0:n]),
                  [r_xb_, r_ident], [r_pst])
            A("act", lambda e: e.activation(out=dstT[:, :, t0:t0 + n], in_=pst[:, :, 0:n], func=AF.Copy), [r_pst], [r_dstT])

        def rmsnorm_T(src, r_src, n, stat, r_stat, dstT, r_dstT, t0):
            rmsnorm_back(rmsnorm_front(src, r_src, n, stat, r_stat), n, dstT, r_dstT, t0)

        def conv_silu(b, rb, blk, N, halo_src, r_halo_src, halo_dst, r_halo_dst, dst, r_dst):
            i = rr("pre")
            p_ = pre[i]
            A("act", lambda e: e.activation(out=p_[:, 3:3 + N], in_=b[:, 0:N], func=AF.Copy), [rb], [r_pre[i]])
            A("pool", lambda e: e.tensor_copy(out=p_[:, 0:3], in_=halo_src[:, blk, :]), [r_halo_src], [r_pre[i]])
            a_ = acc[i]
            A("dve", lambda e: e.tensor_scalar(out=a_[:, 0:N], in0=p_[:, 3:3 + N], scalar1=cw[:, blk, 3:4], scalar2=cb[:, blk:blk + 1],
                                               op0=ALU.mult, op1=ALU.add), [r_pre[i], r_cw, r_cb], [r_acc[i]])
            for jj in (2, 1, 0):
                A("dve", lambda e, jj=jj: e.scalar_tensor_tensor(out=a_[:, 0:N], in0=p_[:, jj:jj + N], scalar=cw[:, blk, jj:jj + 1],
                                                                 in1=a_[:, 0:N], op0=ALU.mult, op1=ALU.add), [r_pre[i], r_cw, r_acc[i]], [r_acc[i]])
            A("pool", lambda e: e.tensor_copy(out=halo_dst[:, blk, :], in_=p_[:, N:N + 3]), [r_pre[i]], [r_halo_dst])
            k = rr("tz")
            A("act", lambda e: e.activation(out=tzt[k][:, 0:N], in_=a_[:, 0:N], func=AF.Tanh), [r_acc[i]], [r_tzt[k]])
            A("dve", lambda e: e.scalar_tensor_tensor(out=dst, in0=tzt[k][:, 0:N], scalar=1.0, in1=a_[:, 0:N], op0=ALU.add, op1=ALU.mult),
              [r_tzt[k], r_acc[i]], [r_dst])

        xm = ot[0]
        A("sp", lambda e: e.dma_start(out=xm[0:16, :], in_=meta), writes=[r_ot[0]], dma="xm")
        uTm = sb("uTm", [128, 8, 16], BF16); r_uTm = P.res("uTm")
        qkm = sb("qkm", [128, 8, 16], BF16); r_qkm = P.res("qkm")
        rmsnorm_T(xm[0:16, :], r_ot[0], 16, stt[0], r_stt[0], uTm, r_uTm, 0)
        wk = load_w(WB["K"])
        for g in range(4):
            b, rb = proj_fm(wk, g * 128, uTm, r_uTm, 16)
            flush()
            rope(b, rb, 16, cosm[:], sinm[:], [r_cosm, r_sinm], 1.0, KmT[:, g, :], r_KmT)
        flush()
        wv = load_w(WB["V"], 256)
        b, rb = proj_tm(wv, 256, uTm, r_uTm, 0, 16)
        A("act", lambda e, b=b: e.activation(out=Vmeta[0:16, :, 0:64], in_=b[0:16, 0:256].rearrange("p (g d) -> p g d", g=4), func=AF.Copy), [rb], [r_Vmeta])
        bgm_, rbgm = bank()
        for kc in range(8):
            A("pe", lambda e, kc=kc: e.matmul(bgm_[0:16, 0:8], lhsT=uTm[:, kc, 0:16], rhs=wgate[:, kc, :], start=(kc == 0), stop=(kc == 7)),
              [r_uTm, r_wgate], [rbgm])
        A("dve", lambda e: e.tensor_tensor(out=gsb[0:16, 0, :], in0=bgm_[0:16, 0:8], in1=gbias[0:16, :], op=ALU.add), [rbgm, r_gbias], [r_gsb])
        A("act", lambda e: e.activation(out=e1[0:16, 0, :], in_=gsb[0:16, 0, 4:8], func=AF.Exp, scale=-1.0), [r_gsb], [r_e1])
        A("act", lambda e: e.activation(out=nlf[0:16, 0:4], in_=e1[0:16, 0, :], func=AF.Ln, bias=1.0), [r_e1], [r_nlf])
        bnbm, rbnbm = bank()
        A("pe", lambda e: e.matmul(bnbm[0:16, 0:4], lhsT=triu_f[0:16, 0:16], rhs=nlf[0:16, 0:4], start=True, stop=True), [r_triu, r_nlf], [rbnbm])
        bnsm, rbnsm = bank()
        A("pe", lambda e: e.matmul(bnsm[:, 0:4], lhsT=ones_f[0:16, :], rhs=nlf[0:16, 0:4], start=True, stop=True), [r_onesf, r_nlf], [rbnsm])
        A("dve", lambda e: e.tensor_tensor(out=gs[0:16, 0:4], in0=gsb[0:16, 0, 0:4], in1=bnbm[0:16, 0:4], op=ALU.add), [r_gsb, rbnbm], [r_gs])
        A("act", lambda e: e.activation(out=vsc[0:16, 0:4], in_=gs[0:16, 0:4], func=AF.Exp, bias=LOGK), [r_gs], [r_vsc])
        A("act", lambda e: e.activation(out=eB[:, 0:4], in_=bnsm[:, 0:4], func=AF.Exp, scale=-1.0), [rbnsm], [r_eB])
        for i in range(2):
            w = load_w(WB["MQK"] + i)
            for c in range(4):
                blk = 4 * i + c
                b, rb = proj_fm(w, c * 128, uTm, r_uTm, 16)
                conv_silu(b, rb, blk, 16, halo_m, r_halom, halo_m, r_halom, qkm[:, blk, :], r_qkm)
        vaugm = vaug[0]
        for i in range(2):
            w = load_w(WB["MV"] + i)
            b, rb = proj_tm(w, 512, uTm, r_uTm, 0, 16)
            for hh in range(2):
                h = 2 * i + hh
                A("act", lambda e, b=b, hh=hh, h=h: e.activation(out=vaugm[0:16, h, 0:256], in_=b[0:16, hh * 256:(hh + 1) * 256], func=AF.Copy,
                                                              scale=vsc[0:16, h:h + 1]), [rb, r_vsc], [r_vaug[0]])
        A("pool", lambda e: e.tensor_copy(out=vaugm[0:16, :, 256:257], in_=vsc[0:16, 0:4].unsqueeze(2)), [r_vsc], [r_vaug[0]])
        for h in range(4):
            A("pe", lambda e, h=h: e.transpose(out=pst[0:16, h, :], in_=qkm[:, 4 + h, :], identity=ident_b[:]), [r_qkm, r_ident], [r_pst])
        A("act", lambda e: e.activation(out=ktok[0:16, :, :], in_=pst[0:16, 0:4, :], func=AF.Copy), [r_pst], [r_ktok])
        for h in range(4):
            bU, rbU = bank()
            A("pe", lambda e, h=h, bU=bU: e.matmul(bU[:, 0:257], lhsT=ktok[0:16, h, :], rhs=vaugm[0:16, h, 0:257], start=True, stop=True),
              [r_ktok, r_vaug[0]], [rbU])
            A("dve", lambda e, h=h, bU=bU: e.tensor_scalar(out=Cm32[:, h, :], in0=bU[:, 0:257], scalar1=eB[:, h:h + 1], scalar2=None, op0=ALU.mult),
              [rbU, r_eB], [r_Cm32])
        A("pool", lambda e: e.memset(vaug[0][:], 0.0), [], [r_vaug[0]])

        def phase0_load(s_, ti_, j):
            t0_ = ti_ * T + j * 128
            A("sp", lambda e: e.dma_start(out=xs[j][:], in_=x[s_, t0_: t0_ + 128, :]), writes=[r_xs[j]], dma="xs%d" % j)

        def phase0_tabs(ti_):
            A("sp", lambda e: e.dma_start(out=cosT[:], in_=c_cos[ti_]), writes=[r_cos], dma="cos")
            A("sp", lambda e: e.dma_start(out=sinT[:], in_=c_sin[ti_]), writes=[r_sin], dma="sin")

        def phase0_norm(j):
            rmsnorm_T(xs[j][:], r_xs[j], 128, stt[j], r_stt[j], uT, r_uT, j * 128)

        def phase0_front(j):
            return rmsnorm_front(xs[j][:], r_xs[j], 128, stt[j], r_stt[j])

        def phase0_back(j, xbr):
            rmsnorm_back(xbr, 128, uT, r_uT, j * 128)

        for s in range(NSEQ):
            A("pool", lambda e: e.tensor_copy(out=C32[:], in_=Cm32[:]), [r_Cm32], r_C32)
            A("pool", lambda e: e.tensor_copy(out=Cbf[:, :, 0:257], in_=Cm32[:]), [r_Cm32], r_Cbf)
            A("pool", lambda e: e.tensor_copy(out=halo[:], in_=halo_m[:]), [r_halom], [r_halo])
            for ti in range(NST):
                tok0 = ti * T
                if s == 0 and ti == 0:
                    phase0_tabs(ti)
                    for j in range(4):
                        phase0_load(s, ti, j)
                    for j in range(4):
                        phase0_norm(j)
                nxt = (s, ti + 1) if ti + 1 < NST else ((s + 1, 0) if s + 1 < NSEQ else None)
                wk = load_w(WB["K"])
                for g in range(4):
                    b, rb = proj_fm(wk, g * 128, uT, r_uT, T)
                    flush()
                    rope(b, rb, T, cosT[:], sinT[:], [r_cos, r_sin], 1.0, Kbuf[:, g, 128:640], r_Kbuf)
                wv = load_w(WB["V"], 256)
                for j in range(4):
                    b, rb = proj_tm(wv, 256, uT, r_uT, j * 128, 128)
                    A("act", lambda e, b=b, j=j: e.activation(out=Vdup[j + 1][:, :, 0:64], in_=b[:, 0:256].rearrange("p (g d) -> p g d", g=4), func=AF.Copy),
                      [rb], [r_Vdup[j + 1]])
                QT, r_QT = bufA, r_bufA
                attT, r_attT = bufB, r_bufB
                for i in range(2):
                    w = load_w(WB["Q"] + i)
                    for c in range(4):
                        blk = 4 * i + c
                        b, rb = proj_fm(w, c * 128, uT, r_uT, T)
                        flush()
                        rope(b, rb, T, cosT[:], sinT[:], [r_cos, r_sin], 0.125, QT[:, blk, :], r_QT)
                flush()
                def att_S(j, g):
                    has_prev = not (ti == 0 and j == 0)
                    jc = slice(j * 128, (j + 1) * 128)
                    pi = rr("pt")
                    bSc, rSc = bank()
                    bSm, rSm = bank()
                    bSp, rSp = bank() if has_prev else (None, None)
                    for hh in range(4):
                        blk = 2 * g + hh // 2
                        rows = slice((hh % 2) * 64, (hh % 2) * 64 + 64)
                        hc = slice(hh * 128, (hh + 1) * 128)
                        A("pe", lambda e, rows=rows, hc=hc, blk=blk: e.matmul(
                            bSc[:, hc], lhsT=Kbuf[rows, g, 128 + j * 128: 256 + j * 128], rhs=QT[rows, blk, jc], start=True, stop=True),
                          [r_Kbuf, r_QT], [rSc])
                        A("pe", lambda e, rows=rows, hc=hc, blk=blk: e.matmul(
                            bSm[0:16, hc], lhsT=KmT[rows, g, :], rhs=QT[rows, blk, jc], start=True, stop=True),
                          [r_KmT, r_QT], [rSm])
                        if has_prev:
                            A("pe", lambda e, rows=rows, hc=hc, blk=blk: e.matmul(
                                bSp[:, hc], lhsT=Kbuf[rows, g, j * 128: 128 + j * 128], rhs=QT[rows, blk, jc], start=True, stop=True),
                              [r_Kbuf, r_QT], [rSp])
                    A("act", lambda e: e.activation(out=ptc[pi][:], in_=bSc[:], func=AF.Exp), [rSc], [r_ptc[pi]])
                    A("act", lambda e: e.activation(out=ptm[g][0:16, :], in_=bSm[0:16, :], func=AF.Exp), [rSm], [r_ptm[g]])
                    A("pool", lambda e: e.tensor_tensor(out=ptc[pi][:], in0=ptc[pi][:], in1=mask_b[:].rearrange("p h q -> p (h q)"), op=ALU.mult),
                      [r_ptc[pi], r_mask], [r_ptc[pi]])
                    if has_prev:
                        A("act", lambda e: e.activation(out=ptp[pi][:], in_=bSp[:], func=AF.Exp), [rSp], [r_ptp[pi]])
                        A("pool", lambda e: e.tensor_tensor(out=ptp[pi][:], in0=ptp[pi][:], in1=maskp_b[:].rearrange("p h q -> p (h q)"), op=ALU.mult),
                          [r_ptp[pi], r_maskp], [r_ptp[pi]])
                    return (j, g, pi, has_prev, jc)

                def att_PV(ctx):
                    j, g, pi, has_prev, jc = ctx
                    ai = j % 2
                    bO, rO = bank()
                    for hh in range(4):
                        hc = slice(hh * 128, (hh + 1) * 128)
                        oc = slice(hh * 65, hh * 65 + 65)
                        A("pe", lambda e, hc=hc, oc=oc: e.matmul(bO[:, oc], lhsT=ptm[g][:, hc], rhs=Vmeta[:, g, 0:65], start=True, stop=False),
                          [r_Vmeta, r_ptm[g]], [rO])
                        if has_prev:
                            A("pe", lambda e, hc=hc, oc=oc: e.matmul(bO[:, oc], lhsT=ptp[pi][:, hc], rhs=Vdup[j][:, g, 0:65], start=False, stop=False),
                              [r_Vdup[j], r_ptp[pi]], [rO])
                        A("pe", lambda e, hc=hc, oc=oc: e.matmul(bO[:, oc], lhsT=ptc[pi][:, hc], rhs=Vdup[j + 1][:, g, 0:65], start=False, stop=True),
                          [r_Vdup[j + 1], r_ptc[pi]], [rO])
                    ri = rr("rec4")
                    bOv = bO[:, 0:260].rearrange("p (h c) -> p h c", c=65)
                    A("dve", lambda e: e.reciprocal(out=rec4[ri][:], in_=bOv[:, :, 64]), [rO], [r_rec4[ri]])
                    A("dve", lambda e: e.tensor_tensor(out=att_tok[ai][:, g * 256:(g + 1) * 256].rearrange("p (h d) -> p h d", h=4), in0=bOv[:, :, 0:64],
                                                       in1=rec4[ri][:].unsqueeze(2).to_broadcast([128, 4, 64]), op=ALU.mult),
                      [rO, r_rec4[ri]], [r_att_tok[ai]])
                    if g == 3:
                        for k in range(8):
                            A("pe", lambda e, k=k: e.transpose(out=pst[:, k, :], in_=att_tok[ai][:, k * 128:(k + 1) * 128], identity=ident_b[:]),
                              [r_att_tok[ai], r_ident], [r_pst])
                        A("act", lambda e: e.activation(out=attT[:, :, jc], in_=pst[:], func=AF.Copy), [r_pst], [r_attT])

                order = [(j, g) for j in range(4) for g in range(4)]
                ctxs = [att_S(*order[0])]
                for idx in range(len(order)):
                    if idx + 1 < len(order):
                        ctxs.append(att_S(*order[idx + 1]))
                    att_PV(ctxs[idx])
                for i in range(2):
                    w = load_w(WB["Z"] + i)
                    for c in range(4):
                        blk = 4 * i + c
                        b, rb = proj_fm(w, c * 128, uT, r_uT, T)
                        k = rr("tz")
                        A("act", lambda e, b=b, k=k: e.activation(out=tzt[k][:], in_=b[:], func=AF.Tanh, scale=0.5), [rb], [r_tzt[k]])
                        A("dve", lambda e, b=b, k=k: e.scalar_tensor_tensor(out=zst[k][:], in0=tzt[k][:], scalar=1.0, in1=b[:], op0=ALU.add, op1=ALU.mult),
                          [r_tzt[k], rb], [r_zst[k]])
                        A("pool", lambda e, k=k, blk=blk: e.tensor_tensor(out=attT[:, blk, :], in0=attT[:, blk, :], in1=zst[k][:], op=ALU.mult),
                          [r_attT, r_zst[k]], [r_attT])
                for i in range(2):
                    wg = load_w(WB["GA"] + i)
                    wa = load_w(WB["AO"] + i)
                    for c in range(4):
                        blk = 4 * i + c
                        b, rb = proj_fm(wg, c * 128, uT, r_uT, T)
                        k = rr("tz")
                        A("act", lambda e, b=b, k=k: e.activation(out=tzt[k][:], in_=b[:], func=AF.Tanh, scale=0.5), [rb], [r_tzt[k]])
                        by, rby = proj_fm(wa, c * 128, attT, r_attT, T)
                        A("dve", lambda e, by=by, k=k, blk=blk: e.scalar_tensor_tensor(out=mrg[:, blk, :], in0=tzt[k][:], scalar=1.0, in1=by[:],
                                                                                    op0=ALU.add, op1=ALU.mult), [r_tzt[k], rby], [r_mrg])
                if nxt is not None:
                    phase0_tabs(nxt[1])
                    for j in range(4):
                        phase0_load(nxt[0], nxt[1], j)
                bg, rbg = bank()
                for j in range(4):
                    for kc in range(8):
                        A("pe", lambda e, kc=kc, j=j, bg=bg: e.matmul(bg[:, j * 8:(j + 1) * 8], lhsT=uT[:, kc, j * 128:(j + 1) * 128], rhs=wgate[:, kc, :],
                                                                 start=(kc == 0), stop=(kc == 7)), [r_uT, r_wgate], [rbg])
                A("dve", lambda e, bg=bg: e.tensor_tensor(out=gsb[:], in0=bg[:, 0:32].rearrange("p (j c) -> p j c", j=4),
                                                        in1=gbias[:].unsqueeze(1).to_broadcast([128, 4, 8]), op=ALU.add), [rbg, r_gbias], [r_gsb])
                A("act", lambda e: e.activation(out=e1[:], in_=gsb[:, :, 4:8], func=AF.Exp, scale=-1.0), [r_gsb], [r_e1])
                A("act", lambda e: e.activation(out=nlf[:], in_=e1[:].rearrange("p j h -> p (j h)"), func=AF.Ln, bias=1.0), [r_e1], [r_nlf])
                bnb, rbnb = bank()
                A("pe", lambda e, bnb=bnb: e.matmul(bnb[:, 0:16], lhsT=triu_f[:], rhs=nlf[:], start=True, stop=True), [r_triu, r_nlf], [rbnb])
                bns, rbns = bank()
                A("pe", lambda e, bns=bns: e.matmul(bns[:, 0:16], lhsT=ones_f[:], rhs=nlf[:], start=True, stop=True), [r_onesf, r_nlf], [rbns])
                A("act", lambda e, bnb=bnb: e.activation(out=thr[:], in_=bnb[:, 0:16], func=AF.Exp), [rbnb], [r_thr])
                A("dve", lambda e, bnb=bnb: e.tensor_tensor(out=gs[:].rearrange("p (j h) -> p j h", j=4), in0=gsb[:, :, 0:4],
                                                          in1=bnb[:, 0:16].rearrange("p (j h) -> p j h", j=4), op=ALU.add), [r_gsb, rbnb], [r_gs])
                A("act", lambda e: e.activation(out=vsc[:], in_=gs[:], func=AF.Exp, bias=LOGK), [r_gs], [r_vsc])
                A("act", lambda e, bns=bns: e.activation(out=eB[:], in_=bns[:, 0:16], func=AF.Exp, scale=-1.0), [rbns], [r_eB])
                qkT, r_qkT = bufC, r_bufC
                for i in range(2):
                    w = load_w(WB["MQK"] + i)
                    for c in range(4):
                        blk = 4 * i + c
                        b, rb = proj_fm(w, c * 128, uT, r_uT, T)
                        conv_silu(b, rb, blk, T, halo, r_halo, halo, r_halo, qkT[:, blk, :], r_qkT)
                for i in range(2):
                    w = load_w(WB["MV"] + i)
                    for j in range(4):
                        b, rb = proj_tm(w, 512, uT, r_uT, j * 128, 128)
                        for hh in range(2):
                            h = 2 * i + hh
                            A("act", lambda e, b=b, hh=hh, h=h, j=j: e.activation(out=vaug[j][:, h, 0:256], in_=b[:, hh * 256:(hh + 1) * 256], func=AF.Copy,
                                                                             scale=vsc[:, j * 4 + h: j * 4 + h + 1]), [rb, r_vsc], [r_vaug[j]])
                for j in range(4):
                    A("pool", lambda e, j=j: e.tensor_copy(out=vaug[j][:, :, 256:257], in_=vsc[:, j * 4:(j + 1) * 4].unsqueeze(2)), [r_vsc], [r_vaug[j]])
                th = bufB[:].rearrange("p k t -> p (k t)").rearrange("p (j f) -> p j f", j=4)
                r_th = r_bufB
                zs = bufA[:].rearrange("p k t -> p (k t)").rearrange("p (j f) -> p j f", j=4)
                r_zs = r_bufA
                for i in range(2):
                    w = load_w(WB["MO"] + i)
                    for j in range(4):
                        b, rb = proj_tm(w, 512, uT, r_uT, j * 128, 128)
                        A("act", lambda e, b=b, j=j, i=i: e.activation(out=th[:, j, i * 512:(i + 1) * 512], in_=b[:], func=AF.Tanh, scale=0.5), [rb], [r_th])
                for i in range(2):
                    w = load_w(WB["MZ"] + i)
                    for j in range(4):
                        b, rb = proj_tm(w, 512, uT, r_uT, j * 128, 128)
                        k = rr("tz")
                        A("act", lambda e, b=b, k=k: e.activation(out=tzt[k][:], in_=b[:], func=AF.Tanh, scale=0.5), [rb], [r_tzt[k]])
                        A("dve", lambda e, b=b, k=k: e.scalar_tensor_tensor(out=zst[k][:], in0=tzt[k][:], scalar=1.0, in1=b[:],
                                                                          op0=ALU.add, op1=ALU.mult), [r_tzt[k], rb], [r_zst[k]])
                        A("pool", lambda e, k=k, j=j, i=i: e.tensor_tensor(out=zs[:, j, i * 512:(i + 1) * 512], in0=zst[k][:], in1=hnB[:, i * 512:(i + 1) * 512], op=ALU.mult),
                          [r_zst[k], r_hnB], [r_zs])
                def rec_pe(j):
                    jc = slice(j * 128, (j + 1) * 128)
                    for h in range(4):
                        A("pe", lambda e, h=h: e.transpose(out=pst[:, h, :], in_=qkT[:, 4 + h, jc], identity=ident_b[:]), [r_qkT, r_ident], [r_pst])
                    A("act", lambda e: e.activation(out=ktok[:], in_=pst[:, 0:4, :], func=AF.Copy), [r_pst], [r_ktok])
                    bS, rS = bank()
                    for h in range(4):
                        A("pe", lambda e, h=h: e.matmul(bS[:, h * 128:(h + 1) * 128], lhsT=qkT[:, 4 + h, jc], rhs=qkT[:, h, jc], start=True, stop=True),
                          [r_qkT], [rS])
                    A("dve", lambda e: e.tensor_tensor(out=PTm[:], in0=bS[:], in1=mask_b[:].rearrange("p h q -> p (h q)"), op=ALU.mult), [rS, r_mask], [r_PTm])
                    bNs = [bank(), bank()]
                    bDn, rDn = bank()
                    for h in range(4):
                        bN, rN = bNs[h // 2]
                        nc_ = slice((h % 2) * 256, (h % 2) * 256 + 256)
                        A("pe", lambda e, h=h, bN=bN, nc_=nc_: e.matmul(bN[:, nc_], lhsT=PTm[:, h * 128:(h + 1) * 128], rhs=vaug[j][:, h, 0:256], start=True, stop=False),
                          [r_PTm, r_vaug[j]], [rN])
                        A("pe", lambda e, h=h, bN=bN, nc_=nc_: e.matmul(bN[:, nc_], lhsT=qkT[:, h, jc], rhs=Cbf[:, h, 0:256], start=False, stop=True),
                          [r_qkT, r_Cbf[h]], [rN])
                        A("pe", lambda e, h=h: e.matmul(bDn[:, h:h + 1], lhsT=PTm[:, h * 128:(h + 1) * 128], rhs=vaug[j][:, h, 256:257], start=True, stop=False),
                          [r_PTm, r_vaug[j]], [rDn])
                        A("pe", lambda e, h=h: e.matmul(bDn[:, h:h + 1], lhsT=qkT[:, h, jc], rhs=Cbf[:, h, 256:257], start=False, stop=True),
                          [r_qkT, r_Cbf[h]], [rDn])
                    bUs = []
                    for h in range(4):
                        bU, rbU = bank()
                        A("pe", lambda e, h=h, bU=bU: e.matmul(bU[:, 0:257], lhsT=ktok[:, h, :], rhs=vaug[j][:, h, 0:257], start=True, stop=True),
                          [r_ktok, r_vaug[j]], [rbU])
                        bUs.append((bU, rbU))
                    return (j, jc, bNs, bDn, rDn, bUs)

                def rec_state(ctx):
                    j, jc, bNs, bDn, rDn, bUs = ctx
                    for h in range(4):
                        bU, rbU = bUs[h]
                        col = j * 4 + h
                        A("dve", lambda e, h=h, col=col: e.tensor_scalar(out=C32[:, h, :], in0=C32[:, h, :], scalar1=eB[:, col:col + 1], scalar2=None, op0=ALU.mult),
                          [r_C32[h], r_eB], [r_C32[h]])
                        A("dve", lambda e, h=h, col=col, bU=bU: e.scalar_tensor_tensor(out=C32[:, h, :], in0=bU[:, 0:257], scalar=eB[:, col:col + 1], in1=C32[:, h, :],
                                                                                    op0=ALU.mult, op1=ALU.add), [rbU, r_eB, r_C32[h]], [r_C32[h]])
                        A("act", lambda e, h=h: e.activation(out=Cbf[:, h, 0:257], in_=C32[:, h, :], func=AF.Copy), [r_C32[h]], [r_Cbf[h]])

                def rec_rest(ctx):
                    j, jc, bNs, bDn, rDn, bUs = ctx
                    c4 = slice(j * 4, (j + 1) * 4)
                    A("dve", lambda e: e.tensor_scalar(out=dd[:, 0, :], in0=bDn[:, 0:4], scalar1=-1.0, scalar2=None, op0=ALU.mult), [rDn], [r_dd[0]])
                    A("dve", lambda e: e.tensor_tensor(out=dd[:, 1, :], in0=bDn[:, 0:4], in1=dd[:, 0, :], op=ALU.max), [rDn, r_dd[0]], [r_dd[0]])
                    A("dve", lambda e: e.tensor_tensor(out=dd[:, 2, :], in0=dd[:, 1, :], in1=thr[:, c4], op=ALU.max), [r_dd[0], r_thr], [r_dd[0]])
                    A("dve", lambda e: e.reciprocal(out=dd[:, 3, :], in_=dd[:, 2, :]), [r_dd[0]], [r_dd[0]])
                    for h in range(4):
                        bN, rN = bNs[h // 2]
                        nc_ = slice((h % 2) * 256, (h % 2) * 256 + 256)
                        A("act", lambda e, h=h, bN=bN, nc_=nc_: e.activation(out=tmpN[:, h * 256:(h + 1) * 256], in_=bN[:, nc_], func=AF.Copy, scale=dd[:, 3, h:h + 1]),
                          [rN, r_dd[0]], [r_tmpN])
                    A("dve", lambda e: e.scalar_tensor_tensor(out=ho[:], in0=th[:, j, :], scalar=1.0, in1=tmpN[:], op0=ALU.add, op1=ALU.mult),
                      [r_th, r_tmpN], [r_ho])
                    for h in range(4):
                        A("dve", lambda e, h=h: e.bn_stats(out=st6[:, h, :], in_=ho[:, h * 256:(h + 1) * 256]), [r_ho], [r_st6])
                    for h in range(4):
                        A("dve", lambda e, h=h: e.bn_aggr(out=mv[:, h, :], in_=st6[:, h, :]), [r_st6], [r_mv])
                    A("pool", lambda e: e.tensor_scalar(out=lnv[:, 0:4], in0=mv[:, :, 1], scalar1=4e-6, scalar2=None, op0=ALU.add), [r_mv], [r_lnv])
                    A("pool", lambda e: e.tensor_tensor(out=lnv[:, 4:8], in0=lnv[:, 0:4], in1=mhalf[:, 0:4], op=ALU.pow), [r_lnv, r_mhalf], [r_lnv])
                    for h in range(4):
                        A("dve", lambda e, h=h: e.tensor_scalar(out=ho[:, h * 256:(h + 1) * 256], in0=ho[:, h * 256:(h + 1) * 256], scalar1=mv[:, h, 0:1],
                                                              scalar2=lnv[:, 4 + h:5 + h], op0=ALU.subtract, op1=ALU.mult), [r_ho, r_mv, r_lnv], [r_ho])
                    hi = rr("hz")
                    A("pool", lambda e: e.tensor_tensor(out=hzs[hi][:], in0=ho[:], in1=zs[:, j, :], op=ALU.mult), [r_ho, r_zs], [r_hzs[hi]])
                    return (jc, hi)

                def rec_tail(t):
                    jc, hi = t
                    for k in range(8):
                        A("pe", lambda e, k=k: e.transpose(out=pst[:, k, :], in_=hzs[hi][:, k * 128:(k + 1) * 128], identity=ident_b[:]), [r_hzs[hi], r_ident], [r_pst])
                    A("act", lambda e: e.activation(out=hzT[:, :, jc], in_=pst[:], func=AF.Copy), [r_pst], [r_hzT])

                tail = None
                for j in range(4):
                    ctx = rec_pe(j)
                    bctr[0] += 3
                    rec_state(ctx)
                    if tail is not None:
                        rec_tail(tail)
                    tail = rec_rest(ctx)
                rec_tail(tail)
                mT, r_mT = bufC, r_bufC
                for i in range(2):
                    wg = load_w(WB["GM"] + i)
                    wm = load_w(WB["MOUT"] + i)
                    for c in range(4):
                        blk = 4 * i + c
                        b, rb = proj_fm(wg, c * 128, uT, r_uT, T)
                        k = rr("tz")
                        A("act", lambda e, b=b, k=k: e.activation(out=tzt[k][:], in_=b[:], func=AF.Tanh, scale=0.5), [rb], [r_tzt[k]])
                        by, rby = proj_fm(wm, c * 128, hzT, r_hzT, T)
                        A("dve", lambda e, by=by, k=k: e.scalar_tensor_tensor(out=zst[k][:], in0=tzt[k][:], scalar=1.0, in1=by[:], op0=ALU.add, op1=ALU.mult),
                          [r_tzt[k], rby], [r_zst[k]])
                        A("pool", lambda e, k=k, blk=blk: e.tensor_tensor(out=mT[:, blk, :], in0=zst[k][:], in1=mrg[:, blk, :], op=ALU.add),
                          [r_zst[k], r_mrg], [r_mT])
                w0 = load_w(WB["OUT"])
                w1 = load_w(WB["OUT"] + 1)
                xr = [(tmpN, r_tmpN), (ho, r_ho)]

                def reload(j):
                    xt_, rx_ = xr[j % 2]
                    t0_ = tok0 + j * 128
                    A("sp", lambda e, s=s: e.dma_start(out=xt_[:], in_=x[s, t0_: t0_ + 128, :]), writes=[rx_], dma="xr%d" % (j % 2))

                reload(0)
                reload(1)
                fronts = {}
                if nxt is not None:
                    fronts[0] = phase0_front(0)
                    fronts[1] = phase0_front(1)
                for j in range(4):
                    oi = rr("ot")
                    b0, rb0 = proj_tm(w0, 512, mT, r_mT, j * 128, 128)
                    b1, rb1 = proj_tm(w1, 512, mT, r_mT, j * 128, 128)
                    A("act", lambda e, b0=b0, oi=oi: e.activation(out=hz[:, 0:512], in_=b0[:], func=AF.Square, accum_out=ss[oi][:, 0:1]), [rb0], [r_hz, r_ss[oi]])
                    A("act", lambda e, b1=b1, oi=oi: e.activation(out=hz[:, 512:1024], in_=b1[:], func=AF.Square, accum_out=ss[oi][:, 1:2]), [rb1], [r_hz, r_ss[oi]])
                    A("dve", lambda e, oi=oi: e.tensor_tensor(out=ss[oi][:, 2:3], in0=ss[oi][:, 0:1], in1=ss[oi][:, 1:2], op=ALU.add), [r_ss[oi]], [r_ss[oi]])
                    A("pool", lambda e, oi=oi: e.tensor_scalar(out=ss[oi][:, 3:4], in0=ss[oi][:, 2:3], scalar1=1024 * 16e-6, scalar2=None, op0=ALU.add), [r_ss[oi]], [r_ss[oi]])
                    A("pool", lambda e, oi=oi: e.tensor_tensor(out=ss[oi][:, 4:5], in0=ss[oi][:, 3:4], in1=mhalf[:, 0:1], op=ALU.pow), [r_ss[oi], r_mhalf], [r_ss[oi]])
                    A("dve", lambda e, b0=b0, oi=oi: e.scalar_tensor_tensor(out=ot[oi][:, 0:512], in0=b0[:], scalar=ss[oi][:, 4:5], in1=npB[:, 0:512],
                                                                          op0=ALU.mult, op1=ALU.mult), [rb0, r_ss[oi], r_npB], [r_ot[oi]])
                    A("dve", lambda e, b1=b1, oi=oi: e.scalar_tensor_tensor(out=ot[oi][:, 512:1024], in0=b1[:], scalar=ss[oi][:, 4:5], in1=npB[:, 512:1024],
                                                                          op0=ALU.mult, op1=ALU.mult), [rb1, r_ss[oi], r_npB], [r_ot[oi]])
                    xt_, rx_ = xr[j % 2]
                    A("pool", lambda e, oi=oi, xt_=xt_: e.tensor_tensor(out=ot[oi][:], in0=ot[oi][:], in1=xt_[:], op=ALU.add), [r_ot[oi], rx_], [r_ot[oi]])
                    A("sp", lambda e, oi=oi, j=j, s=s, tok0=tok0: e.dma_start(out=out[s, tok0 + j * 128: tok0 + (j + 1) * 128, :], in_=ot[oi][:]),
                      reads=[r_ot[oi]], writes=[new_out()], dma="out%d" % oi)
                    if j + 2 < 4:
                        reload(j + 2)
                    if nxt is not None and j < 2:
                        phase0_back(2 * j, fronts[2 * j])
                        phase0_back(2 * j + 1, fronts[2 * j + 1])
                        if j == 0:
                            fronts[2] = phase0_front(2)
                            fronts[3] = phase0_front(3)
                A("pool", lambda e: e.tensor_copy(out=Kbuf[:, :, 0:128], in_=Kbuf[:, :, 512:640]), [r_Kbuf], [r_Kbuf])
                A("pool", lambda e: e.tensor_copy(out=Vdup[0][:], in_=Vdup[4][:]), [r_Vdup[4]], [r_Vdup[0]])
        fin = P.res("fin")
        A("sp", lambda e: e.nop(), reads=r_outs, writes=[fin])
        P.emit()
    return nc


def _consts(NST):
    ident = np.eye(128, dtype=np.float32)
    triu = np.triu(np.ones((128, 128), dtype=np.float32))
    pit = np.zeros((128, 128), dtype=np.float32)
    for m in range(128):
        d = m % 64
        base = m - d
        if d < 8:
            k = base + d + 8
        elif d < 16:
            k = base + d - 8
        else:
            k = m
        pit[k, m] = 1.0
    half = 8
    inv_freq = (500000.0 ** (-np.arange(0, 16, 2, dtype=np.float32) / 16)).astype(np.float32)

    def tables(pos):
        pos = pos.astype(np.float32)
        ang = (pos[:, None] * inv_freq[None, :]).astype(np.float32)
        c = np.cos(ang.astype(np.float64)).astype(np.float32)
        s_ = np.sin(ang.astype(np.float64)).astype(np.float32)
        n = pos.shape[0]
        cosT = np.ones((128, n), dtype=np.float32)
        sinT = np.zeros((128, n), dtype=np.float32)
        for hb in (0, 64):
            cosT[hb:hb + 8] = c.T
            cosT[hb + 8:hb + 16] = c.T
            sinT[hb:hb + 8] = -s_.T
            sinT[hb + 8:hb + 16] = s_.T
        return cosT, sinT

    cos = np.zeros((NST, 128, T), dtype=np.float32)
    sin = np.zeros((NST, 128, T), dtype=np.float32)
    for ti in range(NST):
        cos[ti], sin[ti] = tables(16 + ti * T + np.arange(T))
    cosm, sinm = tables(np.arange(16))
    return dict(c_ident=ident, c_triu=triu, c_pit=pit, c_cos=cos, c_sin=sin, c_cosm=cosm, c_sinm=sinm)


_NC_CACHE = {}


def run(inputs, n_cores, nseq, nst):
    key = (nseq, nst)
    if key not in _NC_CACHE:
        _NC_CACHE[key] = build_nc(nseq, nst)
    nc = _NC_CACHE[key]
    cs = _consts(nst)
    f = lambda a: np.ascontiguousarray(np.asarray(a, dtype=np.float32))
    shared = {
        "meta_tokens": f(inputs["meta_tokens"]),
        "norm_pre": f(inputs["norm_pre"][0]),
        "w_in": f(inputs["w_in"][0]),
        "attn_sinks": f(inputs["attn_sinks"][0]),
        "conv_w": f(inputs["conv_w"][0]),
        "conv_b": f(inputs["conv_b"][0]),
        "mlstm_gate_bias": f(inputs["mlstm_gate_bias"][0]),
        "mlstm_head_norm": f(inputs["mlstm_head_norm"][0]),
        "w_attn_out": f(inputs["w_attn_out"][0]),
        "w_mlstm_out": f(inputs["w_mlstm_out"][0]),
        "w_out": f(inputs["w_out"][0]),
        "norm_post": f(inputs["norm_post"][0]),
    }
    shared.update(cs)
    xfull = np.asarray(inputs["x"], dtype=np.float32)
    in_maps = []
    for c in range(n_cores):
        m = dict(shared)
        m["x"] = np.ascontiguousarray(xfull[c * nseq:(c + 1) * nseq, :nst * T])
        in_maps.append(m)
    res = run_bass_kernel_spmd(nc, in_maps, core_ids=list(range(n_cores)))
    return np.concatenate([np.asarray(r["out"]) for r in res.results], axis=0)


def kernel(**inputs):
    return run(inputs, 8, 2, 8).astype(np.float32)
```

```python
import contextlib
import math
import numpy as np
import concourse.bass as bass
import concourse.mybir as mybir
from concourse.bass_utils import run_bass_kernel_spmd

F32 = mybir.dt.float32
BF16 = mybir.dt.bfloat16
ALU = mybir.AluOpType
AF = mybir.ActivationFunctionType

ENGS = ("pe", "act", "dve", "pool", "sp")


class Res:
    __slots__ = ("name", "w", "readers")

    def __init__(self, name):
        self.name = name
        self.w = None
        self.readers = []


class Op:
    __slots__ = ("eng", "fn", "semkey", "inc", "deps", "signal", "idx", "is_dma")

    def __init__(self, eng, fn, semkey, inc, is_dma):
        self.eng = eng
        self.fn = fn
        self.semkey = semkey
        self.inc = inc
        self.deps = []
        self.signal = False
        self.idx = 0
        self.is_dma = is_dma


class Prog:
    def __init__(self, nc):
        self.nc = nc
        self.streams = {e: [] for e in ENGS}
        self.by_sem = {}

    def res(self, name):
        return Res(name)

    def op(self, eng, fn, reads=(), writes=(), dma=None):
        is_dma = dma is not None
        semkey = ("dma", dma) if is_dma else eng
        o = Op(eng, fn, semkey, 16 if is_dma else 1, is_dma)
        if is_dma:
            o.signal = True
        deps = {}

        def need(p, kind):
            if p is None:
                return
            if (not p.is_dma) and (not is_dma) and p.eng == eng:
                if eng == "pe":
                    return
            deps[id(p)] = p

        for r in reads:
            need(r.w, "raw")
        for w in writes:
            need(w.w, "waw")
            for t in w.readers:
                need(t, "war")
        o.deps = list(deps.values())
        for p in o.deps:
            p.signal = True
        for w in writes:
            w.w = o
            w.readers = []
        for r in reads:
            if not is_dma:
                r.readers = [t for t in r.readers if t.is_dma or t.eng != eng]
            r.readers.append(o)
        self.streams[eng].append(o)
        self.by_sem.setdefault(semkey, []).append(o)
        return o

    def emit(self):
        nc = self.nc
        for k, ops in self.by_sem.items():
            c = 0
            for o in ops:
                if o.signal:
                    c += o.inc
                o.idx = c
        with contextlib.ExitStack() as st:
            sems = {}
            for k in self.by_sem:
                nm = "s_" + (k if isinstance(k, str) else "d_" + str(k[1]))
                sems[k] = st.enter_context(nc.semaphore(nm))
            block = st.enter_context(nc.Block())
            prog = self

            def run(eng_name, handle):
                waited = {}
                for o in prog.streams[eng_name]:
                    need = {}
                    for p in o.deps:
                        if p.idx > need.get(p.semkey, 0):
                            need[p.semkey] = p.idx
                    for k, v in need.items():
                        if waited.get(k, 0) >= v:
                            continue
                        handle.wait_ge(sems[k], v)
                        waited[k] = v
                    ins = o.fn(handle)
                    if o.signal:
                        ins.then_inc(sems[o.semkey], o.inc)

            @block.tensor
            def _(e):
                run("pe", e)

            @block.scalar
            def _(e):
                run("act", e)

            @block.vector
            def _(e):
                run("dve", e)

            @block.gpsimd
            def _(e):
                run("pool", e)

            @block.sync
            def _(e):
                run("sp", e)


C_AQ, C_AK, C_AV, C_AZ = 0, 1024, 1280, 1536
C_MQK, C_MV, C_MI, C_MF, C_MO, C_MZ, C_GA, C_GM = 2560, 3584, 4608, 4612, 4616, 5640, 6664, 7688
IN_W = 8712
T = 512
LOGK = -0.5 * math.log(128.0)


def build_nc(NSEQ, NST):
    nc = bass.Bass("TRN2", target_bir_lowering=False)
    S = NST * T

    def din(name, shape, dt=F32):
        return nc.dram_tensor(name, shape, dt, kind="ExternalInput").ap()

    x = din("x", [NSEQ, S, 1024])
    meta = din("meta_tokens", [16, 1024])
    norm_pre = din("norm_pre", [1024])
    w_in = din("w_in", [1024, IN_W])
    sinks = din("attn_sinks", [16])
    conv_w = din("conv_w", [4, 1024])
    conv_b = din("conv_b", [1024])
    gate_bias = din("mlstm_gate_bias", [8])
    head_norm = din("mlstm_head_norm", [1024])
    w_ao = din("w_attn_out", [1024, 1024])
    w_mo = din("w_mlstm_out", [1024, 1024])
    w_out = din("w_out", [1024, 1024])
    norm_post = din("norm_post", [1024])
    c_ident = din("c_ident", [128, 128])
    c_triu = din("c_triu", [128, 128])
    c_pit = din("c_pit", [128, 128])
    c_cos = din("c_cos", [NST, 128, T])
    c_sin = din("c_sin", [NST, 128, T])
    c_cosm = din("c_cosm", [128, 16])
    c_sinm = din("c_sinm", [128, 16])
    out = nc.dram_tensor("out", [NSEQ, S, 1024], F32, kind="ExternalOutput").ap()

    def dscr(name, shape):
        return nc.dram_tensor(name, shape, BF16, kind="Internal").ap()

    NWB = 24
    wbt = dscr("wbt", [NWB, 128, 8, 512])
    WB = {"K": 0, "V": 1, "Q": 2, "Z": 4, "GA": 6, "MQK": 8, "MV": 10, "MO": 12, "MZ": 14, "GM": 16, "AO": 18, "MOUT": 20, "OUT": 22}

    P = Prog(nc)
    with contextlib.ExitStack() as st:
        def sb(name, shape, dt=F32):
            return st.enter_context(nc.sbuf_tensor(name, shape, dt))

        NB = 7
        banks = [st.enter_context(nc.psum_tensor("pb%d" % i, [128, 512], F32)) for i in range(NB)]
        bank_res = [P.res("pb%d" % i) for i in range(NB)]
        pst = st.enter_context(nc.psum_tensor("pst", [128, 8, 128], BF16))
        r_pst = P.res("pst")
        bctr = [0]

        def bank():
            i = bctr[0] % NB
            bctr[0] += 1
            return banks[i], bank_res[i]

        ident_f = sb("ident_f", [128, 128]); r_identf = P.res("identf")
        ident_b = sb("ident_b", [128, 128], BF16); r_ident = P.res("ident")
        triu_f = sb("triu_f", [128, 128]); r_triu = P.res("triu")
        pit_f = sb("pit_f", [128, 128]); r_pitf = P.res("pitf")
        pit_b = sb("pit_b", [128, 128], BF16); r_pit = P.res("pit")
        ones_f = sb("ones_f", [128, 128]); r_onesf = P.res("onesf")
        ones_b = sb("ones_b", [128, 128], BF16); r_onesb = P.res("onesb")
        mask_b = sb("mask_b", [128, 4, 128], BF16); r_mask = P.res("mask")
        maskp_b = sb("maskp_b", [128, 4, 128], BF16); r_maskp = P.res("maskp")
        gB = sb("gB", [128, 1024]); r_gB = P.res("gB")
        hnB = sb("hnB", [128, 1024]); r_hnB = P.res("hnB")
        npB = sb("npB", [128, 1024]); r_npB = P.res("npB")
        gbias = sb("gbias", [128, 8]); r_gbias = P.res("gbias")
        cw = sb("cw", [128, 8, 4]); r_cw = P.res("cw")
        cb = sb("cb", [128, 8]); r_cb = P.res("cb")
        sk = sb("sk", [33, 16]); r_sk = P.res("sk")
        wgate = sb("wgate", [128, 8, 8], BF16); r_wgate = P.res("wgate")
        cosT = sb("cosT", [128, T]); r_cos = P.res("cos")
        sinT = sb("sinT", [128, T]); r_sin = P.res("sin")
        cosm = sb("cosm", [128, 16]); r_cosm = P.res("cosm")
        sinm = sb("sinm", [128, 16]); r_sinm = P.res("sinm")

        NSLOT = 3
        wring = [sb("wring%d" % i, [128, 8, 512], BF16) for i in range(NSLOT)]
        r_wring = [P.res("wring%d" % i) for i in range(NSLOT)]
        wctr = [0]

        xs = [sb("xs%d" % j, [128, 1024]) for j in range(4)]
        r_xs = [P.res("xs%d" % j) for j in range(4)]
        xbs = [sb("xb%d" % i, [128, 1024], BF16) for i in range(2)]
        r_xbs = [P.res("xb%d" % i) for i in range(2)]
        mhalf = sb("mhalf", [128, 8]); r_mhalf = P.res("mhalf")
        stt = [sb("stt%d" % j, [128, 4]) for j in range(4)]
        r_stt = [P.res("stt%d" % j) for j in range(4)]
        uT = sb("uT", [128, 8, T], BF16); r_uT = P.res("uT")
        bufA = sb("bufA", [128, 8, T], BF16); r_bufA = P.res("bufA")
        bufB = sb("bufB", [128, 8, T], BF16); r_bufB = P.res("bufB")
        bufC = sb("bufC", [128, 8, T], BF16); r_bufC = P.res("bufC")
        hzT = sb("hzT", [128, 8, T], BF16); r_hzT = P.res("hzT")
        mrg = sb("mrg", [128, 8, T], BF16); r_mrg = P.res("mrg")
        Kbuf = sb("Kbuf", [128, 4, 640], BF16); r_Kbuf = P.res("Kbuf")
        KmT = sb("KmT", [128, 4, 16], BF16); r_KmT = P.res("KmT")
        Vdup = [sb("Vaug%d" % i, [128, 4, 66], BF16) for i in range(5)]
        r_Vdup = [P.res("Vaug%d" % i) for i in range(5)]
        Vmeta = sb("Vmeta", [33, 4, 66], BF16); r_Vmeta = P.res("Vmeta")
        att_tok = [sb("att_tok%d" % i, [128, 1024], BF16) for i in range(2)]
        r_att_tok = [P.res("att_tok%d" % i) for i in range(2)]
        rec4 = [sb("rec4_%d" % i, [128, 4]) for i in range(2)]
        r_rec4 = [P.res("rec4_%d" % i) for i in range(2)]
        ptc = [sb("ptc%d" % i, [128, 512], BF16) for i in range(2)]
        r_ptc = [P.res("ptc%d" % i) for i in range(2)]
        ptp = [sb("ptp%d" % i, [128, 512], BF16) for i in range(2)]
        r_ptp = [P.res("ptp%d" % i) for i in range(2)]
        ptm = [sb("ptm%d" % g, [33, 512], BF16) for g in range(4)]
        r_ptm = [P.res("ptm%d" % g) for g in range(4)]
        raw = [sb("raw%d" % i, [128, 512], BF16) for i in range(2)]
        r_raw = [P.res("raw%d" % i) for i in range(2)]
        t1 = [sb("t1_%d" % i, [128, 512]) for i in range(2)]
        r_t1 = [P.res("t1_%d" % i) for i in range(2)]
        t2 = [sb("t2_%d" % i, [128, 512]) for i in range(2)]
        r_t2 = [P.res("t2_%d" % i) for i in range(2)]
        tzt = [sb("tzt%d" % i, [128, 512], BF16) for i in range(2)]
        r_tzt = [P.res("tzt%d" % i) for i in range(2)]
        zst = [sb("zst%d" % i, [128, 512], BF16) for i in range(2)]
        r_zst = [P.res("zst%d" % i) for i in range(2)]
        pre = [sb("pre%d" % i, [128, 515]) for i in range(2)]
        r_pre = [P.res("pre%d" % i) for i in range(2)]
        acc = [sb("acc%d" % i, [128, 512]) for i in range(2)]
        r_acc = [P.res("acc%d" % i) for i in range(2)]
        halo = sb("halo", [128, 8, 3]); r_halo = P.res("halo")
        halo_m = sb("halo_m", [128, 8, 3]); r_halom = P.res("halom")
        gsb = sb("gsb", [128, 4, 8]); r_gsb = P.res("gsb")
        e1 = sb("e1", [128, 4, 4]); r_e1 = P.res("e1")
        nlf = sb("nlf", [128, 16]); r_nlf = P.res("nlf")
        thr = sb("thr", [128, 16]); r_thr = P.res("thr")
        gs = sb("gs", [128, 16]); r_gs = P.res("gs")
        vsc = sb("vsc", [128, 16]); r_vsc = P.res("vsc")
        eB = sb("eB", [128, 16]); r_eB = P.res("eB")
        vaug = [sb("vaug%d" % j, [128, 4, 272], BF16) for j in range(4)]
        r_vaug = [P.res("vaug%d" % j) for j in range(4)]
        ktok = sb("ktok", [128, 4, 128], BF16); r_ktok = P.res("ktok")
        PTm = sb("PTm", [128, 512], BF16); r_PTm = P.res("PTm")
        C32 = sb("C32", [128, 4, 257]); r_C32 = [P.res("C32_%d" % h) for h in range(4)]
        Cbf = sb("Cbf", [128, 4, 272], BF16); r_Cbf = [P.res("Cbf_%d" % h) for h in range(4)]
        Cm32 = sb("Cm32", [128, 4, 257]); r_Cm32 = P.res("Cm32")
        Ue = [sb("Ue%d" % i, [128, 257]) for i in range(2)]
        r_Ue = [P.res("Ue%d" % i) for i in range(2)]
        dd = sb("dd", [128, 4, 4]); r_dd = [P.res("dd%d" % h) for h in range(4)]
        tmpN = sb("tmpN", [128, 1024]); r_tmpN = P.res("tmpN")
        ho = sb("ho", [128, 1024]); r_ho = P.res("ho")
        st6 = sb("st6", [128, 4, 6]); r_st6 = P.res("st6")
        mv = sb("mv", [128, 4, 2]); r_mv = P.res("mv")
        lnv = sb("lnv", [128, 8]); r_lnv = P.res("lnv")
        hzs = [sb("hz%d" % i, [128, 1024], BF16) for i in range(2)]
        r_hzs = [P.res("hz%d" % i) for i in range(2)]
        hz, r_hz = hzs[0], r_hzs[0]
        ot = [sb("ot%d" % i, [128, 1024]) for i in range(2)]
        r_ot = [P.res("ot%d" % i) for i in range(2)]
        ss = [sb("ss%d" % i, [128, 8]) for i in range(2)]
        r_ss = [P.res("ss%d" % i) for i in range(2)]
        r_outs = []

        def new_out():
            r = P.res("out%d" % len(r_outs))
            r_outs.append(r)
            return r

        ctr = {"raw": 0, "tz": 0, "pre": 0, "pt": 0, "ot": 0, "xb": 0, "hz": 0, "rec4": 0, "ue": 0}

        def rr(key, n=2):
            i = ctr[key] % n
            ctr[key] += 1
            return i

        A = P.op

        r_castb = {}

        def cast_blk(bi, src, c0, n):
            r = P.res("cast%d" % bi)
            A("pool", lambda e: e.dma_start(out=wbt[bi][:, :, 0:n], in_=src[:, c0:c0 + n].rearrange("(k p) n -> p k n", p=128)),
              writes=[r], dma="cast%d" % bi)
            r_castb[bi] = [r]

        rk = []
        for g in range(4):
            for d in range(2):
                r = P.res("castK%d%d" % (g, d))
                A("pool", lambda e, g=g, d=d: e.dma_start(out=wbt[0][:, :, g * 128 + d * 64: g * 128 + d * 64 + 64],
                                                      in_=w_in[:, C_AK + g * 64: C_AK + g * 64 + 64].rearrange("(k p) n -> p k n", p=128)),
                  writes=[r], dma="castK")
                rk.append(r)
        r_castb[0] = rk
        cast_blk(1, w_in, C_AV, 256)
        for nm, c0 in (("Q", C_AQ), ("Z", C_AZ), ("GA", C_GA)):
            for i in range(2):
                cast_blk(WB[nm] + i, w_in, c0 + i * 512, 512)
        for i in range(2):
            cast_blk(WB["AO"] + i, w_ao, i * 512, 512)
        r_wgate_c = P.res("wgate_c")
        for nm, c0 in (("MQK", C_MQK), ("MV", C_MV), ("MO", C_MO), ("MZ", C_MZ), ("GM", C_GM)):
            for i in range(2):
                cast_blk(WB[nm] + i, w_in, c0 + i * 512, 512)
        for i in range(2):
            cast_blk(WB["MOUT"] + i, w_mo, i * 512, 512)
        for i in range(2):
            cast_blk(WB["OUT"] + i, w_out, i * 512, 512)

        def ld(dst, src, r, name, **kw):
            A("sp", lambda e: e.dma_start(out=dst, in_=src, **kw), writes=[r], dma=name)

        ld(ident_f[:], c_ident, r_identf, "identf")
        ld(triu_f[:], c_triu, r_triu, "triu")
        ld(pit_f[:], c_pit, r_pitf, "pitf")
        ld(gB[:], norm_pre.partition_broadcast(128), r_gB, "gB")
        ld(hnB[:], head_norm.partition_broadcast(128), r_hnB, "hnB")
        ld(npB[:], norm_post.partition_broadcast(128), r_npB, "npB")
        ld(gbias[:], gate_bias.partition_broadcast(128), r_gbias, "gbias")
        for jj in range(4):
            ld(cw[:, :, jj], conv_w[jj].rearrange("(b p) -> p b", p=128), r_cw, "cw", allow_slow_non_contiguous=True)
        ld(cb[:], conv_b.rearrange("(b p) -> p b", p=128), r_cb, "cb", allow_slow_non_contiguous=True)
        ld(sk[32:33, :], sinks.rearrange("(o n) -> o n", o=1), r_sk, "sk")
        ld(cosm[:], c_cosm, r_cosm, "cosm")
        ld(sinm[:], c_sinm, r_sinm, "sinm")
        A("pool", lambda e: e.dma_start(out=wgate[:], in_=w_in[:, C_MI:C_MI + 8].rearrange("(k p) n -> p k n", p=128),
                                        allow_slow_non_contiguous=True), writes=[r_wgate], dma="wgate")
        A("dve", lambda e: e.tensor_copy(out=ident_b[:], in_=ident_f[:]), [r_identf], [r_ident])
        A("dve", lambda e: e.tensor_copy(out=pit_b[:], in_=pit_f[:]), [r_pitf], [r_pit])
        A("dve", lambda e: e.memset(ones_f[:], 1.0), [], [r_onesf])
        A("dve", lambda e: e.memset(ones_b[:], 1.0), [], [r_onesb])
        A("dve", lambda e: e.tensor_copy(out=mask_b[:], in_=triu_f[:].unsqueeze(1).to_broadcast([128, 4, 128])), [r_triu], [r_mask])
        A("dve", lambda e: e.tensor_scalar(out=maskp_b[:], in0=triu_f[:].unsqueeze(1).to_broadcast([128, 4, 128]),
                                           scalar1=-1.0, scalar2=1.0, op0=ALU.mult, op1=ALU.add), [r_triu], [r_maskp])
        A("dve", lambda e: e.tensor_scalar(out=cw[:], in0=cw[:], scalar1=0.5, scalar2=None, op0=ALU.mult), [r_cw], [r_cw])
        A("dve", lambda e: e.tensor_scalar(out=cb[:], in0=cb[:], scalar1=0.5, scalar2=None, op0=ALU.mult), [r_cb], [r_cb])
        A("pool", lambda e: e.memset(Vmeta[:], 0.0), [], [r_Vmeta])
        A("pool", lambda e: e.memset(Vmeta[0:16, :, 64:65], 1.0), [], [r_Vmeta])
        A("pool", lambda e: e.memset(Vmeta[32:33, :, 64:65], 1.0), [], [r_Vmeta])
        for i in range(5):
            A("pool", lambda e, i=i: e.memset(Vdup[i][:], 0.0), [], [r_Vdup[i]])
            A("pool", lambda e, i=i: e.memset(Vdup[i][:, :, 64:65], 1.0), [], [r_Vdup[i]])
        A("pool", lambda e: e.memset(mhalf[:], -0.5), [], [r_mhalf])
        A("dve", lambda e: e.tensor_scalar(out=gB[:], in0=gB[:], scalar1=32.0, scalar2=None, op0=ALU.mult), [r_gB], [r_gB])
        A("dve", lambda e: e.tensor_scalar(out=npB[:], in0=npB[:], scalar1=32.0, scalar2=None, op0=ALU.mult), [r_npB], [r_npB])
        for g in range(4):
            A("pool", lambda e, g=g: e.memset(ptm[g][:], 0.0), [], [r_ptm[g]])
            A("act", lambda e, g=g: e.activation(out=ptm[g][32:33, :].rearrange("p (h q) -> p h q", h=4),
                                                 in_=sk[32:33, 4 * g:4 * g + 4].unsqueeze(2).to_broadcast([1, 4, 128]),
                                                 func=AF.Exp), [r_sk, r_ptm[g]], [r_ptm[g]])
        for j in range(4):
            A("pool", lambda e, j=j: e.memset(vaug[j][:], 0.0), [], [r_vaug[j]])
        A("pool", lambda e: e.memset(Cbf[:], 0.0), [], r_Cbf)
        A("pool", lambda e: e.memset(halo_m[:], 0.0), [], [r_halom])

        def load_w(bi, n=512):
            i = wctr[0] % NSLOT
            wctr[0] += 1
            A("sp", lambda e: e.dma_start(out=wring[i][:, :, 0:n], in_=wbt[bi][:, :, 0:n]),
              reads=r_castb[bi], writes=[r_wring[i]], dma="w%d" % i)
            return wring[i], r_wring[i]

        def proj_fm(w, c0, rhs, r_rhs, N, n0=0):
            wt, r_w = w
            b, rb = bank()
            for kc in range(8):
                A("pe", lambda e, kc=kc: e.matmul(b[:, 0:N], lhsT=wt[:, kc, c0:c0 + 128], rhs=rhs[:, kc, n0:n0 + N],
                                                 start=(kc == 0), stop=(kc == 7)), [r_w, r_rhs], [rb])
            return b, rb

        def proj_tm(w, n, lhs, r_lhs, t0, nt):
            wt, r_w = w
            b, rb = bank()
            for kc in range(8):
                A("pe", lambda e, kc=kc: e.matmul(b[0:nt, 0:n], lhsT=lhs[:, kc, t0:t0 + nt], rhs=wt[:, kc, 0:n],
                                                 start=(kc == 0), stop=(kc == 7)), [r_w, r_lhs], [rb])
            return b, rb

        pending = []

        def flush():
            while pending:
                pending.pop(0)()

        def rope(b, rb, N, cos_ap, sin_ap, r_tabs, scale, dst, r_dst):
            i = rr("raw")
            A("act", lambda e: e.activation(out=raw[i][:, 0:N], in_=b[:, 0:N], func=AF.Copy, scale=scale), [rb], [r_raw[i]])

            def stage_b():
                b2, rb2 = bank()
                A("pe", lambda e: e.matmul(b2[:, 0:N], lhsT=pit_b[:], rhs=raw[i][:, 0:N], start=True, stop=True), [r_pit, r_raw[i]], [rb2])
                A("dve", lambda e: e.tensor_tensor(out=t1[i][:, 0:N], in0=raw[i][:, 0:N], in1=cos_ap, op=ALU.mult), [r_raw[i]] + r_tabs, [r_t1[i]])
                A("dve", lambda e: e.tensor_tensor(out=t2[i][:, 0:N], in0=b2[:, 0:N], in1=sin_ap, op=ALU.mult), [rb2] + r_tabs, [r_t2[i]])
                A("pool", lambda e: e.tensor_tensor(out=dst, in0=t1[i][:, 0:N], in1=t2[i][:, 0:N], op=ALU.add), [r_t1[i], r_t2[i]], [r_dst])
            pending.append(stage_b)

        def rmsnorm_front(src, r_src, n, stat, r_stat):
            xi = rr("xb")
            xb_, r_xb_ = xbs[xi], r_xbs[xi]
            A("act", lambda e: e.activation(out=xb_[0:n, :], in_=src, func=AF.Square, accum_out=stat[0:n, 0:1]), [r_src], [r_xb_, r_stat])
            A("pool", lambda e: e.tensor_scalar(out=stat[0:n, 1:2], in0=stat[0:n, 0:1], scalar1=1024 * 1e-6, scalar2=None, op0=ALU.add), [r_stat], [r_stat])
            A("pool", lambda e: e.tensor_tensor(out=stat[0:n, 2:3], in0=stat[0:n, 1:2], in1=mhalf[0:n, 0:1], op=ALU.pow), [r_stat, r_mhalf], [r_stat])
            A("dve", lambda e: e.scalar_tensor_tensor(out=xb_[0:n, :], in0=src, scalar=stat[0:n, 2:3], in1=gB[0:n, :],
                                                      op0=ALU.mult, op1=ALU.mult), [r_src, r_stat, r_gB], [r_xb_])
            return xb_, r_xb_

        def rmsnorm_back(xbr, n, dstT, r_dstT, t0):
            xb_, r_xb_ = xbr
            for k in range(8):
                A("pe", lambda e, k=k: e.transpose(out=pst[:, k, 0:n], in_=xb_[0:n, k * 128:(k + 1) * 128], identity=ident_b[0:n, 0:n]),
                  [r_xb_, r_ident], [r_pst])
            A("act", lambda e: e.activation(out=dstT[:, :, t0:t0 + n], in_=pst[:, :, 0:n], func=AF.Copy), [r_pst], [r_dstT])

        def rmsnorm_T(src, r_src, n, stat, r_stat, dstT, r_dstT, t0):
            rmsnorm_back(rmsnorm_front(src, r_src, n, stat, r_stat), n, dstT, r_dstT, t0)

        def conv_silu(b, rb, blk, N, halo_src, r_halo_src, halo_dst, r_halo_dst, dst, r_dst):
            i = rr("pre")
            p_ = pre[i]
            A("act", lambda e: e.activation(out=p_[:, 3:3 + N], in_=b[:, 0:N], func=AF.Copy), [rb], [r_pre[i]])
            A("pool", lambda e: e.tensor_copy(out=p_[:, 0:3], in_=halo_src[:, blk, :]), [r_halo_src], [r_pre[i]])
            a_ = acc[i]
            A("dve", lambda e: e.tensor_scalar(out=a_[:, 0:N], in0=p_[:, 3:3 + N], scalar1=cw[:, blk, 3:4], scalar2=cb[:, blk:blk + 1],
                                               op0=ALU.mult, op1=ALU.add), [r_pre[i], r_cw, r_cb], [r_acc[i]])
            for jj in (2, 1, 0):
                A("dve", lambda e, jj=jj: e.scalar_tensor_tensor(out=a_[:, 0:N], in0=p_[:, jj:jj + N], scalar=cw[:, blk, jj:jj + 1],
                                                                 in1=a_[:, 0:N], op0=ALU.mult, op1=ALU.add), [r_pre[i], r_cw, r_acc[i]], [r_acc[i]])
            A("pool", lambda e: e.tensor_copy(out=halo_dst[:, blk, :], in_=p_[:, N:N + 3]), [r_pre[i]], [r_halo_dst])
            k = rr("tz")
            A("act", lambda e: e.activation(out=tzt[k][:, 0:N], in_=a_[:, 0:N], func=AF.Tanh), [r_acc[i]], [r_tzt[k]])
            A("dve", lambda e: e.scalar_tensor_tensor(out=dst, in0=tzt[k][:, 0:N], scalar=1.0, in1=a_[:, 0:N], op0=ALU.add, op1=ALU.mult),
              [r_tzt[k], r_acc[i]], [r_dst])

        xm = ot[0]
        A("sp", lambda e: e.dma_start(out=xm[0:16, :], in_=meta), writes=[r_ot[0]], dma="xm")
        uTm = sb("uTm", [128, 8, 16], BF16); r_uTm = P.res("uTm")
        qkm = sb("qkm", [128, 8, 16], BF16); r_qkm = P.res("qkm")
        rmsnorm_T(xm[0:16, :], r_ot[0], 16, stt[0], r_stt[0], uTm, r_uTm, 0)
        wk = load_w(WB["K"])
        for g in range(4):
            b, rb = proj_fm(wk, g * 128, uTm, r_uTm, 16)
            flush()
            rope(b, rb, 16, cosm[:], sinm[:], [r_cosm, r_sinm], 1.0, KmT[:, g, :], r_KmT)
        flush()
        wv = load_w(WB["V"], 256)
        b, rb = proj_tm(wv, 256, uTm, r_uTm, 0, 16)
        A("act", lambda e, b=b: e.activation(out=Vmeta[0:16, :, 0:64], in_=b[0:16, 0:256].rearrange("p (g d) -> p g d", g=4), func=AF.Copy), [rb], [r_Vmeta])
        bgm_, rbgm = bank()
        for kc in range(8):
            A("pe", lambda e, kc=kc: e.matmul(bgm_[0:16, 0:8], lhsT=uTm[:, kc, 0:16], rhs=wgate[:, kc, :], start=(kc == 0), stop=(kc == 7)),
              [r_uTm, r_wgate], [rbgm])
        A("dve", lambda e: e.tensor_tensor(out=gsb[0:16, 0, :], in0=bgm_[0:16, 0:8], in1=gbias[0:16, :], op=ALU.add), [rbgm, r_gbias], [r_gsb])
        A("act", lambda e: e.activation(out=e1[0:16, 0, :], in_=gsb[0:16, 0, 4:8], func=AF.Exp, scale=-1.0), [r_gsb], [r_e1])
        A("act", lambda e: e.activation(out=nlf[0:16, 0:4], in_=e1[0:16, 0, :], func=AF.Ln, bias=1.0), [r_e1], [r_nlf])
        bnbm, rbnbm = bank()
        A("pe", lambda e: e.matmul(bnbm[0:16, 0:4], lhsT=triu_f[0:16, 0:16], rhs=nlf[0:16, 0:4], start=True, stop=True), [r_triu, r_nlf], [rbnbm])
        bnsm, rbnsm = bank()
        A("pe", lambda e: e.matmul(bnsm[:, 0:4], lhsT=ones_f[0:16, :], rhs=nlf[0:16, 0:4], start=True, stop=True), [r_onesf, r_nlf], [rbnsm])
        A("dve", lambda e: e.tensor_tensor(out=gs[0:16, 0:4], in0=gsb[0:16, 0, 0:4], in1=bnbm[0:16, 0:4], op=ALU.add), [r_gsb, rbnbm], [r_gs])
        A("act", lambda e: e.activation(out=vsc[0:16, 0:4], in_=gs[0:16, 0:4], func=AF.Exp, bias=LOGK), [r_gs], [r_vsc])
        A("act", lambda e: e.activation(out=eB[:, 0:4], in_=bnsm[:, 0:4], func=AF.Exp, scale=-1.0), [rbnsm], [r_eB])
        for i in range(2):
            w = load_w(WB["MQK"] + i)
            for c in range(4):
                blk = 4 * i + c
                b, rb = proj_fm(w, c * 128, uTm, r_uTm, 16)
                conv_silu(b, rb, blk, 16, halo_m, r_halom, halo_m, r_halom, qkm[:, blk, :], r_qkm)
        vaugm = vaug[0]
        for i in range(2):
            w = load_w(WB["MV"] + i)
            b, rb = proj_tm(w, 512, uTm, r_uTm, 0, 16)
            for hh in range(2):
                h = 2 * i + hh
                A("act", lambda e, b=b, hh=hh, h=h: e.activation(out=vaugm[0:16, h, 0:256], in_=b[0:16, hh * 256:(hh + 1) * 256], func=AF.Copy,
                                                              scale=vsc[0:16, h:h + 1]), [rb, r_vsc], [r_vaug[0]])
        A("pool", lambda e: e.tensor_copy(out=vaugm[0:16, :, 256:257], in_=vsc[0:16, 0:4].unsqueeze(2)), [r_vsc], [r_vaug[0]])
        for h in range(4):
            A("pe", lambda e, h=h: e.transpose(out=pst[0:16, h, :], in_=qkm[:, 4 + h, :], identity=ident_b[:]), [r_qkm, r_ident], [r_pst])
        A("act", lambda e: e.activation(out=ktok[0:16, :, :], in_=pst[0:16, 0:4, :], func=AF.Copy), [r_pst], [r_ktok])
        for h in range(4):
            bU, rbU = bank()
            A("pe", lambda e, h=h, bU=bU: e.matmul(bU[:, 0:257], lhsT=ktok[0:16, h, :], rhs=vaugm[0:16, h, 0:257], start=True, stop=True),
              [r_ktok, r_vaug[0]], [rbU])
            A("dve", lambda e, h=h, bU=bU: e.tensor_scalar(out=Cm32[:, h, :], in0=bU[:, 0:257], scalar1=eB[:, h:h + 1], scalar2=None, op0=ALU.mult),
              [rbU, r_eB], [r_Cm32])
        A("pool", lambda e: e.memset(vaug[0][:], 0.0), [], [r_vaug[0]])

        def phase0_load(s_, ti_, j):
            t0_ = ti_ * T + j * 128
            A("sp", lambda e: e.dma_start(out=xs[j][:], in_=x[s_, t0_: t0_ + 128, :]), writes=[r_xs[j]], dma="xs%d" % j)

        def phase0_tabs(ti_):
            A("sp", lambda e: e.dma_start(out=cosT[:], in_=c_cos[ti_]), writes=[r_cos], dma="cos")
            A("sp", lambda e: e.dma_start(out=sinT[:], in_=c_sin[ti_]), writes=[r_sin], dma="sin")

        def phase0_norm(j):
            rmsnorm_T(xs[j][:], r_xs[j], 128, stt[j], r_stt[j], uT, r_uT, j * 128)

        def phase0_front(j):
            return rmsnorm_front(xs[j][:], r_xs[j], 128, stt[j], r_stt[j])

        def phase0_back(j, xbr):
            rmsnorm_back(xbr, 128, uT, r_uT, j * 128)

        for s in range(NSEQ):
            A("pool", lambda e: e.tensor_copy(out=C32[:], in_=Cm32[:]), [r_Cm32], r_C32)
            A("pool", lambda e: e.tensor_copy(out=Cbf[:, :, 0:257], in_=Cm32[:]), [r_Cm32], r_Cbf)
            A("pool", lambda e: e.tensor_copy(out=halo[:], in_=halo_m[:]), [r_halom], [r_halo])
            for ti in range(NST):
                tok0 = ti * T
                if s == 0 and ti == 0:
                    phase0_tabs(ti)
                    for j in range(4):
                        phase0_load(s, ti, j)
                    for j in range(4):
                        phase0_norm(j)
                nxt = (s, ti + 1) if ti + 1 < NST else ((s + 1, 0) if s + 1 < NSEQ else None)
                wk = load_w(WB["K"])
                for g in range(4):
                    b, rb = proj_fm(wk, g * 128, uT, r_uT, T)
                    flush()
                    rope(b, rb, T, cosT[:], sinT[:], [r_cos, r_sin], 1.0, Kbuf[:, g, 128:640], r_Kbuf)
                wv = load_w(WB["V"], 256)
                for j in range(4):
                    b, rb = proj_tm(wv, 256, uT, r_uT, j * 128, 128)
                    A("act", lambda e, b=b, j=j: e.activation(out=Vdup[j + 1][:, :, 0:64], in_=b[:, 0:256].rearrange("p (g d) -> p g d", g=4), func=AF.Copy),
                      [rb], [r_Vdup[j + 1]])
                QT, r_QT = bufA, r_bufA
                attT, r_attT = bufB, r_bufB
                for i in range(2):
                    w = load_w(WB["Q"] + i)
                    for c in range(4):
                        blk = 4 * i + c
                        b, rb = proj_fm(w, c * 128, uT, r_uT, T)
                        flush()
                        rope(b, rb, T, cosT[:], sinT[:], [r_cos, r_sin], 0.125, QT[:, blk, :], r_QT)
                flush()
                def att_S(j, g):
                    has_prev = not (ti == 0 and j == 0)
                    jc = slice(j * 128, (j + 1) * 128)
                    pi = rr("pt")
                    bSc, rSc = bank()
                    bSm, rSm = bank()
                    bSp, rSp = bank() if has_prev else (None, None)
                    for hh in range(4):
                        blk = 2 * g + hh // 2
                        rows = slice((hh % 2) * 64, (hh % 2) * 64 + 64)
                        hc = slice(hh * 128, (hh + 1) * 128)
                        A("pe", lambda e, rows=rows, hc=hc, blk=blk: e.matmul(
                            bSc[:, hc], lhsT=Kbuf[rows, g, 128 + j * 128: 256 + j * 128], rhs=QT[rows, blk, jc], start=True, stop=True),
                          [r_Kbuf, r_QT], [rSc])
                        A("pe", lambda e, rows=rows, hc=hc, blk=blk: e.matmul(
                            bSm[0:16, hc], lhsT=KmT[rows, g, :], rhs=QT[rows, blk, jc], start=True, stop=True),
                          [r_KmT, r_QT], [rSm])
                        if has_prev:
                            A("pe", lambda e, rows=rows, hc=hc, blk=blk: e.matmul(
                                bSp[:, hc], lhsT=Kbuf[rows, g, j * 128: 128 + j * 128], rhs=QT[rows, blk, jc], start=True, stop=True),
                              [r_Kbuf, r_QT], [rSp])
                    A("act", lambda e: e.activation(out=ptc[pi][:], in_=bSc[:], func=AF.Exp), [rSc], [r_ptc[pi]])
                    A("act", lambda e: e.activation(out=ptm[g][0:16, :], in_=bSm[0:16, :], func=AF.Exp), [rSm], [r_ptm[g]])
                    A("pool", lambda e: e.tensor_tensor(out=ptc[pi][:], in0=ptc[pi][:], in1=mask_b[:].rearrange("p h q -> p (h q)"), op=ALU.mult),
                      [r_ptc[pi], r_mask], [r_ptc[pi]])
                    if has_prev:
                        A("act", lambda e: e.activation(out=ptp[pi][:], in_=bSp[:], func=AF.Exp), [rSp], [r_ptp[pi]])
                        A("pool", lambda e: e.tensor_tensor(out=ptp[pi][:], in0=ptp[pi][:], in1=maskp_b[:].rearrange("p h q -> p (h q)"), op=ALU.mult),
                          [r_ptp[pi], r_maskp], [r_ptp[pi]])
                    return (j, g, pi, has_prev, jc)

                def att_PV(ctx):
                    j, g, pi, has_prev, jc = ctx
                    ai = j % 2
                    bO, rO = bank()
                    for hh in range(4):
                        hc = slice(hh * 128, (hh + 1) * 128)
                        oc = slice(hh * 65, hh * 65 + 65)
                        A("pe", lambda e, hc=hc, oc=oc: e.matmul(bO[:, oc], lhsT=ptm[g][:, hc], rhs=Vmeta[:, g, 0:65], start=True, stop=False),
                          [r_Vmeta, r_ptm[g]], [rO])
                        if has_prev:
                            A("pe", lambda e, hc=hc, oc=oc: e.matmul(bO[:, oc], lhsT=ptp[pi][:, hc], rhs=Vdup[j][:, g, 0:65], start=False, stop=False),
                              [r_Vdup[j], r_ptp[pi]], [rO])
                        A("pe", lambda e, hc=hc, oc=oc: e.matmul(bO[:, oc], lhsT=ptc[pi][:, hc], rhs=Vdup[j + 1][:, g, 0:65], start=False, stop=True),
                          [r_Vdup[j + 1], r_ptc[pi]], [rO])
                    ri = rr("rec4")
                    bOv = bO[:, 0:260].rearrange("p (h c) -> p h c", c=65)
                    A("dve", lambda e: e.reciprocal(out=rec4[ri][:], in_=bOv[:, :, 64]), [rO], [r_rec4[ri]])
                    A("dve", lambda e: e.tensor_tensor(out=att_tok[ai][:, g * 256:(g + 1) * 256].rearrange("p (h d) -> p h d", h=4), in0=bOv[:, :, 0:64],
                                                       in1=rec4[ri][:].unsqueeze(2).to_broadcast([128, 4, 64]), op=ALU.mult),
                      [rO, r_rec4[ri]], [r_att_tok[ai]])
                    if g == 3:
                        for k in range(8):
                            A("pe", lambda e, k=k: e.transpose(out=pst[:, k, :], in_=att_tok[ai][:, k * 128:(k + 1) * 128], identity=ident_b[:]),
                              [r_att_tok[ai], r_ident], [r_pst])
                        A("act", lambda e: e.activation(out=attT[:, :, jc], in_=pst[:], func=AF.Copy), [r_pst], [r_attT])

                order = [(j, g) for j in range(4) for g in range(4)]
                ctxs = [att_S(*order[0])]
                for idx in range(len(order)):
                    if idx + 1 < len(order):
                        ctxs.append(att_S(*order[idx + 1]))
                    att_PV(ctxs[idx])
                for i in range(2):
                    w = load_w(WB["Z"] + i)
                    for c in range(4):
                        blk = 4 * i + c
                        b, rb = proj_fm(w, c * 128, uT, r_uT, T)
                        k = rr("tz")
                        A("act", lambda e, b=b, k=k: e.activation(out=tzt[k][:], in_=b[:], func=AF.Tanh, scale=0.5), [rb], [r_tzt[k]])
                        A("dve", lambda e, b=b, k=k: e.scalar_tensor_tensor(out=zst[k][:], in0=tzt[k][:], scalar=1.0, in1=b[:], op0=ALU.add, op1=ALU.mult),
                          [r_tzt[k], rb], [r_zst[k]])
                        A("pool", lambda e, k=k, blk=blk: e.tensor_tensor(out=attT[:, blk, :], in0=attT[:, blk, :], in1=zst[k][:], op=ALU.mult),
                          [r_attT, r_zst[k]], [r_attT])
                for i in range(2):
                    wg = load_w(WB["GA"] + i)
                    wa = load_w(WB["AO"] + i)
                    for c in range(4):
                        blk = 4 * i + c
                        b, rb = proj_fm(wg, c * 128, uT, r_uT, T)
                        k = rr("tz")
                        A("act", lambda e, b=b, k=k: e.activation(out=tzt[k][:], in_=b[:], func=AF.Tanh, scale=0.5), [rb], [r_tzt[k]])
                        by, rby = proj_fm(wa, c * 128, attT, r_attT, T)
                        A("dve", lambda e, by=by, k=k, blk=blk: e.scalar_tensor_tensor(out=mrg[:, blk, :], in0=tzt[k][:], scalar=1.0, in1=by[:],
                                                                                    op0=ALU.add, op1=ALU.mult), [r_tzt[k], rby], [r_mrg])
                if nxt is not None:
                    phase0_tabs(nxt[1])
                    for j in range(4):
                        phase0_load(nxt[0], nxt[1], j)
                bg, rbg = bank()
                for j in range(4):
                    for kc in range(8):
                        A("pe", lambda e, kc=kc, j=j, bg=bg: e.matmul(bg[:, j * 8:(j + 1) * 8], lhsT=uT[:, kc, j * 128:(j + 1) * 128], rhs=wgate[:, kc, :],
                                                                 start=(kc == 0), stop=(kc == 7)), [r_uT, r_wgate], [rbg])
                A("dve", lambda e, bg=bg: e.tensor_tensor(out=gsb[:], in0=bg[:, 0:32].rearrange("p (j c) -> p j c", j=4),
                                                        in1=gbias[:].unsqueeze(1).to_broadcast([128, 4, 8]), op=ALU.add), [rbg, r_gbias], [r_gsb])
                A("act", lambda e: e.activation(out=e1[:], in_=gsb[:, :, 4:8], func=AF.Exp, scale=-1.0), [r_gsb], [r_e1])
                A("act", lambda e: e.activation(out=nlf[:], in_=e1[:].rearrange("p j h -> p (j h)"), func=AF.Ln, bias=1.0), [r_e1], [r_nlf])
                bnb, rbnb = bank()
                A("pe", lambda e, bnb=bnb: e.matmul(bnb[:, 0:16], lhsT=triu_f[:], rhs=nlf[:], start=True, stop=True), [r_triu, r_nlf], [rbnb])
                bns, rbns = bank()
                A("pe", lambda e, bns=bns: e.matmul(bns[:, 0:16], lhsT=ones_f[:], rhs=nlf[:], start=True, stop=True), [r_onesf, r_nlf], [rbns])
                A("act", lambda e, bnb=bnb: e.activation(out=thr[:], in_=bnb[:, 0:16], func=AF.Exp), [rbnb], [r_thr])
                A("dve", lambda e, bnb=bnb: e.tensor_tensor(out=gs[:].rearrange("p (j h) -> p j h", j=4), in0=gsb[:, :, 0:4],
                                                          in1=bnb[:, 0:16].rearrange("p (j h) -> p j h", j=4), op=ALU.add), [r_gsb, rbnb], [r_gs])
                A("act", lambda e: e.activation(out=vsc[:], in_=gs[:], func=AF.Exp, bias=LOGK), [r_gs], [r_vsc])
                A("act", lambda e, bns=bns: e.activation(out=eB[:], in_=bns[:, 0:16], func=AF.Exp, scale=-1.0), [rbns], [r_eB])
                qkT, r_qkT = bufC, r_bufC
                for i in range(2):
                    w = load_w(WB["MQK"] + i)
                    for c in range(4):
                        blk = 4 * i + c
                        b, rb = proj_fm(w, c * 128, uT, r_uT, T)
                        conv_silu(b, rb, blk, T, halo, r_halo, halo, r_halo, qkT[:, blk, :], r_qkT)
                for i in range(2):
                    w = load_w(WB["MV"] + i)
                    for j in range(4):
                        b, rb = proj_tm(w, 512, uT, r_uT, j * 128, 128)
                        for hh in range(2):
                            h = 2 * i + hh
                            A("act", lambda e, b=b, hh=hh, h=h, j=j: e.activation(out=vaug[j][:, h, 0:256], in_=b[:, hh * 256:(hh + 1) * 256], func=AF.Copy,
                                                                             scale=vsc[:, j * 4 + h: j * 4 + h + 1]), [rb, r_vsc], [r_vaug[j]])
                for j in range(4):
                    A("pool", lambda e, j=j: e.tensor_copy(out=vaug[j][:, :, 256:257], in_=vsc[:, j * 4:(j + 1) * 4].unsqueeze(2)), [r_vsc], [r_vaug[j]])
                th = bufB[:].rearrange("p k t -> p (k t)").rearrange("p (j f) -> p j f", j=4)
                r_th = r_bufB
                zs = bufA[:].rearrange("p k t -> p (k t)").rearrange("p (j f) -> p j f", j=4)
                r_zs = r_bufA
                for i in range(2):
                    w = load_w(WB["MO"] + i)
                    for j in range(4):
                        b, rb = proj_tm(w, 512, uT, r_uT, j * 128, 128)
                        A("act", lambda e, b=b, j=j, i=i: e.activation(out=th[:, j, i * 512:(i + 1) * 512], in_=b[:], func=AF.Tanh, scale=0.5), [rb], [r_th])
                for i in range(2):
                    w = load_w(WB["MZ"] + i)
                    for j in range(4):
                        b, rb = proj_tm(w, 512, uT, r_uT, j * 128, 128)
                        k = rr("tz")
                        A("act", lambda e, b=b, k=k: e.activation(out=tzt[k][:], in_=b[:], func=AF.Tanh, scale=0.5), [rb], [r_tzt[k]])
                        A("dve", lambda e, b=b, k=k: e.scalar_tensor_tensor(out=zst[k][:], in0=tzt[k][:], scalar=1.0, in1=b[:],
                                                                          op0=ALU.add, op1=ALU.mult), [r_tzt[k], rb], [r_zst[k]])
                        A("pool", lambda e, k=k, j=j, i=i: e.tensor_tensor(out=zs[:, j, i * 512:(i + 1) * 512], in0=zst[k][:], in1=hnB[:, i * 512:(i + 1) * 512], op=ALU.mult),
                          [r_zst[k], r_hnB], [r_zs])
                def rec_pe(j):
                    jc = slice(j * 128, (j + 1) * 128)
                    for h in range(4):
                        A("pe", lambda e, h=h: e.transpose(out=pst[:, h, :], in_=qkT[:, 4 + h, jc], identity=ident_b[:]), [r_qkT, r_ident], [r_pst])
                    A("act", lambda e: e.activation(out=ktok[:], in_=pst[:, 0:4, :], func=AF.Copy), [r_pst], [r_ktok])
                    bS, rS = bank()
                    for h in range(4):
                        A("pe", lambda e, h=h: e.matmul(bS[:, h * 128:(h + 1) * 128], lhsT=qkT[:, 4 + h, jc], rhs=qkT[:, h, jc], start=True, stop=True),
                          [r_qkT], [rS])
                    A("dve", lambda e: e.tensor_tensor(out=PTm[:], in0=bS[:], in1=mask_b[:].rearrange("p h q -> p (h q)"), op=ALU.mult), [rS, r_mask], [r_PTm])
                    bNs = [bank(), bank()]
                    bDn, rDn = bank()
                    for h in range(4):
                        bN, rN = bNs[h // 2]
                        nc_ = slice((h % 2) * 256, (h % 2) * 256 + 256)
                        A("pe", lambda e, h=h, bN=bN, nc_=nc_: e.matmul(bN[:, nc_], lhsT=PTm[:, h * 128:(h + 1) * 128], rhs=vaug[j][:, h, 0:256], start=True, stop=False),
                          [r_PTm, r_vaug[j]], [rN])
                        A("pe", lambda e, h=h, bN=bN, nc_=nc_: e.matmul(bN[:, nc_], lhsT=qkT[:, h, jc], rhs=Cbf[:, h, 0:256], start=False, stop=True),
                          [r_qkT, r_Cbf[h]], [rN])
                        A("pe", lambda e, h=h: e.matmul(bDn[:, h:h + 1], lhsT=PTm[:, h * 128:(h + 1) * 128], rhs=vaug[j][:, h, 256:257], start=True, stop=False),
                          [r_PTm, r_vaug[j]], [rDn])
                        A("pe", lambda e, h=h: e.matmul(bDn[:, h:h + 1], lhsT=qkT[:, h, jc], rhs=Cbf[:, h, 256:257], start=False, stop=True),
                          [r_qkT, r_Cbf[h]], [rDn])
                    bUs = []
                    for h in range(4):
                        bU, rbU = bank()
                        A("pe", lambda e, h=h, bU=bU: e.matmul(bU[:, 0:257], lhsT=ktok[:, h, :], rhs=vaug[j][:, h, 0:257], start=True, stop=True),
                          [r_ktok, r_vaug[j]], [rbU])
                        bUs.append((bU, rbU))
                    return (j, jc, bNs, bDn, rDn, bUs)

                def rec_state(ctx):
                    j, jc, bNs, bDn, rDn, bUs = ctx
                    for h in range(4):
                        bU, rbU = bUs[h]
                        col = j * 4 + h
                        A("dve", lambda e, h=h, col=col: e.tensor_scalar(out=C32[:, h, :], in0=C32[:, h, :], scalar1=eB[:, col:col + 1], scalar2=None, op0=ALU.mult),
                          [r_C32[h], r_eB], [r_C32[h]])
                        A("dve", lambda e, h=h, col=col, bU=bU: e.scalar_tensor_tensor(out=C32[:, h, :], in0=bU[:, 0:257], scalar=eB[:, col:col + 1], in1=C32[:, h, :],
                                                                                    op0=ALU.mult, op1=ALU.add), [rbU, r_eB, r_C32[h]], [r_C32[h]])
                        A("act", lambda e, h=h: e.activation(out=Cbf[:, h, 0:257], in_=C32[:, h, :], func=AF.Copy), [r_C32[h]], [r_Cbf[h]])

                def rec_rest(ctx):
                    j, jc, bNs, bDn, rDn, bUs = ctx
                    c4 = slice(j * 4, (j + 1) * 4)
                    A("dve", lambda e: e.tensor_scalar(out=dd[:, 0, :], in0=bDn[:, 0:4], scalar1=-1.0, scalar2=None, op0=ALU.mult), [rDn], [r_dd[0]])
                    A("dve", lambda e: e.tensor_tensor(out=dd[:, 1, :], in0=bDn[:, 0:4], in1=dd[:, 0, :], op=ALU.max), [rDn, r_dd[0]], [r_dd[0]])
                    A("dve", lambda e: e.tensor_tensor(out=dd[:, 2, :], in0=dd[:, 1, :], in1=thr[:, c4], op=ALU.max), [r_dd[0], r_thr], [r_dd[0]])
                    A("dve", lambda e: e.reciprocal(out=dd[:, 3, :], in_=dd[:, 2, :]), [r_dd[0]], [r_dd[0]])
                    for h in range(4):
                        bN, rN = bNs[h // 2]
                        nc_ = slice((h % 2) * 256, (h % 2) * 256 + 256)
                        A("act", lambda e, h=h, bN=bN, nc_=nc_: e.activation(out=tmpN[:, h * 256:(h + 1) * 256], in_=bN[:, nc_], func=AF.Copy, scale=dd[:, 3, h:h + 1]),
                          [rN, r_dd[0]], [r_tmpN])
                    A("dve", lambda e: e.scalar_tensor_tensor(out=ho[:], in0=th[:, j, :], scalar=1.0, in1=tmpN[:], op0=ALU.add, op1=ALU.mult),
                      [r_th, r_tmpN], [r_ho])
                    for h in range(4):
                        A("dve", lambda e, h=h: e.bn_stats(out=st6[:, h, :], in_=ho[:, h * 256:(h + 1) * 256]), [r_ho], [r_st6])
                    for h in range(4):
                        A("dve", lambda e, h=h: e.bn_aggr(out=mv[:, h, :], in_=st6[:, h, :]), [r_st6], [r_mv])
                    A("pool", lambda e: e.tensor_scalar(out=lnv[:, 0:4], in0=mv[:, :, 1], scalar1=4e-6, scalar2=None, op0=ALU.add), [r_mv], [r_lnv])
                    A("pool", lambda e: e.tensor_tensor(out=lnv[:, 4:8], in0=lnv[:, 0:4], in1=mhalf[:, 0:4], op=ALU.pow), [r_lnv, r_mhalf], [r_lnv])
                    for h in range(4):
                        A("dve", lambda e, h=h: e.tensor_scalar(out=ho[:, h * 256:(h + 1) * 256], in0=ho[:, h * 256:(h + 1) * 256], scalar1=mv[:, h, 0:1],
                                                              scalar2=lnv[:, 4 + h:5 + h], op0=ALU.subtract, op1=ALU.mult), [r_ho, r_mv, r_lnv], [r_ho])
                    hi = rr("hz")
                    A("pool", lambda e: e.tensor_tensor(out=hzs[hi][:], in0=ho[:], in1=zs[:, j, :], op=ALU.mult), [r_ho, r_zs], [r_hzs[hi]])
                    return (jc, hi)

                def rec_tail(t):
                    jc, hi = t
                    for k in range(8):
                        A("pe", lambda e, k=k: e.transpose(out=pst[:, k, :], in_=hzs[hi][:, k * 128:(k + 1) * 128], identity=ident_b[:]), [r_hzs[hi], r_ident], [r_pst])
                    A("act", lambda e: e.activation(out=hzT[:, :, jc], in_=pst[:], func=AF.Copy), [r_pst], [r_hzT])

                tail = None
                for j in range(4):
                    ctx = rec_pe(j)
                    bctr[0] += 3
                    rec_state(ctx)
                    if tail is not None:
                        rec_tail(tail)
                    tail = rec_rest(ctx)
                rec_tail(tail)
                mT, r_mT = bufC, r_bufC
                for i in range(2):
                    wg = load_w(WB["GM"] + i)
                    wm = load_w(WB["MOUT"] + i)
                    for c in range(4):
                        blk = 4 * i + c
                        b, rb = proj_fm(wg, c * 128, uT, r_uT, T)
                        k = rr("tz")
                        A("act", lambda e, b=b, k=k: e.activation(out=tzt[k][:], in_=b[:], func=AF.Tanh, scale=0.5), [rb], [r_tzt[k]])
                        by, rby = proj_fm(wm, c * 128, hzT, r_hzT, T)
                        A("dve", lambda e, by=by, k=k: e.scalar_tensor_tensor(out=zst[k][:], in0=tzt[k][:], scalar=1.0, in1=by[:], op0=ALU.add, op1=ALU.mult),
                          [r_tzt[k], rby], [r_zst[k]])
                        A("pool", lambda e, k=k, blk=blk: e.tensor_tensor(out=mT[:, blk, :], in0=zst[k][:], in1=mrg[:, blk, :], op=ALU.add),
                          [r_zst[k], r_mrg], [r_mT])
                w0 = load_w(WB["OUT"])
                w1 = load_w(WB["OUT"] + 1)
                xr = [(tmpN, r_tmpN), (ho, r_ho)]

                def reload(j):
                    xt_, rx_ = xr[j % 2]
                    t0_ = tok0 + j * 128
                    A("sp", lambda e, s=s: e.dma_start(out=xt_[:], in_=x[s, t0_: t0_ + 128, :]), writes=[rx_], dma="xr%d" % (j % 2))

                reload(0)
                reload(1)
                fronts = {}
                if nxt is not None:
                    fronts[0] = phase0_front(0)
                    fronts[1] = phase0_front(1)
                for j in range(4):
                    oi = rr("ot")
                    b0, rb0 = proj_tm(w0, 512, mT, r_mT, j * 128, 128)
                    b1, rb1 = proj_tm(w1, 512, mT, r_mT, j * 128, 128)
                    A("act", lambda e, b0=b0, oi=oi: e.activation(out=hz[:, 0:512], in_=b0[:], func=AF.Square, accum_out=ss[oi][:, 0:1]), [rb0], [r_hz, r_ss[oi]])
                    A("act", lambda e, b1=b1, oi=oi: e.activation(out=hz[:, 512:1024], in_=b1[:], func=AF.Square, accum_out=ss[oi][:, 1:2]), [rb1], [r_hz, r_ss[oi]])
                    A("dve", lambda e, oi=oi: e.tensor_tensor(out=ss[oi][:, 2:3], in0=ss[oi][:, 0:1], in1=ss[oi][:, 1:2], op=ALU.add), [r_ss[oi]], [r_ss[oi]])
                    A("pool", lambda e, oi=oi: e.tensor_scalar(out=ss[oi][:, 3:4], in0=ss[oi][:, 2:3], scalar1=1024 * 16e-6, scalar2=None, op0=ALU.add), [r_ss[oi]], [r_ss[oi]])
                    A("pool", lambda e, oi=oi: e.tensor_tensor(out=ss[oi][:, 4:5], in0=ss[oi][:, 3:4], in1=mhalf[:, 0:1], op=ALU.pow), [r_ss[oi], r_mhalf], [r_ss[oi]])
                    A("dve", lambda e, b0=b0, oi=oi: e.scalar_tensor_tensor(out=ot[oi][:, 0:512], in0=b0[:], scalar=ss[oi][:, 4:5], in1=npB[:, 0:512],
                                                                          op0=ALU.mult, op1=ALU.mult), [rb0, r_ss[oi], r_npB], [r_ot[oi]])
                    A("dve", lambda e, b1=b1, oi=oi: e.scalar_tensor_tensor(out=ot[oi][:, 512:1024], in0=b1[:], scalar=ss[oi][:, 4:5], in1=npB[:, 512:1024],
                                                                          op0=ALU.mult, op1=ALU.mult), [rb1, r_ss[oi], r_npB], [r_ot[oi]])
                    xt_, rx_ = xr[j % 2]
                    A("pool", lambda e, oi=oi, xt_=xt_: e.tensor_tensor(out=ot[oi][:], in0=ot[oi][:], in1=xt_[:], op=ALU.add), [r_ot[oi], rx_], [r_ot[oi]])
                    A("sp", lambda e, oi=oi, j=j, s=s, tok0=tok0: e.dma_start(out=out[s, tok0 + j * 128: tok0 + (j + 1) * 128, :], in_=ot[oi][:]),
                      reads=[r_ot[oi]], writes=[new_out()], dma="out%d" % oi)
                    if j + 2 < 4:
                        reload(j + 2)
                    if nxt is not None and j < 2:
                        phase0_back(2 * j, fronts[2 * j])
                        phase0_back(2 * j + 1, fronts[2 * j + 1])
                        if j == 0:
                            fronts[2] = phase0_front(2)
                            fronts[3] = phase0_front(3)
                A("pool", lambda e: e.tensor_copy(out=Kbuf[:, :, 0:128], in_=Kbuf[:, :, 512:640]), [r_Kbuf], [r_Kbuf])
                A("pool", lambda e: e.tensor_copy(out=Vdup[0][:], in_=Vdup[4][:]), [r_Vdup[4]], [r_Vdup[0]])
        fin = P.res("fin")
        A("sp", lambda e: e.nop(), reads=r_outs, writes=[fin])
        P.emit()
    return nc


def _consts(NST):
    ident = np.eye(128, dtype=np.float32)
    triu = np.triu(np.ones((128, 128), dtype=np.float32))
    pit = np.zeros((128, 128), dtype=np.float32)
    for m in range(128):
        d = m % 64
        base = m - d
        if d < 8:
            k = base + d + 8
        elif d < 16:
            k = base + d - 8
        else:
            k = m
        pit[k, m] = 1.0
    half = 8
    inv_freq = (500000.0 ** (-np.arange(0, 16, 2, dtype=np.float32) / 16)).astype(np.float32)

    def tables(pos):
        pos = pos.astype(np.float32)
        ang = (pos[:, None] * inv_freq[None, :]).astype(np.float32)
        c = np.cos(ang.astype(np.float64)).astype(np.float32)
        s_ = np.sin(ang.astype(np.float64)).astype(np.float32)
        n = pos.shape[0]
        cosT = np.ones((128, n), dtype=np.float32)
        sinT = np.zeros((128, n), dtype=np.float32)
        for hb in (0, 64):
            cosT[hb:hb + 8] = c.T
            cosT[hb + 8:hb + 16] = c.T
            sinT[hb:hb + 8] = -s_.T
            sinT[hb + 8:hb + 16] = s_.T
        return cosT, sinT

    cos = np.zeros((NST, 128, T), dtype=np.float32)
    sin = np.zeros((NST, 128, T), dtype=np.float32)
    for ti in range(NST):
        cos[ti], sin[ti] = tables(16 + ti * T + np.arange(T))
    cosm, sinm = tables(np.arange(16))
    return dict(c_ident=ident, c_triu=triu, c_pit=pit, c_cos=cos, c_sin=sin, c_cosm=cosm, c_sinm=sinm)


_NC_CACHE = {}


def run(inputs, n_cores, nseq, nst):
    key = (nseq, nst)
    if key not in _NC_CACHE:
        _NC_CACHE[key] = build_nc(nseq, nst)
    nc = _NC_CACHE[key]
    cs = _consts(nst)
    f = lambda a: np.ascontiguousarray(np.asarray(a, dtype=np.float32))
    shared = {
        "meta_tokens": f(inputs["meta_tokens"]),
        "norm_pre": f(inputs["norm_pre"][0]),
        "w_in": f(inputs["w_in"][0]),
        "attn_sinks": f(inputs["attn_sinks"][0]),
        "conv_w": f(inputs["conv_w"][0]),
        "conv_b": f(inputs["conv_b"][0]),
        "mlstm_gate_bias": f(inputs["mlstm_gate_bias"][0]),
        "mlstm_head_norm": f(inputs["mlstm_head_norm"][0]),
        "w_attn_out": f(inputs["w_attn_out"][0]),
        "w_mlstm_out": f(inputs["w_mlstm_out"][0]),
        "w_out": f(inputs["w_out"][0]),
        "norm_post": f(inputs["norm_post"][0]),
    }
    shared.update(cs)
    xfull = np.asarray(inputs["x"], dtype=np.float32)
    in_maps = []
    for c in range(n_cores):
        m = dict(shared)
        m["x"] = np.ascontiguousarray(xfull[c * nseq:(c + 1) * nseq, :nst * T])
        in_maps.append(m)
    res = run_bass_kernel_spmd(nc, in_maps, core_ids=list(range(n_cores)))
    return np.concatenate([np.asarray(r["out"]) for r in res.results], axis=0)


def kernel(**inputs):
    return run(inputs, 8, 2, 8).astype(np.float32)
```

```python
import contextlib
import math
import numpy as np
import concourse.bass as bass
import concourse.mybir as mybir
from concourse.bass_utils import run_bass_kernel_spmd

F32 = mybir.dt.float32
BF16 = mybir.dt.bfloat16
ALU = mybir.AluOpType
AF = mybir.ActivationFunctionType

ENGS = ("pe", "act", "dve", "pool", "sp")


class Res:
    __slots__ = ("name", "w", "readers")

    def __init__(self, name):
        self.name = name
        self.w = None
        self.readers = []


class Op:
    __slots__ = ("eng", "fn", "semkey", "inc", "deps", "signal", "idx", "is_dma")

    def __init__(self, eng, fn, semkey, inc, is_dma):
        self.eng = eng
        self.fn = fn
        self.semkey = semkey
        self.inc = inc
        self.deps = []
        self.signal = False
        self.idx = 0
        self.is_dma = is_dma


class Prog:
    def __init__(self, nc):
        self.nc = nc
        self.streams = {e: [] for e in ENGS}
        self.by_sem = {}

    def res(self, name):
        return Res(name)

    def op(self, eng, fn, reads=(), writes=(), dma=None):
        is_dma = dma is not None
        semkey = ("dma", dma) if is_dma else eng
        o = Op(eng, fn, semkey, 16 if is_dma else 1, is_dma)
        if is_dma:
            o.signal = True
        deps = {}

        def need(p, kind):
            if p is None:
                return
            if (not p.is_dma) and (not is_dma) and p.eng == eng:
                if eng == "pe":
                    return
            deps[id(p)] = p

        for r in reads:
            need(r.w, "raw")
        for w in writes:
            need(w.w, "waw")
            for t in w.readers:
                need(t, "war")
        o.deps = list(deps.values())
        for p in o.deps:
            p.signal = True
        for w in writes:
            w.w = o
            w.readers = []
        for r in reads:
            if not is_dma:
                r.readers = [t for t in r.readers if t.is_dma or t.eng != eng]
            r.readers.append(o)
        self.streams[eng].append(o)
        self.by_sem.setdefault(semkey, []).append(o)
        return o

    def emit(self):
        nc = self.nc
        for k, ops in self.by_sem.items():
            c = 0
            for o in ops:
                if o.signal:
                    c += o.inc
                o.idx = c
        with contextlib.ExitStack() as st:
            sems = {}
            for k in self.by_sem:
                nm = "s_" + (k if isinstance(k, str) else "d_" + str(k[1]))
                sems[k] = st.enter_context(nc.semaphore(nm))
            block = st.enter_context(nc.Block())
            prog = self

            def run(eng_name, handle):
                waited = {}
                for o in prog.streams[eng_name]:
                    need = {}
                    for p in o.deps:
                        if p.idx > need.get(p.semkey, 0):
                            need[p.semkey] = p.idx
                    for k, v in need.items():
                        if waited.get(k, 0) >= v:
                            continue
                        handle.wait_ge(sems[k], v)
                        waited[k] = v
                    ins = o.fn(handle)
                    if o.signal:
                        ins.then_inc(sems[o.semkey], o.inc)

            @block.tensor
            def _(e):
                run("pe", e)

            @block.scalar
            def _(e):
                run("act", e)

            @block.vector
            def _(e):
                run("dve", e)

            @block.gpsimd
            def _(e):
                run("pool", e)

            @block.sync
            def _(e):
                run("sp", e)


C_AQ, C_AK, C_AV, C_AZ = 0, 1024, 1280, 1536
C_MQK, C_MV, C_MI, C_MF, C_MO, C_MZ, C_GA, C_GM = 2560, 3584, 4608, 4612, 4616, 5640, 6664, 7688
IN_W = 8712
T = 512
LOGK = -0.5 * math.log(128.0)


def build_nc(NSEQ, NST):
    nc = bass.Bass("TRN2", target_bir_lowering=False)
    S = NST * T

    def din(name, shape, dt=F32):
        return nc.dram_tensor(name, shape, dt, kind="ExternalInput").ap()

    x = din("x", [NSEQ, S, 1024])
    meta = din("meta_tokens", [16, 1024])
    norm_pre = din("norm_pre", [1024])
    w_in = din("w_in", [1024, IN_W])
    sinks = din("attn_sinks", [16])
    conv_w = din("conv_w", [4, 1024])
    conv_b = din("conv_b", [1024])
    gate_bias = din("mlstm_gate_bias", [8])
    head_norm = din("mlstm_head_norm", [1024])
    w_ao = din("w_attn_out", [1024, 1024])
    w_mo = din("w_mlstm_out", [1024, 1024])
    w_out = din("w_out", [1024, 1024])
    norm_post = din("norm_post", [1024])
    c_ident = din("c_ident", [128, 128])
    c_triu = din("c_triu", [128, 128])
    c_pit = din("c_pit", [128, 128])
    c_cos = din("c_cos", [NST, 128, T])
    c_sin = din("c_sin", [NST, 128, T])
    c_cosm = din("c_cosm", [128, 16])
    c_sinm = din("c_sinm", [128, 16])
    out = nc.dram_tensor("out", [NSEQ, S, 1024], F32, kind="ExternalOutput").ap()

    def dscr(name, shape):
        return nc.dram_tensor(name, shape, BF16, kind="Internal").ap()

    NWB = 24
    wbt = dscr("wbt", [NWB, 128, 8, 512])
    WB = {"K": 0, "V": 1, "Q": 2, "Z": 4, "GA": 6, "MQK": 8, "MV": 10, "MO": 12, "MZ": 14, "GM": 16, "AO": 18, "MOUT": 20, "OUT": 22}

    P = Prog(nc)
    with contextlib.ExitStack() as st:
        def sb(name, shape, dt=F32):
            return st.enter_context(nc.sbuf_tensor(name, shape, dt))

        NB = 7
        banks = [st.enter_context(nc.psum_tensor("pb%d" % i, [128, 512], F32)) for i in range(NB)]
        bank_res = [P.res("pb%d" % i) for i in range(NB)]
        pst = st.enter_context(nc.psum_tensor("pst", [128, 8, 128], BF16))
        r_pst = P.res("pst")
        bctr = [0]

        def bank():
            i = bctr[0] % NB
            bctr[0] += 1
            return banks[i], bank_res[i]

        ident_f = sb("ident_f", [128, 128]); r_identf = P.res("identf")
        ident_b = sb("ident_b", [128, 128], BF16); r_ident = P.res("ident")
        triu_f = sb("triu_f", [128, 128]); r_triu = P.res("triu")
        pit_f = sb("pit_f", [128, 128]); r_pitf = P.res("pitf")
        pit_b = sb("pit_b", [128, 128], BF16); r_pit = P.res("pit")
        ones_f = sb("ones_f", [128, 128]); r_onesf = P.res("onesf")
        ones_b = sb("ones_b", [128, 128], BF16); r_onesb = P.res("onesb")
        mask_b = sb("mask_b", [128, 4, 128], BF16); r_mask = P.res("mask")
        maskp_b = sb("maskp_b", [128, 4, 128], BF16); r_maskp = P.res("maskp")
        gB = sb("gB", [128, 1024]); r_gB = P.res("gB")
        hnB = sb("hnB", [128, 1024]); r_hnB = P.res("hnB")
        npB = sb("npB", [128, 1024]); r_npB = P.res("npB")
        gbias = sb("gbias", [128, 8]); r_gbias = P.res("gbias")
        cw = sb("cw", [128, 8, 4]); r_cw = P.res("cw")
        cb = sb("cb", [128, 8]); r_cb = P.res("cb")
        sk = sb("sk", [33, 16]); r_sk = P.res("sk")
        wgate = sb("wgate", [128, 8, 8], BF16); r_wgate = P.res("wgate")
        cosT = sb("cosT", [128, T]); r_cos = P.res("cos")
        sinT = sb("sinT", [128, T]); r_sin = P.res("sin")
        cosm = sb("cosm", [128, 16]); r_cosm = P.res("cosm")
        sinm = sb("sinm", [128, 16]); r_sinm = P.res("sinm")

        NSLOT = 3
        wring = [sb("wring%d" % i, [128, 8, 512], BF16) for i in range(NSLOT)]
        r_wring = [P.res("wring%d" % i) for i in range(NSLOT)]
        wctr = [0]

        xs = [sb("xs%d" % j, [128, 1024]) for j in range(4)]
        r_xs = [P.res("xs%d" % j) for j in range(4)]
        xbs = [sb("xb%d" % i, [128, 1024], BF16) for i in range(2)]
        r_xbs = [P.res("xb%d" % i) for i in range(2)]
        mhalf = sb("mhalf", [128, 8]); r_mhalf = P.res("mhalf")
        stt = [sb("stt%d" % j, [128, 4]) for j in range(4)]
        r_stt = [P.res("stt%d" % j) for j in range(4)]
        uT = sb("uT", [128, 8, T], BF16); r_uT = P.res("uT")
        bufA = sb("bufA", [128, 8, T], BF16); r_bufA = P.res("bufA")
        bufB = sb("bufB", [128, 8, T], BF16); r_bufB = P.res("bufB")
        bufC = sb("bufC", [128, 8, T], BF16); r_bufC = P.res("bufC")
        hzT = sb("hzT", [128, 8, T], BF16); r_hzT = P.res("hzT")
        mrg = sb("mrg", [128, 8, T], BF16); r_mrg = P.res("mrg")
        Kbuf = sb("Kbuf", [128, 4, 640], BF16); r_Kbuf = P.res("Kbuf")
        KmT = sb("KmT", [128, 4, 16], BF16); r_KmT = P.res("KmT")
        Vdup = [sb("Vaug%d" % i, [128, 4, 66], BF16) for i in range(5)]
        r_Vdup = [P.res("Vaug%d" % i) for i in range(5)]
        Vmeta = sb("Vmeta", [33, 4, 66], BF16); r_Vmeta = P.res("Vmeta")
        att_tok = [sb("att_tok%d" % i, [128, 1024], BF16) for i in range(2)]
        r_att_tok = [P.res("att_tok%d" % i) for i in range(2)]
        rec4 = [sb("rec4_%d" % i, [128, 4]) for i in range(2)]
        r_rec4 = [P.res("rec4_%d" % i) for i in range(2)]
        ptc = [sb("ptc%d" % i, [128, 512], BF16) for i in range(2)]
        r_ptc = [P.res("ptc%d" % i) for i in range(2)]
        ptp = [sb("ptp%d" % i, [128, 512], BF16) for i in range(2)]
        r_ptp = [P.res("ptp%d" % i) for i in range(2)]
        ptm = [sb("ptm%d" % g, [33, 512], BF16) for g in range(4)]
        r_ptm = [P.res("ptm%d" % g) for g in range(4)]
        raw = [sb("raw%d" % i, [128, 512], BF16) for i in range(2)]
        r_raw = [P.res("raw%d" % i) for i in range(2)]
        t1 = [sb("t1_%d" % i, [128, 512]) for i in range(2)]
        r_t1 = [P.res("t1_%d" % i) for i in range(2)]
        t2 = [sb("t2_%d" % i, [128, 512]) for i in range(2)]
        r_t2 = [P.res("t2_%d" % i) for i in range(2)]
        tzt = [sb("tzt%d" % i, [128, 512], BF16) for i in range(2)]
        r_tzt = [P.res("tzt%d" % i) for i in range(2)]
        zst = [sb("zst%d" % i, [128, 512], BF16) for i in range(2)]
        r_zst = [P.res("zst%d" % i) for i in range(2)]
        pre = [sb("pre%d" % i, [128, 515]) for i in range(2)]
        r_pre = [P.res("pre%d" % i) for i in range(2)]
        acc = [sb("acc%d" % i, [128, 512]) for i in range(2)]
        r_acc = [P.res("acc%d" % i) for i in range(2)]
        halo = sb("halo", [128, 8, 3]); r_halo = P.res("halo")
        halo_m = sb("halo_m", [128, 8, 3]); r_halom = P.res("halom")
        gsb = sb("gsb", [128, 4, 8]); r_gsb = P.res("gsb")
        e1 = sb("e1", [128, 4, 4]); r_e1 = P.res("e1")
        nlf = sb("nlf", [128, 16]); r_nlf = P.res("nlf")
        thr = sb("thr", [128, 16]); r_thr = P.res("thr")
        gs = sb("gs", [128, 16]); r_gs = P.res("gs")
        vsc = sb("vsc", [128, 16]); r_vsc = P.res("vsc")
        eB = sb("eB", [128, 16]); r_eB = P.res("eB")
        vaug = [sb("vaug%d" % j, [128, 4, 272], BF16) for j in range(4)]
        r_vaug = [P.res("vaug%d" % j) for j in range(4)]
        ktok = sb("ktok", [128, 4, 128], BF16); r_ktok = P.res("ktok")
        PTm = sb("PTm", [128, 512], BF16); r_PTm = P.res("PTm")
        C32 = sb("C32", [128, 4, 257]); r_C32 = [P.res("C32_%d" % h) for h in range(4)]
        Cbf = sb("Cbf", [128, 4, 272], BF16); r_Cbf = [P.res("Cbf_%d" % h) for h in range(4)]
        Cm32 = sb("Cm32", [128, 4, 257]); r_Cm32 = P.res("Cm32")
        Ue = [sb("Ue%d" % i, [128, 257]) for i in range(2)]
        r_Ue = [P.res("Ue%d" % i) for i in range(2)]
        dd = sb("dd", [128, 4, 4]); r_dd = [P.res("dd%d" % h) for h in range(4)]
        tmpN = sb("tmpN", [128, 1024]); r_tmpN = P.res("tmpN")
        ho = sb("ho", [128, 1024]); r_ho = P.res("ho")
        st6 = sb("st6", [128, 4, 6]); r_st6 = P.res("st6")
        mv = sb("mv", [128, 4, 2]); r_mv = P.res("mv")
        lnv = sb("lnv", [128, 8]); r_lnv = P.res("lnv")
        hzs = [sb("hz%d" % i, [128, 1024], BF16) for i in range(2)]
        r_hzs = [P.res("hz%d" % i) for i in range(2)]
        hz, r_hz = hzs[0], r_hzs[0]
        ot = [sb("ot%d" % i, [128, 1024]) for i in range(2)]
        r_ot = [P.res("ot%d" % i) for i in range(2)]
        ss = [sb("ss%d" % i, [128, 8]) for i in range(2)]
        r_ss = [P.res("ss%d" % i) for i in range(2)]
        r_outs = []

        def new_out():
            r = P.res("out%d" % len(r_outs))
            r_outs.append(r)
            return r

        ctr = {"raw": 0, "tz": 0, "pre": 0, "pt": 0, "ot": 0, "xb": 0, "hz": 0, "rec4": 0, "ue": 0}

        def rr(key, n=2):
            i = ctr[key] % n
            ctr[key] += 1
            return i

        A = P.op

        r_castb = {}

        def cast_blk(bi, src, c0, n):
            r = P.res("cast%d" % bi)
            A("pool", lambda e: e.dma_start(out=wbt[bi][:, :, 0:n], in_=src[:, c0:c0 + n].rearrange("(k p) n -> p k n", p=128)),
              writes=[r], dma="cast%d" % bi)
            r_castb[bi] = [r]

        rk = []
        for g in range(4):
            for d in range(2):
                r = P.res("castK%d%d" % (g, d))
                A("pool", lambda e, g=g, d=d: e.dma_start(out=wbt[0][:, :, g * 128 + d * 64: g * 128 + d * 64 + 64],
                                                      in_=w_in[:, C_AK + g * 64: C_AK + g * 64 + 64].rearrange("(k p) n -> p k n", p=128)),
                  writes=[r], dma="castK")
                rk.append(r)
        r_castb[0] = rk
        cast_blk(1, w_in, C_AV, 256)
        for nm, c0 in (("Q", C_AQ), ("Z", C_AZ), ("GA", C_GA)):
            for i in range(2):
                cast_blk(WB[nm] + i, w_in, c0 + i * 512, 512)
        for i in range(2):
            cast_blk(WB["AO"] + i, w_ao, i * 512, 512)
        r_wgate_c = P.res("wgate_c")
        for nm, c0 in (("MQK", C_MQK), ("MV", C_MV), ("MO", C_MO), ("MZ", C_MZ), ("GM", C_GM)):
            for i in range(2):
                cast_blk(WB[nm] + i, w_in, c0 + i * 512, 512)
        for i in range(2):
            cast_blk(WB["MOUT"] + i, w_mo, i * 512, 512)
        for i in range(2):
            cast_blk(WB["OUT"] + i, w_out, i * 512, 512)

        def ld(dst, src, r, name, **kw):
            A("sp", lambda e: e.dma_start(out=dst, in_=src, **kw), writes=[r], dma=name)

        ld(ident_f[:], c_ident, r_identf, "identf")
        ld(triu_f[:], c_triu, r_triu, "triu")
        ld(pit_f[:], c_pit, r_pitf, "pitf")
        ld(gB[:], norm_pre.partition_broadcast(128), r_gB, "gB")
        ld(hnB[:], head_norm.partition_broadcast(128), r_hnB, "hnB")
        ld(npB[:], norm_post.partition_broadcast(128), r_npB, "npB")
        ld(gbias[:], gate_bias.partition_broadcast(128), r_gbias, "gbias")
        for jj in range(4):
            ld(cw[:, :, jj], conv_w[jj].rearrange("(b p) -> p b", p=128), r_cw, "cw", allow_slow_non_contiguous=True)
        ld(cb[:], conv_b.rearrange("(b p) -> p b", p=128), r_cb, "cb", allow_slow_non_contiguous=True)
        ld(sk[32:33, :], sinks.rearrange("(o n) -> o n", o=1), r_sk, "sk")
        ld(cosm[:], c_cosm, r_cosm, "cosm")
        ld(sinm[:], c_sinm, r_sinm, "sinm")
        A("pool", lambda e: e.dma_start(out=wgate[:], in_=w_in[:, C_MI:C_MI + 8].rearrange("(k p) n -> p k n", p=128),
                                        allow_slow_non_contiguous=True), writes=[r_wgate], dma="wgate")
        A("dve", lambda e: e.tensor_copy(out=ident_b[:], in_=ident_f[:]), [r_identf], [r_ident])
        A("dve", lambda e: e.tensor_copy(out=pit_b[:], in_=pit_f[:]), [r_pitf], [r_pit])
        A("dve", lambda e: e.memset(ones_f[:], 1.0), [], [r_onesf])
        A("dve", lambda e: e.memset(ones_b[:], 1.0), [], [r_onesb])
        A("dve", lambda e: e.tensor_copy(out=mask_b[:], in_=triu_f[:].unsqueeze(1).to_broadcast([128, 4, 128])), [r_triu], [r_mask])
        A("dve", lambda e: e.tensor_scalar(out=maskp_b[:], in0=triu_f[:].unsqueeze(1).to_broadcast([128, 4, 128]),
                                           scalar1=-1.0, scalar2=1.0, op0=ALU.mult, op1=ALU.add), [r_triu], [r_maskp])
        A("dve", lambda e: e.tensor_scalar(out=cw[:], in0=cw[:], scalar1=0.5, scalar2=None, op0=ALU.mult), [r_cw], [r_cw])
        A("dve", lambda e: e.tensor_scalar(out=cb[:], in0=cb[:], scalar1=0.5, scalar2=None, op0=ALU.mult), [r_cb], [r_cb])
        A("pool", lambda e: e.memset(Vmeta[:], 0.0), [], [r_Vmeta])
        A("pool", lambda e: e.memset(Vmeta[0:16, :, 64:65], 1.0), [], [r_Vmeta])
        A("pool", lambda e: e.memset(Vmeta[32:33, :, 64:65], 1.0), [], [r_Vmeta])
        for i in range(5):
            A("pool", lambda e, i=i: e.memset(Vdup[i][:], 0.0), [], [r_Vdup[i]])
            A("pool", lambda e, i=i: e.memset(Vdup[i][:, :, 64:65], 1.0), [], [r_Vdup[i]])
        A("pool", lambda e: e.memset(mhalf[:], -0.5), [], [r_mhalf])
        A("dve", lambda e: e.tensor_scalar(out=gB[:], in0=gB[:], scalar1=32.0, scalar2=None, op0=ALU.mult), [r_gB], [r_gB])
        A("dve", lambda e: e.tensor_scalar(out=npB[:], in0=npB[:], scalar1=32.0, scalar2=None, op0=ALU.mult), [r_npB], [r_npB])
        for g in range(4):
            A("pool", lambda e, g=g: e.memset(ptm[g][:], 0.0), [], [r_ptm[g]])
            A("act", lambda e, g=g: e.activation(out=ptm[g][32:33, :].rearrange("p (h q) -> p h q", h=4),
                                                 in_=sk[32:33, 4 * g:4 * g + 4].unsqueeze(2).to_broadcast([1, 4, 128]),
                                                 func=AF.Exp), [r_sk, r_ptm[g]], [r_ptm[g]])
        for j in range(4):
            A("pool", lambda e, j=j: e.memset(vaug[j][:], 0.0), [], [r_vaug[j]])
        A("pool", lambda e: e.memset(Cbf[:], 0.0), [], r_Cbf)
        A("pool", lambda e: e.memset(halo_m[:], 0.0), [], [r_halom])

        def load_w(bi, n=512):
            i = wctr[0] % NSLOT
            wctr[0] += 1
            A("sp", lambda e: e.dma_start(out=wring[i][:, :, 0:n], in_=wbt[bi][:, :, 0:n]),
              reads=r_castb[bi], writes=[r_wring[i]], dma="w%d" % i)
            return wring[i], r_wring[i]

        def proj_fm(w, c0, rhs, r_rhs, N, n0=0):
            wt, r_w = w
            b, rb = bank()
            for kc in range(8):
                A("pe", lambda e, kc=kc: e.matmul(b[:, 0:N], lhsT=wt[:, kc, c0:c0 + 128], rhs=rhs[:, kc, n0:n0 + N],
                                                 start=(kc == 0), stop=(kc == 7)), [r_w, r_rhs], [rb])
            return b, rb

        def proj_tm(w, n, lhs, r_lhs, t0, nt):
            wt, r_w = w
            b, rb = bank()
            for kc in range(8):
                A("pe", lambda e, kc=kc: e.matmul(b[0:nt, 0:n], lhsT=lhs[:, kc, t0:t0 + nt], rhs=wt[:, kc, 0:n],
                                                 start=(kc == 0), stop=(kc == 7)), [r_w, r_lhs], [rb])
            return b, rb

        pending = []

        def flush():
            while pending:
                pending.pop(0)()

        def rope(b, rb, N, cos_ap, sin_ap, r_tabs, scale, dst, r_dst):
            i = rr("raw")
            A("act", lambda e: e.activation(out=raw[i][:, 0:N], in_=b[:, 0:N], func=AF.Copy, scale=scale), [rb], [r_raw[i]])

            def stage_b():
                b2, rb2 = bank()
                A("pe", lambda e: e.matmul(b2[:, 0:N], lhsT=pit_b[:], rhs=raw[i][:, 0:N], start=True, stop=True), [r_pit, r_raw[i]], [rb2])
                A("dve", lambda e: e.tensor_tensor(out=t1[i][:, 0:N], in0=raw[i][:, 0:N], in1=cos_ap, op=ALU.mult), [r_raw[i]] + r_tabs, [r_t1[i]])
                A("dve", lambda e: e.tensor_tensor(out=t2[i][:, 0:N], in0=b2[:, 0:N], in1=sin_ap, op=ALU.mult), [rb2] + r_tabs, [r_t2[i]])
                A("pool", lambda e: e.tensor_tensor(out=dst, in0=t1[i][:, 0:N], in1=t2[i][:, 0:N], op=ALU.add), [r_t1[i], r_t2[i]], [r_dst])
            pending.append(stage_b)

        def rmsnorm_front(src, r_src, n, stat, r_stat):
            xi = rr("xb")
            xb_, r_xb_ = xbs[xi], r_xbs[xi]
            A("act", lambda e: e.activation(out=xb_[0:n, :], in_=src, func=AF.Square, accum_out=stat[0:n, 0:1]), [r_src], [r_xb_, r_stat])
            A("pool", lambda e: e.tensor_scalar(out=stat[0:n, 1:2], in0=stat[0:n, 0:1], scalar1=1024 * 1e-6, scalar2=None, op0=ALU.add), [r_stat], [r_stat])
            A("pool", lambda e: e.tensor_tensor(out=stat[0:n, 2:3], in0=stat[0:n, 1:2], in1=mhalf[0:n, 0:1], op=ALU.pow), [r_stat, r_mhalf], [r_stat])
            A("dve", lambda e: e.scalar_tensor_tensor(out=xb_[0:n, :], in0=src, scalar=stat[0:n, 2:3], in1=gB[0:n, :],
                                                      op0=ALU.mult, op1=ALU.mult), [r_src, r_stat, r_gB], [r_xb_])
            return xb_, r_xb_

        def rmsnorm_back(xbr, n, dstT, r_dstT, t0):
            xb_, r_xb_ = xbr
            for k in range(8):
                A("pe", lambda e, k=k: e.transpose(out=pst[:, k, 0:n], in_=xb_[0:n, k * 128:(k + 1) * 128], identity=ident_b[0:n, 0:n]),
                  [r_xb_, r_ident], [r_pst])
            A("act", lambda e: e.activation(out=dstT[:, :, t0:t0 + n], in_=pst[:, :, 0:n], func=AF.Copy), [r_pst], [r_dstT])

        def rmsnorm_T(src, r_src, n, stat, r_stat, dstT, r_dstT, t0):
            rmsnorm_back(rmsnorm_front(src, r_src, n, stat, r_stat), n, dstT, r_dstT, t0)

        def conv_silu(b, rb, blk, N, halo_src, r_halo_src, halo_dst, r_halo_dst, dst, r_dst):
            i = rr("pre")
            p_ = pre[i]
            A("act", lambda e: e.activation(out=p_[:, 3:3 + N], in_=b[:, 0:N], func=AF.Copy), [rb], [r_pre[i]])
            A("pool", lambda e: e.tensor_copy(out=p_[:, 0:3], in_=halo_src[:, blk, :]), [r_halo_src], [r_pre[i]])
            a_ = acc[i]
            A("act", lambda e: e.activation(out=a_[:, 0:N], in_=b[:, 0:N], func=AF.Identity, scale=cw[:, blk, 3:4], bias=cb[:, blk:blk + 1]),
              [rb, r_cw, r_cb], [r_acc[i]])
            for jj in (2, 1, 0):
                A("dve", lambda e, jj=jj: e.scalar_tensor_tensor(out=a_[:, 0:N], in0=p_[:, jj:jj + N], scalar=cw[:, blk, jj:jj + 1],
                                                                 in1=a_[:, 0:N], op0=ALU.mult, op1=ALU.add), [r_pre[i], r_cw, r_acc[i]], [r_acc[i]])
            A("pool", lambda e: e.tensor_copy(out=halo_dst[:, blk, :], in_=p_[:, N:N + 3]), [r_pre[i]], [r_halo_dst])
            k = rr("tz")
            A("act", lambda e: e.activation(out=tzt[k][:, 0:N], in_=a_[:, 0:N], func=AF.Tanh), [r_acc[i]], [r_tzt[k]])
            A("dve", lambda e: e.scalar_tensor_tensor(out=dst, in0=tzt[k][:, 0:N], scalar=1.0, in1=a_[:, 0:N], op0=ALU.add, op1=ALU.mult),
              [r_tzt[k], r_acc[i]], [r_dst])

        xm = ot[0]
        A("sp", lambda e: e.dma_start(out=xm[0:16, :], in_=meta), writes=[r_ot[0]], dma="xm")
        uTm = sb("uTm", [128, 8, 16], BF16); r_uTm = P.res("uTm")
        qkm = sb("qkm", [128, 8, 16], BF16); r_qkm = P.res("qkm")
        rmsnorm_T(xm[0:16, :], r_ot[0], 16, stt[0], r_stt[0], uTm, r_uTm, 0)
        wk = load_w(WB["K"])
        for g in range(4):
            b, rb = proj_fm(wk, g * 128, uTm, r_uTm, 16)
            flush()
            rope(b, rb, 16, cosm[:], sinm[:], [r_cosm, r_sinm], 1.0, KmT[:, g, :], r_KmT)
        flush()
        wv = load_w(WB["V"], 256)
        b, rb = proj_tm(wv, 256, uTm, r_uTm, 0, 16)
        A("act", lambda e, b=b: e.activation(out=Vmeta[0:16, :, 0:64], in_=b[0:16, 0:256].rearrange("p (g d) -> p g d", g=4), func=AF.Copy), [rb], [r_Vmeta])
        bgm_, rbgm = bank()
        for kc in range(8):
            A("pe", lambda e, kc=kc: e.matmul(bgm_[0:16, 0:8], lhsT=uTm[:, kc, 0:16], rhs=wgate[:, kc, :], start=(kc == 0), stop=(kc == 7)),
              [r_uTm, r_wgate], [rbgm])
        A("dve", lambda e: e.tensor_tensor(out=gsb[0:16, 0, :], in0=bgm_[0:16, 0:8], in1=gbias[0:16, :], op=ALU.add), [rbgm, r_gbias], [r_gsb])
        A("act", lambda e: e.activation(out=e1[0:16, 0, :], in_=gsb[0:16, 0, 4:8], func=AF.Exp, scale=-1.0), [r_gsb], [r_e1])
        A("act", lambda e: e.activation(out=nlf[0:16, 0:4], in_=e1[0:16, 0, :], func=AF.Ln, bias=1.0), [r_e1], [r_nlf])
        bnbm, rbnbm = bank()
        A("pe", lambda e: e.matmul(bnbm[0:16, 0:4], lhsT=triu_f[0:16, 0:16], rhs=nlf[0:16, 0:4], start=True, stop=True), [r_triu, r_nlf], [rbnbm])
        bnsm, rbnsm = bank()
        A("pe", lambda e: e.matmul(bnsm[:, 0:4], lhsT=ones_f[0:16, :], rhs=nlf[0:16, 0:4], start=True, stop=True), [r_onesf, r_nlf], [rbnsm])
        A("dve", lambda e: e.tensor_tensor(out=gs[0:16, 0:4], in0=gsb[0:16, 0, 0:4], in1=bnbm[0:16, 0:4], op=ALU.add), [r_gsb, rbnbm], [r_gs])
        A("act", lambda e: e.activation(out=vsc[0:16, 0:4], in_=gs[0:16, 0:4], func=AF.Exp, bias=LOGK), [r_gs], [r_vsc])
        A("act", lambda e: e.activation(out=eB[:, 0:4], in_=bnsm[:, 0:4], func=AF.Exp, scale=-1.0), [rbnsm], [r_eB])
        for i in range(2):
            w = load_w(WB["MQK"] + i)
            for c in range(4):
                blk = 4 * i + c
                b, rb = proj_fm(w, c * 128, uTm, r_uTm, 16)
                conv_silu(b, rb, blk, 16, halo_m, r_halom, halo_m, r_halom, qkm[:, blk, :], r_qkm)
        vaugm = vaug[0]
        for i in range(2):
            w = load_w(WB["MV"] + i)
            b, rb = proj_tm(w, 512, uTm, r_uTm, 0, 16)
            for hh in range(2):
                h = 2 * i + hh
                A("act", lambda e, b=b, hh=hh, h=h: e.activation(out=vaugm[0:16, h, 0:256], in_=b[0:16, hh * 256:(hh + 1) * 256], func=AF.Copy,
                                                              scale=vsc[0:16, h:h + 1]), [rb, r_vsc], [r_vaug[0]])
        A("pool", lambda e: e.tensor_copy(out=vaugm[0:16, :, 256:257], in_=vsc[0:16, 0:4].unsqueeze(2)), [r_vsc], [r_vaug[0]])
        for h in range(4):
            A("pe", lambda e, h=h: e.transpose(out=pst[0:16, h, :], in_=qkm[:, 4 + h, :], identity=ident_b[:]), [r_qkm, r_ident], [r_pst])
        A("act", lambda e: e.activation(out=ktok[0:16, :, :], in_=pst[0:16, 0:4, :], func=AF.Copy), [r_pst], [r_ktok])
        for h in range(4):
            bU, rbU = bank()
            A("pe", lambda e, h=h, bU=bU: e.matmul(bU[:, 0:257], lhsT=ktok[0:16, h, :], rhs=vaugm[0:16, h, 0:257], start=True, stop=True),
              [r_ktok, r_vaug[0]], [rbU])
            A("dve", lambda e, h=h, bU=bU: e.tensor_scalar(out=Cm32[:, h, :], in0=bU[:, 0:257], scalar1=eB[:, h:h + 1], scalar2=None, op0=ALU.mult),
              [rbU, r_eB], [r_Cm32])
        A("pool", lambda e: e.memset(vaug[0][:], 0.0), [], [r_vaug[0]])

        def phase0_load(s_, ti_, j):
            t0_ = ti_ * T + j * 128
            A("sp", lambda e: e.dma_start(out=xs[j][:], in_=x[s_, t0_: t0_ + 128, :]), writes=[r_xs[j]], dma="xs%d" % j)

        def phase0_tabs(ti_):
            A("sp", lambda e: e.dma_start(out=cosT[:], in_=c_cos[ti_]), writes=[r_cos], dma="cos")
            A("sp", lambda e: e.dma_start(out=sinT[:], in_=c_sin[ti_]), writes=[r_sin], dma="sin")

        def phase0_norm(j):
            rmsnorm_T(xs[j][:], r_xs[j], 128, stt[j], r_stt[j], uT, r_uT, j * 128)

        def phase0_front(j):
            return rmsnorm_front(xs[j][:], r_xs[j], 128, stt[j], r_stt[j])

        def phase0_back(j, xbr):
            rmsnorm_back(xbr, 128, uT, r_uT, j * 128)

        wk_pref = [None]
        for s in range(NSEQ):
            A("pool", lambda e: e.tensor_copy(out=C32[:], in_=Cm32[:]), [r_Cm32], r_C32)
            A("pool", lambda e: e.tensor_copy(out=Cbf[:, :, 0:257], in_=Cm32[:]), [r_Cm32], r_Cbf)
            A("pool", lambda e: e.tensor_copy(out=halo[:], in_=halo_m[:]), [r_halom], [r_halo])
            for ti in range(NST):
                tok0 = ti * T
                if s == 0 and ti == 0:
                    phase0_tabs(ti)
                    for j in range(4):
                        phase0_load(s, ti, j)
                    for j in range(4):
                        phase0_norm(j)
                nxt = (s, ti + 1) if ti + 1 < NST else ((s + 1, 0) if s + 1 < NSEQ else None)
                wk = wk_pref[0] if wk_pref[0] is not None else load_w(WB["K"])
                wk_pref[0] = None
                for g in range(4):
                    b, rb = proj_fm(wk, g * 128, uT, r_uT, T)
                    flush()
                    rope(b, rb, T, cosT[:], sinT[:], [r_cos, r_sin], 1.0, Kbuf[:, g, 128:640], r_Kbuf)
                wv = load_w(WB["V"], 256)
                for j in range(4):
                    b, rb = proj_tm(wv, 256, uT, r_uT, j * 128, 128)
                    A("act", lambda e, b=b, j=j: e.activation(out=Vdup[j + 1][:, :, 0:64], in_=b[:, 0:256].rearrange("p (g d) -> p g d", g=4), func=AF.Copy),
                      [rb], [r_Vdup[j + 1]])
                QT, r_QT = bufA, r_bufA
                attT, r_attT = bufB, r_bufB
                for i in range(2):
                    w = load_w(WB["Q"] + i)
                    for c in range(4):
                        blk = 4 * i + c
                        b, rb = proj_fm(w, c * 128, uT, r_uT, T)
                        flush()
                        rope(b, rb, T, cosT[:], sinT[:], [r_cos, r_sin], 0.125, QT[:, blk, :], r_QT)
                flush()
                def att_S(j, g):
                    has_prev = not (ti == 0 and j == 0)
                    jc = slice(j * 128, (j + 1) * 128)
                    pi = rr("pt")
                    bSc, rSc = bank()
                    bSm, rSm = bank()
                    bSp, rSp = bank() if has_prev else (None, None)
                    for hh in range(4):
                        blk = 2 * g + hh // 2
                        rows = slice((hh % 2) * 64, (hh % 2) * 64 + 64)
                        hc = slice(hh * 128, (hh + 1) * 128)
                        A("pe", lambda e, rows=rows, hc=hc, blk=blk: e.matmul(
                            bSc[:, hc], lhsT=Kbuf[rows, g, 128 + j * 128: 256 + j * 128], rhs=QT[rows, blk, jc], start=True, stop=True),
                          [r_Kbuf, r_QT], [rSc])
                        A("pe", lambda e, rows=rows, hc=hc, blk=blk: e.matmul(
                            bSm[0:16, hc], lhsT=KmT[rows, g, :], rhs=QT[rows, blk, jc], start=True, stop=True),
                          [r_KmT, r_QT], [rSm])
                        if has_prev:
                            A("pe", lambda e, rows=rows, hc=hc, blk=blk: e.matmul(
                                bSp[:, hc], lhsT=Kbuf[rows, g, j * 128: 128 + j * 128], rhs=QT[rows, blk, jc], start=True, stop=True),
                              [r_Kbuf, r_QT], [rSp])
                    A("act", lambda e: e.activation(out=ptc[pi][:], in_=bSc[:], func=AF.Exp), [rSc], [r_ptc[pi]])
                    A("act", lambda e: e.activation(out=ptm[g][0:16, :], in_=bSm[0:16, :], func=AF.Exp), [rSm], [r_ptm[g]])
                    A("pool", lambda e: e.tensor_tensor(out=ptc[pi][:], in0=ptc[pi][:], in1=mask_b[:].rearrange("p h q -> p (h q)"), op=ALU.mult),
                      [r_ptc[pi], r_mask], [r_ptc[pi]])
                    if has_prev:
                        A("act", lambda e: e.activation(out=ptp[pi][:], in_=bSp[:], func=AF.Exp), [rSp], [r_ptp[pi]])
                        A("pool", lambda e: e.tensor_tensor(out=ptp[pi][:], in0=ptp[pi][:], in1=maskp_b[:].rearrange("p h q -> p (h q)"), op=ALU.mult),
                          [r_ptp[pi], r_maskp], [r_ptp[pi]])
                    return (j, g, pi, has_prev, jc)

                def att_PV(ctx):
                    j, g, pi, has_prev, jc = ctx
                    ai = j % 2
                    bO, rO = bank()
                    for hh in range(4):
                        hc = slice(hh * 128, (hh + 1) * 128)
                        oc = slice(hh * 65, hh * 65 + 65)
                        A("pe", lambda e, hc=hc, oc=oc: e.matmul(bO[:, oc], lhsT=ptm[g][:, hc], rhs=Vmeta[:, g, 0:65], start=True, stop=False),
                          [r_Vmeta, r_ptm[g]], [rO])
                        if has_prev:
                            A("pe", lambda e, hc=hc, oc=oc: e.matmul(bO[:, oc], lhsT=ptp[pi][:, hc], rhs=Vdup[j][:, g, 0:65], start=False, stop=False),
                              [r_Vdup[j], r_ptp[pi]], [rO])
                        A("pe", lambda e, hc=hc, oc=oc: e.matmul(bO[:, oc], lhsT=ptc[pi][:, hc], rhs=Vdup[j + 1][:, g, 0:65], start=False, stop=True),
                          [r_Vdup[j + 1], r_ptc[pi]], [rO])
                    ri = rr("rec4")
                    bOv = bO[:, 0:260].rearrange("p (h c) -> p h c", c=65)
                    A("dve", lambda e: e.reciprocal(out=rec4[ri][:], in_=bOv[:, :, 64]), [rO], [r_rec4[ri]])
                    A("dve", lambda e: e.tensor_tensor(out=att_tok[ai][:, g * 256:(g + 1) * 256].rearrange("p (h d) -> p h d", h=4), in0=bOv[:, :, 0:64],
                                                       in1=rec4[ri][:].unsqueeze(2).to_broadcast([128, 4, 64]), op=ALU.mult),
                      [rO, r_rec4[ri]], [r_att_tok[ai]])
                    if g == 3:
                        for k in range(8):
                            A("pe", lambda e, k=k: e.transpose(out=pst[:, k, :], in_=att_tok[ai][:, k * 128:(k + 1) * 128], identity=ident_b[:]),
                              [r_att_tok[ai], r_ident], [r_pst])
                        A("act", lambda e: e.activation(out=attT[:, :, jc], in_=pst[:], func=AF.Copy), [r_pst], [r_attT])

                order = [(j, g) for j in range(4) for g in range(4)]
                ctxs = [att_S(*order[0])]
                for idx in range(len(order)):
                    if idx + 1 < len(order):
                        ctxs.append(att_S(*order[idx + 1]))
                    att_PV(ctxs[idx])
                for i in range(2):
                    w = load_w(WB["Z"] + i)
                    for c in range(4):
                        blk = 4 * i + c
                        b, rb = proj_fm(w, c * 128, uT, r_uT, T)
                        k = rr("tz")
                        A("act", lambda e, b=b, k=k: e.activation(out=tzt[k][:], in_=b[:], func=AF.Tanh, scale=0.5), [rb], [r_tzt[k]])
                        A("dve", lambda e, b=b, k=k: e.scalar_tensor_tensor(out=zst[k][:], in0=tzt[k][:], scalar=1.0, in1=b[:], op0=ALU.add, op1=ALU.mult),
                          [r_tzt[k], rb], [r_zst[k]])
                        A("pool", lambda e, k=k, blk=blk: e.tensor_tensor(out=attT[:, blk, :], in0=attT[:, blk, :], in1=zst[k][:], op=ALU.mult),
                          [r_attT, r_zst[k]], [r_attT])
                for i in range(2):
                    wg = load_w(WB["GA"] + i)
                    wa = load_w(WB["AO"] + i)
                    for c in range(4):
                        blk = 4 * i + c
                        b, rb = proj_fm(wg, c * 128, uT, r_uT, T)
                        k = rr("tz")
                        A("act", lambda e, b=b, k=k: e.activation(out=tzt[k][:], in_=b[:], func=AF.Tanh, scale=0.5), [rb], [r_tzt[k]])
                        by, rby = proj_fm(wa, c * 128, attT, r_attT, T)
                        A("dve", lambda e, by=by, k=k, blk=blk: e.scalar_tensor_tensor(out=mrg[:, blk, :], in0=tzt[k][:], scalar=1.0, in1=by[:],
                                                                                    op0=ALU.add, op1=ALU.mult), [r_tzt[k], rby], [r_mrg])
                if nxt is not None:
                    phase0_tabs(nxt[1])
                    for j in range(4):
                        phase0_load(nxt[0], nxt[1], j)
                bg, rbg = bank()
                for j in range(4):
                    for kc in range(8):
                        A("pe", lambda e, kc=kc, j=j, bg=bg: e.matmul(bg[:, j * 8:(j + 1) * 8], lhsT=uT[:, kc, j * 128:(j + 1) * 128], rhs=wgate[:, kc, :],
                                                                 start=(kc == 0), stop=(kc == 7)), [r_uT, r_wgate], [rbg])
                A("dve", lambda e, bg=bg: e.tensor_tensor(out=gsb[:], in0=bg[:, 0:32].rearrange("p (j c) -> p j c", j=4),
                                                        in1=gbias[:].unsqueeze(1).to_broadcast([128, 4, 8]), op=ALU.add), [rbg, r_gbias], [r_gsb])
                A("act", lambda e: e.activation(out=e1[:], in_=gsb[:, :, 4:8], func=AF.Exp, scale=-1.0), [r_gsb], [r_e1])
                A("act", lambda e: e.activation(out=nlf[:], in_=e1[:].rearrange("p j h -> p (j h)"), func=AF.Ln, bias=1.0), [r_e1], [r_nlf])
                bnb, rbnb = bank()
                A("pe", lambda e, bnb=bnb: e.matmul(bnb[:, 0:16], lhsT=triu_f[:], rhs=nlf[:], start=True, stop=True), [r_triu, r_nlf], [rbnb])
                bns, rbns = bank()
                A("pe", lambda e, bns=bns: e.matmul(bns[:, 0:16], lhsT=ones_f[:], rhs=nlf[:], start=True, stop=True), [r_onesf, r_nlf], [rbns])
                A("act", lambda e, bnb=bnb: e.activation(out=thr[:], in_=bnb[:, 0:16], func=AF.Exp), [rbnb], [r_thr])
                A("dve", lambda e, bnb=bnb: e.tensor_tensor(out=gs[:].rearrange("p (j h) -> p j h", j=4), in0=gsb[:, :, 0:4],
                                                          in1=bnb[:, 0:16].rearrange("p (j h) -> p j h", j=4), op=ALU.add), [r_gsb, rbnb], [r_gs])
                A("act", lambda e: e.activation(out=vsc[:], in_=gs[:], func=AF.Exp, bias=LOGK), [r_gs], [r_vsc])
                A("act", lambda e, bns=bns: e.activation(out=eB[:], in_=bns[:, 0:16], func=AF.Exp, scale=-1.0), [rbns], [r_eB])
                qkT, r_qkT = bufC, r_bufC
                for i in range(2):
                    w = load_w(WB["MQK"] + i)
                    for c in range(4):
                        blk = 4 * i + c
                        b, rb = proj_fm(w, c * 128, uT, r_uT, T)
                        conv_silu(b, rb, blk, T, halo, r_halo, halo, r_halo, qkT[:, blk, :], r_qkT)
                for i in range(2):
                    w = load_w(WB["MV"] + i)
                    for j in range(4):
                        b, rb = proj_tm(w, 512, uT, r_uT, j * 128, 128)
                        for hh in range(2):
                            h = 2 * i + hh
                            A("act", lambda e, b=b, hh=hh, h=h, j=j: e.activation(out=vaug[j][:, h, 0:256], in_=b[:, hh * 256:(hh + 1) * 256], func=AF.Copy,
                                                                             scale=vsc[:, j * 4 + h: j * 4 + h + 1]), [rb, r_vsc], [r_vaug[j]])
                for j in range(4):
                    A("pool", lambda e, j=j: e.tensor_copy(out=vaug[j][:, :, 256:257], in_=vsc[:, j * 4:(j + 1) * 4].unsqueeze(2)), [r_vsc], [r_vaug[j]])
                th = bufB[:].rearrange("p k t -> p (k t)").rearrange("p (j f) -> p j f", j=4)
                r_th = r_bufB
                zs = bufA[:].rearrange("p k t -> p (k t)").rearrange("p (j f) -> p j f", j=4)
                r_zs = r_bufA
                for i in range(2):
                    w = load_w(WB["MO"] + i)
                    for j in range(4):
                        b, rb = proj_tm(w, 512, uT, r_uT, j * 128, 128)
                        A("act", lambda e, b=b, j=j, i=i: e.activation(out=th[:, j, i * 512:(i + 1) * 512], in_=b[:], func=AF.Tanh, scale=0.5), [rb], [r_th])
                for i in range(2):
                    w = load_w(WB["MZ"] + i)
                    for j in range(4):
                        b, rb = proj_tm(w, 512, uT, r_uT, j * 128, 128)
                        k = rr("tz")
                        A("act", lambda e, b=b, k=k: e.activation(out=tzt[k][:], in_=b[:], func=AF.Tanh, scale=0.5), [rb], [r_tzt[k]])
                        A("dve", lambda e, b=b, k=k: e.scalar_tensor_tensor(out=zst[k][:], in0=tzt[k][:], scalar=1.0, in1=b[:],
                                                                          op0=ALU.add, op1=ALU.mult), [r_tzt[k], rb], [r_zst[k]])
                        A("pool", lambda e, k=k, j=j, i=i: e.tensor_tensor(out=zs[:, j, i * 512:(i + 1) * 512], in0=zst[k][:], in1=hnB[:, i * 512:(i + 1) * 512], op=ALU.mult),
                          [r_zst[k], r_hnB], [r_zs])
                def rec_pe(j):
                    jc = slice(j * 128, (j + 1) * 128)
                    for h in range(4):
                        A("pe", lambda e, h=h: e.transpose(out=pst[:, h, :], in_=qkT[:, 4 + h, jc], identity=ident_b[:]), [r_qkT, r_ident], [r_pst])
                    A("act", lambda e: e.activation(out=ktok[:], in_=pst[:, 0:4, :], func=AF.Copy), [r_pst], [r_ktok])
                    bS, rS = bank()
                    for h in range(4):
                        A("pe", lambda e, h=h: e.matmul(bS[:, h * 128:(h + 1) * 128], lhsT=qkT[:, 4 + h, jc], rhs=qkT[:, h, jc], start=True, stop=True),
                          [r_qkT], [rS])
                    A("dve", lambda e: e.tensor_tensor(out=PTm[:], in0=bS[:], in1=mask_b[:].rearrange("p h q -> p (h q)"), op=ALU.mult), [rS, r_mask], [r_PTm])
                    bNs = [bank(), bank()]
                    bDn, rDn = bank()
                    for h in range(4):
                        bN, rN = bNs[h // 2]
                        nc_ = slice((h % 2) * 256, (h % 2) * 256 + 256)
                        A("pe", lambda e, h=h, bN=bN, nc_=nc_: e.matmul(bN[:, nc_], lhsT=PTm[:, h * 128:(h + 1) * 128], rhs=vaug[j][:, h, 0:256], start=True, stop=False),
                          [r_PTm, r_vaug[j]], [rN])
                        A("pe", lambda e, h=h, bN=bN, nc_=nc_: e.matmul(bN[:, nc_], lhsT=qkT[:, h, jc], rhs=Cbf[:, h, 0:256], start=False, stop=True),
                          [r_qkT, r_Cbf[h]], [rN])
                        A("pe", lambda e, h=h: e.matmul(bDn[:, h:h + 1], lhsT=PTm[:, h * 128:(h + 1) * 128], rhs=vaug[j][:, h, 256:257], start=True, stop=False),
                          [r_PTm, r_vaug[j]], [rDn])
                        A("pe", lambda e, h=h: e.matmul(bDn[:, h:h + 1], lhsT=qkT[:, h, jc], rhs=Cbf[:, h, 256:257], start=False, stop=True),
                          [r_qkT, r_Cbf[h]], [rDn])
                    bUs = []
                    for h in range(4):
                        bU, rbU = bank()
                        A("pe", lambda e, h=h, bU=bU: e.matmul(bU[:, 0:257], lhsT=ktok[:, h, :], rhs=vaug[j][:, h, 0:257], start=True, stop=True),
                          [r_ktok, r_vaug[j]], [rbU])
                        bUs.append((bU, rbU))
                    return (j, jc, bNs, bDn, rDn, bUs)

                def rec_state(ctx):
                    j, jc, bNs, bDn, rDn, bUs = ctx
                    for h in range(4):
                        bU, rbU = bUs[h]
                        col = j * 4 + h
                        A("dve", lambda e, h=h, col=col: e.tensor_scalar(out=C32[:, h, :], in0=C32[:, h, :], scalar1=eB[:, col:col + 1], scalar2=None, op0=ALU.mult),
                          [r_C32[h], r_eB], [r_C32[h]])
                        A("dve", lambda e, h=h, col=col, bU=bU: e.scalar_tensor_tensor(out=C32[:, h, :], in0=bU[:, 0:257], scalar=eB[:, col:col + 1], in1=C32[:, h, :],
                                                                                    op0=ALU.mult, op1=ALU.add), [rbU, r_eB, r_C32[h]], [r_C32[h]])
                        A("act", lambda e, h=h: e.activation(out=Cbf[:, h, 0:257], in_=C32[:, h, :], func=AF.Copy), [r_C32[h]], [r_Cbf[h]])

                def rec_rest(ctx):
                    j, jc, bNs, bDn, rDn, bUs = ctx
                    c4 = slice(j * 4, (j + 1) * 4)
                    A("dve", lambda e: e.tensor_scalar(out=dd[:, 0, :], in0=bDn[:, 0:4], scalar1=-1.0, scalar2=None, op0=ALU.mult), [rDn], [r_dd[0]])
                    A("dve", lambda e: e.tensor_tensor(out=dd[:, 1, :], in0=bDn[:, 0:4], in1=dd[:, 0, :], op=ALU.max), [rDn, r_dd[0]], [r_dd[0]])
                    A("dve", lambda e: e.tensor_tensor(out=dd[:, 2, :], in0=dd[:, 1, :], in1=thr[:, c4], op=ALU.max), [r_dd[0], r_thr], [r_dd[0]])
                    A("dve", lambda e: e.reciprocal(out=dd[:, 3, :], in_=dd[:, 2, :]), [r_dd[0]], [r_dd[0]])
                    for h in range(4):
                        bN, rN = bNs[h // 2]
                        nc_ = slice((h % 2) * 256, (h % 2) * 256 + 256)
                        A("act", lambda e, h=h, bN=bN, nc_=nc_: e.activation(out=tmpN[:, h * 256:(h + 1) * 256], in_=bN[:, nc_], func=AF.Copy, scale=dd[:, 3, h:h + 1]),
                          [rN, r_dd[0]], [r_tmpN])
                    A("dve", lambda e: e.scalar_tensor_tensor(out=ho[:], in0=th[:, j, :], scalar=1.0, in1=tmpN[:], op0=ALU.add, op1=ALU.mult),
                      [r_th, r_tmpN], [r_ho])
                    for h in range(4):
                        A("dve", lambda e, h=h: e.bn_stats(out=st6[:, h, :], in_=ho[:, h * 256:(h + 1) * 256]), [r_ho], [r_st6])
                    for h in range(4):
                        A("dve", lambda e, h=h: e.bn_aggr(out=mv[:, h, :], in_=st6[:, h, :]), [r_st6], [r_mv])
                    A("pool", lambda e: e.tensor_scalar(out=lnv[:, 0:4], in0=mv[:, :, 1], scalar1=4e-6, scalar2=None, op0=ALU.add), [r_mv], [r_lnv])
                    A("pool", lambda e: e.tensor_tensor(out=lnv[:, 4:8], in0=lnv[:, 0:4], in1=mhalf[:, 0:4], op=ALU.pow), [r_lnv, r_mhalf], [r_lnv])
                    for h in range(4):
                        A("dve", lambda e, h=h: e.tensor_scalar(out=ho[:, h * 256:(h + 1) * 256], in0=ho[:, h * 256:(h + 1) * 256], scalar1=mv[:, h, 0:1],
                                                              scalar2=lnv[:, 4 + h:5 + h], op0=ALU.subtract, op1=ALU.mult), [r_ho, r_mv, r_lnv], [r_ho])
                    hi = rr("hz")
                    A("pool", lambda e: e.tensor_tensor(out=hzs[hi][:], in0=ho[:], in1=zs[:, j, :], op=ALU.mult), [r_ho, r_zs], [r_hzs[hi]])
                    return (jc, hi)

                def rec_tail(t):
                    jc, hi = t
                    for k in range(8):
                        A("pe", lambda e, k=k: e.transpose(out=pst[:, k, :], in_=hzs[hi][:, k * 128:(k + 1) * 128], identity=ident_b[:]), [r_hzs[hi], r_ident], [r_pst])
                    A("act", lambda e: e.activation(out=hzT[:, :, jc], in_=pst[:], func=AF.Copy), [r_pst], [r_hzT])

                tail = None
                for j in range(4):
                    ctx = rec_pe(j)
                    bctr[0] += 3
                    rec_state(ctx)
                    if tail is not None:
                        rec_tail(tail)
                    tail = rec_rest(ctx)
                rec_tail(tail)
                mT, r_mT = bufC, r_bufC
                for i in range(2):
                    wg = load_w(WB["GM"] + i)
                    wm = load_w(WB["MOUT"] + i)
                    for c in range(4):
                        blk = 4 * i + c
                        b, rb = proj_fm(wg, c * 128, uT, r_uT, T)
                        k = rr("tz")
                        A("act", lambda e, b=b, k=k: e.activation(out=tzt[k][:], in_=b[:], func=AF.Tanh, scale=0.5), [rb], [r_tzt[k]])
                        by, rby = proj_fm(wm, c * 128, hzT, r_hzT, T)
                        A("dve", lambda e, by=by, k=k: e.scalar_tensor_tensor(out=zst[k][:], in0=tzt[k][:], scalar=1.0, in1=by[:], op0=ALU.add, op1=ALU.mult),
                          [r_tzt[k], rby], [r_zst[k]])
                        A("pool", lambda e, k=k, blk=blk: e.tensor_tensor(out=mT[:, blk, :], in0=zst[k][:], in1=mrg[:, blk, :], op=ALU.add),
                          [r_zst[k], r_mrg], [r_mT])
                w0 = load_w(WB["OUT"])
                w1 = load_w(WB["OUT"] + 1)
                if nxt is not None:
                    wk_pref[0] = load_w(WB["K"])
                xr = [(tmpN, r_tmpN), (ho, r_ho)]

                def reload(j):
                    xt_, rx_ = xr[j % 2]
                    t0_ = tok0 + j * 128
                    A("sp", lambda e, s=s: e.dma_start(out=xt_[:], in_=x[s, t0_: t0_ + 128, :]), writes=[rx_], dma="xr%d" % (j % 2))

                reload(0)
                reload(1)
                fronts = {}
                if nxt is not None:
                    fronts[0] = phase0_front(0)
                    fronts[1] = phase0_front(1)
                for j in range(4):
                    oi = rr("ot")
                    b0, rb0 = proj_tm(w0, 512, mT, r_mT, j * 128, 128)
                    b1, rb1 = proj_tm(w1, 512, mT, r_mT, j * 128, 128)
                    A("act", lambda e, b0=b0, oi=oi: e.activation(out=hz[:, 0:512], in_=b0[:], func=AF.Square, accum_out=ss[oi][:, 0:1]), [rb0], [r_hz, r_ss[oi]])
                    A("act", lambda e, b1=b1, oi=oi: e.activation(out=hz[:, 512:1024], in_=b1[:], func=AF.Square, accum_out=ss[oi][:, 1:2]), [rb1], [r_hz, r_ss[oi]])
                    A("dve", lambda e, oi=oi: e.tensor_tensor(out=ss[oi][:, 2:3], in0=ss[oi][:, 0:1], in1=ss[oi][:, 1:2], op=ALU.add), [r_ss[oi]], [r_ss[oi]])
                    A("pool", lambda e, oi=oi: e.tensor_scalar(out=ss[oi][:, 3:4], in0=ss[oi][:, 2:3], scalar1=1024 * 16e-6, scalar2=None, op0=ALU.add), [r_ss[oi]], [r_ss[oi]])
                    A("pool", lambda e, oi=oi: e.tensor_tensor(out=ss[oi][:, 4:5], in0=ss[oi][:, 3:4], in1=mhalf[:, 0:1], op=ALU.pow), [r_ss[oi], r_mhalf], [r_ss[oi]])
                    A("dve", lambda e, b0=b0, oi=oi: e.scalar_tensor_tensor(out=ot[oi][:, 0:512], in0=b0[:], scalar=ss[oi][:, 4:5], in1=npB[:, 0:512],
                                                                          op0=ALU.mult, op1=ALU.mult), [rb0, r_ss[oi], r_npB], [r_ot[oi]])
                    A("dve", lambda e, b1=b1, oi=oi: e.scalar_tensor_tensor(out=ot[oi][:, 512:1024], in0=b1[:], scalar=ss[oi][:, 4:5], in1=npB[:, 512:1024],
                                                                          op0=ALU.mult, op1=ALU.mult), [rb1, r_ss[oi], r_npB], [r_ot[oi]])
                    xt_, rx_ = xr[j % 2]
                    A("pool", lambda e, oi=oi, xt_=xt_: e.tensor_tensor(out=ot[oi][:], in0=ot[oi][:], in1=xt_[:], op=ALU.add), [r_ot[oi], rx_], [r_ot[oi]])
                    A("sp", lambda e, oi=oi, j=j, s=s, tok0=tok0: e.dma_start(out=out[s, tok0 + j * 128: tok0 + (j + 1) * 128, :], in_=ot[oi][:]),
                      reads=[r_ot[oi]], writes=[new_out()], dma="out%d" % oi)
                    if j + 2 < 4:
                        reload(j + 2)
                    if nxt is not None and j < 2:
                        phase0_back(2 * j, fronts[2 * j])
                        phase0_back(2 * j + 1, fronts[2 * j + 1])
                        if j == 0:
                            fronts[2] = phase0_front(2)
                            fronts[3] = phase0_front(3)
                A("pool", lambda e: e.tensor_copy(out=Kbuf[:, :, 0:128], in_=Kbuf[:, :, 512:640]), [r_Kbuf], [r_Kbuf])
                A("pool", lambda e: e.tensor_copy(out=Vdup[0][:], in_=Vdup[4][:]), [r_Vdup[4]], [r_Vdup[0]])
        fin = P.res("fin")
        A("sp", lambda e: e.nop(), reads=r_outs, writes=[fin])
        P.emit()
    return nc


def _consts(NST):
    ident = np.eye(128, dtype=np.float32)
    triu = np.triu(np.ones((128, 128), dtype=np.float32))
    pit = np.zeros((128, 128), dtype=np.float32)
    for m in range(128):
        d = m % 64
        base = m - d
        if d < 8:
            k = base + d + 8
        elif d < 16:
            k = base + d - 8
        else:
            k = m
        pit[k, m] = 1.0
    half = 8
    inv_freq = (500000.0 ** (-np.arange(0, 16, 2, dtype=np.float32) / 16)).astype(np.float32)

    def tables(pos):
        pos = pos.astype(np.float32)
        ang = (pos[:, None] * inv_freq[None, :]).astype(np.float32)
        c = np.cos(ang.astype(np.float64)).astype(np.float32)
        s_ = np.sin(ang.astype(np.float64)).astype(np.float32)
        n = pos.shape[0]
        cosT = np.ones((128, n), dtype=np.float32)
        sinT = np.zeros((128, n), dtype=np.float32)
        for hb in (0, 64):
            cosT[hb:hb + 8] = c.T
            cosT[hb + 8:hb + 16] = c.T
            sinT[hb:hb + 8] = -s_.T
            sinT[hb + 8:hb + 16] = s_.T
        return cosT, sinT

    cos = np.zeros((NST, 128, T), dtype=np.float32)
    sin = np.zeros((NST, 128, T), dtype=np.float32)
    for ti in range(NST):
        cos[ti], sin[ti] = tables(16 + ti * T + np.arange(T))
    cosm, sinm = tables(np.arange(16))
    return dict(c_ident=ident, c_triu=triu, c_pit=pit, c_cos=cos, c_sin=sin, c_cosm=cosm, c_sinm=sinm)


_NC_CACHE = {}


def run(inputs, n_cores, nseq, nst):
    key = (nseq, nst)
    if key not in _NC_CACHE:
        _NC_CACHE[key] = build_nc(nseq, nst)
    nc = _NC_CACHE[key]
    cs = _consts(nst)
    f = lambda a: np.ascontiguousarray(np.asarray(a, dtype=np.float32))
    shared = {
        "meta_tokens": f(inputs["meta_tokens"]),
        "norm_pre": f(inputs["norm_pre"][0]),
        "w_in": f(inputs["w_in"][0]),
        "attn_sinks": f(inputs["attn_sinks"][0]),
        "conv_w": f(inputs["conv_w"][0]),
        "conv_b": f(inputs["conv_b"][0]),
        "mlstm_gate_bias": f(inputs["mlstm_gate_bias"][0]),
        "mlstm_head_norm": f(inputs["mlstm_head_norm"][0]),
        "w_attn_out": f(inputs["w_attn_out"][0]),
        "w_mlstm_out": f(inputs["w_mlstm_out"][0]),
        "w_out": f(inputs["w_out"][0]),
        "norm_post": f(inputs["norm_post"][0]),
    }
    shared.update(cs)
    xfull = np.asarray(inputs["x"], dtype=np.float32)
    in_maps = []
    for c in range(n_cores):
        m = dict(shared)
        m["x"] = np.ascontiguousarray(xfull[c * nseq:(c + 1) * nseq, :nst * T])
        in_maps.append(m)
    res = run_bass_kernel_spmd(nc, in_maps, core_ids=list(range(n_cores)))
    return np.concatenate([np.asarray(r["out"]) for r in res.results], axis=0)


def kernel(**inputs):
    return run(inputs, 8, 2, 8).astype(np.float32)
```

```python
import contextlib
import math
import numpy as np
import concourse.bass as bass
import concourse.mybir as mybir
from concourse.bass_utils import run_bass_kernel_spmd

F32 = mybir.dt.float32
BF16 = mybir.dt.bfloat16
ALU = mybir.AluOpType
AF = mybir.ActivationFunctionType

ENGS = ("pe", "act", "dve", "pool", "sp")


class Res:
    __slots__ = ("name", "w", "readers")

    def __init__(self, name):
        self.name = name
        self.w = None
        self.readers = []


class Op:
    __slots__ = ("eng", "fn", "semkey", "inc", "deps", "signal", "idx", "is_dma")

    def __init__(self, eng, fn, semkey, inc, is_dma):
        self.eng = eng
        self.fn = fn
        self.semkey = semkey
        self.inc = inc
        self.deps = []
        self.signal = False
        self.idx = 0
        self.is_dma = is_dma


class Prog:
    def __init__(self, nc):
        self.nc = nc
        self.streams = {e: [] for e in ENGS}
        self.by_sem = {}

    def res(self, name):
        return Res(name)

    def op(self, eng, fn, reads=(), writes=(), dma=None):
        is_dma = dma is not None
        semkey = ("dma", dma) if is_dma else eng
        o = Op(eng, fn, semkey, 16 if is_dma else 1, is_dma)
        if is_dma:
            o.signal = True
        deps = {}

        def need(p, kind):
            if p is None:
                return
            if (not p.is_dma) and (not is_dma) and p.eng == eng:
                if eng == "pe":
                    return
            deps[id(p)] = p

        for r in reads:
            need(r.w, "raw")
        for w in writes:
            need(w.w, "waw")
            for t in w.readers:
                need(t, "war")
        o.deps = list(deps.values())
        for p in o.deps:
            p.signal = True
        for w in writes:
            w.w = o
            w.readers = []
        for r in reads:
            if not is_dma:
                r.readers = [t for t in r.readers if t.is_dma or t.eng != eng]
            r.readers.append(o)
        self.streams[eng].append(o)
        self.by_sem.setdefault(semkey, []).append(o)
        return o

    def emit(self):
        nc = self.nc
        for k, ops in self.by_sem.items():
            c = 0
            for o in ops:
                if o.signal:
                    c += o.inc
                o.idx = c
        with contextlib.ExitStack() as st:
            sems = {}
            for k in self.by_sem:
                nm = "s_" + (k if isinstance(k, str) else "d_" + str(k[1]))
                sems[k] = st.enter_context(nc.semaphore(nm))
            block = st.enter_context(nc.Block())
            prog = self

            def run(eng_name, handle):
                waited = {}
                for o in prog.streams[eng_name]:
                    need = {}
                    for p in o.deps:
                        if p.idx > need.get(p.semkey, 0):
                            need[p.semkey] = p.idx
                    for k, v in need.items():
                        if waited.get(k, 0) >= v:
                            continue
                        handle.wait_ge(sems[k], v)
                        waited[k] = v
                    ins = o.fn(handle)
                    if o.signal:
                        ins.then_inc(sems[o.semkey], o.inc)

            @block.tensor
            def _(e):
                run("pe", e)

            @block.scalar
            def _(e):
                run("act", e)

            @block.vector
            def _(e):
                run("dve", e)

            @block.gpsimd
            def _(e):
                run("pool", e)

            @block.sync
            def _(e):
                run("sp", e)


C_AQ, C_AK, C_AV, C_AZ = 0, 1024, 1280, 1536
C_MQK, C_MV, C_MI, C_MF, C_MO, C_MZ, C_GA, C_GM = 2560, 3584, 4608, 4612, 4616, 5640, 6664, 7688
IN_W = 8712
T = 512
LOGK = -0.5 * math.log(128.0)


def build_nc(NSEQ, NST):
    nc = bass.Bass("TRN2", target_bir_lowering=False)
    S = NST * T

    def din(name, shape, dt=F32):
        return nc.dram_tensor(name, shape, dt, kind="ExternalInput").ap()

    x = din("x", [NSEQ, S, 1024])
    meta = din("meta_tokens", [16, 1024])
    norm_pre = din("norm_pre", [1024])
    w_in = din("w_in", [1024, IN_W])
    sinks = din("attn_sinks", [16])
    conv_w = din("conv_w", [4, 1024])
    conv_b = din("conv_b", [1024])
    gate_bias = din("mlstm_gate_bias", [8])
    head_norm = din("mlstm_head_norm", [1024])
    w_ao = din("w_attn_out", [1024, 1024])
    w_mo = din("w_mlstm_out", [1024, 1024])
    w_out = din("w_out", [1024, 1024])
    norm_post = din("norm_post", [1024])
    c_ident = din("c_ident", [128, 128])
    c_triu = din("c_triu", [128, 128])
    c_pit = din("c_pit", [128, 128])
    c_cos = din("c_cos", [NST, 128, T])
    c_sin = din("c_sin", [NST, 128, T])
    c_cosm = din("c_cosm", [128, 16])
    c_sinm = din("c_sinm", [128, 16])
    out = nc.dram_tensor("out", [NSEQ, S, 1024], F32, kind="ExternalOutput").ap()

    def dscr(name, shape):
        return nc.dram_tensor(name, shape, BF16, kind="Internal").ap()

    NWB = 24
    wbt = dscr("wbt", [NWB, 128, 8, 512])
    WB = {"K": 0, "V": 1, "Q": 2, "Z": 4, "GA": 6, "MQK": 8, "MV": 10, "MO": 12, "MZ": 14, "GM": 16, "AO": 18, "MOUT": 20, "OUT": 22}

    P = Prog(nc)
    with contextlib.ExitStack() as st:
        def sb(name, shape, dt=F32):
            return st.enter_context(nc.sbuf_tensor(name, shape, dt))

        NB = 7
        banks = [st.enter_context(nc.psum_tensor("pb%d" % i, [128, 512], F32)) for i in range(NB)]
        bank_res = [P.res("pb%d" % i) for i in range(NB)]
        pst = st.enter_context(nc.psum_tensor("pst", [128, 8, 128], BF16))
        r_pst = P.res("pst")
        bctr = [0]

        def bank():
            i = bctr[0] % NB
            bctr[0] += 1
            return banks[i], bank_res[i]

        ident_f = sb("ident_f", [128, 128]); r_identf = P.res("identf")
        ident_b = sb("ident_b", [128, 128], BF16); r_ident = P.res("ident")
        triu_f = sb("triu_f", [128, 128]); r_triu = P.res("triu")
        pit_f = sb("pit_f", [128, 128]); r_pitf = P.res("pitf")
        pit_b = sb("pit_b", [128, 128], BF16); r_pit = P.res("pit")
        ones_f = sb("ones_f", [128, 128]); r_onesf = P.res("onesf")
        ones_b = sb("ones_b", [128, 128], BF16); r_onesb = P.res("onesb")
        mask_b = sb("mask_b", [128, 4, 128], BF16); r_mask = P.res("mask")
        maskp_b = sb("maskp_b", [128, 4, 128], BF16); r_maskp = P.res("maskp")
        gB = sb("gB", [128, 1024]); r_gB = P.res("gB")
        hnB = sb("hnB", [128, 1024]); r_hnB = P.res("hnB")
        npB = sb("npB", [128, 1024]); r_npB = P.res("npB")
        gbias = sb("gbias", [128, 8]); r_gbias = P.res("gbias")
        cw = sb("cw", [128, 8, 4]); r_cw = P.res("cw")
        cb = sb("cb", [128, 8]); r_cb = P.res("cb")
        sk = sb("sk", [33, 16]); r_sk = P.res("sk")
        wgate = sb("wgate", [128, 8, 8], BF16); r_wgate = P.res("wgate")
        cosT = sb("cosT", [128, T]); r_cos = P.res("cos")
        sinT = sb("sinT", [128, T]); r_sin = P.res("sin")
        cosm = sb("cosm", [128, 16]); r_cosm = P.res("cosm")
        sinm = sb("sinm", [128, 16]); r_sinm = P.res("sinm")

        NSLOT = 3
        wring = [sb("wring%d" % i, [128, 8, 512], BF16) for i in range(NSLOT)]
        r_wring = [P.res("wring%d" % i) for i in range(NSLOT)]
        wctr = [0]

        xs = [sb("xs%d" % j, [128, 1024]) for j in range(4)]
        r_xs = [P.res("xs%d" % j) for j in range(4)]
        xbs = [sb("xb%d" % i, [128, 1024], BF16) for i in range(2)]
        r_xbs = [P.res("xb%d" % i) for i in range(2)]
        mhalf = sb("mhalf", [128, 8]); r_mhalf = P.res("mhalf")
        stt = [sb("stt%d" % j, [128, 4]) for j in range(4)]
        r_stt = [P.res("stt%d" % j) for j in range(4)]
        uT = sb("uT", [128, 8, T], BF16); r_uT = P.res("uT")
        bufA = sb("bufA", [128, 8, T], BF16); r_bufA = P.res("bufA")
        bufB = sb("bufB", [128, 8, T], BF16); r_bufB = P.res("bufB")
        bufC = sb("bufC", [128, 8, T], BF16); r_bufC = P.res("bufC")
        hzT = sb("hzT", [128, 8, T], BF16); r_hzT = P.res("hzT")
        mrg = sb("mrg", [128, 8, T], BF16); r_mrg = P.res("mrg")
        Kbuf = sb("Kbuf", [128, 4, 640], BF16); r_Kbuf = P.res("Kbuf")
        KmT = sb("KmT", [128, 4, 16], BF16); r_KmT = P.res("KmT")
        Vdup = [sb("Vaug%d" % i, [128, 4, 66], BF16) for i in range(5)]
        r_Vdup = [P.res("Vaug%d" % i) for i in range(5)]
        Vmeta = sb("Vmeta", [33, 4, 66], BF16); r_Vmeta = P.res("Vmeta")
        att_tok = [sb("att_tok%d" % i, [128, 1024], BF16) for i in range(2)]
        r_att_tok = [P.res("att_tok%d" % i) for i in range(2)]
        rec4 = [sb("rec4_%d" % i, [128, 4]) for i in range(2)]
        r_rec4 = [P.res("rec4_%d" % i) for i in range(2)]
        ptc = [sb("ptc%d" % i, [128, 512], BF16) for i in range(2)]
        r_ptc = [P.res("ptc%d" % i) for i in range(2)]
        ptp = [sb("ptp%d" % i, [128, 512], BF16) for i in range(2)]
        r_ptp = [P.res("ptp%d" % i) for i in range(2)]
        ptm = [sb("ptm%d" % g, [33, 512], BF16) for g in range(4)]
        r_ptm = [P.res("ptm%d" % g) for g in range(4)]
        raw = [sb("raw%d" % i, [128, 512], BF16) for i in range(2)]
        r_raw = [P.res("raw%d" % i) for i in range(2)]
        t1 = [sb("t1_%d" % i, [128, 512]) for i in range(2)]
        r_t1 = [P.res("t1_%d" % i) for i in range(2)]
        t2 = [sb("t2_%d" % i, [128, 512]) for i in range(2)]
        r_t2 = [P.res("t2_%d" % i) for i in range(2)]
        tzt = [sb("tzt%d" % i, [128, 512], BF16) for i in range(2)]
        r_tzt = [P.res("tzt%d" % i) for i in range(2)]
        zst = [sb("zst%d" % i, [128, 512], BF16) for i in range(2)]
        r_zst = [P.res("zst%d" % i) for i in range(2)]
        pre = [sb("pre%d" % i, [128, 515]) for i in range(2)]
        r_pre = [P.res("pre%d" % i) for i in range(2)]
        acc = [sb("acc%d" % i, [128, 512]) for i in range(2)]
        r_acc = [P.res("acc%d" % i) for i in range(2)]
        halo = sb("halo", [128, 8, 3]); r_halo = P.res("halo")
        halo_m = sb("halo_m", [128, 8, 3]); r_halom = P.res("halom")
        gsb = sb("gsb", [128, 4, 8]); r_gsb = P.res("gsb")
        e1 = sb("e1", [128, 4, 4]); r_e1 = P.res("e1")
        nlf = sb("nlf", [128, 16]); r_nlf = P.res("nlf")
        thr = sb("thr", [128, 16]); r_thr = P.res("thr")
        gs = sb("gs", [128, 16]); r_gs = P.res("gs")
        vsc = sb("vsc", [128, 16]); r_vsc = P.res("vsc")
        eB = sb("eB", [128, 16]); r_eB = P.res("eB")
        vaug = [sb("vaug%d" % j, [128, 4, 272], BF16) for j in range(4)]
        r_vaug = [P.res("vaug%d" % j) for j in range(4)]
        ktok = sb("ktok", [128, 4, 128], BF16); r_ktok = P.res("ktok")
        PTm = sb("PTm", [128, 512], BF16); r_PTm = P.res("PTm")
        C32 = sb("C32", [128, 4, 257]); r_C32 = [P.res("C32_%d" % h) for h in range(4)]
        Cbf = sb("Cbf", [128, 4, 272], BF16); r_Cbf = [P.res("Cbf_%d" % h) for h in range(4)]
        Cm32 = sb("Cm32", [128, 4, 257]); r_Cm32 = P.res("Cm32")
        Ue = [sb("Ue%d" % i, [128, 257]) for i in range(2)]
        r_Ue = [P.res("Ue%d" % i) for i in range(2)]
        dd = sb("dd", [128, 4, 4]); r_dd = [P.res("dd%d" % h) for h in range(4)]
        tmpN = sb("tmpN", [128, 1024]); r_tmpN = P.res("tmpN")
        ho = sb("ho", [128, 1024]); r_ho = P.res("ho")
        st6 = sb("st6", [128, 4, 6]); r_st6 = P.res("st6")
        mv = sb("mv", [128, 4, 2]); r_mv = P.res("mv")
        lnv = sb("lnv", [128, 8]); r_lnv = P.res("lnv")
        hzs = [sb("hz%d" % i, [128, 1024], BF16) for i in range(2)]
        r_hzs = [P.res("hz%d" % i) for i in range(2)]
        hz, r_hz = hzs[0], r_hzs[0]
        ot = [sb("ot%d" % i, [128, 1024]) for i in range(2)]
        r_ot = [P.res("ot%d" % i) for i in range(2)]
        ss = [sb("ss%d" % i, [128, 8]) for i in range(2)]
        r_ss = [P.res("ss%d" % i) for i in range(2)]
        r_outs = []

        def new_out():
            r = P.res("out%d" % len(r_outs))
            r_outs.append(r)
            return r

        ctr = {"raw": 0, "tz": 0, "pre": 0, "pt": 0, "ot": 0, "xb": 0, "hz": 0, "rec4": 0, "ue": 0}

        def rr(key, n=2):
            i = ctr[key] % n
            ctr[key] += 1
            return i

        A = P.op

        r_castb = {}

        def cast_blk(bi, src, c0, n):
            r = P.res("cast%d" % bi)
            A("pool", lambda e: e.dma_start(out=wbt[bi][:, :, 0:n], in_=src[:, c0:c0 + n].rearrange("(k p) n -> p k n", p=128)),
              writes=[r], dma="cast%d" % bi)
            r_castb[bi] = [r]

        rk = []
        for g in range(4):
            for d in range(2):
                r = P.res("castK%d%d" % (g, d))
                A("pool", lambda e, g=g, d=d: e.dma_start(out=wbt[0][:, :, g * 128 + d * 64: g * 128 + d * 64 + 64],
                                                      in_=w_in[:, C_AK + g * 64: C_AK + g * 64 + 64].rearrange("(k p) n -> p k n", p=128)),
                  writes=[r], dma="castK")
                rk.append(r)
        r_castb[0] = rk
        cast_blk(1, w_in, C_AV, 256)
        for nm, c0 in (("Q", C_AQ), ("Z", C_AZ), ("GA", C_GA)):
            for i in range(2):
                cast_blk(WB[nm] + i, w_in, c0 + i * 512, 512)
        for i in range(2):
            cast_blk(WB["AO"] + i, w_ao, i * 512, 512)
        r_wgate_c = P.res("wgate_c")
        for nm, c0 in (("MQK", C_MQK), ("MV", C_MV), ("MO", C_MO), ("MZ", C_MZ), ("GM", C_GM)):
            for i in range(2):
                cast_blk(WB[nm] + i, w_in, c0 + i * 512, 512)
        for i in range(2):
            cast_blk(WB["MOUT"] + i, w_mo, i * 512, 512)
        for i in range(2):
            cast_blk(WB["OUT"] + i, w_out, i * 512, 512)

        def ld(dst, src, r, name, **kw):
            A("sp", lambda e: e.dma_start(out=dst, in_=src, **kw), writes=[r], dma=name)

        ld(ident_f[:], c_ident, r_identf, "identf")
        ld(triu_f[:], c_triu, r_triu, "triu")
        ld(pit_f[:], c_pit, r_pitf, "pitf")
        ld(gB[:], norm_pre.partition_broadcast(128), r_gB, "gB")
        ld(hnB[:], head_norm.partition_broadcast(128), r_hnB, "hnB")
        ld(npB[:], norm_post.partition_broadcast(128), r_npB, "npB")
        ld(gbias[:], gate_bias.partition_broadcast(128), r_gbias, "gbias")
        for jj in range(4):
            ld(cw[:, :, jj], conv_w[jj].rearrange("(b p) -> p b", p=128), r_cw, "cw", allow_slow_non_contiguous=True)
        ld(cb[:], conv_b.rearrange("(b p) -> p b", p=128), r_cb, "cb", allow_slow_non_contiguous=True)
        ld(sk[32:33, :], sinks.rearrange("(o n) -> o n", o=1), r_sk, "sk")
        ld(cosm[:], c_cosm, r_cosm, "cosm")
        ld(sinm[:], c_sinm, r_sinm, "sinm")
        A("pool", lambda e: e.dma_start(out=wgate[:], in_=w_in[:, C_MI:C_MI + 8].rearrange("(k p) n -> p k n", p=128),
                                        allow_slow_non_contiguous=True), writes=[r_wgate], dma="wgate")
        A("dve", lambda e: e.tensor_copy(out=ident_b[:], in_=ident_f[:]), [r_identf], [r_ident])
        A("dve", lambda e: e.tensor_copy(out=pit_b[:], in_=pit_f[:]), [r_pitf], [r_pit])
        A("dve", lambda e: e.memset(ones_f[:], 1.0), [], [r_onesf])
        A("dve", lambda e: e.memset(ones_b[:], 1.0), [], [r_onesb])
        A("dve", lambda e: e.tensor_copy(out=mask_b[:], in_=triu_f[:].unsqueeze(1).to_broadcast([128, 4, 128])), [r_triu], [r_mask])
        A("dve", lambda e: e.tensor_scalar(out=maskp_b[:], in0=triu_f[:].unsqueeze(1).to_broadcast([128, 4, 128]),
                                           scalar1=-1.0, scalar2=1.0, op0=ALU.mult, op1=ALU.add), [r_triu], [r_maskp])
        A("dve", lambda e: e.tensor_scalar(out=cw[:], in0=cw[:], scalar1=0.5, scalar2=None, op0=ALU.mult), [r_cw], [r_cw])
        A("dve", lambda e: e.tensor_scalar(out=cb[:], in0=cb[:], scalar1=0.5, scalar2=None, op0=ALU.mult), [r_cb], [r_cb])
        A("pool", lambda e: e.memset(Vmeta[:], 0.0), [], [r_Vmeta])
        A("pool", lambda e: e.memset(Vmeta[0:16, :, 64:65], 1.0), [], [r_Vmeta])
        A("pool", lambda e: e.memset(Vmeta[32:33, :, 64:65], 1.0), [], [r_Vmeta])
        for i in range(5):
            A("pool", lambda e, i=i: e.memset(Vdup[i][:], 0.0), [], [r_Vdup[i]])
            A("pool", lambda e, i=i: e.memset(Vdup[i][:, :, 64:65], 1.0), [], [r_Vdup[i]])
        A("pool", lambda e: e.memset(mhalf[:], -0.5), [], [r_mhalf])
        A("dve", lambda e: e.tensor_scalar(out=gB[:], in0=gB[:], scalar1=32.0, scalar2=None, op0=ALU.mult), [r_gB], [r_gB])
        A("dve", lambda e: e.tensor_scalar(out=npB[:], in0=npB[:], scalar1=32.0, scalar2=None, op0=ALU.mult), [r_npB], [r_npB])
        for g in range(4):
            A("pool", lambda e, g=g: e.memset(ptm[g][:], 0.0), [], [r_ptm[g]])
            A("act", lambda e, g=g: e.activation(out=ptm[g][32:33, :].rearrange("p (h q) -> p h q", h=4),
                                                 in_=sk[32:33, 4 * g:4 * g + 4].unsqueeze(2).to_broadcast([1, 4, 128]),
                                                 func=AF.Exp), [r_sk, r_ptm[g]], [r_ptm[g]])
        for j in range(4):
            A("pool", lambda e, j=j: e.memset(vaug[j][:], 0.0), [], [r_vaug[j]])
        A("pool", lambda e: e.memset(Cbf[:], 0.0), [], r_Cbf)
        A("pool", lambda e: e.memset(halo_m[:], 0.0), [], [r_halom])

        def load_w(bi, n=512):
            i = wctr[0] % NSLOT
            wctr[0] += 1
            A("sp", lambda e: e.dma_start(out=wring[i][:, :, 0:n], in_=wbt[bi][:, :, 0:n]),
              reads=r_castb[bi], writes=[r_wring[i]], dma="w%d" % i)
            return wring[i], r_wring[i]

        def proj_fm(w, c0, rhs, r_rhs, N, n0=0):
            wt, r_w = w
            b, rb = bank()
            for kc in range(8):
                A("pe", lambda e, kc=kc: e.matmul(b[:, 0:N], lhsT=wt[:, kc, c0:c0 + 128], rhs=rhs[:, kc, n0:n0 + N],
                                                 start=(kc == 0), stop=(kc == 7)), [r_w, r_rhs], [rb])
            return b, rb

        def proj_tm(w, n, lhs, r_lhs, t0, nt):
            wt, r_w = w
            b, rb = bank()
            for kc in range(8):
                A("pe", lambda e, kc=kc: e.matmul(b[0:nt, 0:n], lhsT=lhs[:, kc, t0:t0 + nt], rhs=wt[:, kc, 0:n],
                                                 start=(kc == 0), stop=(kc == 7)), [r_w, r_lhs], [rb])
            return b, rb

        pending = []

        def flush():
            while pending:
                pending.pop(0)()

        def rope(b, rb, N, cos_ap, sin_ap, r_tabs, scale, dst, r_dst):
            i = rr("raw")
            A("act", lambda e: e.activation(out=raw[i][:, 0:N], in_=b[:, 0:N], func=AF.Copy, scale=scale), [rb], [r_raw[i]])

            def stage_b():
                b2, rb2 = bank()
                A("pe", lambda e: e.matmul(b2[:, 0:N], lhsT=pit_b[:], rhs=raw[i][:, 0:N], start=True, stop=True), [r_pit, r_raw[i]], [rb2])
                A("dve", lambda e: e.tensor_tensor(out=t1[i][:, 0:N], in0=raw[i][:, 0:N], in1=cos_ap, op=ALU.mult), [r_raw[i]] + r_tabs, [r_t1[i]])
                A("dve", lambda e: e.tensor_tensor(out=t2[i][:, 0:N], in0=b2[:, 0:N], in1=sin_ap, op=ALU.mult), [rb2] + r_tabs, [r_t2[i]])
                A("pool", lambda e: e.tensor_tensor(out=dst, in0=t1[i][:, 0:N], in1=t2[i][:, 0:N], op=ALU.add), [r_t1[i], r_t2[i]], [r_dst])
            pending.append(stage_b)

        def rmsnorm_front(src, r_src, n, stat, r_stat):
            xi = rr("xb")
            xb_, r_xb_ = xbs[xi], r_xbs[xi]
            A("act", lambda e: e.activation(out=xb_[0:n, :], in_=src, func=AF.Square, accum_out=stat[0:n, 0:1]), [r_src], [r_xb_, r_stat])
            A("pool", lambda e: e.tensor_scalar(out=stat[0:n, 1:2], in0=stat[0:n, 0:1], scalar1=1024 * 1e-6, scalar2=None, op0=ALU.add), [r_stat], [r_stat])
            A("pool", lambda e: e.tensor_tensor(out=stat[0:n, 2:3], in0=stat[0:n, 1:2], in1=mhalf[0:n, 0:1], op=ALU.pow), [r_stat, r_mhalf], [r_stat])
            A("dve", lambda e: e.scalar_tensor_tensor(out=xb_[0:n, :], in0=src, scalar=stat[0:n, 2:3], in1=gB[0:n, :],
                                                      op0=ALU.mult, op1=ALU.mult), [r_src, r_stat, r_gB], [r_xb_])
            return xb_, r_xb_

        def rmsnorm_back(xbr, n, dstT, r_dstT, t0):
            xb_, r_xb_ = xbr
            for k in range(8):
                A("pe", lambda e, k=k: e.transpose(out=pst[:, k, 0:n], in_=xb_[0:n, k * 128:(k + 1) * 128], identity=ident_b[0:n, 0:n]),
                  [r_xb_, r_ident], [r_pst])
            A("act", lambda e: e.activation(out=dstT[:, :, t0:t0 + n], in_=pst[:, :, 0:n], func=AF.Copy), [r_pst], [r_dstT])

        def rmsnorm_T(src, r_src, n, stat, r_stat, dstT, r_dstT, t0):
            rmsnorm_back(rmsnorm_front(src, r_src, n, stat, r_stat), n, dstT, r_dstT, t0)

        def conv_silu(b, rb, blk, N, halo_src, r_halo_src, halo_dst, r_halo_dst, dst, r_dst):
            i = rr("pre")
            p_ = pre[i]
            A("act", lambda e: e.activation(out=p_[:, 3:3 + N], in_=b[:, 0:N], func=AF.Copy), [rb], [r_pre[i]])
            A("pool", lambda e: e.tensor_copy(out=p_[:, 0:3], in_=halo_src[:, blk, :]), [r_halo_src], [r_pre[i]])
            a_ = acc[i]
            A("act", lambda e: e.activation(out=a_[:, 0:N], in_=b[:, 0:N], func=AF.Identity, scale=cw[:, blk, 3:4], bias=cb[:, blk:blk + 1]),
              [rb, r_cw, r_cb], [r_acc[i]])
            for jj in (2, 1, 0):
                A("dve", lambda e, jj=jj: e.scalar_tensor_tensor(out=a_[:, 0:N], in0=p_[:, jj:jj + N], scalar=cw[:, blk, jj:jj + 1],
                                                                 in1=a_[:, 0:N], op0=ALU.mult, op1=ALU.add), [r_pre[i], r_cw, r_acc[i]], [r_acc[i]])
            A("pool", lambda e: e.tensor_copy(out=halo_dst[:, blk, :], in_=p_[:, N:N + 3]), [r_pre[i]], [r_halo_dst])
            k = rr("tz")
            A("act", lambda e: e.activation(out=tzt[k][:, 0:N], in_=a_[:, 0:N], func=AF.Tanh), [r_acc[i]], [r_tzt[k]])
            A("dve", lambda e: e.scalar_tensor_tensor(out=dst, in0=tzt[k][:, 0:N], scalar=1.0, in1=a_[:, 0:N], op0=ALU.add, op1=ALU.mult),
              [r_tzt[k], r_acc[i]], [r_dst])

        xm = ot[0]
        A("sp", lambda e: e.dma_start(out=xm[0:16, :], in_=meta), writes=[r_ot[0]], dma="xm")
        uTm = sb("uTm", [128, 8, 16], BF16); r_uTm = P.res("uTm")
        qkm = sb("qkm", [128, 8, 16], BF16); r_qkm = P.res("qkm")
        rmsnorm_T(xm[0:16, :], r_ot[0], 16, stt[0], r_stt[0], uTm, r_uTm, 0)
        wk = load_w(WB["K"])
        for g in range(4):
            b, rb = proj_fm(wk, g * 128, uTm, r_uTm, 16)
            flush()
            rope(b, rb, 16, cosm[:], sinm[:], [r_cosm, r_sinm], 1.0, KmT[:, g, :], r_KmT)
        flush()
        wv = load_w(WB["V"], 256)
        b, rb = proj_tm(wv, 256, uTm, r_uTm, 0, 16)
        A("act", lambda e, b=b: e.activation(out=Vmeta[0:16, :, 0:64], in_=b[0:16, 0:256].rearrange("p (g d) -> p g d", g=4), func=AF.Copy), [rb], [r_Vmeta])
        bgm_, rbgm = bank()
        for kc in range(8):
            A("pe", lambda e, kc=kc: e.matmul(bgm_[0:16, 0:8], lhsT=uTm[:, kc, 0:16], rhs=wgate[:, kc, :], start=(kc == 0), stop=(kc == 7)),
              [r_uTm, r_wgate], [rbgm])
        A("dve", lambda e: e.tensor_tensor(out=gsb[0:16, 0, :], in0=bgm_[0:16, 0:8], in1=gbias[0:16, :], op=ALU.add), [rbgm, r_gbias], [r_gsb])
        A("act", lambda e: e.activation(out=e1[0:16, 0, :], in_=gsb[0:16, 0, 4:8], func=AF.Exp, scale=-1.0), [r_gsb], [r_e1])
        A("act", lambda e: e.activation(out=nlf[0:16, 0:4], in_=e1[0:16, 0, :], func=AF.Ln, bias=1.0), [r_e1], [r_nlf])
        bnbm, rbnbm = bank()
        A("pe", lambda e: e.matmul(bnbm[0:16, 0:4], lhsT=triu_f[0:16, 0:16], rhs=nlf[0:16, 0:4], start=True, stop=True), [r_triu, r_nlf], [rbnbm])
        bnsm, rbnsm = bank()
        A("pe", lambda e: e.matmul(bnsm[:, 0:4], lhsT=ones_f[0:16, :], rhs=nlf[0:16, 0:4], start=True, stop=True), [r_onesf, r_nlf], [rbnsm])
        A("dve", lambda e: e.tensor_tensor(out=gs[0:16, 0:4], in0=gsb[0:16, 0, 0:4], in1=bnbm[0:16, 0:4], op=ALU.add), [r_gsb, rbnbm], [r_gs])
        A("act", lambda e: e.activation(out=vsc[0:16, 0:4], in_=gs[0:16, 0:4], func=AF.Exp, bias=LOGK), [r_gs], [r_vsc])
        A("act", lambda e: e.activation(out=eB[:, 0:4], in_=bnsm[:, 0:4], func=AF.Exp, scale=-1.0), [rbnsm], [r_eB])
        for i in range(2):
            w = load_w(WB["MQK"] + i)
            for c in range(4):
                blk = 4 * i + c
                b, rb = proj_fm(w, c * 128, uTm, r_uTm, 16)
                conv_silu(b, rb, blk, 16, halo_m, r_halom, halo_m, r_halom, qkm[:, blk, :], r_qkm)
        vaugm = vaug[0]
        for i in range(2):
            w = load_w(WB["MV"] + i)
            b, rb = proj_tm(w, 512, uTm, r_uTm, 0, 16)
            for hh in range(2):
                h = 2 * i + hh
                A("act", lambda e, b=b, hh=hh, h=h: e.activation(out=vaugm[0:16, h, 0:256], in_=b[0:16, hh * 256:(hh + 1) * 256], func=AF.Copy,
                                                              scale=vsc[0:16, h:h + 1]), [rb, r_vsc], [r_vaug[0]])
        A("pool", lambda e: e.tensor_copy(out=vaugm[0:16, :, 256:257], in_=vsc[0:16, 0:4].unsqueeze(2)), [r_vsc], [r_vaug[0]])
        for h in range(4):
            A("pe", lambda e, h=h: e.transpose(out=pst[0:16, h, :], in_=qkm[:, 4 + h, :], identity=ident_b[:]), [r_qkm, r_ident], [r_pst])
        A("act", lambda e: e.activation(out=ktok[0:16, :, :], in_=pst[0:16, 0:4, :], func=AF.Copy), [r_pst], [r_ktok])
        for h in range(4):
            bU, rbU = bank()
            A("pe", lambda e, h=h, bU=bU: e.matmul(bU[:, 0:257], lhsT=ktok[0:16, h, :], rhs=vaugm[0:16, h, 0:257], start=True, stop=True),
              [r_ktok, r_vaug[0]], [rbU])
            A("dve", lambda e, h=h, bU=bU: e.tensor_scalar(out=Cm32[:, h, :], in0=bU[:, 0:257], scalar1=eB[:, h:h + 1], scalar2=None, op0=ALU.mult),
              [rbU, r_eB], [r_Cm32])
        A("pool", lambda e: e.memset(vaug[0][:], 0.0), [], [r_vaug[0]])

        def phase0_load(s_, ti_, j):
            t0_ = ti_ * T + j * 128
            A("sp", lambda e: e.dma_start(out=xs[j][:], in_=x[s_, t0_: t0_ + 128, :]), writes=[r_xs[j]], dma="xs%d" % j)

        def phase0_tabs(ti_):
            A("sp", lambda e: e.dma_start(out=cosT[:], in_=c_cos[ti_]), writes=[r_cos], dma="cos")
            A("sp", lambda e: e.dma_start(out=sinT[:], in_=c_sin[ti_]), writes=[r_sin], dma="sin")

        def phase0_norm(j):
            rmsnorm_T(xs[j][:], r_xs[j], 128, stt[j], r_stt[j], uT, r_uT, j * 128)

        def phase0_front(j):
            return rmsnorm_front(xs[j][:], r_xs[j], 128, stt[j], r_stt[j])

        def phase0_back(j, xbr):
            rmsnorm_back(xbr, 128, uT, r_uT, j * 128)

        wk_pref = [None]
        for s in range(NSEQ):
            A("pool", lambda e: e.tensor_copy(out=C32[:], in_=Cm32[:]), [r_Cm32], r_C32)
            A("pool", lambda e: e.tensor_copy(out=Cbf[:, :, 0:257], in_=Cm32[:]), [r_Cm32], r_Cbf)
            A("pool", lambda e: e.tensor_copy(out=halo[:], in_=halo_m[:]), [r_halom], [r_halo])
            for ti in range(NST):
                tok0 = ti * T
                if s == 0 and ti == 0:
                    phase0_tabs(ti)
                    for j in range(4):
                        phase0_load(s, ti, j)
                    for j in range(4):
                        phase0_norm(j)
                nxt = (s, ti + 1) if ti + 1 < NST else ((s + 1, 0) if s + 1 < NSEQ else None)
                wk = wk_pref[0] if wk_pref[0] is not None else load_w(WB["K"])
                wk_pref[0] = None
                for g in range(4):
                    b, rb = proj_fm(wk, g * 128, uT, r_uT, T)
                    flush()
                    rope(b, rb, T, cosT[:], sinT[:], [r_cos, r_sin], 1.0, Kbuf[:, g, 128:640], r_Kbuf)
                wv = load_w(WB["V"], 256)
                for j in range(4):
                    b, rb = proj_tm(wv, 256, uT, r_uT, j * 128, 128)
                    A("act", lambda e, b=b, j=j: e.activation(out=Vdup[j + 1][:, :, 0:64], in_=b[:, 0:256].rearrange("p (g d) -> p g d", g=4), func=AF.Copy),
                      [rb], [r_Vdup[j + 1]])
                QT, r_QT = bufA, r_bufA
                attT, r_attT = bufB, r_bufB
                for i in range(2):
                    w = load_w(WB["Q"] + i)
                    for c in range(4):
                        blk = 4 * i + c
                        b, rb = proj_fm(w, c * 128, uT, r_uT, T)
                        flush()
                        rope(b, rb, T, cosT[:], sinT[:], [r_cos, r_sin], 0.125, QT[:, blk, :], r_QT)
                flush()
                def att_S(j, g):
                    has_prev = not (ti == 0 and j == 0)
                    jc = slice(j * 128, (j + 1) * 128)
                    pi = rr("pt")
                    bSc, rSc = bank()
                    bSm, rSm = bank()
                    bSp, rSp = bank() if has_prev else (None, None)
                    for hh in range(4):
                        blk = 2 * g + hh // 2
                        rows = slice((hh % 2) * 64, (hh % 2) * 64 + 64)
                        hc = slice(hh * 128, (hh + 1) * 128)
                        A("pe", lambda e, rows=rows, hc=hc, blk=blk: e.matmul(
                            bSc[:, hc], lhsT=Kbuf[rows, g, 128 + j * 128: 256 + j * 128], rhs=QT[rows, blk, jc], start=True, stop=True),
                          [r_Kbuf, r_QT], [rSc])
                        A("pe", lambda e, rows=rows, hc=hc, blk=blk: e.matmul(
                            bSm[0:16, hc], lhsT=KmT[rows, g, :], rhs=QT[rows, blk, jc], start=True, stop=True),
                          [r_KmT, r_QT], [rSm])
                        if has_prev:
                            A("pe", lambda e, rows=rows, hc=hc, blk=blk: e.matmul(
                                bSp[:, hc], lhsT=Kbuf[rows, g, j * 128: 128 + j * 128], rhs=QT[rows, blk, jc], start=True, stop=True),
                              [r_Kbuf, r_QT], [rSp])
                    A("act", lambda e: e.activation(out=ptc[pi][:], in_=bSc[:], func=AF.Exp), [rSc], [r_ptc[pi]])
                    A("act", lambda e: e.activation(out=ptm[g][0:16, :], in_=bSm[0:16, :], func=AF.Exp), [rSm], [r_ptm[g]])
                    A("pool", lambda e: e.tensor_tensor(out=ptc[pi][:], in0=ptc[pi][:], in1=mask_b[:].rearrange("p h q -> p (h q)"), op=ALU.mult),
                      [r_ptc[pi], r_mask], [r_ptc[pi]])
                    if has_prev:
                        A("act", lambda e: e.activation(out=ptp[pi][:], in_=bSp[:], func=AF.Exp), [rSp], [r_ptp[pi]])
                        A("pool", lambda e: e.tensor_tensor(out=ptp[pi][:], in0=ptp[pi][:], in1=maskp_b[:].rearrange("p h q -> p (h q)"), op=ALU.mult),
                          [r_ptp[pi], r_maskp], [r_ptp[pi]])
                    return (j, g, pi, has_prev, jc)

                def att_PV(ctx):
                    j, g, pi, has_prev, jc = ctx
                    ai = j % 2
                    bO, rO = bank()
                    for hh in range(4):
                        hc = slice(hh * 128, (hh + 1) * 128)
                        oc = slice(hh * 65, hh * 65 + 65)
                        A("pe", lambda e, hc=hc, oc=oc: e.matmul(bO[:, oc], lhsT=ptm[g][:, hc], rhs=Vmeta[:, g, 0:65], start=True, stop=False),
                          [r_Vmeta, r_ptm[g]], [rO])
                        if has_prev:
                            A("pe", lambda e, hc=hc, oc=oc: e.matmul(bO[:, oc], lhsT=ptp[pi][:, hc], rhs=Vdup[j][:, g, 0:65], start=False, stop=False),
                              [r_Vdup[j], r_ptp[pi]], [rO])
                        A("pe", lambda e, hc=hc, oc=oc: e.matmul(bO[:, oc], lhsT=ptc[pi][:, hc], rhs=Vdup[j + 1][:, g, 0:65], start=False, stop=True),
                          [r_Vdup[j + 1], r_ptc[pi]], [rO])
                    ri = rr("rec4")
                    bOv = bO[:, 0:260].rearrange("p (h c) -> p h c", c=65)
                    A("dve", lambda e: e.reciprocal(out=rec4[ri][:], in_=bOv[:, :, 64]), [rO], [r_rec4[ri]])
                    A("dve", lambda e: e.tensor_tensor(out=att_tok[ai][:, g * 256:(g + 1) * 256].rearrange("p (h d) -> p h d", h=4), in0=bOv[:, :, 0:64],
                                                       in1=rec4[ri][:].unsqueeze(2).to_broadcast([128, 4, 64]), op=ALU.mult),
                      [rO, r_rec4[ri]], [r_att_tok[ai]])
                    if g == 3:
                        for k in range(8):
                            A("pe", lambda e, k=k: e.transpose(out=pst[:, k, :], in_=att_tok[ai][:, k * 128:(k + 1) * 128], identity=ident_b[:]),
                              [r_att_tok[ai], r_ident], [r_pst])
                        A("act", lambda e: e.activation(out=attT[:, :, jc], in_=pst[:], func=AF.Copy), [r_pst], [r_attT])

                order = [(j, g) for j in range(4) for g in range(4)]
                ctxs = [att_S(*order[0])]
                for idx in range(len(order)):
                    if idx + 1 < len(order):
                        ctxs.append(att_S(*order[idx + 1]))
                    att_PV(ctxs[idx])
                for i in range(2):
                    w = load_w(WB["Z"] + i)
                    for c in range(4):
                        blk = 4 * i + c
                        b, rb = proj_fm(w, c * 128, uT, r_uT, T)
                        k = rr("tz")
                        A("act", lambda e, b=b, k=k: e.activation(out=tzt[k][:], in_=b[:], func=AF.Tanh, scale=0.5), [rb], [r_tzt[k]])
                        A("dve", lambda e, b=b, k=k: e.scalar_tensor_tensor(out=zst[k][:], in0=tzt[k][:], scalar=1.0, in1=b[:], op0=ALU.add, op1=ALU.mult),
                          [r_tzt[k], rb], [r_zst[k]])
                        A("pool", lambda e, k=k, blk=blk: e.tensor_tensor(out=attT[:, blk, :], in0=attT[:, blk, :], in1=zst[k][:], op=ALU.mult),
                          [r_attT, r_zst[k]], [r_attT])
                for i in range(2):
                    wg = load_w(WB["GA"] + i)
                    wa = load_w(WB["AO"] + i)
                    for c in range(4):
                        blk = 4 * i + c
                        b, rb = proj_fm(wg, c * 128, uT, r_uT, T)
                        k = rr("tz")
                        A("act", lambda e, b=b, k=k: e.activation(out=tzt[k][:], in_=b[:], func=AF.Tanh, scale=0.5), [rb], [r_tzt[k]])
                        by, rby = proj_fm(wa, c * 128, attT, r_attT, T)
                        A("dve", lambda e, by=by, k=k, blk=blk: e.scalar_tensor_tensor(out=mrg[:, blk, :], in0=tzt[k][:], scalar=1.0, in1=by[:],
                                                                                    op0=ALU.add, op1=ALU.mult), [r_tzt[k], rby], [r_mrg])
                if nxt is not None:
                    phase0_tabs(nxt[1])
                    for j in range(4):
                        phase0_load(nxt[0], nxt[1], j)
                bg, rbg = bank()
                for j in range(4):
                    for kc in range(8):
                        A("pe", lambda e, kc=kc, j=j, bg=bg: e.matmul(bg[:, j * 8:(j + 1) * 8], lhsT=uT[:, kc, j * 128:(j + 1) * 128], rhs=wgate[:, kc, :],
                                                                 start=(kc == 0), stop=(kc == 7)), [r_uT, r_wgate], [rbg])
                A("dve", lambda e, bg=bg: e.tensor_tensor(out=gsb[:], in0=bg[:, 0:32].rearrange("p (j c) -> p j c", j=4),
                                                        in1=gbias[:].unsqueeze(1).to_broadcast([128, 4, 8]), op=ALU.add), [rbg, r_gbias], [r_gsb])
                A("act", lambda e: e.activation(out=e1[:], in_=gsb[:, :, 4:8], func=AF.Exp, scale=-1.0), [r_gsb], [r_e1])
                A("act", lambda e: e.activation(out=nlf[:], in_=e1[:].rearrange("p j h -> p (j h)"), func=AF.Ln, bias=1.0), [r_e1], [r_nlf])
                bnb, rbnb = bank()
                A("pe", lambda e, bnb=bnb: e.matmul(bnb[:, 0:16], lhsT=triu_f[:], rhs=nlf[:], start=True, stop=True), [r_triu, r_nlf], [rbnb])
                bns, rbns = bank()
                A("pe", lambda e, bns=bns: e.matmul(bns[:, 0:16], lhsT=ones_f[:], rhs=nlf[:], start=True, stop=True), [r_onesf, r_nlf], [rbns])
                A("act", lambda e, bnb=bnb: e.activation(out=thr[:], in_=bnb[:, 0:16], func=AF.Exp), [rbnb], [r_thr])
                A("dve", lambda e, bnb=bnb: e.tensor_tensor(out=gs[:].rearrange("p (j h) -> p j h", j=4), in0=gsb[:, :, 0:4],
                                                          in1=bnb[:, 0:16].rearrange("p (j h) -> p j h", j=4), op=ALU.add), [r_gsb, rbnb], [r_gs])
                A("act", lambda e: e.activation(out=vsc[:], in_=gs[:], func=AF.Exp, bias=LOGK), [r_gs], [r_vsc])
                A("act", lambda e, bns=bns: e.activation(out=eB[:], in_=bns[:, 0:16], func=AF.Exp, scale=-1.0), [rbns], [r_eB])
                qkT, r_qkT = bufC, r_bufC
                for i in range(2):
                    w = load_w(WB["MQK"] + i)
                    for c in range(4):
                        blk = 4 * i + c
                        b, rb = proj_fm(w, c * 128, uT, r_uT, T)
                        conv_silu(b, rb, blk, T, halo, r_halo, halo, r_halo, qkT[:, blk, :], r_qkT)
                for i in range(2):
                    w = load_w(WB["MV"] + i)
                    for j in range(4):
                        b, rb = proj_tm(w, 512, uT, r_uT, j * 128, 128)
                        for hh in range(2):
                            h = 2 * i + hh
                            A("act", lambda e, b=b, hh=hh, h=h, j=j: e.activation(out=vaug[j][:, h, 0:256], in_=b[:, hh * 256:(hh + 1) * 256], func=AF.Copy,
                                                                             scale=vsc[:, j * 4 + h: j * 4 + h + 1]), [rb, r_vsc], [r_vaug[j]])
                for j in range(4):
                    A("pool", lambda e, j=j: e.tensor_copy(out=vaug[j][:, :, 256:257], in_=vsc[:, j * 4:(j + 1) * 4].unsqueeze(2)), [r_vsc], [r_vaug[j]])
                th = bufB[:].rearrange("p k t -> p (k t)").rearrange("p (j f) -> p j f", j=4)
                r_th = r_bufB
                zs = bufA[:].rearrange("p k t -> p (k t)").rearrange("p (j f) -> p j f", j=4)
                r_zs = r_bufA
                for i in range(2):
                    w = load_w(WB["MO"] + i)
                    for j in range(4):
                        b, rb = proj_tm(w, 512, uT, r_uT, j * 128, 128)
                        A("act", lambda e, b=b, j=j, i=i: e.activation(out=th[:, j, i * 512:(i + 1) * 512], in_=b[:], func=AF.Tanh, scale=0.5), [rb], [r_th])
                for i in range(2):
                    w = load_w(WB["MZ"] + i)
                    for j in range(4):
                        b, rb = proj_tm(w, 512, uT, r_uT, j * 128, 128)
                        k = rr("tz")
                        A("act", lambda e, b=b, k=k: e.activation(out=tzt[k][:], in_=b[:], func=AF.Tanh, scale=0.5), [rb], [r_tzt[k]])
                        A("dve", lambda e, b=b, k=k: e.scalar_tensor_tensor(out=zst[k][:], in0=tzt[k][:], scalar=1.0, in1=b[:],
                                                                          op0=ALU.add, op1=ALU.mult), [r_tzt[k], rb], [r_zst[k]])
                        A("pool", lambda e, k=k, j=j, i=i: e.tensor_tensor(out=zs[:, j, i * 512:(i + 1) * 512], in0=zst[k][:], in1=hnB[:, i * 512:(i + 1) * 512], op=ALU.mult),
                          [r_zst[k], r_hnB], [r_zs])
                def rec_pe_a(j):
                    jc = slice(j * 128, (j + 1) * 128)
                    for h in range(4):
                        A("pe", lambda e, h=h: e.transpose(out=pst[:, h, :], in_=qkT[:, 4 + h, jc], identity=ident_b[:]), [r_qkT, r_ident], [r_pst])
                    A("act", lambda e: e.activation(out=ktok[:], in_=pst[:, 0:4, :], func=AF.Copy), [r_pst], [r_ktok])
                    bS, rS = bank()
                    for h in range(4):
                        A("pe", lambda e, h=h: e.matmul(bS[:, h * 128:(h + 1) * 128], lhsT=qkT[:, 4 + h, jc], rhs=qkT[:, h, jc], start=True, stop=True),
                          [r_qkT], [rS])
                    A("dve", lambda e: e.tensor_tensor(out=PTm[:], in0=bS[:], in1=mask_b[:].rearrange("p h q -> p (h q)"), op=ALU.mult), [rS, r_mask], [r_PTm])
                    return jc

                def rec_pe_b(j, jc):
                    bNs = [bank(), bank()]
                    bDn, rDn = bank()
                    for h in range(4):
                        bN, rN = bNs[h // 2]
                        nc_ = slice((h % 2) * 256, (h % 2) * 256 + 256)
                        A("pe", lambda e, h=h, bN=bN, nc_=nc_: e.matmul(bN[:, nc_], lhsT=PTm[:, h * 128:(h + 1) * 128], rhs=vaug[j][:, h, 0:256], start=True, stop=False),
                          [r_PTm, r_vaug[j]], [rN])
                        A("pe", lambda e, h=h, bN=bN, nc_=nc_: e.matmul(bN[:, nc_], lhsT=qkT[:, h, jc], rhs=Cbf[:, h, 0:256], start=False, stop=True),
                          [r_qkT, r_Cbf[h]], [rN])
                        A("pe", lambda e, h=h: e.matmul(bDn[:, h:h + 1], lhsT=PTm[:, h * 128:(h + 1) * 128], rhs=vaug[j][:, h, 256:257], start=True, stop=False),
                          [r_PTm, r_vaug[j]], [rDn])
                        A("pe", lambda e, h=h: e.matmul(bDn[:, h:h + 1], lhsT=qkT[:, h, jc], rhs=Cbf[:, h, 256:257], start=False, stop=True),
                          [r_qkT, r_Cbf[h]], [rDn])
                    bUs = []
                    for h in range(4):
                        bU, rbU = bank()
                        A("pe", lambda e, h=h, bU=bU: e.matmul(bU[:, 0:257], lhsT=ktok[:, h, :], rhs=vaug[j][:, h, 0:257], start=True, stop=True),
                          [r_ktok, r_vaug[j]], [rbU])
                        bUs.append((bU, rbU))
                    return (j, jc, bNs, bDn, rDn, bUs)

                def rec_state(ctx):
                    j, jc, bNs, bDn, rDn, bUs = ctx
                    for h in range(4):
                        bU, rbU = bUs[h]
                        col = j * 4 + h
                        A("dve", lambda e, h=h, col=col: e.tensor_scalar(out=C32[:, h, :], in0=C32[:, h, :], scalar1=eB[:, col:col + 1], scalar2=None, op0=ALU.mult),
                          [r_C32[h], r_eB], [r_C32[h]])
                        A("dve", lambda e, h=h, col=col, bU=bU: e.scalar_tensor_tensor(out=C32[:, h, :], in0=bU[:, 0:257], scalar=eB[:, col:col + 1], in1=C32[:, h, :],
                                                                                    op0=ALU.mult, op1=ALU.add), [rbU, r_eB, r_C32[h]], [r_C32[h]])
                        A("act", lambda e, h=h: e.activation(out=Cbf[:, h, 0:257], in_=C32[:, h, :], func=AF.Copy), [r_C32[h]], [r_Cbf[h]])

                def rec_den(ctx):
                    j, jc, bNs, bDn, rDn, bUs = ctx
                    c4 = slice(j * 4, (j + 1) * 4)
                    A("dve", lambda e: e.tensor_scalar(out=dd[:, 0, :], in0=bDn[:, 0:4], scalar1=-1.0, scalar2=None, op0=ALU.mult), [rDn], [r_dd[0]])
                    A("dve", lambda e: e.tensor_tensor(out=dd[:, 1, :], in0=bDn[:, 0:4], in1=dd[:, 0, :], op=ALU.max), [rDn, r_dd[0]], [r_dd[0]])
                    A("dve", lambda e: e.tensor_tensor(out=dd[:, 2, :], in0=dd[:, 1, :], in1=thr[:, c4], op=ALU.max), [r_dd[0], r_thr], [r_dd[0]])
                    A("dve", lambda e: e.reciprocal(out=dd[:, 3, :], in_=dd[:, 2, :]), [r_dd[0]], [r_dd[0]])
                    for h in range(4):
                        bN, rN = bNs[h // 2]
                        nc_ = slice((h % 2) * 256, (h % 2) * 256 + 256)
                        A("act", lambda e, h=h, bN=bN, nc_=nc_: e.activation(out=tmpN[:, h * 256:(h + 1) * 256], in_=bN[:, nc_], func=AF.Copy, scale=dd[:, 3, h:h + 1]),
                          [rN, r_dd[0]], [r_tmpN])

                def rec_rest(ctx):
                    j, jc, bNs, bDn, rDn, bUs = ctx
                    A("dve", lambda e: e.scalar_tensor_tensor(out=ho[:], in0=th[:, j, :], scalar=1.0, in1=tmpN[:], op0=ALU.add, op1=ALU.mult),
                      [r_th, r_tmpN], [r_ho])
                    for h in range(4):
                        A("dve", lambda e, h=h: e.bn_stats(out=st6[:, h, :], in_=ho[:, h * 256:(h + 1) * 256]), [r_ho], [r_st6])
                    for h in range(4):
                        A("dve", lambda e, h=h: e.bn_aggr(out=mv[:, h, :], in_=st6[:, h, :]), [r_st6], [r_mv])
                    A("pool", lambda e: e.tensor_scalar(out=lnv[:, 0:4], in0=mv[:, :, 1], scalar1=4e-6, scalar2=None, op0=ALU.add), [r_mv], [r_lnv])
                    A("pool", lambda e: e.tensor_tensor(out=lnv[:, 4:8], in0=lnv[:, 0:4], in1=mhalf[:, 0:4], op=ALU.pow), [r_lnv, r_mhalf], [r_lnv])
                    for h in range(4):
                        A("dve", lambda e, h=h: e.tensor_scalar(out=ho[:, h * 256:(h + 1) * 256], in0=ho[:, h * 256:(h + 1) * 256], scalar1=mv[:, h, 0:1],
                                                              scalar2=lnv[:, 4 + h:5 + h], op0=ALU.subtract, op1=ALU.mult), [r_ho, r_mv, r_lnv], [r_ho])
                    hi = rr("hz")
                    A("pool", lambda e: e.tensor_tensor(out=hzs[hi][:], in0=ho[:], in1=zs[:, j, :], op=ALU.mult), [r_ho, r_zs], [r_hzs[hi]])
                    return (jc, hi)

                def rec_tail(t):
                    jc, hi = t
                    for k in range(8):
                        A("pe", lambda e, k=k: e.transpose(out=pst[:, k, :], in_=hzs[hi][:, k * 128:(k + 1) * 128], identity=ident_b[:]), [r_hzs[hi], r_ident], [r_pst])
                    A("act", lambda e: e.activation(out=hzT[:, :, jc], in_=pst[:], func=AF.Copy), [r_pst], [r_hzT])

                tail = None
                jc_next = rec_pe_a(0)
                for j in range(4):
                    ctx = rec_pe_b(j, jc_next)
                    bctr[0] += 3
                    rec_den(ctx)
                    rec_state(ctx)
                    if j + 1 < 4:
                        jc_next = rec_pe_a(j + 1)
                    if tail is not None:
                        rec_tail(tail)
                    tail = rec_rest(ctx)
                rec_tail(tail)
                mT, r_mT = bufC, r_bufC
                for i in range(2):
                    wg = load_w(WB["GM"] + i)
                    wm = load_w(WB["MOUT"] + i)
                    for c in range(4):
                        blk = 4 * i + c
                        b, rb = proj_fm(wg, c * 128, uT, r_uT, T)
                        k = rr("tz")
                        A("act", lambda e, b=b, k=k: e.activation(out=tzt[k][:], in_=b[:], func=AF.Tanh, scale=0.5), [rb], [r_tzt[k]])
                        by, rby = proj_fm(wm, c * 128, hzT, r_hzT, T)
                        A("dve", lambda e, by=by, k=k: e.scalar_tensor_tensor(out=zst[k][:], in0=tzt[k][:], scalar=1.0, in1=by[:], op0=ALU.add, op1=ALU.mult),
                          [r_tzt[k], rby], [r_zst[k]])
                        A("pool", lambda e, k=k, blk=blk: e.tensor_tensor(out=mT[:, blk, :], in0=zst[k][:], in1=mrg[:, blk, :], op=ALU.add),
                          [r_zst[k], r_mrg], [r_mT])
                w0 = load_w(WB["OUT"])
                w1 = load_w(WB["OUT"] + 1)
                if nxt is not None:
                    wk_pref[0] = load_w(WB["K"])
                xr = [(tmpN, r_tmpN), (ho, r_ho)]

                def reload(j):
                    xt_, rx_ = xr[j % 2]
                    t0_ = tok0 + j * 128
                    A("sp", lambda e, s=s: e.dma_start(out=xt_[:], in_=x[s, t0_: t0_ + 128, :]), writes=[rx_], dma="xr%d" % (j % 2))

                reload(0)
                reload(1)
                fronts = {}
                if nxt is not None:
                    fronts[0] = phase0_front(0)
                    fronts[1] = phase0_front(1)
                for j in range(4):
                    oi = rr("ot")
                    b0, rb0 = proj_tm(w0, 512, mT, r_mT, j * 128, 128)
                    b1, rb1 = proj_tm(w1, 512, mT, r_mT, j * 128, 128)
                    A("act", lambda e, b0=b0, oi=oi: e.activation(out=hz[:, 0:512], in_=b0[:], func=AF.Square, accum_out=ss[oi][:, 0:1]), [rb0], [r_hz, r_ss[oi]])
                    A("act", lambda e, b1=b1, oi=oi: e.activation(out=hz[:, 512:1024], in_=b1[:], func=AF.Square, accum_out=ss[oi][:, 1:2]), [rb1], [r_hz, r_ss[oi]])
                    A("dve", lambda e, oi=oi: e.tensor_tensor(out=ss[oi][:, 2:3], in0=ss[oi][:, 0:1], in1=ss[oi][:, 1:2], op=ALU.add), [r_ss[oi]], [r_ss[oi]])
                    A("pool", lambda e, oi=oi: e.tensor_scalar(out=ss[oi][:, 3:4], in0=ss[oi][:, 2:3], scalar1=1024 * 16e-6, scalar2=None, op0=ALU.add), [r_ss[oi]], [r_ss[oi]])
                    A("pool", lambda e, oi=oi: e.tensor_tensor(out=ss[oi][:, 4:5], in0=ss[oi][:, 3:4], in1=mhalf[:, 0:1], op=ALU.pow), [r_ss[oi], r_mhalf], [r_ss[oi]])
                    A("dve", lambda e, b0=b0, oi=oi: e.scalar_tensor_tensor(out=ot[oi][:, 0:512], in0=b0[:], scalar=ss[oi][:, 4:5], in1=npB[:, 0:512],
                                                                          op0=ALU.mult, op1=ALU.mult), [rb0, r_ss[oi], r_npB], [r_ot[oi]])
                    A("dve", lambda e, b1=b1, oi=oi: e.scalar_tensor_tensor(out=ot[oi][:, 512:1024], in0=b1[:], scalar=ss[oi][:, 4:5], in1=npB[:, 512:1024],
                                                                          op0=ALU.mult, op1=ALU.mult), [rb1, r_ss[oi], r_npB], [r_ot[oi]])
                    xt_, rx_ = xr[j % 2]
                    A("pool", lambda e, oi=oi, xt_=xt_: e.tensor_tensor(out=ot[oi][:], in0=ot[oi][:], in1=xt_[:], op=ALU.add), [r_ot[oi], rx_], [r_ot[oi]])
                    A("sp", lambda e, oi=oi, j=j, s=s, tok0=tok0: e.dma_start(out=out[s, tok0 + j * 128: tok0 + (j + 1) * 128, :], in_=ot[oi][:]),
                      reads=[r_ot[oi]], writes=[new_out()], dma="out%d" % oi)
                    if j + 2 < 4:
                        reload(j + 2)
                    if nxt is not None and j < 2:
                        phase0_back(2 * j, fronts[2 * j])
                        phase0_back(2 * j + 1, fronts[2 * j + 1])
                        if j == 0:
                            fronts[2] = phase0_front(2)
                            fronts[3] = phase0_front(3)
                A("pool", lambda e: e.tensor_copy(out=Kbuf[:, :, 0:128], in_=Kbuf[:, :, 512:640]), [r_Kbuf], [r_Kbuf])
                A("pool", lambda e: e.tensor_copy(out=Vdup[0][:], in_=Vdup[4][:]), [r_Vdup[4]], [r_Vdup[0]])
        fin = P.res("fin")
        A("sp", lambda e: e.nop(), reads=r_outs, writes=[fin])
        P.emit()
    return nc


def _consts(NST):
    ident = np.eye(128, dtype=np.float32)
    triu = np.triu(np.ones((128, 128), dtype=np.float32))
    pit = np.zeros((128, 128), dtype=np.float32)
    for m in range(128):
        d = m % 64
        base = m - d
        if d < 8:
            k = base + d + 8
        elif d < 16:
            k = base + d - 8
        else:
            k = m
        pit[k, m] = 1.0
    half = 8
    inv_freq = (500000.0 ** (-np.arange(0, 16, 2, dtype=np.float32) / 16)).astype(np.float32)

    def tables(pos):
        pos = pos.astype(np.float32)
        ang = (pos[:, None] * inv_freq[None, :]).astype(np.float32)
        c = np.cos(ang.astype(np.float64)).astype(np.float32)
        s_ = np.sin(ang.astype(np.float64)).astype(np.float32)
        n = pos.shape[0]
        cosT = np.ones((128, n), dtype=np.float32)
        sinT = np.zeros((128, n), dtype=np.float32)
        for hb in (0, 64):
            cosT[hb:hb + 8] = c.T
            cosT[hb + 8:hb + 16] = c.T
            sinT[hb:hb + 8] = -s_.T
            sinT[hb + 8:hb + 16] = s_.T
        return cosT, sinT

    cos = np.zeros((NST, 128, T), dtype=np.float32)
    sin = np.zeros((NST, 128, T), dtype=np.float32)
    for ti in range(NST):
        cos[ti], sin[ti] = tables(16 + ti * T + np.arange(T))
    cosm, sinm = tables(np.arange(16))
    return dict(c_ident=ident, c_triu=triu, c_pit=pit, c_cos=cos, c_sin=sin, c_cosm=cosm, c_sinm=sinm)


_NC_CACHE = {}


def run(inputs, n_cores, nseq, nst):
    key = (nseq, nst)
    if key not in _NC_CACHE:
        _NC_CACHE[key] = build_nc(nseq, nst)
    nc = _NC_CACHE[key]
    cs = _consts(nst)
    f = lambda a: np.ascontiguousarray(np.asarray(a, dtype=np.float32))
    shared = {
        "meta_tokens": f(inputs["meta_tokens"]),
        "norm_pre": f(inputs["norm_pre"][0]),
        "w_in": f(inputs["w_in"][0]),
        "attn_sinks": f(inputs["attn_sinks"][0]),
        "conv_w": f(inputs["conv_w"][0]),
        "conv_b": f(inputs["conv_b"][0]),
        "mlstm_gate_bias": f(inputs["mlstm_gate_bias"][0]),
        "mlstm_head_norm": f(inputs["mlstm_head_norm"][0]),
        "w_attn_out": f(inputs["w_attn_out"][0]),
        "w_mlstm_out": f(inputs["w_mlstm_out"][0]),
        "w_out": f(inputs["w_out"][0]),
        "norm_post": f(inputs["norm_post"][0]),
    }
    shared.update(cs)
    xfull = np.asarray(inputs["x"], dtype=np.float32)
    in_maps = []
    for c in range(n_cores):
        m = dict(shared)
        m["x"] = np.ascontiguousarray(xfull[c * nseq:(c + 1) * nseq, :nst * T])
        in_maps.append(m)
    res = run_bass_kernel_spmd(nc, in_maps, core_ids=list(range(n_cores)))
    return np.concatenate([np.asarray(r["out"]) for r in res.results], axis=0)


def kernel(**inputs):
    return run(inputs, 8, 2, 8).astype(np.float32)
```

```python
import contextlib
import math
import numpy as np
import concourse.bass as bass
import concourse.mybir as mybir
from concourse.bass_utils import run_bass_kernel_spmd

F32 = mybir.dt.float32
BF16 = mybir.dt.bfloat16
ALU = mybir.AluOpType
AF = mybir.ActivationFunctionType

ENGS = ("pe", "act", "dve", "pool", "sp")


class Res:
    __slots__ = ("name", "w", "readers")

    def __init__(self, name):
        self.name = name
        self.w = None
        self.readers = []


class Op:
    __slots__ = ("eng", "fn", "semkey", "inc", "deps", "signal", "idx", "is_dma")

    def __init__(self, eng, fn, semkey, inc, is_dma):
        self.eng = eng
        self.fn = fn
        self.semkey = semkey
        self.inc = inc
        self.deps = []
        self.signal = False
        self.idx = 0
        self.is_dma = is_dma


class Prog:
    def __init__(self, nc):
        self.nc = nc
        self.streams = {e: [] for e in ENGS}
        self.by_sem = {}

    def res(self, name):
        return Res(name)

    def op(self, eng, fn, reads=(), writes=(), dma=None):
        is_dma = dma is not None
        semkey = ("dma", dma) if is_dma else eng
        o = Op(eng, fn, semkey, 16 if is_dma else 1, is_dma)
        if is_dma:
            o.signal = True
        deps = {}

        def need(p, kind):
            if p is None:
                return
            if (not p.is_dma) and (not is_dma) and p.eng == eng:
                if eng == "pe":
                    return
            deps[id(p)] = p

        for r in reads:
            need(r.w, "raw")
        for w in writes:
            need(w.w, "waw")
            for t in w.readers:
                need(t, "war")
        o.deps = list(deps.values())
        for p in o.deps:
            p.signal = True
        for w in writes:
            w.w = o
            w.readers = []
        for r in reads:
            if not is_dma:
                r.readers = [t for t in r.readers if t.is_dma or t.eng != eng]
            r.readers.append(o)
        self.streams[eng].append(o)
        self.by_sem.setdefault(semkey, []).append(o)
        return o

    def emit(self):
        nc = self.nc
        for k, ops in self.by_sem.items():
            c = 0
            for o in ops:
                if o.signal:
                    c += o.inc
                o.idx = c
        with contextlib.ExitStack() as st:
            sems = {}
            for k in self.by_sem:
                nm = "s_" + (k if isinstance(k, str) else "d_" + str(k[1]))
                sems[k] = st.enter_context(nc.semaphore(nm))
            block = st.enter_context(nc.Block())
            prog = self

            def run(eng_name, handle):
                waited = {}
                for o in prog.streams[eng_name]:
                    need = {}
                    for p in o.deps:
                        if p.idx > need.get(p.semkey, 0):
                            need[p.semkey] = p.idx
                    for k, v in need.items():
                        if waited.get(k, 0) >= v:
                            continue
                        handle.wait_ge(sems[k], v)
                        waited[k] = v
                    ins = o.fn(handle)
                    if o.signal:
                        ins.then_inc(sems[o.semkey], o.inc)

            @block.tensor
            def _(e):
                run("pe", e)

            @block.scalar
            def _(e):
                run("act", e)

            @block.vector
            def _(e):
                run("dve", e)

            @block.gpsimd
            def _(e):
                run("pool", e)

            @block.sync
            def _(e):
                run("sp", e)


C_AQ, C_AK, C_AV, C_AZ = 0, 1024, 1280, 1536
C_MQK, C_MV, C_MI, C_MF, C_MO, C_MZ, C_GA, C_GM = 2560, 3584, 4608, 4612, 4616, 5640, 6664, 7688
IN_W = 8712
T = 512
LOGK = -0.5 * math.log(128.0)


def build_nc(NSEQ, NST):
    nc = bass.Bass("TRN2", target_bir_lowering=False)
    S = NST * T

    def din(name, shape, dt=F32):
        return nc.dram_tensor(name, shape, dt, kind="ExternalInput").ap()

    x = din("x", [NSEQ, S, 1024])
    meta = din("meta_tokens", [16, 1024])
    norm_pre = din("norm_pre", [1024])
    w_in = din("w_in", [1024, IN_W])
    sinks = din("attn_sinks", [16])
    conv_w = din("conv_w", [4, 1024])
    conv_b = din("conv_b", [1024])
    gate_bias = din("mlstm_gate_bias", [8])
    head_norm = din("mlstm_head_norm", [1024])
    w_ao = din("w_attn_out", [1024, 1024])
    w_mo = din("w_mlstm_out", [1024, 1024])
    w_out = din("w_out", [1024, 1024])
    norm_post = din("norm_post", [1024])
    c_ident = din("c_ident", [128, 128])
    c_triu = din("c_triu", [128, 128])
    c_pit = din("c_pit", [128, 128])
    c_cos = din("c_cos", [NST, 128, T])
    c_sin = din("c_sin", [NST, 128, T])
    c_cosm = din("c_cosm", [128, 16])
    c_sinm = din("c_sinm", [128, 16])
    out = nc.dram_tensor("out", [NSEQ, S, 1024], F32, kind="ExternalOutput").ap()

    def dscr(name, shape):
        return nc.dram_tensor(name, shape, BF16, kind="Internal").ap()

    NWB = 24
    wbt = dscr("wbt", [NWB, 128, 8, 512])
    WB = {"K": 0, "V": 1, "Q": 2, "Z": 4, "GA": 6, "MQK": 8, "MV": 10, "MO": 12, "MZ": 14, "GM": 16, "AO": 18, "MOUT": 20, "OUT": 22}

    P = Prog(nc)
    with contextlib.ExitStack() as st:
        def sb(name, shape, dt=F32):
            return st.enter_context(nc.sbuf_tensor(name, shape, dt))

        NB = 7
        banks = [st.enter_context(nc.psum_tensor("pb%d" % i, [128, 512], F32)) for i in range(NB)]
        bank_res = [P.res("pb%d" % i) for i in range(NB)]
        pst = st.enter_context(nc.psum_tensor("pst", [128, 8, 128], BF16))
        r_pst = P.res("pst")
        bctr = [0]

        def bank():
            i = bctr[0] % NB
            bctr[0] += 1
            return banks[i], bank_res[i]

        ident_f = sb("ident_f", [128, 128]); r_identf = P.res("identf")
        ident_b = sb("ident_b", [128, 128], BF16); r_ident = P.res("ident")
        triu_f = sb("triu_f", [128, 128]); r_triu = P.res("triu")
        pit_f = sb("pit_f", [128, 128]); r_pitf = P.res("pitf")
        pit_b = sb("pit_b", [128, 128], BF16); r_pit = P.res("pit")
        ones_f = sb("ones_f", [128, 128]); r_onesf = P.res("onesf")
        ones_b = sb("ones_b", [128, 128], BF16); r_onesb = P.res("onesb")
        mask_b = sb("mask_b", [128, 4, 128], BF16); r_mask = P.res("mask")
        maskp_b = sb("maskp_b", [128, 4, 128], BF16); r_maskp = P.res("maskp")
        gB = sb("gB", [128, 1024]); r_gB = P.res("gB")
        hnB = sb("hnB", [128, 1024]); r_hnB = P.res("hnB")
        npB = sb("npB", [128, 1024]); r_npB = P.res("npB")
        gbias = sb("gbias", [128, 8]); r_gbias = P.res("gbias")
        cw = sb("cw", [128, 8, 4]); r_cw = P.res("cw")
        cb = sb("cb", [128, 8]); r_cb = P.res("cb")
        sk = sb("sk", [33, 16]); r_sk = P.res("sk")
        wgate = sb("wgate", [128, 8, 8], BF16); r_wgate = P.res("wgate")
        cosT = sb("cosT", [128, T]); r_cos = P.res("cos")
        sinT = sb("sinT", [128, T]); r_sin = P.res("sin")
        cosm = sb("cosm", [128, 16]); r_cosm = P.res("cosm")
        sinm = sb("sinm", [128, 16]); r_sinm = P.res("sinm")

        NSLOT = 3
        wring = [sb("wring%d" % i, [128, 8, 512], BF16) for i in range(NSLOT)]
        r_wring = [P.res("wring%d" % i) for i in range(NSLOT)]
        wctr = [0]

        xs = [sb("xs%d" % j, [128, 1024]) for j in range(4)]
        r_xs = [P.res("xs%d" % j) for j in range(4)]
        xbs = [sb("xb%d" % i, [128, 1024], BF16) for i in range(2)]
        r_xbs = [P.res("xb%d" % i) for i in range(2)]
        mhalf = sb("mhalf", [128, 8]); r_mhalf = P.res("mhalf")
        stt = [sb("stt%d" % j, [128, 4]) for j in range(4)]
        r_stt = [P.res("stt%d" % j) for j in range(4)]
        uT = sb("uT", [128, 8, T], BF16); r_uT = P.res("uT")
        bufA = sb("bufA", [128, 8, T], BF16); r_bufA = P.res("bufA")
        bufB = sb("bufB", [128, 8, T], BF16); r_bufB = P.res("bufB")
        bufC = sb("bufC", [128, 8, T], BF16); r_bufC = P.res("bufC")
        hzT = sb("hzT", [128, 8, T], BF16); r_hzT = P.res("hzT")
        mrg = sb("mrg", [128, 8, T], BF16); r_mrg = P.res("mrg")
        Kbuf = sb("Kbuf", [128, 4, 640], BF16); r_Kbuf = P.res("Kbuf")
        KmT = sb("KmT", [128, 4, 16], BF16); r_KmT = P.res("KmT")
        Vdup = [sb("Vaug%d" % i, [128, 4, 66], BF16) for i in range(5)]
        r_Vdup = [P.res("Vaug%d" % i) for i in range(5)]
        Vmeta = sb("Vmeta", [33, 4, 66], BF16); r_Vmeta = P.res("Vmeta")
        att_tok = [sb("att_tok%d" % i, [128, 1024], BF16) for i in range(2)]
        r_att_tok = [P.res("att_tok%d" % i) for i in range(2)]
        rec4 = [sb("rec4_%d" % i, [128, 4]) for i in range(2)]
        r_rec4 = [P.res("rec4_%d" % i) for i in range(2)]
        ptc = [sb("ptc%d" % i, [128, 512], BF16) for i in range(2)]
        r_ptc = [P.res("ptc%d" % i) for i in range(2)]
        ptp = [sb("ptp%d" % i, [128, 512], BF16) for i in range(2)]
        r_ptp = [P.res("ptp%d" % i) for i in range(2)]
        ptm = [sb("ptm%d" % g, [33, 512], BF16) for g in range(4)]
        r_ptm = [P.res("ptm%d" % g) for g in range(4)]
        raw = [sb("raw%d" % i, [128, 512], BF16) for i in range(2)]
        r_raw = [P.res("raw%d" % i) for i in range(2)]
        t1 = [sb("t1_%d" % i, [128, 512]) for i in range(2)]
        r_t1 = [P.res("t1_%d" % i) for i in range(2)]
        t2 = [sb("t2_%d" % i, [128, 512]) for i in range(2)]
        r_t2 = [P.res("t2_%d" % i) for i in range(2)]
        tzt = [sb("tzt%d" % i, [128, 512], BF16) for i in range(2)]
        r_tzt = [P.res("tzt%d" % i) for i in range(2)]
        zst = [sb("zst%d" % i, [128, 512], BF16) for i in range(2)]
        r_zst = [P.res("zst%d" % i) for i in range(2)]
        pre = [sb("pre%d" % i, [128, 515]) for i in range(2)]
        r_pre = [P.res("pre%d" % i) for i in range(2)]
        acc = [sb("acc%d" % i, [128, 512]) for i in range(2)]
        r_acc = [P.res("acc%d" % i) for i in range(2)]
        halo = sb("halo", [128, 8, 3]); r_halo = P.res("halo")
        halo_m = sb("halo_m", [128, 8, 3]); r_halom = P.res("halom")
        gsb = sb("gsb", [128, 4, 8]); r_gsb = P.res("gsb")
        e1 = sb("e1", [128, 4, 4]); r_e1 = P.res("e1")
        nlf = sb("nlf", [128, 16]); r_nlf = P.res("nlf")
        thr = sb("thr", [128, 16]); r_thr = P.res("thr")
        gs = sb("gs", [128, 16]); r_gs = P.res("gs")
        vsc = sb("vsc", [128, 16]); r_vsc = P.res("vsc")
        eB = sb("eB", [128, 16]); r_eB = P.res("eB")
        vaug = [sb("vaug%d" % j, [128, 4, 272], BF16) for j in range(4)]
        r_vaug = [P.res("vaug%d" % j) for j in range(4)]
        ktok = sb("ktok", [128, 4, 128], BF16); r_ktok = P.res("ktok")
        PTm = sb("PTm", [128, 512], BF16); r_PTm = P.res("PTm")
        C32 = sb("C32", [128, 4, 257]); r_C32 = [P.res("C32_%d" % h) for h in range(4)]
        Cbf = sb("Cbf", [128, 4, 272], BF16); r_Cbf = [P.res("Cbf_%d" % h) for h in range(4)]
        Cm32 = sb("Cm32", [128, 4, 257]); r_Cm32 = P.res("Cm32")
        Ue = [sb("Ue%d" % i, [128, 257]) for i in range(2)]
        r_Ue = [P.res("Ue%d" % i) for i in range(2)]
        dd = sb("dd", [128, 4, 4]); r_dd = [P.res("dd%d" % h) for h in range(4)]
        tmpN = sb("tmpN", [128, 1024]); r_tmpN = P.res("tmpN")
        ho = sb("ho", [128, 1024]); r_ho = P.res("ho")
        st6 = sb("st6", [128, 4, 6]); r_st6 = P.res("st6")
        mv = sb("mv", [128, 4, 2]); r_mv = P.res("mv")
        lnv = sb("lnv", [128, 8]); r_lnv = P.res("lnv")
        hzs = [sb("hz%d" % i, [128, 1024], BF16) for i in range(2)]
        r_hzs = [P.res("hz%d" % i) for i in range(2)]
        hz, r_hz = hzs[0], r_hzs[0]
        ot = [sb("ot%d" % i, [128, 1024]) for i in range(2)]
        r_ot = [P.res("ot%d" % i) for i in range(2)]
        ss = [sb("ss%d" % i, [128, 8]) for i in range(2)]
        r_ss = [P.res("ss%d" % i) for i in range(2)]
        r_outs = []

        def new_out():
            r = P.res("out%d" % len(r_outs))
            r_outs.append(r)
            return r

        ctr = {"raw": 0, "tz": 0, "pre": 0, "pt": 0, "ot": 0, "xb": 0, "hz": 0, "rec4": 0, "ue": 0}

        def rr(key, n=2):
            i = ctr[key] % n
            ctr[key] += 1
            return i

        A = P.op

        r_castb = {}

        def cast_blk(bi, src, c0, n):
            r = P.res("cast%d" % bi)
            A("pool", lambda e: e.dma_start(out=wbt[bi][:, :, 0:n], in_=src[:, c0:c0 + n].rearrange("(k p) n -> p k n", p=128)),
              writes=[r], dma="cast%d" % bi)
            r_castb[bi] = [r]

        rk = []
        for g in range(4):
            for d in range(2):
                r = P.res("castK%d%d" % (g, d))
                A("pool", lambda e, g=g, d=d: e.dma_start(out=wbt[0][:, :, g * 128 + d * 64: g * 128 + d * 64 + 64],
                                                      in_=w_in[:, C_AK + g * 64: C_AK + g * 64 + 64].rearrange("(k p) n -> p k n", p=128)),
                  writes=[r], dma="castK")
                rk.append(r)
        r_castb[0] = rk
        cast_blk(1, w_in, C_AV, 256)
        for nm, c0 in (("Q", C_AQ), ("Z", C_AZ), ("GA", C_GA)):
            for i in range(2):
                cast_blk(WB[nm] + i, w_in, c0 + i * 512, 512)
        for i in range(2):
            cast_blk(WB["AO"] + i, w_ao, i * 512, 512)
        r_wgate_c = P.res("wgate_c")
        for nm, c0 in (("MQK", C_MQK), ("MV", C_MV), ("MO", C_MO), ("MZ", C_MZ), ("GM", C_GM)):
            for i in range(2):
                cast_blk(WB[nm] + i, w_in, c0 + i * 512, 512)
        for i in range(2):
            cast_blk(WB["MOUT"] + i, w_mo, i * 512, 512)
        for i in range(2):
            cast_blk(WB["OUT"] + i, w_out, i * 512, 512)

        def ld(dst, src, r, name, **kw):
            A("sp", lambda e: e.dma_start(out=dst, in_=src, **kw), writes=[r], dma=name)

        ld(ident_f[:], c_ident, r_identf, "identf")
        ld(triu_f[:], c_triu, r_triu, "triu")
        ld(pit_f[:], c_pit, r_pitf, "pitf")
        ld(gB[:], norm_pre.partition_broadcast(128), r_gB, "gB")
        ld(hnB[:], head_norm.partition_broadcast(128), r_hnB, "hnB")
        ld(npB[:], norm_post.partition_broadcast(128), r_npB, "npB")
        ld(gbias[:], gate_bias.partition_broadcast(128), r_gbias, "gbias")
        for jj in range(4):
            ld(cw[:, :, jj], conv_w[jj].rearrange("(b p) -> p b", p=128), r_cw, "cw", allow_slow_non_contiguous=True)
        ld(cb[:], conv_b.rearrange("(b p) -> p b", p=128), r_cb, "cb", allow_slow_non_contiguous=True)
        ld(sk[32:33, :], sinks.rearrange("(o n) -> o n", o=1), r_sk, "sk")
        ld(cosm[:], c_cosm, r_cosm, "cosm")
        ld(sinm[:], c_sinm, r_sinm, "sinm")
        A("pool", lambda e: e.dma_start(out=wgate[:], in_=w_in[:, C_MI:C_MI + 8].rearrange("(k p) n -> p k n", p=128),
                                        allow_slow_non_contiguous=True), writes=[r_wgate], dma="wgate")
        A("dve", lambda e: e.tensor_copy(out=ident_b[:], in_=ident_f[:]), [r_identf], [r_ident])
        A("dve", lambda e: e.tensor_copy(out=pit_b[:], in_=pit_f[:]), [r_pitf], [r_pit])
        A("dve", lambda e: e.memset(ones_f[:], 1.0), [], [r_onesf])
        A("dve", lambda e: e.memset(ones_b[:], 1.0), [], [r_onesb])
        A("dve", lambda e: e.tensor_copy(out=mask_b[:], in_=triu_f[:].unsqueeze(1).to_broadcast([128, 4, 128])), [r_triu], [r_mask])
        A("dve", lambda e: e.tensor_scalar(out=maskp_b[:], in0=triu_f[:].unsqueeze(1).to_broadcast([128, 4, 128]),
                                           scalar1=-1.0, scalar2=1.0, op0=ALU.mult, op1=ALU.add), [r_triu], [r_maskp])
        A("dve", lambda e: e.tensor_scalar(out=cw[:], in0=cw[:], scalar1=0.5, scalar2=None, op0=ALU.mult), [r_cw], [r_cw])
        A("dve", lambda e: e.tensor_scalar(out=cb[:], in0=cb[:], scalar1=0.5, scalar2=None, op0=ALU.mult), [r_cb], [r_cb])
        A("pool", lambda e: e.memset(Vmeta[:], 0.0), [], [r_Vmeta])
        A("pool", lambda e: e.memset(Vmeta[0:16, :, 64:65], 1.0), [], [r_Vmeta])
        A("pool", lambda e: e.memset(Vmeta[32:33, :, 64:65], 1.0), [], [r_Vmeta])
        for i in range(5):
            A("pool", lambda e, i=i: e.memset(Vdup[i][:], 0.0), [], [r_Vdup[i]])
            A("pool", lambda e, i=i: e.memset(Vdup[i][:, :, 64:65], 1.0), [], [r_Vdup[i]])
        A("pool", lambda e: e.memset(mhalf[:], -0.5), [], [r_mhalf])
        A("dve", lambda e: e.tensor_scalar(out=gB[:], in0=gB[:], scalar1=32.0, scalar2=None, op0=ALU.mult), [r_gB], [r_gB])
        A("dve", lambda e: e.tensor_scalar(out=npB[:], in0=npB[:], scalar1=32.0, scalar2=None, op0=ALU.mult), [r_npB], [r_npB])
        for g in range(4):
            A("pool", lambda e, g=g: e.memset(ptm[g][:], 0.0), [], [r_ptm[g]])
            A("act", lambda e, g=g: e.activation(out=ptm[g][32:33, :].rearrange("p (h q) -> p h q", h=4),
                                                 in_=sk[32:33, 4 * g:4 * g + 4].unsqueeze(2).to_broadcast([1, 4, 128]),
                                                 func=AF.Exp), [r_sk, r_ptm[g]], [r_ptm[g]])
        for j in range(4):
            A("pool", lambda e, j=j: e.memset(vaug[j][:], 0.0), [], [r_vaug[j]])
        A("pool", lambda e: e.memset(Cbf[:], 0.0), [], r_Cbf)
        A("pool", lambda e: e.memset(halo_m[:], 0.0), [], [r_halom])

        def load_w(bi, n=512):
            i = wctr[0] % NSLOT
            wctr[0] += 1
            A("sp", lambda e: e.dma_start(out=wring[i][:, :, 0:n], in_=wbt[bi][:, :, 0:n]),
              reads=r_castb[bi], writes=[r_wring[i]], dma="w%d" % i)
            return wring[i], r_wring[i]

        def proj_fm(w, c0, rhs, r_rhs, N, n0=0):
            wt, r_w = w
            b, rb = bank()
            for kc in range(8):
                A("pe", lambda e, kc=kc: e.matmul(b[:, 0:N], lhsT=wt[:, kc, c0:c0 + 128], rhs=rhs[:, kc, n0:n0 + N],
                                                 start=(kc == 0), stop=(kc == 7)), [r_w, r_rhs], [rb])
            return b, rb

        def proj_tm(w, n, lhs, r_lhs, t0, nt):
            wt, r_w = w
            b, rb = bank()
            for kc in range(8):
                A("pe", lambda e, kc=kc: e.matmul(b[0:nt, 0:n], lhsT=lhs[:, kc, t0:t0 + nt], rhs=wt[:, kc, 0:n],
                                                 start=(kc == 0), stop=(kc == 7)), [r_w, r_lhs], [rb])
            return b, rb

        pending = []

        def flush():
            while pending:
                pending.pop(0)()

        def rope(b, rb, N, cos_ap, sin_ap, r_tabs, scale, dst, r_dst):
            i = rr("raw")
            A("act", lambda e: e.activation(out=raw[i][:, 0:N], in_=b[:, 0:N], func=AF.Copy, scale=scale), [rb], [r_raw[i]])

            def stage_b():
                b2, rb2 = bank()
                A("pe", lambda e: e.matmul(b2[:, 0:N], lhsT=pit_b[:], rhs=raw[i][:, 0:N], start=True, stop=True), [r_pit, r_raw[i]], [rb2])
                A("dve", lambda e: e.tensor_tensor(out=t1[i][:, 0:N], in0=raw[i][:, 0:N], in1=cos_ap, op=ALU.mult), [r_raw[i]] + r_tabs, [r_t1[i]])
                A("dve", lambda e: e.tensor_tensor(out=t2[i][:, 0:N], in0=b2[:, 0:N], in1=sin_ap, op=ALU.mult), [rb2] + r_tabs, [r_t2[i]])
                A("pool", lambda e: e.tensor_tensor(out=dst, in0=t1[i][:, 0:N], in1=t2[i][:, 0:N], op=ALU.add), [r_t1[i], r_t2[i]], [r_dst])
            pending.append(stage_b)

        def rmsnorm_front(src, r_src, n, stat, r_stat):
            xi = rr("xb")
            xb_, r_xb_ = xbs[xi], r_xbs[xi]
            A("act", lambda e: e.activation(out=xb_[0:n, :], in_=src, func=AF.Square, accum_out=stat[0:n, 0:1]), [r_src], [r_xb_, r_stat])
            A("pool", lambda e: e.tensor_scalar(out=stat[0:n, 1:2], in0=stat[0:n, 0:1], scalar1=1024 * 1e-6, scalar2=None, op0=ALU.add), [r_stat], [r_stat])
            A("pool", lambda e: e.tensor_tensor(out=stat[0:n, 2:3], in0=stat[0:n, 1:2], in1=mhalf[0:n, 0:1], op=ALU.pow), [r_stat, r_mhalf], [r_stat])
            A("dve", lambda e: e.scalar_tensor_tensor(out=xb_[0:n, :], in0=src, scalar=stat[0:n, 2:3], in1=gB[0:n, :],
                                                      op0=ALU.mult, op1=ALU.mult), [r_src, r_stat, r_gB], [r_xb_])
            return xb_, r_xb_

        def rmsnorm_back(xbr, n, dstT, r_dstT, t0):
            xb_, r_xb_ = xbr
            for k in range(8):
                A("pe", lambda e, k=k: e.transpose(out=pst[:, k, 0:n], in_=xb_[0:n, k * 128:(k + 1) * 128], identity=ident_b[0:n, 0:n]),
                  [r_xb_, r_ident], [r_pst])
            A("act", lambda e: e.activation(out=dstT[:, :, t0:t0 + n], in_=pst[:, :, 0:n], func=AF.Copy), [r_pst], [r_dstT])

        def rmsnorm_T(src, r_src, n, stat, r_stat, dstT, r_dstT, t0):
            rmsnorm_back(rmsnorm_front(src, r_src, n, stat, r_stat), n, dstT, r_dstT, t0)

        def conv_silu(b, rb, blk, N, halo_src, r_halo_src, halo_dst, r_halo_dst, dst, r_dst):
            i = rr("pre")
            p_ = pre[i]
            A("act", lambda e: e.activation(out=p_[:, 3:3 + N], in_=b[:, 0:N], func=AF.Copy), [rb], [r_pre[i]])
            A("pool", lambda e: e.tensor_copy(out=p_[:, 0:3], in_=halo_src[:, blk, :]), [r_halo_src], [r_pre[i]])
            a_ = acc[i]
            A("act", lambda e: e.activation(out=a_[:, 0:N], in_=b[:, 0:N], func=AF.Identity, scale=cw[:, blk, 3:4], bias=cb[:, blk:blk + 1]),
              [rb, r_cw, r_cb], [r_acc[i]])
            for jj in (2, 1, 0):
                A("dve", lambda e, jj=jj: e.scalar_tensor_tensor(out=a_[:, 0:N], in0=p_[:, jj:jj + N], scalar=cw[:, blk, jj:jj + 1],
                                                                 in1=a_[:, 0:N], op0=ALU.mult, op1=ALU.add), [r_pre[i], r_cw, r_acc[i]], [r_acc[i]])
            A("pool", lambda e: e.tensor_copy(out=halo_dst[:, blk, :], in_=p_[:, N:N + 3]), [r_pre[i]], [r_halo_dst])
            k = rr("tz")
            A("act", lambda e: e.activation(out=tzt[k][:, 0:N], in_=a_[:, 0:N], func=AF.Tanh), [r_acc[i]], [r_tzt[k]])
            A("dve", lambda e: e.scalar_tensor_tensor(out=dst, in0=tzt[k][:, 0:N], scalar=1.0, in1=a_[:, 0:N], op0=ALU.add, op1=ALU.mult),
              [r_tzt[k], r_acc[i]], [r_dst])

        xm = ot[0]
        A("sp", lambda e: e.dma_start(out=xm[0:16, :], in_=meta), writes=[r_ot[0]], dma="xm")
        uTm = sb("uTm", [128, 8, 16], BF16); r_uTm = P.res("uTm")
        qkm = sb("qkm", [128, 8, 16], BF16); r_qkm = P.res("qkm")
        rmsnorm_T(xm[0:16, :], r_ot[0], 16, stt[0], r_stt[0], uTm, r_uTm, 0)
        wk = load_w(WB["K"])
        for g in range(4):
            b, rb = proj_fm(wk, g * 128, uTm, r_uTm, 16)
            flush()
            rope(b, rb, 16, cosm[:], sinm[:], [r_cosm, r_sinm], 1.0, KmT[:, g, :], r_KmT)
        flush()
        wv = load_w(WB["V"], 256)
        b, rb = proj_tm(wv, 256, uTm, r_uTm, 0, 16)
        A("act", lambda e, b=b: e.activation(out=Vmeta[0:16, :, 0:64], in_=b[0:16, 0:256].rearrange("p (g d) -> p g d", g=4), func=AF.Copy), [rb], [r_Vmeta])
        bgm_, rbgm = bank()
        for kc in range(8):
            A("pe", lambda e, kc=kc: e.matmul(bgm_[0:16, 0:8], lhsT=uTm[:, kc, 0:16], rhs=wgate[:, kc, :], start=(kc == 0), stop=(kc == 7)),
              [r_uTm, r_wgate], [rbgm])
        A("dve", lambda e: e.tensor_tensor(out=gsb[0:16, 0, :], in0=bgm_[0:16, 0:8], in1=gbias[0:16, :], op=ALU.add), [rbgm, r_gbias], [r_gsb])
        A("act", lambda e: e.activation(out=e1[0:16, 0, :], in_=gsb[0:16, 0, 4:8], func=AF.Exp, scale=-1.0), [r_gsb], [r_e1])
        A("act", lambda e: e.activation(out=nlf[0:16, 0:4], in_=e1[0:16, 0, :], func=AF.Ln, bias=1.0), [r_e1], [r_nlf])
        bnbm, rbnbm = bank()
        A("pe", lambda e: e.matmul(bnbm[0:16, 0:4], lhsT=triu_f[0:16, 0:16], rhs=nlf[0:16, 0:4], start=True, stop=True), [r_triu, r_nlf], [rbnbm])
        bnsm, rbnsm = bank()
        A("pe", lambda e: e.matmul(bnsm[:, 0:4], lhsT=ones_f[0:16, :], rhs=nlf[0:16, 0:4], start=True, stop=True), [r_onesf, r_nlf], [rbnsm])
        A("dve", lambda e: e.tensor_tensor(out=gs[0:16, 0:4], in0=gsb[0:16, 0, 0:4], in1=bnbm[0:16, 0:4], op=ALU.add), [r_gsb, rbnbm], [r_gs])
        A("act", lambda e: e.activation(out=vsc[0:16, 0:4], in_=gs[0:16, 0:4], func=AF.Exp, bias=LOGK), [r_gs], [r_vsc])
        A("act", lambda e: e.activation(out=eB[:, 0:4], in_=bnsm[:, 0:4], func=AF.Exp, scale=-1.0), [rbnsm], [r_eB])
        for i in range(2):
            w = load_w(WB["MQK"] + i)
            for c in range(4):
                blk = 4 * i + c
                b, rb = proj_fm(w, c * 128, uTm, r_uTm, 16)
                conv_silu(b, rb, blk, 16, halo_m, r_halom, halo_m, r_halom, qkm[:, blk, :], r_qkm)
        vaugm = vaug[0]
        for i in range(2):
            w = load_w(WB["MV"] + i)
            b, rb = proj_tm(w, 512, uTm, r_uTm, 0, 16)
            for hh in range(2):
                h = 2 * i + hh
                A("act", lambda e, b=b, hh=hh, h=h: e.activation(out=vaugm[0:16, h, 0:256], in_=b[0:16, hh * 256:(hh + 1) * 256], func=AF.Copy,
                                                              scale=vsc[0:16, h:h + 1]), [rb, r_vsc], [r_vaug[0]])
        A("pool", lambda e: e.tensor_copy(out=vaugm[0:16, :, 256:257], in_=vsc[0:16, 0:4].unsqueeze(2)), [r_vsc], [r_vaug[0]])
        for h in range(4):
            A("pe", lambda e, h=h: e.transpose(out=pst[0:16, h, :], in_=qkm[:, 4 + h, :], identity=ident_b[:]), [r_qkm, r_ident], [r_pst])
        A("act", lambda e: e.activation(out=ktok[0:16, :, :], in_=pst[0:16, 0:4, :], func=AF.Copy), [r_pst], [r_ktok])
        for h in range(4):
            bU, rbU = bank()
            A("pe", lambda e, h=h, bU=bU: e.matmul(bU[:, 0:257], lhsT=ktok[0:16, h, :], rhs=vaugm[0:16, h, 0:257], start=True, stop=True),
              [r_ktok, r_vaug[0]], [rbU])
            A("dve", lambda e, h=h, bU=bU: e.tensor_scalar(out=Cm32[:, h, :], in0=bU[:, 0:257], scalar1=eB[:, h:h + 1], scalar2=None, op0=ALU.mult),
              [rbU, r_eB], [r_Cm32])
        A("pool", lambda e: e.memset(vaug[0][:], 0.0), [], [r_vaug[0]])

        def phase0_load(s_, ti_, j):
            t0_ = ti_ * T + j * 128
            A("sp", lambda e: e.dma_start(out=xs[j][:], in_=x[s_, t0_: t0_ + 128, :]), writes=[r_xs[j]], dma="xs%d" % j)

        def phase0_tabs(ti_):
            A("sp", lambda e: e.dma_start(out=cosT[:], in_=c_cos[ti_]), writes=[r_cos], dma="cos")
            A("sp", lambda e: e.dma_start(out=sinT[:], in_=c_sin[ti_]), writes=[r_sin], dma="sin")

        def phase0_norm(j):
            rmsnorm_T(xs[j][:], r_xs[j], 128, stt[j], r_stt[j], uT, r_uT, j * 128)

        def phase0_front(j):
            return rmsnorm_front(xs[j][:], r_xs[j], 128, stt[j], r_stt[j])

        def phase0_back(j, xbr):
            rmsnorm_back(xbr, 128, uT, r_uT, j * 128)

        wk_pref = [None]
        for s in range(NSEQ):
            A("pool", lambda e: e.tensor_copy(out=C32[:], in_=Cm32[:]), [r_Cm32], r_C32)
            A("pool", lambda e: e.tensor_copy(out=Cbf[:, :, 0:257], in_=Cm32[:]), [r_Cm32], r_Cbf)
            A("pool", lambda e: e.tensor_copy(out=halo[:], in_=halo_m[:]), [r_halom], [r_halo])
            for ti in range(NST):
                tok0 = ti * T
                if s == 0 and ti == 0:
                    phase0_tabs(ti)
                    for j in range(4):
                        phase0_load(s, ti, j)
                    for j in range(4):
                        phase0_norm(j)
                nxt = (s, ti + 1) if ti + 1 < NST else ((s + 1, 0) if s + 1 < NSEQ else None)
                wk = wk_pref[0] if wk_pref[0] is not None else load_w(WB["K"])
                wk_pref[0] = None
                for g in range(4):
                    b, rb = proj_fm(wk, g * 128, uT, r_uT, T)
                    flush()
                    rope(b, rb, T, cosT[:], sinT[:], [r_cos, r_sin], 1.0, Kbuf[:, g, 128:640], r_Kbuf)
                wv = load_w(WB["V"], 256)
                for j in range(4):
                    b, rb = proj_tm(wv, 256, uT, r_uT, j * 128, 128)
                    A("act", lambda e, b=b, j=j: e.activation(out=Vdup[j + 1][:, :, 0:64], in_=b[:, 0:256].rearrange("p (g d) -> p g d", g=4), func=AF.Copy),
                      [rb], [r_Vdup[j + 1]])
                QT, r_QT = bufA, r_bufA
                attT, r_attT = bufB, r_bufB
                for i in range(2):
                    w = load_w(WB["Q"] + i)
                    for c in range(4):
                        blk = 4 * i + c
                        b, rb = proj_fm(w, c * 128, uT, r_uT, T)
                        flush()
                        rope(b, rb, T, cosT[:], sinT[:], [r_cos, r_sin], 0.125, QT[:, blk, :], r_QT)
                flush()
                def att_S(j, g):
                    has_prev = not (ti == 0 and j == 0)
                    jc = slice(j * 128, (j + 1) * 128)
                    pi = rr("pt")
                    bSc, rSc = bank()
                    bSm, rSm = bank()
                    bSp, rSp = bank() if has_prev else (None, None)
                    for hh in range(4):
                        blk = 2 * g + hh // 2
                        rows = slice((hh % 2) * 64, (hh % 2) * 64 + 64)
                        hc = slice(hh * 128, (hh + 1) * 128)
                        A("pe", lambda e, rows=rows, hc=hc, blk=blk: e.matmul(
                            bSc[:, hc], lhsT=Kbuf[rows, g, 128 + j * 128: 256 + j * 128], rhs=QT[rows, blk, jc], start=True, stop=True),
                          [r_Kbuf, r_QT], [rSc])
                        A("pe", lambda e, rows=rows, hc=hc, blk=blk: e.matmul(
                            bSm[0:16, hc], lhsT=KmT[rows, g, :], rhs=QT[rows, blk, jc], start=True, stop=True),
                          [r_KmT, r_QT], [rSm])
                        if has_prev:
                            A("pe", lambda e, rows=rows, hc=hc, blk=blk: e.matmul(
                                bSp[:, hc], lhsT=Kbuf[rows, g, j * 128: 128 + j * 128], rhs=QT[rows, blk, jc], start=True, stop=True),
                              [r_Kbuf, r_QT], [rSp])
                    A("act", lambda e: e.activation(out=ptc[pi][:], in_=bSc[:], func=AF.Exp), [rSc], [r_ptc[pi]])
                    A("act", lambda e: e.activation(out=ptm[g][0:16, :], in_=bSm[0:16, :], func=AF.Exp), [rSm], [r_ptm[g]])
                    A("pool", lambda e: e.tensor_tensor(out=ptc[pi][:], in0=ptc[pi][:], in1=mask_b[:].rearrange("p h q -> p (h q)"), op=ALU.mult),
                      [r_ptc[pi], r_mask], [r_ptc[pi]])
                    if has_prev:
                        A("act", lambda e: e.activation(out=ptp[pi][:], in_=bSp[:], func=AF.Exp), [rSp], [r_ptp[pi]])
                        A("pool", lambda e: e.tensor_tensor(out=ptp[pi][:], in0=ptp[pi][:], in1=maskp_b[:].rearrange("p h q -> p (h q)"), op=ALU.mult),
                          [r_ptp[pi], r_maskp], [r_ptp[pi]])
                    return (j, g, pi, has_prev, jc)

                def att_PV(ctx):
                    j, g, pi, has_prev, jc = ctx
                    ai = j % 2
                    bO, rO = bank()
                    for hh in range(4):
                        hc = slice(hh * 128, (hh + 1) * 128)
                        oc = slice(hh * 65, hh * 65 + 65)
                        A("pe", lambda e, hc=hc, oc=oc: e.matmul(bO[:, oc], lhsT=ptm[g][:, hc], rhs=Vmeta[:, g, 0:65], start=True, stop=False),
                          [r_Vmeta, r_ptm[g]], [rO])
                        if has_prev:
                            A("pe", lambda e, hc=hc, oc=oc: e.matmul(bO[:, oc], lhsT=ptp[pi][:, hc], rhs=Vdup[j][:, g, 0:65], start=False, stop=False),
                              [r_Vdup[j], r_ptp[pi]], [rO])
                        A("pe", lambda e, hc=hc, oc=oc: e.matmul(bO[:, oc], lhsT=ptc[pi][:, hc], rhs=Vdup[j + 1][:, g, 0:65], start=False, stop=True),
                          [r_Vdup[j + 1], r_ptc[pi]], [rO])
                    ri = rr("rec4")
                    bOv = bO[:, 0:260].rearrange("p (h c) -> p h c", c=65)
                    A("dve", lambda e: e.reciprocal(out=rec4[ri][:], in_=bOv[:, :, 64]), [rO], [r_rec4[ri]])
                    A("dve", lambda e: e.tensor_tensor(out=att_tok[ai][:, g * 256:(g + 1) * 256].rearrange("p (h d) -> p h d", h=4), in0=bOv[:, :, 0:64],
                                                       in1=rec4[ri][:].unsqueeze(2).to_broadcast([128, 4, 64]), op=ALU.mult),
                      [rO, r_rec4[ri]], [r_att_tok[ai]])
                    if g == 3:
                        for k in range(8):
                            A("pe", lambda e, k=k: e.transpose(out=pst[:, k, :], in_=att_tok[ai][:, k * 128:(k + 1) * 128], identity=ident_b[:]),
                              [r_att_tok[ai], r_ident], [r_pst])
                        A("act", lambda e: e.activation(out=attT[:, :, jc], in_=pst[:], func=AF.Copy), [r_pst], [r_attT])

                order = [(j, g) for j in range(4) for g in range(4)]
                ctxs = [att_S(*order[0])]
                for idx in range(len(order)):
                    if idx + 1 < len(order):
                        ctxs.append(att_S(*order[idx + 1]))
                    att_PV(ctxs[idx])
                for i in range(2):
                    w = load_w(WB["Z"] + i)
                    for c in range(4):
                        blk = 4 * i + c
                        b, rb = proj_fm(w, c * 128, uT, r_uT, T)
                        k = rr("tz")
                        A("act", lambda e, b=b, k=k: e.activation(out=tzt[k][:], in_=b[:], func=AF.Tanh, scale=0.5), [rb], [r_tzt[k]])
                        A("dve", lambda e, b=b, k=k: e.scalar_tensor_tensor(out=zst[k][:], in0=tzt[k][:], scalar=1.0, in1=b[:], op0=ALU.add, op1=ALU.mult),
                          [r_tzt[k], rb], [r_zst[k]])
                        A("pool", lambda e, k=k, blk=blk: e.tensor_tensor(out=attT[:, blk, :], in0=attT[:, blk, :], in1=zst[k][:], op=ALU.mult),
                          [r_attT, r_zst[k]], [r_attT])
                for i in range(2):
                    wg = load_w(WB["GA"] + i)
                    wa = load_w(WB["AO"] + i)
                    for c in range(4):
                        blk = 4 * i + c
                        b, rb = proj_fm(wg, c * 128, uT, r_uT, T)
                        k = rr("tz")
                        A("act", lambda e, b=b, k=k: e.activation(out=tzt[k][:], in_=b[:], func=AF.Tanh, scale=0.5), [rb], [r_tzt[k]])
                        by, rby = proj_fm(wa, c * 128, attT, r_attT, T)
                        A("dve", lambda e, by=by, k=k, blk=blk: e.scalar_tensor_tensor(out=mrg[:, blk, :], in0=tzt[k][:], scalar=1.0, in1=by[:],
                                                                                    op0=ALU.add, op1=ALU.mult), [r_tzt[k], rby], [r_mrg])
                if nxt is not None:
                    phase0_tabs(nxt[1])
                    for j in range(4):
                        phase0_load(nxt[0], nxt[1], j)
                bg, rbg = bank()
                for j in range(4):
                    for kc in range(8):
                        A("pe", lambda e, kc=kc, j=j, bg=bg: e.matmul(bg[:, j * 8:(j + 1) * 8], lhsT=uT[:, kc, j * 128:(j + 1) * 128], rhs=wgate[:, kc, :],
                                                                 start=(kc == 0), stop=(kc == 7)), [r_uT, r_wgate], [rbg])
                A("dve", lambda e, bg=bg: e.tensor_tensor(out=gsb[:], in0=bg[:, 0:32].rearrange("p (j c) -> p j c", j=4),
                                                        in1=gbias[:].unsqueeze(1).to_broadcast([128, 4, 8]), op=ALU.add), [rbg, r_gbias], [r_gsb])
                A("act", lambda e: e.activation(out=e1[:], in_=gsb[:, :, 4:8], func=AF.Exp, scale=-1.0), [r_gsb], [r_e1])
                A("act", lambda e: e.activation(out=nlf[:], in_=e1[:].rearrange("p j h -> p (j h)"), func=AF.Ln, bias=1.0), [r_e1], [r_nlf])
                bnb, rbnb = bank()
                A("pe", lambda e, bnb=bnb: e.matmul(bnb[:, 0:16], lhsT=triu_f[:], rhs=nlf[:], start=True, stop=True), [r_triu, r_nlf], [rbnb])
                bns, rbns = bank()
                A("pe", lambda e, bns=bns: e.matmul(bns[:, 0:16], lhsT=ones_f[:], rhs=nlf[:], start=True, stop=True), [r_onesf, r_nlf], [rbns])
                A("act", lambda e, bnb=bnb: e.activation(out=thr[:], in_=bnb[:, 0:16], func=AF.Exp), [rbnb], [r_thr])
                A("dve", lambda e, bnb=bnb: e.tensor_tensor(out=gs[:].rearrange("p (j h) -> p j h", j=4), in0=gsb[:, :, 0:4],
                                                          in1=bnb[:, 0:16].rearrange("p (j h) -> p j h", j=4), op=ALU.add), [r_gsb, rbnb], [r_gs])
                A("act", lambda e: e.activation(out=vsc[:], in_=gs[:], func=AF.Exp, bias=LOGK), [r_gs], [r_vsc])
                A("act", lambda e, bns=bns: e.activation(out=eB[:], in_=bns[:, 0:16], func=AF.Exp, scale=-1.0), [rbns], [r_eB])
                qkT, r_qkT = bufC, r_bufC
                for i in range(2):
                    w = load_w(WB["MQK"] + i)
                    for c in range(4):
                        blk = 4 * i + c
                        b, rb = proj_fm(w, c * 128, uT, r_uT, T)
                        conv_silu(b, rb, blk, T, halo, r_halo, halo, r_halo, qkT[:, blk, :], r_qkT)
                for i in range(2):
                    w = load_w(WB["MV"] + i)
                    for j in range(4):
                        b, rb = proj_tm(w, 512, uT, r_uT, j * 128, 128)
                        for hh in range(2):
                            h = 2 * i + hh
                            A("act", lambda e, b=b, hh=hh, h=h, j=j: e.activation(out=vaug[j][:, h, 0:256], in_=b[:, hh * 256:(hh + 1) * 256], func=AF.Copy,
                                                                             scale=vsc[:, j * 4 + h: j * 4 + h + 1]), [rb, r_vsc], [r_vaug[j]])
                for j in range(4):
                    A("pool", lambda e, j=j: e.tensor_copy(out=vaug[j][:, :, 256:257], in_=vsc[:, j * 4:(j + 1) * 4].unsqueeze(2)), [r_vsc], [r_vaug[j]])
                th = bufB[:].rearrange("p k t -> p (k t)").rearrange("p (j f) -> p j f", j=4)
                r_th = r_bufB
                zs = bufA[:].rearrange("p k t -> p (k t)").rearrange("p (j f) -> p j f", j=4)
                r_zs = r_bufA
                for i in range(2):
                    w = load_w(WB["MO"] + i)
                    for j in range(4):
                        b, rb = proj_tm(w, 512, uT, r_uT, j * 128, 128)
                        A("act", lambda e, b=b, j=j, i=i: e.activation(out=th[:, j, i * 512:(i + 1) * 512], in_=b[:], func=AF.Tanh, scale=0.5), [rb], [r_th])
                for i in range(2):
                    w = load_w(WB["MZ"] + i)
                    for j in range(4):
                        b, rb = proj_tm(w, 512, uT, r_uT, j * 128, 128)
                        k = rr("tz")
                        A("act", lambda e, b=b, k=k: e.activation(out=tzt[k][:], in_=b[:], func=AF.Tanh, scale=0.5), [rb], [r_tzt[k]])
                        A("dve", lambda e, b=b, k=k: e.scalar_tensor_tensor(out=zst[k][:], in0=tzt[k][:], scalar=1.0, in1=b[:],
                                                                          op0=ALU.add, op1=ALU.mult), [r_tzt[k], rb], [r_zst[k]])
                        A("pool", lambda e, k=k, j=j, i=i: e.tensor_tensor(out=zs[:, j, i * 512:(i + 1) * 512], in0=zst[k][:], in1=hnB[:, i * 512:(i + 1) * 512], op=ALU.mult),
                          [r_zst[k], r_hnB], [r_zs])
                tgm = [(raw[0][:], r_raw[0]), (raw[1][:], r_raw[1]), (ptc[0][:], r_ptc[0]), (ptc[1][:], r_ptc[1]),
                       (ptp[0][:], r_ptp[0]), (ptp[1][:], r_ptp[1]), (att_tok[0][:, 0:512], r_att_tok[0]), (att_tok[1][:, 0:512], r_att_tok[1])]
                wgm = [None, None]
                def rec_pe_a(j):
                    jc = slice(j * 128, (j + 1) * 128)
                    for h in range(4):
                        A("pe", lambda e, h=h: e.transpose(out=pst[:, h, :], in_=qkT[:, 4 + h, jc], identity=ident_b[:]), [r_qkT, r_ident], [r_pst])
                    A("act", lambda e: e.activation(out=ktok[:], in_=pst[:, 0:4, :], func=AF.Copy), [r_pst], [r_ktok])
                    bS, rS = bank()
                    for h in range(4):
                        A("pe", lambda e, h=h: e.matmul(bS[:, h * 128:(h + 1) * 128], lhsT=qkT[:, 4 + h, jc], rhs=qkT[:, h, jc], start=True, stop=True),
                          [r_qkT], [rS])
                    A("dve", lambda e: e.tensor_tensor(out=PTm[:], in0=bS[:], in1=mask_b[:].rearrange("p h q -> p (h q)"), op=ALU.mult), [rS, r_mask], [r_PTm])
                    return jc

                def rec_pe_b(j, jc):
                    bNs = [bank(), bank()]
                    bDn, rDn = bank()
                    for h in range(4):
                        bN, rN = bNs[h // 2]
                        nc_ = slice((h % 2) * 256, (h % 2) * 256 + 256)
                        A("pe", lambda e, h=h, bN=bN, nc_=nc_: e.matmul(bN[:, nc_], lhsT=PTm[:, h * 128:(h + 1) * 128], rhs=vaug[j][:, h, 0:256], start=True, stop=False),
                          [r_PTm, r_vaug[j]], [rN])
                        A("pe", lambda e, h=h, bN=bN, nc_=nc_: e.matmul(bN[:, nc_], lhsT=qkT[:, h, jc], rhs=Cbf[:, h, 0:256], start=False, stop=True),
                          [r_qkT, r_Cbf[h]], [rN])
                        A("pe", lambda e, h=h: e.matmul(bDn[:, h:h + 1], lhsT=PTm[:, h * 128:(h + 1) * 128], rhs=vaug[j][:, h, 256:257], start=True, stop=False),
                          [r_PTm, r_vaug[j]], [rDn])
                        A("pe", lambda e, h=h: e.matmul(bDn[:, h:h + 1], lhsT=qkT[:, h, jc], rhs=Cbf[:, h, 256:257], start=False, stop=True),
                          [r_qkT, r_Cbf[h]], [rDn])
                    bUs = []
                    for h in range(4):
                        bU, rbU = bank()
                        A("pe", lambda e, h=h, bU=bU: e.matmul(bU[:, 0:257], lhsT=ktok[:, h, :], rhs=vaug[j][:, h, 0:257], start=True, stop=True),
                          [r_ktok, r_vaug[j]], [rbU])
                        bUs.append((bU, rbU))
                    return (j, jc, bNs, bDn, rDn, bUs)

                def rec_state(ctx):
                    j, jc, bNs, bDn, rDn, bUs = ctx
                    for h in range(4):
                        bU, rbU = bUs[h]
                        col = j * 4 + h
                        A("dve", lambda e, h=h, col=col: e.tensor_scalar(out=C32[:, h, :], in0=C32[:, h, :], scalar1=eB[:, col:col + 1], scalar2=None, op0=ALU.mult),
                          [r_C32[h], r_eB], [r_C32[h]])
                        A("dve", lambda e, h=h, col=col, bU=bU: e.scalar_tensor_tensor(out=C32[:, h, :], in0=bU[:, 0:257], scalar=eB[:, col:col + 1], in1=C32[:, h, :],
                                                                                    op0=ALU.mult, op1=ALU.add), [rbU, r_eB, r_C32[h]], [r_C32[h]])
                        A("act", lambda e, h=h: e.activation(out=Cbf[:, h, 0:257], in_=C32[:, h, :], func=AF.Copy), [r_C32[h]], [r_Cbf[h]])

                def rec_den(ctx):
                    j, jc, bNs, bDn, rDn, bUs = ctx
                    c4 = slice(j * 4, (j + 1) * 4)
                    A("dve", lambda e: e.tensor_scalar(out=dd[:, 0, :], in0=bDn[:, 0:4], scalar1=-1.0, scalar2=None, op0=ALU.mult), [rDn], [r_dd[0]])
                    A("dve", lambda e: e.tensor_tensor(out=dd[:, 1, :], in0=bDn[:, 0:4], in1=dd[:, 0, :], op=ALU.max), [rDn, r_dd[0]], [r_dd[0]])
                    A("dve", lambda e: e.tensor_tensor(out=dd[:, 2, :], in0=dd[:, 1, :], in1=thr[:, c4], op=ALU.max), [r_dd[0], r_thr], [r_dd[0]])
                    A("dve", lambda e: e.reciprocal(out=dd[:, 3, :], in_=dd[:, 2, :]), [r_dd[0]], [r_dd[0]])
                    for h in range(4):
                        bN, rN = bNs[h // 2]
                        nc_ = slice((h % 2) * 256, (h % 2) * 256 + 256)
                        A("act", lambda e, h=h, bN=bN, nc_=nc_: e.activation(out=tmpN[:, h * 256:(h + 1) * 256], in_=bN[:, nc_], func=AF.Copy, scale=dd[:, 3, h:h + 1]),
                          [rN, r_dd[0]], [r_tmpN])

                def rec_rest(ctx):
                    j, jc, bNs, bDn, rDn, bUs = ctx
                    A("dve", lambda e: e.scalar_tensor_tensor(out=ho[:], in0=th[:, j, :], scalar=1.0, in1=tmpN[:], op0=ALU.add, op1=ALU.mult),
                      [r_th, r_tmpN], [r_ho])
                    for h in range(4):
                        A("dve", lambda e, h=h: e.bn_stats(out=st6[:, h, :], in_=ho[:, h * 256:(h + 1) * 256]), [r_ho], [r_st6])
                    for h in range(4):
                        A("dve", lambda e, h=h: e.bn_aggr(out=mv[:, h, :], in_=st6[:, h, :]), [r_st6], [r_mv])
                    A("pool", lambda e: e.tensor_scalar(out=lnv[:, 0:4], in0=mv[:, :, 1], scalar1=4e-6, scalar2=None, op0=ALU.add), [r_mv], [r_lnv])
                    A("pool", lambda e: e.tensor_tensor(out=lnv[:, 4:8], in0=lnv[:, 0:4], in1=mhalf[:, 0:4], op=ALU.pow), [r_lnv, r_mhalf], [r_lnv])
                    for h in range(4):
                        A("dve", lambda e, h=h: e.tensor_scalar(out=ho[:, h * 256:(h + 1) * 256], in0=ho[:, h * 256:(h + 1) * 256], scalar1=mv[:, h, 0:1],
                                                              scalar2=lnv[:, 4 + h:5 + h], op0=ALU.subtract, op1=ALU.mult), [r_ho, r_mv, r_lnv], [r_ho])
                    hi = rr("hz")
                    A("pool", lambda e: e.tensor_tensor(out=hzs[hi][:], in0=ho[:], in1=zs[:, j, :], op=ALU.mult), [r_ho, r_zs], [r_hzs[hi]])
                    return (jc, hi)

                def rec_tail(t):
                    jc, hi = t
                    for k in range(8):
                        A("pe", lambda e, k=k: e.transpose(out=pst[:, k, :], in_=hzs[hi][:, k * 128:(k + 1) * 128], identity=ident_b[:]), [r_hzs[hi], r_ident], [r_pst])
                    A("act", lambda e: e.activation(out=hzT[:, :, jc], in_=pst[:], func=AF.Copy), [r_pst], [r_hzT])

                tail = None
                jc_next = rec_pe_a(0)
                for j in range(4):
                    ctx = rec_pe_b(j, jc_next)
                    bctr[0] += 3
                    rec_den(ctx)
                    rec_state(ctx)
                    if j + 1 < 4:
                        jc_next = rec_pe_a(j + 1)
                    if tail is not None:
                        rec_tail(tail)
                    tail = rec_rest(ctx)
                    for blk in (2 * j, 2 * j + 1):
                        i_, c_ = divmod(blk, 4)
                        if c_ == 0:
                            wgm[i_] = load_w(WB["GM"] + i_)
                        b, rb = proj_fm(wgm[i_], c_ * 128, uT, r_uT, T)
                        tg_, r_tg_ = tgm[blk]
                        A("act", lambda e, b=b, tg_=tg_: e.activation(out=tg_, in_=b[:], func=AF.Tanh, scale=0.5), [rb], [r_tg_])
                rec_tail(tail)
                mT, r_mT = bufC, r_bufC
                for i in range(2):
                    wm = load_w(WB["MOUT"] + i)
                    for c in range(4):
                        blk = 4 * i + c
                        tg_, r_tg_ = tgm[blk]
                        k = rr("tz")
                        by, rby = proj_fm(wm, c * 128, hzT, r_hzT, T)
                        A("dve", lambda e, by=by, k=k, tg_=tg_: e.scalar_tensor_tensor(out=zst[k][:], in0=tg_, scalar=1.0, in1=by[:], op0=ALU.add, op1=ALU.mult),
                          [r_tg_, rby], [r_zst[k]])
                        A("pool", lambda e, k=k, blk=blk: e.tensor_tensor(out=mT[:, blk, :], in0=zst[k][:], in1=mrg[:, blk, :], op=ALU.add),
                          [r_zst[k], r_mrg], [r_mT])
                w0 = load_w(WB["OUT"])
                w1 = load_w(WB["OUT"] + 1)
                if nxt is not None:
                    wk_pref[0] = load_w(WB["K"])
                xr = [(tmpN, r_tmpN), (ho, r_ho)]

                def reload(j):
                    xt_, rx_ = xr[j % 2]
                    t0_ = tok0 + j * 128
                    A("sp", lambda e, s=s: e.dma_start(out=xt_[:], in_=x[s, t0_: t0_ + 128, :]), writes=[rx_], dma="xr%d" % (j % 2))

                reload(0)
                reload(1)
                fronts = {}
                if nxt is not None:
                    fronts[0] = phase0_front(0)
                    fronts[1] = phase0_front(1)
                for j in range(4):
                    oi = rr("ot")
                    b0, rb0 = proj_tm(w0, 512, mT, r_mT, j * 128, 128)
                    b1, rb1 = proj_tm(w1, 512, mT, r_mT, j * 128, 128)
                    A("act", lambda e, b0=b0, oi=oi: e.activation(out=hz[:, 0:512], in_=b0[:], func=AF.Square, accum_out=ss[oi][:, 0:1]), [rb0], [r_hz, r_ss[oi]])
                    A("act", lambda e, b1=b1, oi=oi: e.activation(out=hz[:, 512:1024], in_=b1[:], func=AF.Square, accum_out=ss[oi][:, 1:2]), [rb1], [r_hz, r_ss[oi]])
                    A("dve", lambda e, oi=oi: e.tensor_tensor(out=ss[oi][:, 2:3], in0=ss[oi][:, 0:1], in1=ss[oi][:, 1:2], op=ALU.add), [r_ss[oi]], [r_ss[oi]])
                    A("pool", lambda e, oi=oi: e.tensor_scalar(out=ss[oi][:, 3:4], in0=ss[oi][:, 2:3], scalar1=1024 * 16e-6, scalar2=None, op0=ALU.add), [r_ss[oi]], [r_ss[oi]])
                    A("pool", lambda e, oi=oi: e.tensor_tensor(out=ss[oi][:, 4:5], in0=ss[oi][:, 3:4], in1=mhalf[:, 0:1], op=ALU.pow), [r_ss[oi], r_mhalf], [r_ss[oi]])
                    A("dve", lambda e, b0=b0, oi=oi: e.scalar_tensor_tensor(out=ot[oi][:, 0:512], in0=b0[:], scalar=ss[oi][:, 4:5], in1=npB[:, 0:512],
                                                                          op0=ALU.mult, op1=ALU.mult), [rb0, r_ss[oi], r_npB], [r_ot[oi]])
                    A("dve", lambda e, b1=b1, oi=oi: e.scalar_tensor_tensor(out=ot[oi][:, 512:1024], in0=b1[:], scalar=ss[oi][:, 4:5], in1=npB[:, 512:1024],
                                                                          op0=ALU.mult, op1=ALU.mult), [rb1, r_ss[oi], r_npB], [r_ot[oi]])
                    xt_, rx_ = xr[j % 2]
                    A("pool", lambda e, oi=oi, xt_=xt_: e.tensor_tensor(out=ot[oi][:], in0=ot[oi][:], in1=xt_[:], op=ALU.add), [r_ot[oi], rx_], [r_ot[oi]])
                    A("sp", lambda e, oi=oi, j=j, s=s, tok0=tok0: e.dma_start(out=out[s, tok0 + j * 128: tok0 + (j + 1) * 128, :], in_=ot[oi][:]),
                      reads=[r_ot[oi]], writes=[new_out()], dma="out%d" % oi)
                    if j + 2 < 4:
                        reload(j + 2)
                    if nxt is not None and j < 2:
                        phase0_back(2 * j, fronts[2 * j])
                        phase0_back(2 * j + 1, fronts[2 * j + 1])
                        if j == 0:
                            fronts[2] = phase0_front(2)
                            fronts[3] = phase0_front(3)
                A("pool", lambda e: e.tensor_copy(out=Kbuf[:, :, 0:128], in_=Kbuf[:, :, 512:640]), [r_Kbuf], [r_Kbuf])
                A("pool", lambda e: e.tensor_copy(out=Vdup[0][:], in_=Vdup[4][:]), [r_Vdup[4]], [r_Vdup[0]])
        fin = P.res("fin")
        A("sp", lambda e: e.nop(), reads=r_outs, writes=[fin])
        P.emit()
    return nc


def _consts(NST):
    ident = np.eye(128, dtype=np.float32)
    triu = np.triu(np.ones((128, 128), dtype=np.float32))
    pit = np.zeros((128, 128), dtype=np.float32)
    for m in range(128):
        d = m % 64
        base = m - d
        if d < 8:
            k = base + d + 8
        elif d < 16:
            k = base + d - 8
        else:
            k = m
        pit[k, m] = 1.0
    half = 8
    inv_freq = (500000.0 ** (-np.arange(0, 16, 2, dtype=np.float32) / 16)).astype(np.float32)

    def tables(pos):
        pos = pos.astype(np.float32)
        ang = (pos[:, None] * inv_freq[None, :]).astype(np.float32)
        c = np.cos(ang.astype(np.float64)).astype(np.float32)
        s_ = np.sin(ang.astype(np.float64)).astype(np.float32)
        n = pos.shape[0]
        cosT = np.ones((128, n), dtype=np.float32)
        sinT = np.zeros((128, n), dtype=np.float32)
        for hb in (0, 64):
            cosT[hb:hb + 8] = c.T
            cosT[hb + 8:hb + 16] = c.T
            sinT[hb:hb + 8] = -s_.T
            sinT[hb + 8:hb + 16] = s_.T
        return cosT, sinT

    cos = np.zeros((NST, 128, T), dtype=np.float32)
    sin = np.zeros((NST, 128, T), dtype=np.float32)
    for ti in range(NST):
        cos[ti], sin[ti] = tables(16 + ti * T + np.arange(T))
    cosm, sinm = tables(np.arange(16))
    return dict(c_ident=ident, c_triu=triu, c_pit=pit, c_cos=cos, c_sin=sin, c_cosm=cosm, c_sinm=sinm)


_NC_CACHE = {}


def run(inputs, n_cores, nseq, nst):
    key = (nseq, nst)
    if key not in _NC_CACHE:
        _NC_CACHE[key] = build_nc(nseq, nst)
    nc = _NC_CACHE[key]
    cs = _consts(nst)
    f = lambda a: np.ascontiguousarray(np.asarray(a, dtype=np.float32))
    shared = {
        "meta_tokens": f(inputs["meta_tokens"]),
        "norm_pre": f(inputs["norm_pre"][0]),
        "w_in": f(inputs["w_in"][0]),
        "attn_sinks": f(inputs["attn_sinks"][0]),
        "conv_w": f(inputs["conv_w"][0]),
        "conv_b": f(inputs["conv_b"][0]),
        "mlstm_gate_bias": f(inputs["mlstm_gate_bias"][0]),
        "mlstm_head_norm": f(inputs["mlstm_head_norm"][0]),
        "w_attn_out": f(inputs["w_attn_out"][0]),
        "w_mlstm_out": f(inputs["w_mlstm_out"][0]),
        "w_out": f(inputs["w_out"][0]),
        "norm_post": f(inputs["norm_post"][0]),
    }
    shared.update(cs)
    xfull = np.asarray(inputs["x"], dtype=np.float32)
    in_maps = []
    for c in range(n_cores):
        m = dict(shared)
        m["x"] = np.ascontiguousarray(xfull[c * nseq:(c + 1) * nseq, :nst * T])
        in_maps.append(m)
    res = run_bass_kernel_spmd(nc, in_maps, core_ids=list(range(n_cores)))
    return np.concatenate([np.asarray(r["out"]) for r in res.results], axis=0)


def kernel(**inputs):
    return run(inputs, 8, 2, 8).astype(np.float32)
```

```python
import contextlib
import math
import numpy as np
import concourse.bass as bass
import concourse.mybir as mybir
from concourse.bass_utils import run_bass_kernel_spmd

F32 = mybir.dt.float32
BF16 = mybir.dt.bfloat16
ALU = mybir.AluOpType
AF = mybir.ActivationFunctionType

ENGS = ("pe", "act", "dve", "pool", "sp")


class Res:
    __slots__ = ("name", "w", "readers")

    def __init__(self, name):
        self.name = name
        self.w = None
        self.readers = []


class Op:
    __slots__ = ("eng", "fn", "semkey", "inc", "deps", "signal", "idx", "is_dma")

    def __init__(self, eng, fn, semkey, inc, is_dma):
        self.eng = eng
        self.fn = fn
        self.semkey = semkey
        self.inc = inc
        self.deps = []
        self.signal = False
        self.idx = 0
        self.is_dma = is_dma


class Prog:
    def __init__(self, nc):
        self.nc = nc
        self.streams = {e: [] for e in ENGS}
        self.by_sem = {}

    def res(self, name):
        return Res(name)

    def op(self, eng, fn, reads=(), writes=(), dma=None):
        is_dma = dma is not None
        semkey = ("dma", dma) if is_dma else eng
        o = Op(eng, fn, semkey, 16 if is_dma else 1, is_dma)
        if is_dma:
            o.signal = True
        deps = {}

        def need(p, kind):
            if p is None:
                return
            if (not p.is_dma) and (not is_dma) and p.eng == eng:
                if eng == "pe":
                    return
            deps[id(p)] = p

        for r in reads:
            need(r.w, "raw")
        for w in writes:
            need(w.w, "waw")
            for t in w.readers:
                need(t, "war")
        o.deps = list(deps.values())
        for p in o.deps:
            p.signal = True
        for w in writes:
            w.w = o
            w.readers = []
        for r in reads:
            if not is_dma:
                r.readers = [t for t in r.readers if t.is_dma or t.eng != eng]
            r.readers.append(o)
        self.streams[eng].append(o)
        self.by_sem.setdefault(semkey, []).append(o)
        return o

    def emit(self):
        nc = self.nc
        for k, ops in self.by_sem.items():
            for i, o in enumerate(ops):
                o.idx = i + 1
        used = set()
        plan = {}
        for eng_name in ENGS:
            waited = {}
            for o in self.streams[eng_name]:
                need = {}
                for p in o.deps:
                    if p.idx > need.get(p.semkey, 0):
                        need[p.semkey] = p.idx
                w = []
                for k, v in need.items():
                    if waited.get(k, 0) >= v:
                        continue
                    w.append((k, v))
                    waited[k] = v
                    used.add((k, v))
                plan[id(o)] = w
        pos2idx = {}
        for k, ops in self.by_sem.items():
            c = 0
            m = {}
            for o in ops:
                o.signal = o.is_dma or ((k, o.idx) in used)
                if o.signal:
                    c += o.inc
                m[o.idx] = c
            pos2idx[k] = m
        self.n_signals = sum(1 for ops in self.by_sem.values() for o in ops if o.signal)
        self.n_waits = sum(len(w) for w in plan.values())
        with contextlib.ExitStack() as st:
            sems = {}
            for k in self.by_sem:
                nm = "s_" + (k if isinstance(k, str) else "d_" + str(k[1]))
                sems[k] = st.enter_context(nc.semaphore(nm))
            block = st.enter_context(nc.Block())
            prog = self

            def run(eng_name, handle):
                waited = {}
                for o in prog.streams[eng_name]:
                    for k, v in plan[id(o)]:
                        handle.wait_ge(sems[k], pos2idx[k][v])
                    ins = o.fn(handle)
                    if o.signal:
                        ins.then_inc(sems[o.semkey], o.inc)

            @block.tensor
            def _(e):
                run("pe", e)

            @block.scalar
            def _(e):
                run("act", e)

            @block.vector
            def _(e):
                run("dve", e)

            @block.gpsimd
            def _(e):
                run("pool", e)

            @block.sync
            def _(e):
                run("sp", e)


C_AQ, C_AK, C_AV, C_AZ = 0, 1024, 1280, 1536
C_MQK, C_MV, C_MI, C_MF, C_MO, C_MZ, C_GA, C_GM = 2560, 3584, 4608, 4612, 4616, 5640, 6664, 7688
IN_W = 8712
T = 512
LOGK = -0.5 * math.log(128.0)


def build_nc(NSEQ, NST):
    nc = bass.Bass("TRN2", target_bir_lowering=False)
    S = NST * T

    def din(name, shape, dt=F32):
        return nc.dram_tensor(name, shape, dt, kind="ExternalInput").ap()

    x = din("x", [NSEQ, S, 1024])
    meta = din("meta_tokens", [16, 1024])
    norm_pre = din("norm_pre", [1024])
    w_in = din("w_in", [1024, IN_W])
    sinks = din("attn_sinks", [16])
    conv_w = din("conv_w", [4, 1024])
    conv_b = din("conv_b", [1024])
    gate_bias = din("mlstm_gate_bias", [8])
    head_norm = din("mlstm_head_norm", [1024])
    w_ao = din("w_attn_out", [1024, 1024])
    w_mo = din("w_mlstm_out", [1024, 1024])
    w_out = din("w_out", [1024, 1024])
    norm_post = din("norm_post", [1024])
    c_ident = din("c_ident", [128, 128])
    c_triu = din("c_triu", [128, 128])
    c_pit = din("c_pit", [128, 128])
    c_cos = din("c_cos", [NST, 128, T])
    c_sin = din("c_sin", [NST, 128, T])
    c_cosm = din("c_cosm", [128, 16])
    c_sinm = din("c_sinm", [128, 16])
    out = nc.dram_tensor("out", [NSEQ, S, 1024], F32, kind="ExternalOutput").ap()

    def dscr(name, shape):
        return nc.dram_tensor(name, shape, BF16, kind="Internal").ap()

    NWB = 24
    wbt = dscr("wbt", [NWB, 128, 8, 512])
    WB = {"K": 0, "V": 1, "Q": 2, "Z": 4, "GA": 6, "MQK": 8, "MV": 10, "MO": 12, "MZ": 14, "GM": 16, "AO": 18, "MOUT": 20, "OUT": 22}

    P = Prog(nc)
    with contextlib.ExitStack() as st:
        def sb(name, shape, dt=F32):
            return st.enter_context(nc.sbuf_tensor(name, shape, dt))

        NB = 7
        banks = [st.enter_context(nc.psum_tensor("pb%d" % i, [128, 512], F32)) for i in range(NB)]
        bank_res = [P.res("pb%d" % i) for i in range(NB)]
        pst = st.enter_context(nc.psum_tensor("pst", [128, 8, 128], BF16))
        r_pst = P.res("pst")
        bctr = [0]

        def bank():
            i = bctr[0] % NB
            bctr[0] += 1
            return banks[i], bank_res[i]

        ident_f = sb("ident_f", [128, 128]); r_identf = P.res("identf")
        ident_b = sb("ident_b", [128, 128], BF16); r_ident = P.res("ident")
        triu_f = sb("triu_f", [128, 128]); r_triu = P.res("triu")
        pit_f = sb("pit_f", [128, 128]); r_pitf = P.res("pitf")
        pit_b = sb("pit_b", [128, 128], BF16); r_pit = P.res("pit")
        ones_f = sb("ones_f", [128, 128]); r_onesf = P.res("onesf")
        ones_b = sb("ones_b", [128, 128], BF16); r_onesb = P.res("onesb")
        mask_b = sb("mask_b", [128, 4, 128], BF16); r_mask = P.res("mask")
        maskp_b = sb("maskp_b", [128, 4, 128], BF16); r_maskp = P.res("maskp")
        gB = sb("gB", [128, 1024]); r_gB = P.res("gB")
        hnB = sb("hnB", [128, 1024]); r_hnB = P.res("hnB")
        npB = sb("npB", [128, 1024]); r_npB = P.res("npB")
        gbias = sb("gbias", [128, 8]); r_gbias = P.res("gbias")
        cw = sb("cw", [128, 8, 4]); r_cw = P.res("cw")
        cb = sb("cb", [128, 8]); r_cb = P.res("cb")
        sk = sb("sk", [33, 16]); r_sk = P.res("sk")
        wgate = sb("wgate", [128, 8, 8], BF16); r_wgate = P.res("wgate")
        cosT = sb("cosT", [128, T]); r_cos = P.res("cos")
        sinT = sb("sinT", [128, T]); r_sin = P.res("sin")
        cosm = sb("cosm", [128, 16]); r_cosm = P.res("cosm")
        sinm = sb("sinm", [128, 16]); r_sinm = P.res("sinm")

        NSLOT = 3
        wring = [sb("wring%d" % i, [128, 8, 512], BF16) for i in range(NSLOT)]
        r_wring = [P.res("wring%d" % i) for i in range(NSLOT)]
        wctr = [0]

        xs = [sb("xs%d" % j, [128, 1024]) for j in range(4)]
        r_xs = [P.res("xs%d" % j) for j in range(4)]
        xbs = [sb("xb%d" % i, [128, 1024], BF16) for i in range(2)]
        r_xbs = [P.res("xb%d" % i) for i in range(2)]
        mhalf = sb("mhalf", [128, 8]); r_mhalf = P.res("mhalf")
        stt = [sb("stt%d" % j, [128, 4]) for j in range(4)]
        r_stt = [P.res("stt%d" % j) for j in range(4)]
        uT = sb("uT", [128, 8, T], BF16); r_uT = P.res("uT")
        bufA = sb("bufA", [128, 8, T], BF16); r_bufA = P.res("bufA")
        bufB = sb("bufB", [128, 8, T], BF16); r_bufB = P.res("bufB")
        bufC = sb("bufC", [128, 8, T], BF16); r_bufC = P.res("bufC")
        hzT = sb("hzT", [128, 8, T], BF16); r_hzT = P.res("hzT")
        mrg = sb("mrg", [128, 8, T], BF16); r_mrg = P.res("mrg")
        Kbuf = sb("Kbuf", [128, 4, 640], BF16); r_Kbuf = P.res("Kbuf")
        KmT = sb("KmT", [128, 4, 16], BF16); r_KmT = P.res("KmT")
        Vdup = [sb("Vaug%d" % i, [128, 4, 66], BF16) for i in range(5)]
        r_Vdup = [P.res("Vaug%d" % i) for i in range(5)]
        Vmeta = sb("Vmeta", [33, 4, 66], BF16); r_Vmeta = P.res("Vmeta")
        att_tok = [sb("att_tok%d" % i, [128, 1024], BF16) for i in range(2)]
        r_att_tok = [P.res("att_tok%d" % i) for i in range(2)]
        rec4 = [sb("rec4_%d" % i, [128, 4]) for i in range(2)]
        r_rec4 = [P.res("rec4_%d" % i) for i in range(2)]
        ptc = [sb("ptc%d" % i, [128, 512], BF16) for i in range(2)]
        r_ptc = [P.res("ptc%d" % i) for i in range(2)]
        ptp = [sb("ptp%d" % i, [128, 512], BF16) for i in range(2)]
        r_ptp = [P.res("ptp%d" % i) for i in range(2)]
        ptm = [sb("ptm%d" % g, [33, 512], BF16) for g in range(4)]
        r_ptm = [P.res("ptm%d" % g) for g in range(4)]
        raw = [sb("raw%d" % i, [128, 512], BF16) for i in range(2)]
        r_raw = [P.res("raw%d" % i) for i in range(2)]
        t1 = [sb("t1_%d" % i, [128, 512]) for i in range(2)]
        r_t1 = [P.res("t1_%d" % i) for i in range(2)]
        t2 = [sb("t2_%d" % i, [128, 512]) for i in range(2)]
        r_t2 = [P.res("t2_%d" % i) for i in range(2)]
        tzt = [sb("tzt%d" % i, [128, 512], BF16) for i in range(2)]
        r_tzt = [P.res("tzt%d" % i) for i in range(2)]
        zst = [sb("zst%d" % i, [128, 512], BF16) for i in range(2)]
        r_zst = [P.res("zst%d" % i) for i in range(2)]
        pre = [sb("pre%d" % i, [128, 515]) for i in range(2)]
        r_pre = [P.res("pre%d" % i) for i in range(2)]
        acc = [sb("acc%d" % i, [128, 512]) for i in range(2)]
        r_acc = [P.res("acc%d" % i) for i in range(2)]
        halo = sb("halo", [128, 8, 3]); r_halo = P.res("halo")
        halo_m = sb("halo_m", [128, 8, 3]); r_halom = P.res("halom")
        gsb = sb("gsb", [128, 4, 8]); r_gsb = P.res("gsb")
        e1 = sb("e1", [128, 4, 4]); r_e1 = P.res("e1")
        nlf = sb("nlf", [128, 16]); r_nlf = P.res("nlf")
        thr = sb("thr", [128, 16]); r_thr = P.res("thr")
        gs = sb("gs", [128, 16]); r_gs = P.res("gs")
        vsc = sb("vsc", [128, 16]); r_vsc = P.res("vsc")
        eB = sb("eB", [128, 16]); r_eB = P.res("eB")
        vaug = [sb("vaug%d" % j, [128, 4, 272], BF16) for j in range(4)]
        r_vaug = [P.res("vaug%d" % j) for j in range(4)]
        ktok = sb("ktok", [128, 4, 128], BF16); r_ktok = P.res("ktok")
        PTm = sb("PTm", [128, 512], BF16); r_PTm = P.res("PTm")
        C32 = sb("C32", [128, 4, 257]); r_C32 = [P.res("C32_%d" % h) for h in range(4)]
        Cbf = sb("Cbf", [128, 4, 272], BF16); r_Cbf = [P.res("Cbf_%d" % h) for h in range(4)]
        Cm32 = sb("Cm32", [128, 4, 257]); r_Cm32 = P.res("Cm32")
        Ue = [sb("Ue%d" % i, [128, 257]) for i in range(2)]
        r_Ue = [P.res("Ue%d" % i) for i in range(2)]
        dd = sb("dd", [128, 4, 4]); r_dd = [P.res("dd%d" % h) for h in range(4)]
        tmpN = sb("tmpN", [128, 1024]); r_tmpN = P.res("tmpN")
        ho = sb("ho", [128, 1024]); r_ho = P.res("ho")
        st6 = sb("st6", [128, 4, 6]); r_st6 = P.res("st6")
        mv = sb("mv", [128, 4, 2]); r_mv = P.res("mv")
        lnv = sb("lnv", [128, 8]); r_lnv = P.res("lnv")
        hzs = [sb("hz%d" % i, [128, 1024], BF16) for i in range(2)]
        r_hzs = [P.res("hz%d" % i) for i in range(2)]
        hz, r_hz = hzs[0], r_hzs[0]
        ot = [sb("ot%d" % i, [128, 1024]) for i in range(2)]
        r_ot = [P.res("ot%d" % i) for i in range(2)]
        ss = [sb("ss%d" % i, [128, 8]) for i in range(2)]
        r_ss = [P.res("ss%d" % i) for i in range(2)]
        r_outs = []

        def new_out():
            r = P.res("out%d" % len(r_outs))
            r_outs.append(r)
            return r

        ctr = {"raw": 0, "tz": 0, "pre": 0, "pt": 0, "ot": 0, "xb": 0, "hz": 0, "rec4": 0, "ue": 0}

        def rr(key, n=2):
            i = ctr[key] % n
            ctr[key] += 1
            return i

        A = P.op

        r_castb = {}

        def cast_blk(bi, src, c0, n):
            r = P.res("cast%d" % bi)
            A("pool", lambda e: e.dma_start(out=wbt[bi][:, :, 0:n], in_=src[:, c0:c0 + n].rearrange("(k p) n -> p k n", p=128)),
              writes=[r], dma="cast%d" % bi)
            r_castb[bi] = [r]

        rk = []
        for g in range(4):
            for d in range(2):
                r = P.res("castK%d%d" % (g, d))
                A("pool", lambda e, g=g, d=d: e.dma_start(out=wbt[0][:, :, g * 128 + d * 64: g * 128 + d * 64 + 64],
                                                      in_=w_in[:, C_AK + g * 64: C_AK + g * 64 + 64].rearrange("(k p) n -> p k n", p=128)),
                  writes=[r], dma="castK")
                rk.append(r)
        r_castb[0] = rk
        cast_blk(1, w_in, C_AV, 256)
        for nm, c0 in (("Q", C_AQ), ("Z", C_AZ), ("GA", C_GA)):
            for i in range(2):
                cast_blk(WB[nm] + i, w_in, c0 + i * 512, 512)
        for i in range(2):
            cast_blk(WB["AO"] + i, w_ao, i * 512, 512)
        r_wgate_c = P.res("wgate_c")
        for nm, c0 in (("MQK", C_MQK), ("MV", C_MV), ("MO", C_MO), ("MZ", C_MZ), ("GM", C_GM)):
            for i in range(2):
                cast_blk(WB[nm] + i, w_in, c0 + i * 512, 512)
        for i in range(2):
            cast_blk(WB["MOUT"] + i, w_mo, i * 512, 512)
        for i in range(2):
            cast_blk(WB["OUT"] + i, w_out, i * 512, 512)

        def ld(dst, src, r, name, **kw):
            A("sp", lambda e: e.dma_start(out=dst, in_=src, **kw), writes=[r], dma=name)

        ld(ident_f[:], c_ident, r_identf, "identf")
        ld(triu_f[:], c_triu, r_triu, "triu")
        ld(pit_f[:], c_pit, r_pitf, "pitf")
        ld(gB[:], norm_pre.partition_broadcast(128), r_gB, "gB")
        ld(hnB[:], head_norm.partition_broadcast(128), r_hnB, "hnB")
        ld(npB[:], norm_post.partition_broadcast(128), r_npB, "npB")
        ld(gbias[:], gate_bias.partition_broadcast(128), r_gbias, "gbias")
        for jj in range(4):
            ld(cw[:, :, jj], conv_w[jj].rearrange("(b p) -> p b", p=128), r_cw, "cw", allow_slow_non_contiguous=True)
        ld(cb[:], conv_b.rearrange("(b p) -> p b", p=128), r_cb, "cb", allow_slow_non_contiguous=True)
        ld(sk[32:33, :], sinks.rearrange("(o n) -> o n", o=1), r_sk, "sk")
        ld(cosm[:], c_cosm, r_cosm, "cosm")
        ld(sinm[:], c_sinm, r_sinm, "sinm")
        A("pool", lambda e: e.dma_start(out=wgate[:], in_=w_in[:, C_MI:C_MI + 8].rearrange("(k p) n -> p k n", p=128),
                                        allow_slow_non_contiguous=True), writes=[r_wgate], dma="wgate")
        A("dve", lambda e: e.tensor_copy(out=ident_b[:], in_=ident_f[:]), [r_identf], [r_ident])
        A("dve", lambda e: e.tensor_copy(out=pit_b[:], in_=pit_f[:]), [r_pitf], [r_pit])
        A("dve", lambda e: e.memset(ones_f[:], 1.0), [], [r_onesf])
        A("dve", lambda e: e.memset(ones_b[:], 1.0), [], [r_onesb])
        A("dve", lambda e: e.tensor_copy(out=mask_b[:], in_=triu_f[:].unsqueeze(1).to_broadcast([128, 4, 128])), [r_triu], [r_mask])
        A("dve", lambda e: e.tensor_scalar(out=maskp_b[:], in0=triu_f[:].unsqueeze(1).to_broadcast([128, 4, 128]),
                                           scalar1=-1.0, scalar2=1.0, op0=ALU.mult, op1=ALU.add), [r_triu], [r_maskp])
        A("dve", lambda e: e.tensor_scalar(out=cw[:], in0=cw[:], scalar1=0.5, scalar2=None, op0=ALU.mult), [r_cw], [r_cw])
        A("dve", lambda e: e.tensor_scalar(out=cb[:], in0=cb[:], scalar1=0.5, scalar2=None, op0=ALU.mult), [r_cb], [r_cb])
        A("pool", lambda e: e.memset(Vmeta[:], 0.0), [], [r_Vmeta])
        A("pool", lambda e: e.memset(Vmeta[0:16, :, 64:65], 1.0), [], [r_Vmeta])
        A("pool", lambda e: e.memset(Vmeta[32:33, :, 64:65], 1.0), [], [r_Vmeta])
        for i in range(5):
            A("pool", lambda e, i=i: e.memset(Vdup[i][:], 0.0), [], [r_Vdup[i]])
            A("pool", lambda e, i=i: e.memset(Vdup[i][:, :, 64:65], 1.0), [], [r_Vdup[i]])
        A("pool", lambda e: e.memset(mhalf[:], -0.5), [], [r_mhalf])
        A("dve", lambda e: e.tensor_scalar(out=gB[:], in0=gB[:], scalar1=32.0, scalar2=None, op0=ALU.mult), [r_gB], [r_gB])
        A("dve", lambda e: e.tensor_scalar(out=npB[:], in0=npB[:], scalar1=32.0, scalar2=None, op0=ALU.mult), [r_npB], [r_npB])
        for g in range(4):
            A("pool", lambda e, g=g: e.memset(ptm[g][:], 0.0), [], [r_ptm[g]])
            A("act", lambda e, g=g: e.activation(out=ptm[g][32:33, :].rearrange("p (h q) -> p h q", h=4),
                                                 in_=sk[32:33, 4 * g:4 * g + 4].unsqueeze(2).to_broadcast([1, 4, 128]),
                                                 func=AF.Exp), [r_sk, r_ptm[g]], [r_ptm[g]])
        for j in range(4):
            A("pool", lambda e, j=j: e.memset(vaug[j][:], 0.0), [], [r_vaug[j]])
        A("pool", lambda e: e.memset(Cbf[:], 0.0), [], r_Cbf)
        A("pool", lambda e: e.memset(halo_m[:], 0.0), [], [r_halom])

        def load_w(bi, n=512):
            i = wctr[0] % NSLOT
            wctr[0] += 1
            A("sp", lambda e: e.dma_start(out=wring[i][:, :, 0:n], in_=wbt[bi][:, :, 0:n]),
              reads=r_castb[bi], writes=[r_wring[i]], dma="w%d" % i)
            return wring[i], r_wring[i]

        def proj_fm(w, c0, rhs, r_rhs, N, n0=0):
            wt, r_w = w
            b, rb = bank()
            for kc in range(8):
                A("pe", lambda e, kc=kc: e.matmul(b[:, 0:N], lhsT=wt[:, kc, c0:c0 + 128], rhs=rhs[:, kc, n0:n0 + N],
                                                 start=(kc == 0), stop=(kc == 7)), [r_w, r_rhs], [rb])
            return b, rb

        def proj_tm(w, n, lhs, r_lhs, t0, nt):
            wt, r_w = w
            b, rb = bank()
            for kc in range(8):
                A("pe", lambda e, kc=kc: e.matmul(b[0:nt, 0:n], lhsT=lhs[:, kc, t0:t0 + nt], rhs=wt[:, kc, 0:n],
                                                 start=(kc == 0), stop=(kc == 7)), [r_w, r_lhs], [rb])
            return b, rb

        pending = []

        def flush():
            while pending:
                pending.pop(0)()

        def rope(b, rb, N, cos_ap, sin_ap, r_tabs, scale, dst, r_dst):
            i = rr("raw")
            A("act", lambda e: e.activation(out=raw[i][:, 0:N], in_=b[:, 0:N], func=AF.Copy, scale=scale), [rb], [r_raw[i]])

            def stage_b():
                b2, rb2 = bank()
                A("pe", lambda e: e.matmul(b2[:, 0:N], lhsT=pit_b[:], rhs=raw[i][:, 0:N], start=True, stop=True), [r_pit, r_raw[i]], [rb2])
                A("dve", lambda e: e.tensor_tensor(out=t1[i][:, 0:N], in0=raw[i][:, 0:N], in1=cos_ap, op=ALU.mult), [r_raw[i]] + r_tabs, [r_t1[i]])
                A("dve", lambda e: e.tensor_tensor(out=t2[i][:, 0:N], in0=b2[:, 0:N], in1=sin_ap, op=ALU.mult), [rb2] + r_tabs, [r_t2[i]])
                A("pool", lambda e: e.tensor_tensor(out=dst, in0=t1[i][:, 0:N], in1=t2[i][:, 0:N], op=ALU.add), [r_t1[i], r_t2[i]], [r_dst])
            pending.append(stage_b)

        def rmsnorm_front(src, r_src, n, stat, r_stat):
            xi = rr("xb")
            xb_, r_xb_ = xbs[xi], r_xbs[xi]
            A("act", lambda e: e.activation(out=xb_[0:n, :], in_=src, func=AF.Square, accum_out=stat[0:n, 0:1]), [r_src], [r_xb_, r_stat])
            A("pool", lambda e: e.tensor_scalar(out=stat[0:n, 1:2], in0=stat[0:n, 0:1], scalar1=1024 * 1e-6, scalar2=None, op0=ALU.add), [r_stat], [r_stat])
            A("pool", lambda e: e.tensor_tensor(out=stat[0:n, 2:3], in0=stat[0:n, 1:2], in1=mhalf[0:n, 0:1], op=ALU.pow), [r_stat, r_mhalf], [r_stat])
            A("dve", lambda e: e.scalar_tensor_tensor(out=xb_[0:n, :], in0=src, scalar=stat[0:n, 2:3], in1=gB[0:n, :],
                                                      op0=ALU.mult, op1=ALU.mult), [r_src, r_stat, r_gB], [r_xb_])
            return xb_, r_xb_

        def rmsnorm_back(xbr, n, dstT, r_dstT, t0):
            xb_, r_xb_ = xbr
            for k in range(8):
                A("pe", lambda e, k=k: e.transpose(out=pst[:, k, 0:n], in_=xb_[0:n, k * 128:(k + 1) * 128], identity=ident_b[0:n, 0:n]),
                  [r_xb_, r_ident], [r_pst])
            A("act", lambda e: e.activation(out=dstT[:, :, t0:t0 + n], in_=pst[:, :, 0:n], func=AF.Copy), [r_pst], [r_dstT])

        def rmsnorm_T(src, r_src, n, stat, r_stat, dstT, r_dstT, t0):
            rmsnorm_back(rmsnorm_front(src, r_src, n, stat, r_stat), n, dstT, r_dstT, t0)

        def conv_silu(b, rb, blk, N, halo_src, r_halo_src, halo_dst, r_halo_dst, dst, r_dst):
            i = rr("pre")
            p_ = pre[i]
            A("act", lambda e: e.activation(out=p_[:, 3:3 + N], in_=b[:, 0:N], func=AF.Copy), [rb], [r_pre[i]])
            A("pool", lambda e: e.tensor_copy(out=p_[:, 0:3], in_=halo_src[:, blk, :]), [r_halo_src], [r_pre[i]])
            a_ = acc[i]
            A("act", lambda e: e.activation(out=a_[:, 0:N], in_=b[:, 0:N], func=AF.Identity, scale=cw[:, blk, 3:4], bias=cb[:, blk:blk + 1]),
              [rb, r_cw, r_cb], [r_acc[i]])
            for jj in (2, 1, 0):
                A("dve", lambda e, jj=jj: e.scalar_tensor_tensor(out=a_[:, 0:N], in0=p_[:, jj:jj + N], scalar=cw[:, blk, jj:jj + 1],
                                                                 in1=a_[:, 0:N], op0=ALU.mult, op1=ALU.add), [r_pre[i], r_cw, r_acc[i]], [r_acc[i]])
            A("pool", lambda e: e.tensor_copy(out=halo_dst[:, blk, :], in_=p_[:, N:N + 3]), [r_pre[i]], [r_halo_dst])
            k = rr("tz")
            A("act", lambda e: e.activation(out=tzt[k][:, 0:N], in_=a_[:, 0:N], func=AF.Tanh), [r_acc[i]], [r_tzt[k]])
            A("dve", lambda e: e.scalar_tensor_tensor(out=dst, in0=tzt[k][:, 0:N], scalar=1.0, in1=a_[:, 0:N], op0=ALU.add, op1=ALU.mult),
              [r_tzt[k], r_acc[i]], [r_dst])

        xm = ot[0]
        A("sp", lambda e: e.dma_start(out=xm[0:16, :], in_=meta), writes=[r_ot[0]], dma="xm")
        uTm = sb("uTm", [128, 8, 16], BF16); r_uTm = P.res("uTm")
        qkm = sb("qkm", [128, 8, 16], BF16); r_qkm = P.res("qkm")
        rmsnorm_T(xm[0:16, :], r_ot[0], 16, stt[0], r_stt[0], uTm, r_uTm, 0)
        wk = load_w(WB["K"])
        for g in range(4):
            b, rb = proj_fm(wk, g * 128, uTm, r_uTm, 16)
            flush()
            rope(b, rb, 16, cosm[:], sinm[:], [r_cosm, r_sinm], 1.0, KmT[:, g, :], r_KmT)
        flush()
        wv = load_w(WB["V"], 256)
        b, rb = proj_tm(wv, 256, uTm, r_uTm, 0, 16)
        A("act", lambda e, b=b: e.activation(out=Vmeta[0:16, :, 0:64], in_=b[0:16, 0:256].rearrange("p (g d) -> p g d", g=4), func=AF.Copy), [rb], [r_Vmeta])
        bgm_, rbgm = bank()
        for kc in range(8):
            A("pe", lambda e, kc=kc: e.matmul(bgm_[0:16, 0:8], lhsT=uTm[:, kc, 0:16], rhs=wgate[:, kc, :], start=(kc == 0), stop=(kc == 7)),
              [r_uTm, r_wgate], [rbgm])
        A("dve", lambda e: e.tensor_tensor(out=gsb[0:16, 0, :], in0=bgm_[0:16, 0:8], in1=gbias[0:16, :], op=ALU.add), [rbgm, r_gbias], [r_gsb])
        A("act", lambda e: e.activation(out=e1[0:16, 0, :], in_=gsb[0:16, 0, 4:8], func=AF.Exp, scale=-1.0), [r_gsb], [r_e1])
        A("act", lambda e: e.activation(out=nlf[0:16, 0:4], in_=e1[0:16, 0, :], func=AF.Ln, bias=1.0), [r_e1], [r_nlf])
        bnbm, rbnbm = bank()
        A("pe", lambda e: e.matmul(bnbm[0:16, 0:4], lhsT=triu_f[0:16, 0:16], rhs=nlf[0:16, 0:4], start=True, stop=True), [r_triu, r_nlf], [rbnbm])
        bnsm, rbnsm = bank()
        A("pe", lambda e: e.matmul(bnsm[:, 0:4], lhsT=ones_f[0:16, :], rhs=nlf[0:16, 0:4], start=True, stop=True), [r_onesf, r_nlf], [rbnsm])
        A("dve", lambda e: e.tensor_tensor(out=gs[0:16, 0:4], in0=gsb[0:16, 0, 0:4], in1=bnbm[0:16, 0:4], op=ALU.add), [r_gsb, rbnbm], [r_gs])
        A("act", lambda e: e.activation(out=vsc[0:16, 0:4], in_=gs[0:16, 0:4], func=AF.Exp, bias=LOGK), [r_gs], [r_vsc])
        A("act", lambda e: e.activation(out=eB[:, 0:4], in_=bnsm[:, 0:4], func=AF.Exp, scale=-1.0), [rbnsm], [r_eB])
        for i in range(2):
            w = load_w(WB["MQK"] + i)
            for c in range(4):
                blk = 4 * i + c
                b, rb = proj_fm(w, c * 128, uTm, r_uTm, 16)
                conv_silu(b, rb, blk, 16, halo_m, r_halom, halo_m, r_halom, qkm[:, blk, :], r_qkm)
        vaugm = vaug[0]
        for i in range(2):
            w = load_w(WB["MV"] + i)
            b, rb = proj_tm(w, 512, uTm, r_uTm, 0, 16)
            for hh in range(2):
                h = 2 * i + hh
                A("act", lambda e, b=b, hh=hh, h=h: e.activation(out=vaugm[0:16, h, 0:256], in_=b[0:16, hh * 256:(hh + 1) * 256], func=AF.Copy,
                                                              scale=vsc[0:16, h:h + 1]), [rb, r_vsc], [r_vaug[0]])
        A("pool", lambda e: e.tensor_copy(out=vaugm[0:16, :, 256:257], in_=vsc[0:16, 0:4].unsqueeze(2)), [r_vsc], [r_vaug[0]])
        for h in range(4):
            A("pe", lambda e, h=h: e.transpose(out=pst[0:16, h, :], in_=qkm[:, 4 + h, :], identity=ident_b[:]), [r_qkm, r_ident], [r_pst])
        A("act", lambda e: e.activation(out=ktok[0:16, :, :], in_=pst[0:16, 0:4, :], func=AF.Copy), [r_pst], [r_ktok])
        for h in range(4):
            bU, rbU = bank()
            A("pe", lambda e, h=h, bU=bU: e.matmul(bU[:, 0:257], lhsT=ktok[0:16, h, :], rhs=vaugm[0:16, h, 0:257], start=True, stop=True),
              [r_ktok, r_vaug[0]], [rbU])
            A("dve", lambda e, h=h, bU=bU: e.tensor_scalar(out=Cm32[:, h, :], in0=bU[:, 0:257], scalar1=eB[:, h:h + 1], scalar2=None, op0=ALU.mult),
              [rbU, r_eB], [r_Cm32])
        A("pool", lambda e: e.memset(vaug[0][:], 0.0), [], [r_vaug[0]])

        def phase0_load(s_, ti_, j):
            t0_ = ti_ * T + j * 128
            A("sp", lambda e: e.dma_start(out=xs[j][:], in_=x[s_, t0_: t0_ + 128, :]), writes=[r_xs[j]], dma="xs%d" % j)

        def phase0_tabs(ti_):
            A("sp", lambda e: e.dma_start(out=cosT[:], in_=c_cos[ti_]), writes=[r_cos], dma="cos")
            A("sp", lambda e: e.dma_start(out=sinT[:], in_=c_sin[ti_]), writes=[r_sin], dma="sin")

        def phase0_norm(j):
            rmsnorm_T(xs[j][:], r_xs[j], 128, stt[j], r_stt[j], uT, r_uT, j * 128)

        def phase0_front(j):
            return rmsnorm_front(xs[j][:], r_xs[j], 128, stt[j], r_stt[j])

        def phase0_back(j, xbr):
            rmsnorm_back(xbr, 128, uT, r_uT, j * 128)

        wk_pref = [None]
        for s in range(NSEQ):
            A("pool", lambda e: e.tensor_copy(out=C32[:], in_=Cm32[:]), [r_Cm32], r_C32)
            A("pool", lambda e: e.tensor_copy(out=Cbf[:, :, 0:257], in_=Cm32[:]), [r_Cm32], r_Cbf)
            A("pool", lambda e: e.tensor_copy(out=halo[:], in_=halo_m[:]), [r_halom], [r_halo])
            for ti in range(NST):
                tok0 = ti * T
                if s == 0 and ti == 0:
                    phase0_tabs(ti)
                    for j in range(4):
                        phase0_load(s, ti, j)
                    for j in range(4):
                        phase0_norm(j)
                nxt = (s, ti + 1) if ti + 1 < NST else ((s + 1, 0) if s + 1 < NSEQ else None)
                wk = wk_pref[0] if wk_pref[0] is not None else load_w(WB["K"])
                wk_pref[0] = None
                for g in range(4):
                    b, rb = proj_fm(wk, g * 128, uT, r_uT, T)
                    flush()
                    rope(b, rb, T, cosT[:], sinT[:], [r_cos, r_sin], 1.0, Kbuf[:, g, 128:640], r_Kbuf)
                wv = load_w(WB["V"], 256)
                for j in range(4):
                    b, rb = proj_tm(wv, 256, uT, r_uT, j * 128, 128)
                    A("act", lambda e, b=b, j=j: e.activation(out=Vdup[j + 1][:, :, 0:64], in_=b[:, 0:256].rearrange("p (g d) -> p g d", g=4), func=AF.Copy),
                      [rb], [r_Vdup[j + 1]])
                QT, r_QT = bufA, r_bufA
                attT, r_attT = bufB, r_bufB
                for i in range(2):
                    w = load_w(WB["Q"] + i)
                    for c in range(4):
                        blk = 4 * i + c
                        b, rb = proj_fm(w, c * 128, uT, r_uT, T)
                        flush()
                        rope(b, rb, T, cosT[:], sinT[:], [r_cos, r_sin], 0.125, QT[:, blk, :], r_QT)
                flush()
                def att_S(j, g):
                    has_prev = not (ti == 0 and j == 0)
                    jc = slice(j * 128, (j + 1) * 128)
                    pi = rr("pt")
                    bSc, rSc = bank()
                    bSm, rSm = bank()
                    bSp, rSp = bank() if has_prev else (None, None)
                    for hh in range(4):
                        blk = 2 * g + hh // 2
                        rows = slice((hh % 2) * 64, (hh % 2) * 64 + 64)
                        hc = slice(hh * 128, (hh + 1) * 128)
                        A("pe", lambda e, rows=rows, hc=hc, blk=blk: e.matmul(
                            bSc[:, hc], lhsT=Kbuf[rows, g, 128 + j * 128: 256 + j * 128], rhs=QT[rows, blk, jc], start=True, stop=True),
                          [r_Kbuf, r_QT], [rSc])
                        A("pe", lambda e, rows=rows, hc=hc, blk=blk: e.matmul(
                            bSm[0:16, hc], lhsT=KmT[rows, g, :], rhs=QT[rows, blk, jc], start=True, stop=True),
                          [r_KmT, r_QT], [rSm])
                        if has_prev:
                            A("pe", lambda e, rows=rows, hc=hc, blk=blk: e.matmul(
                                bSp[:, hc], lhsT=Kbuf[rows, g, j * 128: 128 + j * 128], rhs=QT[rows, blk, jc], start=True, stop=True),
                              [r_Kbuf, r_QT], [rSp])
                    A("act", lambda e: e.activation(out=ptc[pi][:], in_=bSc[:], func=AF.Exp), [rSc], [r_ptc[pi]])
                    A("act", lambda e: e.activation(out=ptm[g][0:16, :], in_=bSm[0:16, :], func=AF.Exp), [rSm], [r_ptm[g]])
                    A("pool", lambda e: e.tensor_tensor(out=ptc[pi][:], in0=ptc[pi][:], in1=mask_b[:].rearrange("p h q -> p (h q)"), op=ALU.mult),
                      [r_ptc[pi], r_mask], [r_ptc[pi]])
                    if has_prev:
                        A("act", lambda e: e.activation(out=ptp[pi][:], in_=bSp[:], func=AF.Exp), [rSp], [r_ptp[pi]])
                        A("pool", lambda e: e.tensor_tensor(out=ptp[pi][:], in0=ptp[pi][:], in1=maskp_b[:].rearrange("p h q -> p (h q)"), op=ALU.mult),
                          [r_ptp[pi], r_maskp], [r_ptp[pi]])
                    return (j, g, pi, has_prev, jc)

                def att_PV(ctx):
                    j, g, pi, has_prev, jc = ctx
                    ai = j % 2
                    bO, rO = bank()
                    for hh in range(4):
                        hc = slice(hh * 128, (hh + 1) * 128)
                        oc = slice(hh * 65, hh * 65 + 65)
                        A("pe", lambda e, hc=hc, oc=oc: e.matmul(bO[:, oc], lhsT=ptm[g][:, hc], rhs=Vmeta[:, g, 0:65], start=True, stop=False),
                          [r_Vmeta, r_ptm[g]], [rO])
                        if has_prev:
                            A("pe", lambda e, hc=hc, oc=oc: e.matmul(bO[:, oc], lhsT=ptp[pi][:, hc], rhs=Vdup[j][:, g, 0:65], start=False, stop=False),
                              [r_Vdup[j], r_ptp[pi]], [rO])
                        A("pe", lambda e, hc=hc, oc=oc: e.matmul(bO[:, oc], lhsT=ptc[pi][:, hc], rhs=Vdup[j + 1][:, g, 0:65], start=False, stop=True),
                          [r_Vdup[j + 1], r_ptc[pi]], [rO])
                    ri = rr("rec4")
                    bOv = bO[:, 0:260].rearrange("p (h c) -> p h c", c=65)
                    A("dve", lambda e: e.reciprocal(out=rec4[ri][:], in_=bOv[:, :, 64]), [rO], [r_rec4[ri]])
                    A("dve", lambda e: e.tensor_tensor(out=att_tok[ai][:, g * 256:(g + 1) * 256].rearrange("p (h d) -> p h d", h=4), in0=bOv[:, :, 0:64],
                                                       in1=rec4[ri][:].unsqueeze(2).to_broadcast([128, 4, 64]), op=ALU.mult),
                      [rO, r_rec4[ri]], [r_att_tok[ai]])
                    if g == 3:
                        for k in range(8):
                            A("pe", lambda e, k=k: e.transpose(out=pst[:, k, :], in_=att_tok[ai][:, k * 128:(k + 1) * 128], identity=ident_b[:]),
                              [r_att_tok[ai], r_ident], [r_pst])
                        A("act", lambda e: e.activation(out=attT[:, :, jc], in_=pst[:], func=AF.Copy), [r_pst], [r_attT])

                order = [(j, g) for j in range(4) for g in range(4)]
                ctxs = [att_S(*order[0])]
                for idx in range(len(order)):
                    if idx + 1 < len(order):
                        ctxs.append(att_S(*order[idx + 1]))
                    att_PV(ctxs[idx])
                for i in range(2):
                    w = load_w(WB["Z"] + i)
                    for c in range(4):
                        blk = 4 * i + c
                        b, rb = proj_fm(w, c * 128, uT, r_uT, T)
                        k = rr("tz")
                        A("act", lambda e, b=b, k=k: e.activation(out=tzt[k][:], in_=b[:], func=AF.Tanh, scale=0.5), [rb], [r_tzt[k]])
                        A("dve", lambda e, b=b, k=k: e.scalar_tensor_tensor(out=zst[k][:], in0=tzt[k][:], scalar=1.0, in1=b[:], op0=ALU.add, op1=ALU.mult),
                          [r_tzt[k], rb], [r_zst[k]])
                        A("pool", lambda e, k=k, blk=blk: e.tensor_tensor(out=attT[:, blk, :], in0=attT[:, blk, :], in1=zst[k][:], op=ALU.mult),
                          [r_attT, r_zst[k]], [r_attT])
                for i in range(2):
                    wg = load_w(WB["GA"] + i)
                    wa = load_w(WB["AO"] + i)
                    for c in range(4):
                        blk = 4 * i + c
                        b, rb = proj_fm(wg, c * 128, uT, r_uT, T)
                        k = rr("tz")
                        A("act", lambda e, b=b, k=k: e.activation(out=tzt[k][:], in_=b[:], func=AF.Tanh, scale=0.5), [rb], [r_tzt[k]])
                        by, rby = proj_fm(wa, c * 128, attT, r_attT, T)
                        A("dve", lambda e, by=by, k=k, blk=blk: e.scalar_tensor_tensor(out=mrg[:, blk, :], in0=tzt[k][:], scalar=1.0, in1=by[:],
                                                                                    op0=ALU.add, op1=ALU.mult), [r_tzt[k], rby], [r_mrg])
                if nxt is not None:
                    phase0_tabs(nxt[1])
                    for j in range(4):
                        phase0_load(nxt[0], nxt[1], j)
                bg, rbg = bank()
                for j in range(4):
                    for kc in range(8):
                        A("pe", lambda e, kc=kc, j=j, bg=bg: e.matmul(bg[:, j * 8:(j + 1) * 8], lhsT=uT[:, kc, j * 128:(j + 1) * 128], rhs=wgate[:, kc, :],
                                                                 start=(kc == 0), stop=(kc == 7)), [r_uT, r_wgate], [rbg])
                A("dve", lambda e, bg=bg: e.tensor_tensor(out=gsb[:], in0=bg[:, 0:32].rearrange("p (j c) -> p j c", j=4),
                                                        in1=gbias[:].unsqueeze(1).to_broadcast([128, 4, 8]), op=ALU.add), [rbg, r_gbias], [r_gsb])
                A("act", lambda e: e.activation(out=e1[:], in_=gsb[:, :, 4:8], func=AF.Exp, scale=-1.0), [r_gsb], [r_e1])
                A("act", lambda e: e.activation(out=nlf[:], in_=e1[:].rearrange("p j h -> p (j h)"), func=AF.Ln, bias=1.0), [r_e1], [r_nlf])
                bnb, rbnb = bank()
                A("pe", lambda e, bnb=bnb: e.matmul(bnb[:, 0:16], lhsT=triu_f[:], rhs=nlf[:], start=True, stop=True), [r_triu, r_nlf], [rbnb])
                bns, rbns = bank()
                A("pe", lambda e, bns=bns: e.matmul(bns[:, 0:16], lhsT=ones_f[:], rhs=nlf[:], start=True, stop=True), [r_onesf, r_nlf], [rbns])
                A("act", lambda e, bnb=bnb: e.activation(out=thr[:], in_=bnb[:, 0:16], func=AF.Exp), [rbnb], [r_thr])
                A("dve", lambda e, bnb=bnb: e.tensor_tensor(out=gs[:].rearrange("p (j h) -> p j h", j=4), in0=gsb[:, :, 0:4],
                                                          in1=bnb[:, 0:16].rearrange("p (j h) -> p j h", j=4), op=ALU.add), [r_gsb, rbnb], [r_gs])
                A("act", lambda e: e.activation(out=vsc[:], in_=gs[:], func=AF.Exp, bias=LOGK), [r_gs], [r_vsc])
                A("act", lambda e, bns=bns: e.activation(out=eB[:], in_=bns[:, 0:16], func=AF.Exp, scale=-1.0), [rbns], [r_eB])
                qkT, r_qkT = bufC, r_bufC
                for i in range(2):
                    w = load_w(WB["MQK"] + i)
                    for c in range(4):
                        blk = 4 * i + c
                        b, rb = proj_fm(w, c * 128, uT, r_uT, T)
                        conv_silu(b, rb, blk, T, halo, r_halo, halo, r_halo, qkT[:, blk, :], r_qkT)
                for i in range(2):
                    w = load_w(WB["MV"] + i)
                    for j in range(4):
                        b, rb = proj_tm(w, 512, uT, r_uT, j * 128, 128)
                        for hh in range(2):
                            h = 2 * i + hh
                            A("act", lambda e, b=b, hh=hh, h=h, j=j: e.activation(out=vaug[j][:, h, 0:256], in_=b[:, hh * 256:(hh + 1) * 256], func=AF.Copy,
                                                                             scale=vsc[:, j * 4 + h: j * 4 + h + 1]), [rb, r_vsc], [r_vaug[j]])
                for j in range(4):
                    A("pool", lambda e, j=j: e.tensor_copy(out=vaug[j][:, :, 256:257], in_=vsc[:, j * 4:(j + 1) * 4].unsqueeze(2)), [r_vsc], [r_vaug[j]])
                th = bufB[:].rearrange("p k t -> p (k t)").rearrange("p (j f) -> p j f", j=4)
                r_th = r_bufB
                zs = bufA[:].rearrange("p k t -> p (k t)").rearrange("p (j f) -> p j f", j=4)
                r_zs = r_bufA
                for i in range(2):
                    w = load_w(WB["MO"] + i)
                    for j in range(4):
                        b, rb = proj_tm(w, 512, uT, r_uT, j * 128, 128)
                        A("act", lambda e, b=b, j=j, i=i: e.activation(out=th[:, j, i * 512:(i + 1) * 512], in_=b[:], func=AF.Tanh, scale=0.5), [rb], [r_th])
                for i in range(2):
                    w = load_w(WB["MZ"] + i)
                    for j in range(4):
                        b, rb = proj_tm(w, 512, uT, r_uT, j * 128, 128)
                        k = rr("tz")
                        A("act", lambda e, b=b, k=k: e.activation(out=tzt[k][:], in_=b[:], func=AF.Tanh, scale=0.5), [rb], [r_tzt[k]])
                        A("dve", lambda e, b=b, k=k: e.scalar_tensor_tensor(out=zst[k][:], in0=tzt[k][:], scalar=1.0, in1=b[:],
                                                                          op0=ALU.add, op1=ALU.mult), [r_tzt[k], rb], [r_zst[k]])
                        A("pool", lambda e, k=k, j=j, i=i: e.tensor_tensor(out=zs[:, j, i * 512:(i + 1) * 512], in0=zst[k][:], in1=hnB[:, i * 512:(i + 1) * 512], op=ALU.mult),
                          [r_zst[k], r_hnB], [r_zs])
                tgm = [(raw[0][:], r_raw[0]), (raw[1][:], r_raw[1]), (ptc[0][:], r_ptc[0]), (ptc[1][:], r_ptc[1]),
                       (ptp[0][:], r_ptp[0]), (ptp[1][:], r_ptp[1]), (att_tok[0][:, 0:512], r_att_tok[0]), (att_tok[1][:, 0:512], r_att_tok[1])]
                wgm = [None, None]
                def rec_pe_a(j):
                    jc = slice(j * 128, (j + 1) * 128)
                    for h in range(4):
                        A("pe", lambda e, h=h: e.transpose(out=pst[:, h, :], in_=qkT[:, 4 + h, jc], identity=ident_b[:]), [r_qkT, r_ident], [r_pst])
                    A("act", lambda e: e.activation(out=ktok[:], in_=pst[:, 0:4, :], func=AF.Copy), [r_pst], [r_ktok])
                    bS, rS = bank()
                    for h in range(4):
                        A("pe", lambda e, h=h: e.matmul(bS[:, h * 128:(h + 1) * 128], lhsT=qkT[:, 4 + h, jc], rhs=qkT[:, h, jc], start=True, stop=True),
                          [r_qkT], [rS])
                    A("dve", lambda e: e.tensor_tensor(out=PTm[:], in0=bS[:], in1=mask_b[:].rearrange("p h q -> p (h q)"), op=ALU.mult), [rS, r_mask], [r_PTm])
                    return jc

                def rec_pe_b(j, jc):
                    bNs = [bank(), bank()]
                    bDn, rDn = bank()
                    for h in range(4):
                        bN, rN = bNs[h // 2]
                        nc_ = slice((h % 2) * 256, (h % 2) * 256 + 256)
                        A("pe", lambda e, h=h, bN=bN, nc_=nc_: e.matmul(bN[:, nc_], lhsT=PTm[:, h * 128:(h + 1) * 128], rhs=vaug[j][:, h, 0:256], start=True, stop=False),
                          [r_PTm, r_vaug[j]], [rN])
                        A("pe", lambda e, h=h, bN=bN, nc_=nc_: e.matmul(bN[:, nc_], lhsT=qkT[:, h, jc], rhs=Cbf[:, h, 0:256], start=False, stop=True),
                          [r_qkT, r_Cbf[h]], [rN])
                        A("pe", lambda e, h=h: e.matmul(bDn[:, h:h + 1], lhsT=PTm[:, h * 128:(h + 1) * 128], rhs=vaug[j][:, h, 256:257], start=True, stop=False),
                          [r_PTm, r_vaug[j]], [rDn])
                        A("pe", lambda e, h=h: e.matmul(bDn[:, h:h + 1], lhsT=qkT[:, h, jc], rhs=Cbf[:, h, 256:257], start=False, stop=True),
                          [r_qkT, r_Cbf[h]], [rDn])
                    bUs = []
                    for h in range(4):
                        bU, rbU = bank()
                        A("pe", lambda e, h=h, bU=bU: e.matmul(bU[:, 0:257], lhsT=ktok[:, h, :], rhs=vaug[j][:, h, 0:257], start=True, stop=True),
                          [r_ktok, r_vaug[j]], [rbU])
                        bUs.append((bU, rbU))
                    return (j, jc, bNs, bDn, rDn, bUs)

                def rec_state(ctx):
                    j, jc, bNs, bDn, rDn, bUs = ctx
                    for h in range(4):
                        bU, rbU = bUs[h]
                        col = j * 4 + h
                        A("dve", lambda e, h=h, col=col: e.tensor_scalar(out=C32[:, h, :], in0=C32[:, h, :], scalar1=eB[:, col:col + 1], scalar2=None, op0=ALU.mult),
                          [r_C32[h], r_eB], [r_C32[h]])
                        A("dve", lambda e, h=h, col=col, bU=bU: e.scalar_tensor_tensor(out=C32[:, h, :], in0=bU[:, 0:257], scalar=eB[:, col:col + 1], in1=C32[:, h, :],
                                                                                    op0=ALU.mult, op1=ALU.add), [rbU, r_eB, r_C32[h]], [r_C32[h]])
                        A("act", lambda e, h=h: e.activation(out=Cbf[:, h, 0:257], in_=C32[:, h, :], func=AF.Copy), [r_C32[h]], [r_Cbf[h]])

                def rec_den(ctx):
                    j, jc, bNs, bDn, rDn, bUs = ctx
                    c4 = slice(j * 4, (j + 1) * 4)
                    A("dve", lambda e: e.tensor_scalar(out=dd[:, 0, :], in0=bDn[:, 0:4], scalar1=-1.0, scalar2=None, op0=ALU.mult), [rDn], [r_dd[0]])
                    A("dve", lambda e: e.tensor_tensor(out=dd[:, 1, :], in0=bDn[:, 0:4], in1=dd[:, 0, :], op=ALU.max), [rDn, r_dd[0]], [r_dd[0]])
                    A("dve", lambda e: e.tensor_tensor(out=dd[:, 2, :], in0=dd[:, 1, :], in1=thr[:, c4], op=ALU.max), [r_dd[0], r_thr], [r_dd[0]])
                    A("dve", lambda e: e.reciprocal(out=dd[:, 3, :], in_=dd[:, 2, :]), [r_dd[0]], [r_dd[0]])
                    for h in range(4):
                        bN, rN = bNs[h // 2]
                        nc_ = slice((h % 2) * 256, (h % 2) * 256 + 256)
                        A("act", lambda e, h=h, bN=bN, nc_=nc_: e.activation(out=tmpN[:, h * 256:(h + 1) * 256], in_=bN[:, nc_], func=AF.Copy, scale=dd[:, 3, h:h + 1]),
                          [rN, r_dd[0]], [r_tmpN])

                def rec_rest(ctx):
                    j, jc, bNs, bDn, rDn, bUs = ctx
                    A("dve", lambda e: e.scalar_tensor_tensor(out=ho[:], in0=th[:, j, :], scalar=1.0, in1=tmpN[:], op0=ALU.add, op1=ALU.mult),
                      [r_th, r_tmpN], [r_ho])
                    for h in range(4):
                        A("dve", lambda e, h=h: e.bn_stats(out=st6[:, h, :], in_=ho[:, h * 256:(h + 1) * 256]), [r_ho], [r_st6])
                    for h in range(4):
                        A("dve", lambda e, h=h: e.bn_aggr(out=mv[:, h, :], in_=st6[:, h, :]), [r_st6], [r_mv])
                    A("pool", lambda e: e.tensor_scalar(out=lnv[:, 0:4], in0=mv[:, :, 1], scalar1=4e-6, scalar2=None, op0=ALU.add), [r_mv], [r_lnv])
                    A("pool", lambda e: e.tensor_tensor(out=lnv[:, 4:8], in0=lnv[:, 0:4], in1=mhalf[:, 0:4], op=ALU.pow), [r_lnv, r_mhalf], [r_lnv])
                    for h in range(4):
                        A("dve", lambda e, h=h: e.tensor_scalar(out=ho[:, h * 256:(h + 1) * 256], in0=ho[:, h * 256:(h + 1) * 256], scalar1=mv[:, h, 0:1],
                                                              scalar2=lnv[:, 4 + h:5 + h], op0=ALU.subtract, op1=ALU.mult), [r_ho, r_mv, r_lnv], [r_ho])
                    hi = rr("hz")
                    A("pool", lambda e: e.tensor_tensor(out=hzs[hi][:], in0=ho[:], in1=zs[:, j, :], op=ALU.mult), [r_ho, r_zs], [r_hzs[hi]])
                    return (jc, hi)

                def rec_tail(t):
                    jc, hi = t
                    for k in range(8):
                        A("pe", lambda e, k=k: e.transpose(out=pst[:, k, :], in_=hzs[hi][:, k * 128:(k + 1) * 128], identity=ident_b[:]), [r_hzs[hi], r_ident], [r_pst])
                    A("act", lambda e: e.activation(out=hzT[:, :, jc], in_=pst[:], func=AF.Copy), [r_pst], [r_hzT])

                tail = None
                jc_next = rec_pe_a(0)
                for j in range(4):
                    ctx = rec_pe_b(j, jc_next)
                    bctr[0] += 3
                    rec_den(ctx)
                    rec_state(ctx)
                    if j + 1 < 4:
                        jc_next = rec_pe_a(j + 1)
                    if tail is not None:
                        rec_tail(tail)
                    tail = rec_rest(ctx)
                    for blk in (2 * j, 2 * j + 1):
                        i_, c_ = divmod(blk, 4)
                        if c_ == 0:
                            wgm[i_] = load_w(WB["GM"] + i_)
                        b, rb = proj_fm(wgm[i_], c_ * 128, uT, r_uT, T)
                        tg_, r_tg_ = tgm[blk]
                        A("act", lambda e, b=b, tg_=tg_: e.activation(out=tg_, in_=b[:], func=AF.Tanh, scale=0.5), [rb], [r_tg_])
                rec_tail(tail)
                mT, r_mT = bufC, r_bufC
                for i in range(2):
                    wm = load_w(WB["MOUT"] + i)
                    for c in range(4):
                        blk = 4 * i + c
                        tg_, r_tg_ = tgm[blk]
                        k = rr("tz")
                        by, rby = proj_fm(wm, c * 128, hzT, r_hzT, T)
                        A("dve", lambda e, by=by, k=k, tg_=tg_: e.scalar_tensor_tensor(out=zst[k][:], in0=tg_, scalar=1.0, in1=by[:], op0=ALU.add, op1=ALU.mult),
                          [r_tg_, rby], [r_zst[k]])
                        A("pool", lambda e, k=k, blk=blk: e.tensor_tensor(out=mT[:, blk, :], in0=zst[k][:], in1=mrg[:, blk, :], op=ALU.add),
                          [r_zst[k], r_mrg], [r_mT])
                w0 = load_w(WB["OUT"])
                w1 = load_w(WB["OUT"] + 1)
                if nxt is not None:
                    wk_pref[0] = load_w(WB["K"])
                xr = [(tmpN, r_tmpN), (ho, r_ho)]

                def reload(j):
                    xt_, rx_ = xr[j % 2]
                    t0_ = tok0 + j * 128
                    A("sp", lambda e, s=s: e.dma_start(out=xt_[:], in_=x[s, t0_: t0_ + 128, :]), writes=[rx_], dma="xr%d" % (j % 2))

                reload(0)
                reload(1)
                fronts = {}
                if nxt is not None:
                    fronts[0] = phase0_front(0)
                    fronts[1] = phase0_front(1)
                for j in range(4):
                    oi = rr("ot")
                    b0, rb0 = proj_tm(w0, 512, mT, r_mT, j * 128, 128)
                    b1, rb1 = proj_tm(w1, 512, mT, r_mT, j * 128, 128)
                    A("act", lambda e, b0=b0, oi=oi: e.activation(out=hz[:, 0:512], in_=b0[:], func=AF.Square, accum_out=ss[oi][:, 0:1]), [rb0], [r_hz, r_ss[oi]])
                    A("act", lambda e, b1=b1, oi=oi: e.activation(out=hz[:, 512:1024], in_=b1[:], func=AF.Square, accum_out=ss[oi][:, 1:2]), [rb1], [r_hz, r_ss[oi]])
                    A("dve", lambda e, oi=oi: e.tensor_tensor(out=ss[oi][:, 2:3], in0=ss[oi][:, 0:1], in1=ss[oi][:, 1:2], op=ALU.add), [r_ss[oi]], [r_ss[oi]])
                    A("pool", lambda e, oi=oi: e.tensor_scalar(out=ss[oi][:, 3:4], in0=ss[oi][:, 2:3], scalar1=1024 * 16e-6, scalar2=None, op0=ALU.add), [r_ss[oi]], [r_ss[oi]])
                    A("pool", lambda e, oi=oi: e.tensor_tensor(out=ss[oi][:, 4:5], in0=ss[oi][:, 3:4], in1=mhalf[:, 0:1], op=ALU.pow), [r_ss[oi], r_mhalf], [r_ss[oi]])
                    A("dve", lambda e, b0=b0, oi=oi: e.scalar_tensor_tensor(out=ot[oi][:, 0:512], in0=b0[:], scalar=ss[oi][:, 4:5], in1=npB[:, 0:512],
                                                                          op0=ALU.mult, op1=ALU.mult), [rb0, r_ss[oi], r_npB], [r_ot[oi]])
                    A("dve", lambda e, b1=b1, oi=oi: e.scalar_tensor_tensor(out=ot[oi][:, 512:1024], in0=b1[:], scalar=ss[oi][:, 4:5], in1=npB[:, 512:1024],
                                                                          op0=ALU.mult, op1=ALU.mult), [rb1, r_ss[oi], r_npB], [r_ot[oi]])
                    xt_, rx_ = xr[j % 2]
                    A("pool", lambda e, oi=oi, xt_=xt_: e.tensor_tensor(out=ot[oi][:], in0=ot[oi][:], in1=xt_[:], op=ALU.add), [r_ot[oi], rx_], [r_ot[oi]])
                    A("sp", lambda e, oi=oi, j=j, s=s, tok0=tok0: e.dma_start(out=out[s, tok0 + j * 128: tok0 + (j + 1) * 128, :], in_=ot[oi][:]),
                      reads=[r_ot[oi]], writes=[new_out()], dma="out%d" % oi)
                    if j + 2 < 4:
                        reload(j + 2)
                    if nxt is not None and j < 2:
                        phase0_back(2 * j, fronts[2 * j])
                        phase0_back(2 * j + 1, fronts[2 * j + 1])
                        if j == 0:
                            fronts[2] = phase0_front(2)
                            fronts[3] = phase0_front(3)
                A("pool", lambda e: e.tensor_copy(out=Kbuf[:, :, 0:128], in_=Kbuf[:, :, 512:640]), [r_Kbuf], [r_Kbuf])
                A("pool", lambda e: e.tensor_copy(out=Vdup[0][:], in_=Vdup[4][:]), [r_Vdup[4]], [r_Vdup[0]])
        fin = P.res("fin")
        A("sp", lambda e: e.nop(), reads=r_outs, writes=[fin])
        P.emit()
    return nc


def _consts(NST):
    ident = np.eye(128, dtype=np.float32)
    triu = np.triu(np.ones((128, 128), dtype=np.float32))
    pit = np.zeros((128, 128), dtype=np.float32)
    for m in range(128):
        d = m % 64
        base = m - d
        if d < 8:
            k = base + d + 8
        elif d < 16:
            k = base + d - 8
        else:
            k = m
        pit[k, m] = 1.0
    half = 8
    inv_freq = (500000.0 ** (-np.arange(0, 16, 2, dtype=np.float32) / 16)).astype(np.float32)

    def tables(pos):
        pos = pos.astype(np.float32)
        ang = (pos[:, None] * inv_freq[None, :]).astype(np.float32)
        c = np.cos(ang.astype(np.float64)).astype(np.float32)
        s_ = np.sin(ang.astype(np.float64)).astype(np.float32)
        n = pos.shape[0]
        cosT = np.ones((128, n), dtype=np.float32)
        sinT = np.zeros((128, n), dtype=np.float32)
        for hb in (0, 64):
            cosT[hb:hb + 8] = c.T
            cosT[hb + 8:hb + 16] = c.T
            sinT[hb:hb + 8] = -s_.T
            sinT[hb + 8:hb + 16] = s_.T
        return cosT, sinT

    cos = np.zeros((NST, 128, T), dtype=np.float32)
    sin = np.zeros((NST, 128, T), dtype=np.float32)
    for ti in range(NST):
        cos[ti], sin[ti] = tables(16 + ti * T + np.arange(T))
    cosm, sinm = tables(np.arange(16))
    return dict(c_ident=ident, c_triu=triu, c_pit=pit, c_cos=cos, c_sin=sin, c_cosm=cosm, c_sinm=sinm)


_NC_CACHE = {}


def run(inputs, n_cores, nseq, nst):
    key = (nseq, nst)
    if key not in _NC_CACHE:
        _NC_CACHE[key] = build_nc(nseq, nst)
    nc = _NC_CACHE[key]
    cs = _consts(nst)
    f = lambda a: np.ascontiguousarray(np.asarray(a, dtype=np.float32))
    shared = {
        "meta_tokens": f(inputs["meta_tokens"]),
        "norm_pre": f(inputs["norm_pre"][0]),
        "w_in": f(inputs["w_in"][0]),
        "attn_sinks": f(inputs["attn_sinks"][0]),
        "conv_w": f(inputs["conv_w"][0]),
        "conv_b": f(inputs["conv_b"][0]),
        "mlstm_gate_bias": f(inputs["mlstm_gate_bias"][0]),
        "mlstm_head_norm": f(inputs["mlstm_head_norm"][0]),
        "w_attn_out": f(inputs["w_attn_out"][0]),
        "w_mlstm_out": f(inputs["w_mlstm_out"][0]),
        "w_out": f(inputs["w_out"][0]),
        "norm_post": f(inputs["norm_post"][0]),
    }
    shared.update(cs)
    xfull = np.asarray(inputs["x"], dtype=np.float32)
    in_maps = []
    for c in range(n_cores):
        m = dict(shared)
        m["x"] = np.ascontiguousarray(xfull[c * nseq:(c + 1) * nseq, :nst * T])
        in_maps.append(m)
    res = run_bass_kernel_spmd(nc, in_maps, core_ids=list(range(n_cores)))
    return np.concatenate([np.asarray(r["out"]) for r in res.results], axis=0)


def kernel(**inputs):
    return run(inputs, 8, 2, 8).astype(np.float32)
```

```python
import contextlib
import math
import numpy as np
import concourse.bass as bass
import concourse.mybir as mybir
from concourse.bass_utils import run_bass_kernel_spmd

F32 = mybir.dt.float32
BF16 = mybir.dt.bfloat16
ALU = mybir.AluOpType
AF = mybir.ActivationFunctionType

ENGS = ("pe", "act", "dve", "pool", "sp")


class Res:
    __slots__ = ("name", "w", "readers")

    def __init__(self, name):
        self.name = name
        self.w = None
        self.readers = []


class Op:
    __slots__ = ("eng", "fn", "semkey", "inc", "deps", "signal", "idx", "is_dma")

    def __init__(self, eng, fn, semkey, inc, is_dma):
        self.eng = eng
        self.fn = fn
        self.semkey = semkey
        self.inc = inc
        self.deps = []
        self.signal = False
        self.idx = 0
        self.is_dma = is_dma


class Prog:
    def __init__(self, nc):
        self.nc = nc
        self.streams = {e: [] for e in ENGS}
        self.by_sem = {}

    def res(self, name):
        return Res(name)

    def op(self, eng, fn, reads=(), writes=(), dma=None):
        is_dma = dma is not None
        semkey = ("dma", dma) if is_dma else eng
        o = Op(eng, fn, semkey, 16 if is_dma else 1, is_dma)
        if is_dma:
            o.signal = True
        deps = {}

        def need(p, kind):
            if p is None:
                return
            if (not p.is_dma) and (not is_dma) and p.eng == eng:
                if eng == "pe":
                    return
            deps[id(p)] = p

        for r in reads:
            need(r.w, "raw")
        for w in writes:
            need(w.w, "waw")
            for t in w.readers:
                need(t, "war")
        o.deps = list(deps.values())
        for p in o.deps:
            p.signal = True
        for w in writes:
            w.w = o
            w.readers = []
        for r in reads:
            if not is_dma:
                r.readers = [t for t in r.readers if t.is_dma or t.eng != eng]
            r.readers.append(o)
        self.streams[eng].append(o)
        self.by_sem.setdefault(semkey, []).append(o)
        return o

    def emit(self):
        nc = self.nc
        for k, ops in self.by_sem.items():
            for i, o in enumerate(ops):
                o.idx = i + 1
        used = set()
        plan = {}
        for eng_name in ENGS:
            waited = {}
            for o in self.streams[eng_name]:
                need = {}
                for p in o.deps:
                    if p.idx > need.get(p.semkey, 0):
                        need[p.semkey] = p.idx
                w = []
                for k, v in need.items():
                    if waited.get(k, 0) >= v:
                        continue
                    w.append((k, v))
                    waited[k] = v
                    used.add((k, v))
                plan[id(o)] = w
        pos2idx = {}
        for k, ops in self.by_sem.items():
            c = 0
            m = {}
            for o in ops:
                o.signal = o.is_dma or ((k, o.idx) in used)
                if o.signal:
                    c += o.inc
                m[o.idx] = c
            pos2idx[k] = m
        self.n_signals = sum(1 for ops in self.by_sem.values() for o in ops if o.signal)
        self.n_waits = sum(len(w) for w in plan.values())
        with contextlib.ExitStack() as st:
            sems = {}
            for k in self.by_sem:
                nm = "s_" + (k if isinstance(k, str) else "d_" + str(k[1]))
                sems[k] = st.enter_context(nc.semaphore(nm))
            block = st.enter_context(nc.Block())
            prog = self

            def run(eng_name, handle):
                waited = {}
                for o in prog.streams[eng_name]:
                    for k, v in plan[id(o)]:
                        handle.wait_ge(sems[k], pos2idx[k][v])
                    ins = o.fn(handle)
                    if o.signal:
                        ins.then_inc(sems[o.semkey], o.inc)

            @block.tensor
            def _(e):
                run("pe", e)

            @block.scalar
            def _(e):
                run("act", e)

            @block.vector
            def _(e):
                run("dve", e)

            @block.gpsimd
            def _(e):
                run("pool", e)

            @block.sync
            def _(e):
                run("sp", e)


C_AQ, C_AK, C_AV, C_AZ = 0, 1024, 1280, 1536
C_MQK, C_MV, C_MI, C_MF, C_MO, C_MZ, C_GA, C_GM = 2560, 3584, 4608, 4612, 4616, 5640, 6664, 7688
IN_W = 8712
T = 512
LOGK = -0.5 * math.log(128.0)


def build_nc(NSEQ, NST):
    nc = bass.Bass("TRN2", target_bir_lowering=False)
    S = NST * T

    def din(name, shape, dt=F32):
        return nc.dram_tensor(name, shape, dt, kind="ExternalInput").ap()

    x = din("x", [NSEQ, S, 1024])
    meta = din("meta_tokens", [16, 1024])
    norm_pre = din("norm_pre", [1024])
    w_in = din("w_in", [1024, IN_W])
    sinks = din("attn_sinks", [16])
    conv_w = din("conv_w", [4, 1024])
    conv_b = din("conv_b", [1024])
    gate_bias = din("mlstm_gate_bias", [8])
    head_norm = din("mlstm_head_norm", [1024])
    w_ao = din("w_attn_out", [1024, 1024])
    w_mo = din("w_mlstm_out", [1024, 1024])
    w_out = din("w_out", [1024, 1024])
    norm_post = din("norm_post", [1024])
    c_ident = din("c_ident", [128, 128])
    c_triu = din("c_triu", [128, 128])
    c_pit = din("c_pit", [128, 128])
    c_cos = din("c_cos", [NST, 128, T])
    c_sin = din("c_sin", [NST, 128, T])
    c_cosm = din("c_cosm", [128, 16])
    c_sinm = din("c_sinm", [128, 16])
    out = nc.dram_tensor("out", [NSEQ, S, 1024], F32, kind="ExternalOutput").ap()

    def dscr(name, shape):
        return nc.dram_tensor(name, shape, BF16, kind="Internal").ap()

    NWB = 24
    wbt = dscr("wbt", [NWB, 128, 8, 512])
    WB = {"K": 0, "V": 1, "Q": 2, "Z": 4, "GA": 6, "MQK": 8, "MV": 10, "MO": 12, "MZ": 14, "GM": 16, "AO": 18, "MOUT": 20, "OUT": 22}

    P = Prog(nc)
    with contextlib.ExitStack() as st:
        def sb(name, shape, dt=F32):
            return st.enter_context(nc.sbuf_tensor(name, shape, dt))

        NB = 7
        banks = [st.enter_context(nc.psum_tensor("pb%d" % i, [128, 512], F32)) for i in range(NB)]
        bank_res = [P.res("pb%d" % i) for i in range(NB)]
        pst = st.enter_context(nc.psum_tensor("pst", [128, 8, 128], BF16))
        r_pst = P.res("pst")
        bctr = [0]

        def bank():
            i = bctr[0] % NB
            bctr[0] += 1
            return banks[i], bank_res[i]

        ident_f = sb("ident_f", [128, 128]); r_identf = P.res("identf")
        ident_b = sb("ident_b", [128, 128], BF16); r_ident = P.res("ident")
        triu_f = sb("triu_f", [128, 128]); r_triu = P.res("triu")
        pit_f = sb("pit_f", [128, 128]); r_pitf = P.res("pitf")
        pit_b = sb("pit_b", [128, 128], BF16); r_pit = P.res("pit")
        ones_f = sb("ones_f", [128, 128]); r_onesf = P.res("onesf")
        ones_b = sb("ones_b", [128, 128], BF16); r_onesb = P.res("onesb")
        mask_b = sb("mask_b", [128, 4, 128], BF16); r_mask = P.res("mask")
        maskp_b = sb("maskp_b", [128, 4, 128], BF16); r_maskp = P.res("maskp")
        gB = sb("gB", [128, 1024]); r_gB = P.res("gB")
        hnB = sb("hnB", [128, 1024]); r_hnB = P.res("hnB")
        npB = sb("npB", [128, 1024]); r_npB = P.res("npB")
        gbias = sb("gbias", [128, 8]); r_gbias = P.res("gbias")
        cw = sb("cw", [128, 8, 4]); r_cw = P.res("cw")
        cb = sb("cb", [128, 8]); r_cb = P.res("cb")
        sk = sb("sk", [33, 16]); r_sk = P.res("sk")
        wgate = sb("wgate", [128, 8, 8], BF16); r_wgate = P.res("wgate")
        cosT = sb("cosT", [128, T]); r_cos = P.res("cos")
        sinT = sb("sinT", [128, T]); r_sin = P.res("sin")
        cosm = sb("cosm", [128, 16]); r_cosm = P.res("cosm")
        sinm = sb("sinm", [128, 16]); r_sinm = P.res("sinm")

        NSLOT = 3
        wring = [sb("wring%d" % i, [128, 8, 512], BF16) for i in range(NSLOT)]
        r_wring = [P.res("wring%d" % i) for i in range(NSLOT)]
        wctr = [0]

        xs = [sb("xs%d" % j, [128, 1024]) for j in range(4)]
        r_xs = [P.res("xs%d" % j) for j in range(4)]
        xbs = [sb("xb%d" % i, [128, 1024], BF16) for i in range(2)]
        r_xbs = [P.res("xb%d" % i) for i in range(2)]
        mhalf = sb("mhalf", [128, 8]); r_mhalf = P.res("mhalf")
        stt = [sb("stt%d" % j, [128, 4]) for j in range(4)]
        r_stt = [P.res("stt%d" % j) for j in range(4)]
        uT = sb("uT", [128, 8, T], BF16); r_uT = P.res("uT")
        bufA = sb("bufA", [128, 8, T], BF16); r_bufA = P.res("bufA")
        bufB = sb("bufB", [128, 8, T], BF16); r_bufB = P.res("bufB")
        bufC = sb("bufC", [128, 8, T], BF16); r_bufC = P.res("bufC")
        hzT = sb("hzT", [128, 8, T], BF16); r_hzT = P.res("hzT")
        mrg = sb("mrg", [128, 8, T], BF16); r_mrg = P.res("mrg")
        Kbuf = sb("Kbuf", [128, 4, 640], BF16); r_Kbuf = P.res("Kbuf")
        KmT = sb("KmT", [128, 4, 16], BF16); r_KmT = P.res("KmT")
        Vdup = [sb("Vaug%d" % i, [128, 4, 66], BF16) for i in range(5)]
        r_Vdup = [P.res("Vaug%d" % i) for i in range(5)]
        Vmeta = sb("Vmeta", [33, 4, 66], BF16); r_Vmeta = P.res("Vmeta")
        att_tok = [sb("att_tok%d" % i, [128, 1024], BF16) for i in range(2)]
        r_att_tok = [P.res("att_tok%d" % i) for i in range(2)]
        rec4 = [sb("rec4_%d" % i, [128, 4]) for i in range(2)]
        r_rec4 = [P.res("rec4_%d" % i) for i in range(2)]
        ptc = [sb("ptc%d" % i, [128, 512], BF16) for i in range(2)]
        r_ptc = [P.res("ptc%d" % i) for i in range(2)]
        ptp = [sb("ptp%d" % i, [128, 512], BF16) for i in range(2)]
        r_ptp = [P.res("ptp%d" % i) for i in range(2)]
        ptm = [sb("ptm%d" % g, [33, 512], BF16) for g in range(4)]
        r_ptm = [P.res("ptm%d" % g) for g in range(4)]
        raw = [sb("raw%d" % i, [128, 512], BF16) for i in range(2)]
        r_raw = [P.res("raw%d" % i) for i in range(2)]
        t1 = [sb("t1_%d" % i, [128, 512]) for i in range(2)]
        r_t1 = [P.res("t1_%d" % i) for i in range(2)]
        t2 = [sb("t2_%d" % i, [128, 512]) for i in range(2)]
        r_t2 = [P.res("t2_%d" % i) for i in range(2)]
        tzt = [sb("tzt%d" % i, [128, 512], BF16) for i in range(2)]
        r_tzt = [P.res("tzt%d" % i) for i in range(2)]
        zst = [sb("zst%d" % i, [128, 512], BF16) for i in range(2)]
        r_zst = [P.res("zst%d" % i) for i in range(2)]
        pre = [sb("pre%d" % i, [128, 515]) for i in range(2)]
        r_pre = [P.res("pre%d" % i) for i in range(2)]
        acc = [sb("acc%d" % i, [128, 512]) for i in range(2)]
        r_acc = [P.res("acc%d" % i) for i in range(2)]
        halo = sb("halo", [128, 8, 3]); r_halo = P.res("halo")
        halo_m = sb("halo_m", [128, 8, 3]); r_halom = P.res("halom")
        gsb = sb("gsb", [128, 4, 8]); r_gsb = P.res("gsb")
        e1 = sb("e1", [128, 4, 4]); r_e1 = P.res("e1")
        nlf = sb("nlf", [128, 16]); r_nlf = P.res("nlf")
        thr = sb("thr", [128, 16]); r_thr = P.res("thr")
        gs = sb("gs", [128, 16]); r_gs = P.res("gs")
        vsc = sb("vsc", [128, 16]); r_vsc = P.res("vsc")
        eB = sb("eB", [128, 16]); r_eB = P.res("eB")
        vaug = [sb("vaug%d" % j, [128, 4, 272], BF16) for j in range(4)]
        r_vaug = [P.res("vaug%d" % j) for j in range(4)]
        ktok = sb("ktok", [128, 4, 128], BF16); r_ktok = P.res("ktok")
        PTm = sb("PTm", [128, 512], BF16); r_PTm = P.res("PTm")
        C32 = sb("C32", [128, 4, 257]); r_C32 = [P.res("C32_%d" % h) for h in range(4)]
        Cbf = sb("Cbf", [128, 4, 272], BF16); r_Cbf = [P.res("Cbf_%d" % h) for h in range(4)]
        Cm32 = sb("Cm32", [128, 4, 257]); r_Cm32 = P.res("Cm32")
        Ue = [sb("Ue%d" % i, [128, 257]) for i in range(2)]
        r_Ue = [P.res("Ue%d" % i) for i in range(2)]
        dd = sb("dd", [128, 4, 4]); r_dd = [P.res("dd%d" % h) for h in range(4)]
        tmpN = sb("tmpN", [128, 1024]); r_tmpN = P.res("tmpN")
        ho = sb("ho", [128, 1024]); r_ho = P.res("ho")
        st6 = sb("st6", [128, 4, 6]); r_st6 = P.res("st6")
        mv = sb("mv", [128, 4, 2]); r_mv = P.res("mv")
        lnv = sb("lnv", [128, 8]); r_lnv = P.res("lnv")
        hzs = [sb("hz%d" % i, [128, 1024], BF16) for i in range(2)]
        r_hzs = [P.res("hz%d" % i) for i in range(2)]
        hz, r_hz = hzs[0], r_hzs[0]
        ot = [sb("ot%d" % i, [128, 1024]) for i in range(2)]
        r_ot = [P.res("ot%d" % i) for i in range(2)]
        ss = [sb("ss%d" % i, [128, 8]) for i in range(2)]
        r_ss = [P.res("ss%d" % i) for i in range(2)]
        r_outs = []

        def new_out():
            r = P.res("out%d" % len(r_outs))
            r_outs.append(r)
            return r

        ctr = {"raw": 0, "tz": 0, "pre": 0, "pt": 0, "ot": 0, "xb": 0, "hz": 0, "rec4": 0, "ue": 0}

        def rr(key, n=2):
            i = ctr[key] % n
            ctr[key] += 1
            return i

        A = P.op

        r_castb = {}

        def cast_blk(bi, src, c0, n):
            r = P.res("cast%d" % bi)
            A("pool", lambda e: e.dma_start(out=wbt[bi][:, :, 0:n], in_=src[:, c0:c0 + n].rearrange("(k p) n -> p k n", p=128)),
              writes=[r], dma="cast%d" % bi)
            r_castb[bi] = [r]

        rk = []
        for g in range(4):
            for d in range(2):
                r = P.res("castK%d%d" % (g, d))
                A("pool", lambda e, g=g, d=d: e.dma_start(out=wbt[0][:, :, g * 128 + d * 64: g * 128 + d * 64 + 64],
                                                      in_=w_in[:, C_AK + g * 64: C_AK + g * 64 + 64].rearrange("(k p) n -> p k n", p=128)),
                  writes=[r], dma="castK")
                rk.append(r)
        r_castb[0] = rk
        cast_blk(1, w_in, C_AV, 256)
        for nm, c0 in (("Q", C_AQ), ("Z", C_AZ), ("GA", C_GA)):
            for i in range(2):
                cast_blk(WB[nm] + i, w_in, c0 + i * 512, 512)
        for i in range(2):
            cast_blk(WB["AO"] + i, w_ao, i * 512, 512)
        r_wgate_c = P.res("wgate_c")
        for nm, c0 in (("MQK", C_MQK), ("MV", C_MV), ("MO", C_MO), ("MZ", C_MZ), ("GM", C_GM)):
            for i in range(2):
                cast_blk(WB[nm] + i, w_in, c0 + i * 512, 512)
        for i in range(2):
            cast_blk(WB["MOUT"] + i, w_mo, i * 512, 512)
        for i in range(2):
            cast_blk(WB["OUT"] + i, w_out, i * 512, 512)

        def ld(dst, src, r, name, **kw):
            A("sp", lambda e: e.dma_start(out=dst, in_=src, **kw), writes=[r], dma=name)

        ld(ident_f[:], c_ident, r_identf, "identf")
        ld(triu_f[:], c_triu, r_triu, "triu")
        ld(pit_f[:], c_pit, r_pitf, "pitf")
        ld(gB[:], norm_pre.partition_broadcast(128), r_gB, "gB")
        ld(hnB[:], head_norm.partition_broadcast(128), r_hnB, "hnB")
        ld(npB[:], norm_post.partition_broadcast(128), r_npB, "npB")
        ld(gbias[:], gate_bias.partition_broadcast(128), r_gbias, "gbias")
        for jj in range(4):
            ld(cw[:, :, jj], conv_w[jj].rearrange("(b p) -> p b", p=128), r_cw, "cw", allow_slow_non_contiguous=True)
        ld(cb[:], conv_b.rearrange("(b p) -> p b", p=128), r_cb, "cb", allow_slow_non_contiguous=True)
        ld(sk[32:33, :], sinks.rearrange("(o n) -> o n", o=1), r_sk, "sk")
        ld(cosm[:], c_cosm, r_cosm, "cosm")
        ld(sinm[:], c_sinm, r_sinm, "sinm")
        A("pool", lambda e: e.dma_start(out=wgate[:], in_=w_in[:, C_MI:C_MI + 8].rearrange("(k p) n -> p k n", p=128),
                                        allow_slow_non_contiguous=True), writes=[r_wgate], dma="wgate")
        A("dve", lambda e: e.tensor_copy(out=ident_b[:], in_=ident_f[:]), [r_identf], [r_ident])
        A("dve", lambda e: e.tensor_copy(out=pit_b[:], in_=pit_f[:]), [r_pitf], [r_pit])
        A("dve", lambda e: e.memset(ones_f[:], 1.0), [], [r_onesf])
        A("dve", lambda e: e.memset(ones_b[:], 1.0), [], [r_onesb])
        A("dve", lambda e: e.tensor_copy(out=mask_b[:], in_=triu_f[:].unsqueeze(1).to_broadcast([128, 4, 128])), [r_triu], [r_mask])
        A("dve", lambda e: e.tensor_scalar(out=maskp_b[:], in0=triu_f[:].unsqueeze(1).to_broadcast([128, 4, 128]),
                                           scalar1=-1.0, scalar2=1.0, op0=ALU.mult, op1=ALU.add), [r_triu], [r_maskp])
        A("dve", lambda e: e.tensor_scalar(out=cw[:], in0=cw[:], scalar1=0.5, scalar2=None, op0=ALU.mult), [r_cw], [r_cw])
        A("dve", lambda e: e.tensor_scalar(out=cb[:], in0=cb[:], scalar1=0.5, scalar2=None, op0=ALU.mult), [r_cb], [r_cb])
        A("pool", lambda e: e.memset(Vmeta[:], 0.0), [], [r_Vmeta])
        A("pool", lambda e: e.memset(Vmeta[0:16, :, 64:65], 1.0), [], [r_Vmeta])
        A("pool", lambda e: e.memset(Vmeta[32:33, :, 64:65], 1.0), [], [r_Vmeta])
        for i in range(5):
            A("pool", lambda e, i=i: e.memset(Vdup[i][:], 0.0), [], [r_Vdup[i]])
            A("pool", lambda e, i=i: e.memset(Vdup[i][:, :, 64:65], 1.0), [], [r_Vdup[i]])
        A("pool", lambda e: e.memset(mhalf[:], -0.5), [], [r_mhalf])
        A("dve", lambda e: e.tensor_scalar(out=gB[:], in0=gB[:], scalar1=32.0, scalar2=None, op0=ALU.mult), [r_gB], [r_gB])
        A("dve", lambda e: e.tensor_scalar(out=npB[:], in0=npB[:], scalar1=32.0, scalar2=None, op0=ALU.mult), [r_npB], [r_npB])
        for g in range(4):
            A("pool", lambda e, g=g: e.memset(ptm[g][:], 0.0), [], [r_ptm[g]])
            A("act", lambda e, g=g: e.activation(out=ptm[g][32:33, :].rearrange("p (h q) -> p h q", h=4),
                                                 in_=sk[32:33, 4 * g:4 * g + 4].unsqueeze(2).to_broadcast([1, 4, 128]),
                                                 func=AF.Exp), [r_sk, r_ptm[g]], [r_ptm[g]])
        for j in range(4):
            A("pool", lambda e, j=j: e.memset(vaug[j][:], 0.0), [], [r_vaug[j]])
        A("pool", lambda e: e.memset(Cbf[:], 0.0), [], r_Cbf)
        A("pool", lambda e: e.memset(halo_m[:], 0.0), [], [r_halom])

        def load_w(bi, n=512):
            i = wctr[0] % NSLOT
            wctr[0] += 1
            A("sp", lambda e: e.dma_start(out=wring[i][:, :, 0:n], in_=wbt[bi][:, :, 0:n]),
              reads=r_castb[bi], writes=[r_wring[i]], dma="w%d" % i)
            return wring[i], r_wring[i]

        def proj_fm(w, c0, rhs, r_rhs, N, n0=0):
            wt, r_w = w
            b, rb = bank()
            for kc in range(8):
                A("pe", lambda e, kc=kc: e.matmul(b[:, 0:N], lhsT=wt[:, kc, c0:c0 + 128], rhs=rhs[:, kc, n0:n0 + N],
                                                 start=(kc == 0), stop=(kc == 7)), [r_w, r_rhs], [rb])
            return b, rb

        def proj_tm(w, n, lhs, r_lhs, t0, nt):
            wt, r_w = w
            b, rb = bank()
            for kc in range(8):
                A("pe", lambda e, kc=kc: e.matmul(b[0:nt, 0:n], lhsT=lhs[:, kc, t0:t0 + nt], rhs=wt[:, kc, 0:n],
                                                 start=(kc == 0), stop=(kc == 7)), [r_w, r_lhs], [rb])
            return b, rb

        pending = []

        def flush():
            while pending:
                pending.pop(0)()

        def rope(b, rb, N, cos_ap, sin_ap, r_tabs, scale, dst, r_dst):
            i = rr("raw")
            A("act", lambda e: e.activation(out=raw[i][:, 0:N], in_=b[:, 0:N], func=AF.Copy, scale=scale), [rb], [r_raw[i]])

            def stage_b():
                b2, rb2 = bank()
                A("pe", lambda e: e.matmul(b2[:, 0:N], lhsT=pit_b[:], rhs=raw[i][:, 0:N], start=True, stop=True), [r_pit, r_raw[i]], [rb2])
                A("dve", lambda e: e.tensor_tensor(out=t1[i][:, 0:N], in0=raw[i][:, 0:N], in1=cos_ap, op=ALU.mult), [r_raw[i]] + r_tabs, [r_t1[i]])
                A("dve", lambda e: e.tensor_tensor(out=t2[i][:, 0:N], in0=b2[:, 0:N], in1=sin_ap, op=ALU.mult), [rb2] + r_tabs, [r_t2[i]])
                A("pool", lambda e: e.tensor_tensor(out=dst, in0=t1[i][:, 0:N], in1=t2[i][:, 0:N], op=ALU.add), [r_t1[i], r_t2[i]], [r_dst])
            pending.append(stage_b)

        def rmsnorm_front(src, r_src, n, stat, r_stat):
            xi = rr("xb")
            xb_, r_xb_ = xbs[xi], r_xbs[xi]
            A("act", lambda e: e.activation(out=xb_[0:n, :], in_=src, func=AF.Square, accum_out=stat[0:n, 0:1]), [r_src], [r_xb_, r_stat])
            A("pool", lambda e: e.tensor_scalar(out=stat[0:n, 1:2], in0=stat[0:n, 0:1], scalar1=1024 * 1e-6, scalar2=None, op0=ALU.add), [r_stat], [r_stat])
            A("pool", lambda e: e.tensor_tensor(out=stat[0:n, 2:3], in0=stat[0:n, 1:2], in1=mhalf[0:n, 0:1], op=ALU.pow), [r_stat, r_mhalf], [r_stat])
            A("dve", lambda e: e.scalar_tensor_tensor(out=xb_[0:n, :], in0=src, scalar=stat[0:n, 2:3], in1=gB[0:n, :],
                                                      op0=ALU.mult, op1=ALU.mult), [r_src, r_stat, r_gB], [r_xb_])
            return xb_, r_xb_

        def rmsnorm_back(xbr, n, dstT, r_dstT, t0):
            xb_, r_xb_ = xbr
            for k in range(8):
                A("pe", lambda e, k=k: e.transpose(out=pst[:, k, 0:n], in_=xb_[0:n, k * 128:(k + 1) * 128], identity=ident_b[0:n, 0:n]),
                  [r_xb_, r_ident], [r_pst])
            A("act", lambda e: e.activation(out=dstT[:, :, t0:t0 + n], in_=pst[:, :, 0:n], func=AF.Copy), [r_pst], [r_dstT])

        def rmsnorm_T(src, r_src, n, stat, r_stat, dstT, r_dstT, t0):
            rmsnorm_back(rmsnorm_front(src, r_src, n, stat, r_stat), n, dstT, r_dstT, t0)

        def conv_silu(b, rb, blk, N, halo_src, r_halo_src, halo_dst, r_halo_dst, dst, r_dst):
            i = rr("pre")
            p_ = pre[i]
            A("act", lambda e: e.activation(out=p_[:, 3:3 + N], in_=b[:, 0:N], func=AF.Copy), [rb], [r_pre[i]])
            A("pool", lambda e: e.tensor_copy(out=p_[:, 0:3], in_=halo_src[:, blk, :]), [r_halo_src], [r_pre[i]])
            a_ = acc[i]
            A("act", lambda e: e.activation(out=a_[:, 0:N], in_=b[:, 0:N], func=AF.Identity, scale=cw[:, blk, 3:4], bias=cb[:, blk:blk + 1]),
              [rb, r_cw, r_cb], [r_acc[i]])
            for jj in (2, 1, 0):
                A("dve", lambda e, jj=jj: e.scalar_tensor_tensor(out=a_[:, 0:N], in0=p_[:, jj:jj + N], scalar=cw[:, blk, jj:jj + 1],
                                                                 in1=a_[:, 0:N], op0=ALU.mult, op1=ALU.add), [r_pre[i], r_cw, r_acc[i]], [r_acc[i]])
            A("pool", lambda e: e.tensor_copy(out=halo_dst[:, blk, :], in_=p_[:, N:N + 3]), [r_pre[i]], [r_halo_dst])
            A("act", lambda e: e.activation(out=dst, in_=a_[:, 0:N], func=AF.Silu, scale=2.0), [r_acc[i]], [r_dst])

        xm = ot[0]
        A("sp", lambda e: e.dma_start(out=xm[0:16, :], in_=meta), writes=[r_ot[0]], dma="xm")
        uTm = sb("uTm", [128, 8, 16], BF16); r_uTm = P.res("uTm")
        qkm = sb("qkm", [128, 8, 16], BF16); r_qkm = P.res("qkm")
        rmsnorm_T(xm[0:16, :], r_ot[0], 16, stt[0], r_stt[0], uTm, r_uTm, 0)
        wk = load_w(WB["K"])
        for g in range(4):
            b, rb = proj_fm(wk, g * 128, uTm, r_uTm, 16)
            flush()
            rope(b, rb, 16, cosm[:], sinm[:], [r_cosm, r_sinm], 1.0, KmT[:, g, :], r_KmT)
        flush()
        wv = load_w(WB["V"], 256)
        b, rb = proj_tm(wv, 256, uTm, r_uTm, 0, 16)
        A("act", lambda e, b=b: e.activation(out=Vmeta[0:16, :, 0:64], in_=b[0:16, 0:256].rearrange("p (g d) -> p g d", g=4), func=AF.Copy), [rb], [r_Vmeta])
        bgm_, rbgm = bank()
        for kc in range(8):
            A("pe", lambda e, kc=kc: e.matmul(bgm_[0:16, 0:8], lhsT=uTm[:, kc, 0:16], rhs=wgate[:, kc, :], start=(kc == 0), stop=(kc == 7)),
              [r_uTm, r_wgate], [rbgm])
        A("dve", lambda e: e.tensor_tensor(out=gsb[0:16, 0, :], in0=bgm_[0:16, 0:8], in1=gbias[0:16, :], op=ALU.add), [rbgm, r_gbias], [r_gsb])
        A("act", lambda e: e.activation(out=e1[0:16, 0, :], in_=gsb[0:16, 0, 4:8], func=AF.Exp, scale=-1.0), [r_gsb], [r_e1])
        A("act", lambda e: e.activation(out=nlf[0:16, 0:4], in_=e1[0:16, 0, :], func=AF.Ln, bias=1.0), [r_e1], [r_nlf])
        bnbm, rbnbm = bank()
        A("pe", lambda e: e.matmul(bnbm[0:16, 0:4], lhsT=triu_f[0:16, 0:16], rhs=nlf[0:16, 0:4], start=True, stop=True), [r_triu, r_nlf], [rbnbm])
        bnsm, rbnsm = bank()
        A("pe", lambda e: e.matmul(bnsm[:, 0:4], lhsT=ones_f[0:16, :], rhs=nlf[0:16, 0:4], start=True, stop=True), [r_onesf, r_nlf], [rbnsm])
        A("dve", lambda e: e.tensor_tensor(out=gs[0:16, 0:4], in0=gsb[0:16, 0, 0:4], in1=bnbm[0:16, 0:4], op=ALU.add), [r_gsb, rbnbm], [r_gs])
        A("act", lambda e: e.activation(out=vsc[0:16, 0:4], in_=gs[0:16, 0:4], func=AF.Exp, bias=LOGK), [r_gs], [r_vsc])
        A("act", lambda e: e.activation(out=eB[:, 0:4], in_=bnsm[:, 0:4], func=AF.Exp, scale=-1.0), [rbnsm], [r_eB])
        for i in range(2):
            w = load_w(WB["MQK"] + i)
            for c in range(4):
                blk = 4 * i + c
                b, rb = proj_fm(w, c * 128, uTm, r_uTm, 16)
                conv_silu(b, rb, blk, 16, halo_m, r_halom, halo_m, r_halom, qkm[:, blk, :], r_qkm)
        vaugm = vaug[0]
        for i in range(2):
            w = load_w(WB["MV"] + i)
            b, rb = proj_tm(w, 512, uTm, r_uTm, 0, 16)
            for hh in range(2):
                h = 2 * i + hh
                A("act", lambda e, b=b, hh=hh, h=h: e.activation(out=vaugm[0:16, h, 0:256], in_=b[0:16, hh * 256:(hh + 1) * 256], func=AF.Copy,
                                                              scale=vsc[0:16, h:h + 1]), [rb, r_vsc], [r_vaug[0]])
        A("pool", lambda e: e.tensor_copy(out=vaugm[0:16, :, 256:257], in_=vsc[0:16, 0:4].unsqueeze(2)), [r_vsc], [r_vaug[0]])
        for h in range(4):
            A("pe", lambda e, h=h: e.transpose(out=pst[0:16, h, :], in_=qkm[:, 4 + h, :], identity=ident_b[:]), [r_qkm, r_ident], [r_pst])
        A("act", lambda e: e.activation(out=ktok[0:16, :, :], in_=pst[0:16, 0:4, :], func=AF.Copy), [r_pst], [r_ktok])
        for h in range(4):
            bU, rbU = bank()
            A("pe", lambda e, h=h, bU=bU: e.matmul(bU[:, 0:257], lhsT=ktok[0:16, h, :], rhs=vaugm[0:16, h, 0:257], start=True, stop=True),
              [r_ktok, r_vaug[0]], [rbU])
            A("dve", lambda e, h=h, bU=bU: e.tensor_scalar(out=Cm32[:, h, :], in0=bU[:, 0:257], scalar1=eB[:, h:h + 1], scalar2=None, op0=ALU.mult),
              [rbU, r_eB], [r_Cm32])
        A("pool", lambda e: e.memset(vaug[0][:], 0.0), [], [r_vaug[0]])

        def phase0_load(s_, ti_, j):
            t0_ = ti_ * T + j * 128
            A("sp", lambda e: e.dma_start(out=xs[j][:], in_=x[s_, t0_: t0_ + 128, :]), writes=[r_xs[j]], dma="xs%d" % j)

        def phase0_tabs(ti_):
            A("sp", lambda e: e.dma_start(out=cosT[:], in_=c_cos[ti_]), writes=[r_cos], dma="cos")
            A("sp", lambda e: e.dma_start(out=sinT[:], in_=c_sin[ti_]), writes=[r_sin], dma="sin")

        def phase0_norm(j):
            rmsnorm_T(xs[j][:], r_xs[j], 128, stt[j], r_stt[j], uT, r_uT, j * 128)

        def phase0_front(j):
            return rmsnorm_front(xs[j][:], r_xs[j], 128, stt[j], r_stt[j])

        def phase0_back(j, xbr):
            rmsnorm_back(xbr, 128, uT, r_uT, j * 128)

        wk_pref = [None]
        for s in range(NSEQ):
            A("pool", lambda e: e.tensor_copy(out=C32[:], in_=Cm32[:]), [r_Cm32], r_C32)
            A("pool", lambda e: e.tensor_copy(out=Cbf[:, :, 0:257], in_=Cm32[:]), [r_Cm32], r_Cbf)
            A("pool", lambda e: e.tensor_copy(out=halo[:], in_=halo_m[:]), [r_halom], [r_halo])
            for ti in range(NST):
                tok0 = ti * T
                if s == 0 and ti == 0:
                    phase0_tabs(ti)
                    for j in range(4):
                        phase0_load(s, ti, j)
                    for j in range(4):
                        phase0_norm(j)
                nxt = (s, ti + 1) if ti + 1 < NST else ((s + 1, 0) if s + 1 < NSEQ else None)
                wk = wk_pref[0] if wk_pref[0] is not None else load_w(WB["K"])
                wk_pref[0] = None
                for g in range(4):
                    b, rb = proj_fm(wk, g * 128, uT, r_uT, T)
                    flush()
                    rope(b, rb, T, cosT[:], sinT[:], [r_cos, r_sin], 1.0, Kbuf[:, g, 128:640], r_Kbuf)
                wv = load_w(WB["V"], 256)
                for j in range(4):
                    b, rb = proj_tm(wv, 256, uT, r_uT, j * 128, 128)
                    A("act", lambda e, b=b, j=j: e.activation(out=Vdup[j + 1][:, :, 0:64], in_=b[:, 0:256].rearrange("p (g d) -> p g d", g=4), func=AF.Copy),
                      [rb], [r_Vdup[j + 1]])
                QT, r_QT = bufA, r_bufA
                attT, r_attT = bufB, r_bufB
                for i in range(2):
                    w = load_w(WB["Q"] + i)
                    for c in range(4):
                        blk = 4 * i + c
                        b, rb = proj_fm(w, c * 128, uT, r_uT, T)
                        flush()
                        rope(b, rb, T, cosT[:], sinT[:], [r_cos, r_sin], 0.125, QT[:, blk, :], r_QT)
                flush()
                def att_S(j, g):
                    has_prev = not (ti == 0 and j == 0)
                    jc = slice(j * 128, (j + 1) * 128)
                    pi = rr("pt")
                    bSc, rSc = bank()
                    bSm, rSm = bank()
                    bSp, rSp = bank() if has_prev else (None, None)
                    for hh in range(4):
                        blk = 2 * g + hh // 2
                        rows = slice((hh % 2) * 64, (hh % 2) * 64 + 64)
                        hc = slice(hh * 128, (hh + 1) * 128)
                        A("pe", lambda e, rows=rows, hc=hc, blk=blk: e.matmul(
                            bSc[:, hc], lhsT=Kbuf[rows, g, 128 + j * 128: 256 + j * 128], rhs=QT[rows, blk, jc], start=True, stop=True),
                          [r_Kbuf, r_QT], [rSc])
                        A("pe", lambda e, rows=rows, hc=hc, blk=blk: e.matmul(
                            bSm[0:16, hc], lhsT=KmT[rows, g, :], rhs=QT[rows, blk, jc], start=True, stop=True),
                          [r_KmT, r_QT], [rSm])
                        if has_prev:
                            A("pe", lambda e, rows=rows, hc=hc, blk=blk: e.matmul(
                                bSp[:, hc], lhsT=Kbuf[rows, g, j * 128: 128 + j * 128], rhs=QT[rows, blk, jc], start=True, stop=True),
                              [r_Kbuf, r_QT], [rSp])
                    A("act", lambda e: e.activation(out=ptc[pi][:], in_=bSc[:], func=AF.Exp), [rSc], [r_ptc[pi]])
                    A("act", lambda e: e.activation(out=ptm[g][0:16, :], in_=bSm[0:16, :], func=AF.Exp), [rSm], [r_ptm[g]])
                    A("pool", lambda e: e.tensor_tensor(out=ptc[pi][:], in0=ptc[pi][:], in1=mask_b[:].rearrange("p h q -> p (h q)"), op=ALU.mult),
                      [r_ptc[pi], r_mask], [r_ptc[pi]])
                    if has_prev:
                        A("act", lambda e: e.activation(out=ptp[pi][:], in_=bSp[:], func=AF.Exp), [rSp], [r_ptp[pi]])
                        A("pool", lambda e: e.tensor_tensor(out=ptp[pi][:], in0=ptp[pi][:], in1=maskp_b[:].rearrange("p h q -> p (h q)"), op=ALU.mult),
                          [r_ptp[pi], r_maskp], [r_ptp[pi]])
                    return (j, g, pi, has_prev, jc)

                def att_PV(ctx):
                    j, g, pi, has_prev, jc = ctx
                    ai = j % 2
                    bO, rO = bank()
                    for hh in range(4):
                        hc = slice(hh * 128, (hh + 1) * 128)
                        oc = slice(hh * 65, hh * 65 + 65)
                        A("pe", lambda e, hc=hc, oc=oc: e.matmul(bO[:, oc], lhsT=ptm[g][:, hc], rhs=Vmeta[:, g, 0:65], start=True, stop=False),
                          [r_Vmeta, r_ptm[g]], [rO])
                        if has_prev:
                            A("pe", lambda e, hc=hc, oc=oc: e.matmul(bO[:, oc], lhsT=ptp[pi][:, hc], rhs=Vdup[j][:, g, 0:65], start=False, stop=False),
                              [r_Vdup[j], r_ptp[pi]], [rO])
                        A("pe", lambda e, hc=hc, oc=oc: e.matmul(bO[:, oc], lhsT=ptc[pi][:, hc], rhs=Vdup[j + 1][:, g, 0:65], start=False, stop=True),
                          [r_Vdup[j + 1], r_ptc[pi]], [rO])
                    ri = rr("rec4")
                    bOv = bO[:, 0:260].rearrange("p (h c) -> p h c", c=65)
                    A("dve", lambda e: e.reciprocal(out=rec4[ri][:], in_=bOv[:, :, 64]), [rO], [r_rec4[ri]])
                    A("dve", lambda e: e.tensor_tensor(out=att_tok[ai][:, g * 256:(g + 1) * 256].rearrange("p (h d) -> p h d", h=4), in0=bOv[:, :, 0:64],
                                                       in1=rec4[ri][:].unsqueeze(2).to_broadcast([128, 4, 64]), op=ALU.mult),
                      [rO, r_rec4[ri]], [r_att_tok[ai]])
                    if g == 3:
                        for k in range(8):
                            A("pe", lambda e, k=k: e.transpose(out=pst[:, k, :], in_=att_tok[ai][:, k * 128:(k + 1) * 128], identity=ident_b[:]),
                              [r_att_tok[ai], r_ident], [r_pst])
                        A("act", lambda e: e.activation(out=attT[:, :, jc], in_=pst[:], func=AF.Copy), [r_pst], [r_attT])

                order = [(j, g) for j in range(4) for g in range(4)]
                ctxs = [att_S(*order[0])]
                for idx in range(len(order)):
                    if idx + 1 < len(order):
                        ctxs.append(att_S(*order[idx + 1]))
                    att_PV(ctxs[idx])
                for i in range(2):
                    w = load_w(WB["Z"] + i)
                    for c in range(4):
                        blk = 4 * i + c
                        b, rb = proj_fm(w, c * 128, uT, r_uT, T)
                        k = rr("tz")
                        A("act", lambda e, b=b, k=k: e.activation(out=zst[k][:], in_=b[:], func=AF.Silu), [rb], [r_zst[k]])
                        A("pool", lambda e, k=k, blk=blk: e.tensor_tensor(out=attT[:, blk, :], in0=attT[:, blk, :], in1=zst[k][:], op=ALU.mult),
                          [r_attT, r_zst[k]], [r_attT])
                for i in range(2):
                    wg = load_w(WB["GA"] + i)
                    wa = load_w(WB["AO"] + i)
                    for c in range(4):
                        blk = 4 * i + c
                        b, rb = proj_fm(wg, c * 128, uT, r_uT, T)
                        k = rr("tz")
                        A("act", lambda e, b=b, k=k: e.activation(out=tzt[k][:], in_=b[:], func=AF.Tanh, scale=0.5), [rb], [r_tzt[k]])
                        by, rby = proj_fm(wa, c * 128, attT, r_attT, T)
                        A("dve", lambda e, by=by, k=k, blk=blk: e.scalar_tensor_tensor(out=mrg[:, blk, :], in0=tzt[k][:], scalar=1.0, in1=by[:],
                                                                                    op0=ALU.add, op1=ALU.mult), [r_tzt[k], rby], [r_mrg])
                if nxt is not None:
                    phase0_tabs(nxt[1])
                    for j in range(4):
                        phase0_load(nxt[0], nxt[1], j)
                bg, rbg = bank()
                for j in range(4):
                    for kc in range(8):
                        A("pe", lambda e, kc=kc, j=j, bg=bg: e.matmul(bg[:, j * 8:(j + 1) * 8], lhsT=uT[:, kc, j * 128:(j + 1) * 128], rhs=wgate[:, kc, :],
                                                                 start=(kc == 0), stop=(kc == 7)), [r_uT, r_wgate], [rbg])
                A("dve", lambda e, bg=bg: e.tensor_tensor(out=gsb[:], in0=bg[:, 0:32].rearrange("p (j c) -> p j c", j=4),
                                                        in1=gbias[:].unsqueeze(1).to_broadcast([128, 4, 8]), op=ALU.add), [rbg, r_gbias], [r_gsb])
                A("act", lambda e: e.activation(out=e1[:], in_=gsb[:, :, 4:8], func=AF.Exp, scale=-1.0), [r_gsb], [r_e1])
                A("act", lambda e: e.activation(out=nlf[:], in_=e1[:].rearrange("p j h -> p (j h)"), func=AF.Ln, bias=1.0), [r_e1], [r_nlf])
                bnb, rbnb = bank()
                A("pe", lambda e, bnb=bnb: e.matmul(bnb[:, 0:16], lhsT=triu_f[:], rhs=nlf[:], start=True, stop=True), [r_triu, r_nlf], [rbnb])
                bns, rbns = bank()
                A("pe", lambda e, bns=bns: e.matmul(bns[:, 0:16], lhsT=ones_f[:], rhs=nlf[:], start=True, stop=True), [r_onesf, r_nlf], [rbns])
                A("act", lambda e, bnb=bnb: e.activation(out=thr[:], in_=bnb[:, 0:16], func=AF.Exp), [rbnb], [r_thr])
                A("dve", lambda e, bnb=bnb: e.tensor_tensor(out=gs[:].rearrange("p (j h) -> p j h", j=4), in0=gsb[:, :, 0:4],
                                                          in1=bnb[:, 0:16].rearrange("p (j h) -> p j h", j=4), op=ALU.add), [r_gsb, rbnb], [r_gs])
                A("act", lambda e: e.activation(out=vsc[:], in_=gs[:], func=AF.Exp, bias=LOGK), [r_gs], [r_vsc])
                A("act", lambda e, bns=bns: e.activation(out=eB[:], in_=bns[:, 0:16], func=AF.Exp, scale=-1.0), [rbns], [r_eB])
                qkT, r_qkT = bufC, r_bufC
                for i in range(2):
                    w = load_w(WB["MQK"] + i)
                    for c in range(4):
                        blk = 4 * i + c
                        b, rb = proj_fm(w, c * 128, uT, r_uT, T)
                        conv_silu(b, rb, blk, T, halo, r_halo, halo, r_halo, qkT[:, blk, :], r_qkT)
                for i in range(2):
                    w = load_w(WB["MV"] + i)
                    for j in range(4):
                        b, rb = proj_tm(w, 512, uT, r_uT, j * 128, 128)
                        for hh in range(2):
                            h = 2 * i + hh
                            A("act", lambda e, b=b, hh=hh, h=h, j=j: e.activation(out=vaug[j][:, h, 0:256], in_=b[:, hh * 256:(hh + 1) * 256], func=AF.Copy,
                                                                             scale=vsc[:, j * 4 + h: j * 4 + h + 1]), [rb, r_vsc], [r_vaug[j]])
                for j in range(4):
                    A("pool", lambda e, j=j: e.tensor_copy(out=vaug[j][:, :, 256:257], in_=vsc[:, j * 4:(j + 1) * 4].unsqueeze(2)), [r_vsc], [r_vaug[j]])
                th = bufB[:].rearrange("p k t -> p (k t)").rearrange("p (j f) -> p j f", j=4)
                r_th = r_bufB
                zs = bufA[:].rearrange("p k t -> p (k t)").rearrange("p (j f) -> p j f", j=4)
                r_zs = r_bufA
                for i in range(2):
                    w = load_w(WB["MO"] + i)
                    for j in range(4):
                        b, rb = proj_tm(w, 512, uT, r_uT, j * 128, 128)
                        A("act", lambda e, b=b, j=j, i=i: e.activation(out=th[:, j, i * 512:(i + 1) * 512], in_=b[:], func=AF.Tanh, scale=0.5), [rb], [r_th])
                for i in range(2):
                    w = load_w(WB["MZ"] + i)
                    for j in range(4):
                        b, rb = proj_tm(w, 512, uT, r_uT, j * 128, 128)
                        k = rr("tz")
                        A("act", lambda e, b=b, k=k: e.activation(out=zst[k][:], in_=b[:], func=AF.Silu), [rb], [r_zst[k]])
                        A("pool", lambda e, k=k, j=j, i=i: e.tensor_tensor(out=zs[:, j, i * 512:(i + 1) * 512], in0=zst[k][:], in1=hnB[:, i * 512:(i + 1) * 512], op=ALU.mult),
                          [r_zst[k], r_hnB], [r_zs])
                tgm = [(raw[0][:], r_raw[0]), (raw[1][:], r_raw[1]), (ptc[0][:], r_ptc[0]), (ptc[1][:], r_ptc[1]),
                       (ptp[0][:], r_ptp[0]), (ptp[1][:], r_ptp[1]), (att_tok[0][:, 0:512], r_att_tok[0]), (att_tok[1][:, 0:512], r_att_tok[1])]
                wgm = [None, None]
                def rec_pe_a(j):
                    jc = slice(j * 128, (j + 1) * 128)
                    for h in range(4):
                        A("pe", lambda e, h=h: e.transpose(out=pst[:, h, :], in_=qkT[:, 4 + h, jc], identity=ident_b[:]), [r_qkT, r_ident], [r_pst])
                    A("act", lambda e: e.activation(out=ktok[:], in_=pst[:, 0:4, :], func=AF.Copy), [r_pst], [r_ktok])
                    bS, rS = bank()
                    for h in range(4):
                        A("pe", lambda e, h=h: e.matmul(bS[:, h * 128:(h + 1) * 128], lhsT=qkT[:, 4 + h, jc], rhs=qkT[:, h, jc], start=True, stop=True),
                          [r_qkT], [rS])
                    A("dve", lambda e: e.tensor_tensor(out=PTm[:], in0=bS[:], in1=mask_b[:].rearrange("p h q -> p (h q)"), op=ALU.mult), [rS, r_mask], [r_PTm])
                    return jc

                def rec_pe_b(j, jc):
                    bNs = [bank(), bank()]
                    bDn, rDn = bank()
                    for h in range(4):
                        bN, rN = bNs[h // 2]
                        nc_ = slice((h % 2) * 256, (h % 2) * 256 + 256)
                        A("pe", lambda e, h=h, bN=bN, nc_=nc_: e.matmul(bN[:, nc_], lhsT=PTm[:, h * 128:(h + 1) * 128], rhs=vaug[j][:, h, 0:256], start=True, stop=False),
                          [r_PTm, r_vaug[j]], [rN])
                        A("pe", lambda e, h=h, bN=bN, nc_=nc_: e.matmul(bN[:, nc_], lhsT=qkT[:, h, jc], rhs=Cbf[:, h, 0:256], start=False, stop=True),
                          [r_qkT, r_Cbf[h]], [rN])
                        A("pe", lambda e, h=h: e.matmul(bDn[:, h:h + 1], lhsT=PTm[:, h * 128:(h + 1) * 128], rhs=vaug[j][:, h, 256:257], start=True, stop=False),
                          [r_PTm, r_vaug[j]], [rDn])
                        A("pe", lambda e, h=h: e.matmul(bDn[:, h:h + 1], lhsT=qkT[:, h, jc], rhs=Cbf[:, h, 256:257], start=False, stop=True),
                          [r_qkT, r_Cbf[h]], [rDn])
                    bUs = []
                    for h in range(4):
                        bU, rbU = bank()
                        A("pe", lambda e, h=h, bU=bU: e.matmul(bU[:, 0:257], lhsT=ktok[:, h, :], rhs=vaug[j][:, h, 0:257], start=True, stop=True),
                          [r_ktok, r_vaug[j]], [rbU])
                        bUs.append((bU, rbU))
                    return (j, jc, bNs, bDn, rDn, bUs)

                def rec_state(ctx):
                    j, jc, bNs, bDn, rDn, bUs = ctx
                    for h in range(4):
                        bU, rbU = bUs[h]
                        col = j * 4 + h
                        A("dve", lambda e, h=h, col=col: e.tensor_scalar(out=C32[:, h, :], in0=C32[:, h, :], scalar1=eB[:, col:col + 1], scalar2=None, op0=ALU.mult),
                          [r_C32[h], r_eB], [r_C32[h]])
                        A("dve", lambda e, h=h, col=col, bU=bU: e.scalar_tensor_tensor(out=C32[:, h, :], in0=bU[:, 0:257], scalar=eB[:, col:col + 1], in1=C32[:, h, :],
                                                                                    op0=ALU.mult, op1=ALU.add), [rbU, r_eB, r_C32[h]], [r_C32[h]])
                        A("act", lambda e, h=h: e.activation(out=Cbf[:, h, 0:257], in_=C32[:, h, :], func=AF.Copy), [r_C32[h]], [r_Cbf[h]])

                def rec_den(ctx):
                    j, jc, bNs, bDn, rDn, bUs = ctx
                    c4 = slice(j * 4, (j + 1) * 4)
                    A("dve", lambda e: e.tensor_scalar(out=dd[:, 0, :], in0=bDn[:, 0:4], scalar1=-1.0, scalar2=None, op0=ALU.mult), [rDn], [r_dd[0]])
                    A("dve", lambda e: e.tensor_tensor(out=dd[:, 1, :], in0=bDn[:, 0:4], in1=dd[:, 0, :], op=ALU.max), [rDn, r_dd[0]], [r_dd[0]])
                    A("dve", lambda e: e.tensor_tensor(out=dd[:, 2, :], in0=dd[:, 1, :], in1=thr[:, c4], op=ALU.max), [r_dd[0], r_thr], [r_dd[0]])
                    A("dve", lambda e: e.reciprocal(out=dd[:, 3, :], in_=dd[:, 2, :]), [r_dd[0]], [r_dd[0]])
                    for h in range(4):
                        bN, rN = bNs[h // 2]
                        nc_ = slice((h % 2) * 256, (h % 2) * 256 + 256)
                        A("act", lambda e, h=h, bN=bN, nc_=nc_: e.activation(out=tmpN[:, h * 256:(h + 1) * 256], in_=bN[:, nc_], func=AF.Copy, scale=dd[:, 3, h:h + 1]),
                          [rN, r_dd[0]], [r_tmpN])

                def rec_rest(ctx):
                    j, jc, bNs, bDn, rDn, bUs = ctx
                    A("dve", lambda e: e.scalar_tensor_tensor(out=ho[:], in0=th[:, j, :], scalar=1.0, in1=tmpN[:], op0=ALU.add, op1=ALU.mult),
                      [r_th, r_tmpN], [r_ho])
                    for h in range(4):
                        A("dve", lambda e, h=h: e.bn_stats(out=st6[:, h, :], in_=ho[:, h * 256:(h + 1) * 256]), [r_ho], [r_st6])
                    for h in range(4):
                        A("dve", lambda e, h=h: e.bn_aggr(out=mv[:, h, :], in_=st6[:, h, :]), [r_st6], [r_mv])
                    A("pool", lambda e: e.tensor_scalar(out=lnv[:, 0:4], in0=mv[:, :, 1], scalar1=4e-6, scalar2=None, op0=ALU.add), [r_mv], [r_lnv])
                    A("pool", lambda e: e.tensor_tensor(out=lnv[:, 4:8], in0=lnv[:, 0:4], in1=mhalf[:, 0:4], op=ALU.pow), [r_lnv, r_mhalf], [r_lnv])
                    for h in range(4):
                        A("dve", lambda e, h=h: e.tensor_scalar(out=ho[:, h * 256:(h + 1) * 256], in0=ho[:, h * 256:(h + 1) * 256], scalar1=mv[:, h, 0:1],
                                                              scalar2=lnv[:, 4 + h:5 + h], op0=ALU.subtract, op1=ALU.mult), [r_ho, r_mv, r_lnv], [r_ho])
                    hi = rr("hz")
                    A("pool", lambda e: e.tensor_tensor(out=hzs[hi][:], in0=ho[:], in1=zs[:, j, :], op=ALU.mult), [r_ho, r_zs], [r_hzs[hi]])
                    return (jc, hi)

                def rec_tail(t):
                    jc, hi = t
                    for k in range(8):
                        A("pe", lambda e, k=k: e.transpose(out=pst[:, k, :], in_=hzs[hi][:, k * 128:(k + 1) * 128], identity=ident_b[:]), [r_hzs[hi], r_ident], [r_pst])
                    A("act", lambda e: e.activation(out=hzT[:, :, jc], in_=pst[:], func=AF.Copy), [r_pst], [r_hzT])

                tail = None
                jc_next = rec_pe_a(0)
                for j in range(4):
                    ctx = rec_pe_b(j, jc_next)
                    bctr[0] += 3
                    rec_den(ctx)
                    rec_state(ctx)
                    if j + 1 < 4:
                        jc_next = rec_pe_a(j + 1)
                    if tail is not None:
                        rec_tail(tail)
                    tail = rec_rest(ctx)
                    for blk in (2 * j, 2 * j + 1):
                        i_, c_ = divmod(blk, 4)
                        if c_ == 0:
                            wgm[i_] = load_w(WB["GM"] + i_)
                        b, rb = proj_fm(wgm[i_], c_ * 128, uT, r_uT, T)
                        tg_, r_tg_ = tgm[blk]
                        A("act", lambda e, b=b, tg_=tg_: e.activation(out=tg_, in_=b[:], func=AF.Tanh, scale=0.5), [rb], [r_tg_])
                rec_tail(tail)
                mT, r_mT = bufC, r_bufC
                for i in range(2):
                    wm = load_w(WB["MOUT"] + i)
                    for c in range(4):
                        blk = 4 * i + c
                        tg_, r_tg_ = tgm[blk]
                        k = rr("tz")
                        by, rby = proj_fm(wm, c * 128, hzT, r_hzT, T)
                        A("dve", lambda e, by=by, k=k, tg_=tg_: e.scalar_tensor_tensor(out=zst[k][:], in0=tg_, scalar=1.0, in1=by[:], op0=ALU.add, op1=ALU.mult),
                          [r_tg_, rby], [r_zst[k]])
                        A("pool", lambda e, k=k, blk=blk: e.tensor_tensor(out=mT[:, blk, :], in0=zst[k][:], in1=mrg[:, blk, :], op=ALU.add),
                          [r_zst[k], r_mrg], [r_mT])
                w0 = load_w(WB["OUT"])
                w1 = load_w(WB["OUT"] + 1)
                if nxt is not None:
                    wk_pref[0] = load_w(WB["K"])
                xr = [(tmpN, r_tmpN), (ho, r_ho)]

                def reload(j):
                    xt_, rx_ = xr[j % 2]
                    t0_ = tok0 + j * 128
                    A("sp", lambda e, s=s: e.dma_start(out=xt_[:], in_=x[s, t0_: t0_ + 128, :]), writes=[rx_], dma="xr%d" % (j % 2))

                reload(0)
                reload(1)
                fronts = {}
                if nxt is not None:
                    fronts[0] = phase0_front(0)
                    fronts[1] = phase0_front(1)
                for j in range(4):
                    oi = rr("ot")
                    b0, rb0 = proj_tm(w0, 512, mT, r_mT, j * 128, 128)
                    b1, rb1 = proj_tm(w1, 512, mT, r_mT, j * 128, 128)
                    A("act", lambda e, b0=b0, oi=oi: e.activation(out=hz[:, 0:512], in_=b0[:], func=AF.Square, accum_out=ss[oi][:, 0:1]), [rb0], [r_hz, r_ss[oi]])
                    A("act", lambda e, b1=b1, oi=oi: e.activation(out=hz[:, 512:1024], in_=b1[:], func=AF.Square, accum_out=ss[oi][:, 1:2]), [rb1], [r_hz, r_ss[oi]])
                    A("dve", lambda e, oi=oi: e.tensor_tensor(out=ss[oi][:, 2:3], in0=ss[oi][:, 0:1], in1=ss[oi][:, 1:2], op=ALU.add), [r_ss[oi]], [r_ss[oi]])
                    A("pool", lambda e, oi=oi: e.tensor_scalar(out=ss[oi][:, 3:4], in0=ss[oi][:, 2:3], scalar1=1024 * 4e-6, scalar2=None, op0=ALU.add), [r_ss[oi]], [r_ss[oi]])
                    A("pool", lambda e, oi=oi: e.tensor_tensor(out=ss[oi][:, 4:5], in0=ss[oi][:, 3:4], in1=mhalf[:, 0:1], op=ALU.pow), [r_ss[oi], r_mhalf], [r_ss[oi]])
                    A("dve", lambda e, b0=b0, oi=oi: e.scalar_tensor_tensor(out=ot[oi][:, 0:512], in0=b0[:], scalar=ss[oi][:, 4:5], in1=npB[:, 0:512],
                                                                          op0=ALU.mult, op1=ALU.mult), [rb0, r_ss[oi], r_npB], [r_ot[oi]])
                    A("dve", lambda e, b1=b1, oi=oi: e.scalar_tensor_tensor(out=ot[oi][:, 512:1024], in0=b1[:], scalar=ss[oi][:, 4:5], in1=npB[:, 512:1024],
                                                                          op0=ALU.mult, op1=ALU.mult), [rb1, r_ss[oi], r_npB], [r_ot[oi]])
                    xt_, rx_ = xr[j % 2]
                    A("pool", lambda e, oi=oi, xt_=xt_: e.tensor_tensor(out=ot[oi][:], in0=ot[oi][:], in1=xt_[:], op=ALU.add), [r_ot[oi], rx_], [r_ot[oi]])
                    A("sp", lambda e, oi=oi, j=j, s=s, tok0=tok0: e.dma_start(out=out[s, tok0 + j * 128: tok0 + (j + 1) * 128, :], in_=ot[oi][:]),
                      reads=[r_ot[oi]], writes=[new_out()], dma="out%d" % oi)
                    if j + 2 < 4:
                        reload(j + 2)
                    if nxt is not None and j < 2:
                        phase0_back(2 * j, fronts[2 * j])
                        phase0_back(2 * j + 1, fronts[2 * j + 1])
                        if j == 0:
                            fronts[2] = phase0_front(2)
                            fronts[3] = phase0_front(3)
                A("pool", lambda e: e.tensor_copy(out=Kbuf[:, :, 0:128], in_=Kbuf[:, :, 512:640]), [r_Kbuf], [r_Kbuf])
                A("pool", lambda e: e.tensor_copy(out=Vdup[0][:], in_=Vdup[4][:]), [r_Vdup[4]], [r_Vdup[0]])
        fin = P.res("fin")
        A("sp", lambda e: e.nop(), reads=r_outs, writes=[fin])
        P.emit()
    return nc


def _consts(NST):
    ident = np.eye(128, dtype=np.float32)
    triu = np.triu(np.ones((128, 128), dtype=np.float32))
    pit = np.zeros((128, 128), dtype=np.float32)
    for m in range(128):
        d = m % 64
        base = m - d
        if d < 8:
            k = base + d + 8
        elif d < 16:
            k = base + d - 8
        else:
            k = m
        pit[k, m] = 1.0
    half = 8
    inv_freq = (500000.0 ** (-np.arange(0, 16, 2, dtype=np.float32) / 16)).astype(np.float32)

    def tables(pos):
        pos = pos.astype(np.float32)
        ang = (pos[:, None] * inv_freq[None, :]).astype(np.float32)
        c = np.cos(ang.astype(np.float64)).astype(np.float32)
        s_ = np.sin(ang.astype(np.float64)).astype(np.float32)
        n = pos.shape[0]
        cosT = np.ones((128, n), dtype=np.float32)
        sinT = np.zeros((128, n), dtype=np.float32)
        for hb in (0, 64):
            cosT[hb:hb + 8] = c.T
            cosT[hb + 8:hb + 16] = c.T
            sinT[hb:hb + 8] = -s_.T
            sinT[hb + 8:hb + 16] = s_.T
        return cosT, sinT

    cos = np.zeros((NST, 128, T), dtype=np.float32)
    sin = np.zeros((NST, 128, T), dtype=np.float32)
    for ti in range(NST):
        cos[ti], sin[ti] = tables(16 + ti * T + np.arange(T))
    cosm, sinm = tables(np.arange(16))
    return dict(c_ident=ident, c_triu=triu, c_pit=pit, c_cos=cos, c_sin=sin, c_cosm=cosm, c_sinm=sinm)


_NC_CACHE = {}


def run(inputs, n_cores, nseq, nst):
    key = (nseq, nst)
    if key not in _NC_CACHE:
        _NC_CACHE[key] = build_nc(nseq, nst)
    nc = _NC_CACHE[key]
    cs = _consts(nst)
    f = lambda a: np.ascontiguousarray(np.asarray(a, dtype=np.float32))
    shared = {
        "meta_tokens": f(inputs["meta_tokens"]),
        "norm_pre": f(inputs["norm_pre"][0]),
        "w_in": f(inputs["w_in"][0]),
        "attn_sinks": f(inputs["attn_sinks"][0]),
        "conv_w": f(inputs["conv_w"][0]),
        "conv_b": f(inputs["conv_b"][0]),
        "mlstm_gate_bias": f(inputs["mlstm_gate_bias"][0]),
        "mlstm_head_norm": f(inputs["mlstm_head_norm"][0]),
        "w_attn_out": f(inputs["w_attn_out"][0]),
        "w_mlstm_out": f(inputs["w_mlstm_out"][0]),
        "w_out": f(inputs["w_out"][0]),
        "norm_post": f(inputs["norm_post"][0]),
    }
    shared.update(cs)
    xfull = np.asarray(inputs["x"], dtype=np.float32)
    in_maps = []
    for c in range(n_cores):
        m = dict(shared)
        m["x"] = np.ascontiguousarray(xfull[c * nseq:(c + 1) * nseq, :nst * T])
        in_maps.append(m)
    res = run_bass_kernel_spmd(nc, in_maps, core_ids=list(range(n_cores)))
    return np.concatenate([np.asarray(r["out"]) for r in res.results], axis=0)


def kernel(**inputs):
    return run(inputs, 8, 2, 8).astype(np.float32)
```
